# Optimizing a Trainium2 kernel written in Bass

```python
import jax, jax.numpy as jnp
from jax import lax
import numpy as np

D_MODEL = 1024
BATCH = 16
SEQ = 4096
DEPTH = 4

N_MIXERS = 4
HEAD_DIM = 64
FOX_HEADS = D_MODEL // HEAD_DIM
SB_HEADS = D_MODEL // HEAD_DIM
Q_BLOCK = 128
GM_CHUNK = 128
GM_WIDTH = D_MODEL
GM_GROUPS = 8
GM_GROUP_DIM = GM_WIDTH // GM_GROUPS
CONV_WIDTH = 31
FFN_HIDDEN = -(-8 * D_MODEL // (3 * 256)) * 256
DN_ALPHA = (2.0 * DEPTH) ** 0.25
DN_BETA = (8.0 * DEPTH) ** -0.25
N_GM = (DEPTH + 3) // N_MIXERS
N_FOX = (DEPTH + 2) // N_MIXERS
N_SB = (DEPTH + 1) // N_MIXERS
N_CV = DEPTH // N_MIXERS
LN_EPS = 1e-5
NEG_INF = -1e30

kernel_name = "hybrid_interleaved_gmlp_fox_stickbreak_conformer"


def _layer_norm(x, g, b):
    xf = x.astype(jnp.float32)
    mu = jnp.mean(xf, axis=-1, keepdims=True)
    xc = xf - mu
    var = jnp.mean(xc * xc, axis=-1, keepdims=True)
    return (xc * lax.rsqrt(var + LN_EPS) * g + b).astype(x.dtype)


def _gmlp_mixer(h, w_in, b_in, ln_g, ln_b, w_s, b_s, w_out):
    B, S, _ = h.shape
    z = jax.nn.gelu(h @ w_in + b_in, approximate=False)
    u, v = jnp.split(z, 2, axis=-1)
    v = _layer_norm(v, ln_g, ln_b)
    nc = S // GM_CHUNK
    v5 = v.reshape(B, nc, GM_CHUNK, GM_GROUPS, GM_GROUP_DIM)
    causal = jnp.tril(jnp.ones((GM_CHUNK, GM_CHUNK), dtype=w_s.dtype))
    w_m = w_s * causal
    sv = jnp.einsum('gts,bnsgc->bntgc', w_m, v5) + b_s.T[None, None, :, :, None]
    y = u * sv.reshape(B, S, GM_WIDTH)
    return y @ w_out


def _fox_mixer(h, w_in, b_f, w_out):
    B, S, D = h.shape
    H = FOX_HEADS
    scale = HEAD_DIM ** -0.5
    proj = h @ w_in
    q, k, v, f_logit = jnp.split(proj, [D, 2 * D, 3 * D], axis=-1)
    to_heads = lambda t: t.reshape(B, S, H, HEAD_DIM).transpose(0, 2, 1, 3)
    q, k, v = to_heads(q), to_heads(k), to_heads(v)
    log_f = jax.nn.log_sigmoid((f_logit + b_f).astype(jnp.float32))
    F = jnp.cumsum(log_f, axis=1).transpose(0, 2, 1)
    nb = S // Q_BLOCK
    kpos = jnp.arange(S)
    qb = q.reshape(B, H, nb, Q_BLOCK, HEAD_DIM).transpose(2, 0, 1, 3, 4)
    Fb = F.reshape(B, H, nb, Q_BLOCK).transpose(2, 0, 1, 3)
    pb = kpos.reshape(nb, Q_BLOCK)

    def block(args):
        q_blk, F_blk, q_pos = args
        s = jnp.einsum('bhqd,bhkd->bhqk', q_blk, k).astype(jnp.float32) * scale
        s = s + F_blk[..., None] - F[:, :, None, :]
        s = jnp.where(kpos[None, :] <= q_pos[:, None], s, NEG_INF)
        p = jax.nn.softmax(s, axis=-1).astype(v.dtype)
        return jnp.einsum('bhqk,bhkd->bhqd', p, v)

    o = lax.map(block, (qb, Fb, pb))
    o = o.transpose(1, 0, 3, 2, 4).reshape(B, S, D)
    return o @ w_out


def _stick_breaking_mixer(h, w_in, w_out):
    B, S, D = h.shape
    H = SB_HEADS
    scale = HEAD_DIM ** -0.5
    q, k, v = jnp.split(h @ w_in, 3, axis=-1)
    to_heads = lambda t: t.reshape(B, S, H, HEAD_DIM).transpose(0, 2, 1, 3)
    q, k, v = to_heads(q), to_heads(k), to_heads(v)
    nb = S // Q_BLOCK
    kpos = jnp.arange(S)
    qb = q.reshape(B, H, nb, Q_BLOCK, HEAD_DIM).transpose(2, 0, 1, 3, 4)
    pb = kpos.reshape(nb, Q_BLOCK)

    def block(args):
        q_blk, q_pos = args
        z = jnp.einsum('bhqd,bhkd->bhqk', q_blk, k).astype(jnp.float32) * scale
        mask = kpos[None, :] < q_pos[:, None]
        log_beta = jax.nn.log_sigmoid(z)
        log_1m = jnp.where(mask, jax.nn.log_sigmoid(-z), 0.0)
        rest = lax.cumsum(log_1m, axis=3, reverse=True) - log_1m
        a = jnp.where(mask, jnp.exp(log_beta + rest), 0.0).astype(v.dtype)
        return jnp.einsum('bhqk,bhkd->bhqd', a, v)

    o = lax.map(block, (qb, pb))
    o = o.transpose(1, 0, 3, 2, 4).reshape(B, S, D)
    return o @ w_out


def _conformer_conv_mixer(h, w_in, b_in, dw, dw_b, ln_g, ln_b, w_out, b_out):
    D = h.shape[-1]
    a, g = jnp.split(h @ w_in + b_in, 2, axis=-1)
    y = a * jax.nn.sigmoid(g)
    y = lax.conv_general_dilated(
        y, dw[:, None, :], window_strides=(1,), padding=[(CONV_WIDTH - 1, 0)],
        dimension_numbers=('NWC', 'WIO', 'NWC'), feature_group_count=D) + dw_b
    y = jax.nn.silu(_layer_norm(y, ln_g, ln_b))
    return y @ w_out + b_out


def _swiglu(h, w_in, w_out):
    g, u = jnp.split(h @ w_in, 2, axis=-1)
    return (jax.nn.silu(g) * u) @ w_out


def setup_inputs(seed: int = 0) -> dict:
    key = jax.random.key(seed)
    ks = iter(jax.random.split(key, 40))
    D = D_MODEL
    f32 = jnp.float32
    nrm = lambda shape, s: jax.random.normal(next(ks), shape, f32) * s
    gain = lambda shape: 1.0 + nrm(shape, 0.02)
    return {
        "x": nrm((BATCH, SEQ, D), 1.0),
        "c": nrm((BATCH, D), 1.0),
        "mod_w": nrm((DEPTH, D, 6 * D), 0.1 * D ** -0.5),
        "mod_b": nrm((DEPTH, 6 * D), 0.01),
        "ln1_g": gain((DEPTH, D)),
        "ln1_b": nrm((DEPTH, D), 0.02),
        "ln2_g": gain((DEPTH, D)),
        "ln2_b": nrm((DEPTH, D), 0.02),
        "ffn_w_in": nrm((DEPTH, D, 2 * FFN_HIDDEN), D ** -0.5),
        "ffn_w_out": nrm((DEPTH, FFN_HIDDEN, D), FFN_HIDDEN ** -0.5 * DN_BETA),
        "gm_w_in": nrm((N_GM, D, 2 * GM_WIDTH), D ** -0.5),
        "gm_b_in": nrm((N_GM, 2 * GM_WIDTH), 0.02),
        "gm_ln_g": gain((N_GM, GM_WIDTH)),
        "gm_ln_b": nrm((N_GM, GM_WIDTH), 0.02),
        "gm_w_s": nrm((N_GM, GM_GROUPS, GM_CHUNK, GM_CHUNK), 0.5 * GM_CHUNK ** -0.5),
        "gm_b_s": gain((N_GM, GM_GROUPS, GM_CHUNK)),
        "gm_w_out": nrm((N_GM, GM_WIDTH, D), GM_WIDTH ** -0.5 * DN_BETA),
        "fox_w_in": nrm((N_FOX, D, 3 * D + FOX_HEADS), D ** -0.5),
        "fox_b_f": jax.random.uniform(next(ks), (N_FOX, FOX_HEADS), f32, 1.0, 4.0),
        "fox_w_out": nrm((N_FOX, D, D), D ** -0.5 * DN_BETA),
        "sb_w_in": nrm((N_SB, D, 3 * D), D ** -0.5),
        "sb_w_out": nrm((N_SB, D, D), D ** -0.5 * DN_BETA),
        "cv_w_in": nrm((N_CV, D, 2 * D), D ** -0.5),
        "cv_b_in": nrm((N_CV, 2 * D), 0.02),
        "cv_dw": nrm((N_CV, CONV_WIDTH, D), CONV_WIDTH ** -0.5),
        "cv_dw_b": nrm((N_CV, D), 0.02),
        "cv_ln_g": gain((N_CV, D)),
        "cv_ln_b": nrm((N_CV, D), 0.02),
        "cv_w_out": nrm((N_CV, D, D), D ** -0.5 * DN_BETA),
        "cv_b_out": nrm((N_CV, D), 0.02),
    }


def reference(x, c, mod_w, mod_b, ln1_g, ln1_b, ln2_g, ln2_b, ffn_w_in, ffn_w_out,
              gm_w_in, gm_b_in, gm_ln_g, gm_ln_b, gm_w_s, gm_b_s, gm_w_out,
              fox_w_in, fox_b_f, fox_w_out,
              sb_w_in, sb_w_out,
              cv_w_in, cv_b_in, cv_dw, cv_dw_b, cv_ln_g, cv_ln_b, cv_w_out, cv_b_out):
    c_act = jax.nn.silu(c)
    for l in range(DEPTH):
        m, j = l % N_MIXERS, l // N_MIXERS
        mod = c_act @ mod_w[l] + mod_b[l]
        sh1, sc1, g1, sh2, sc2, g2 = [t[:, None, :] for t in jnp.split(mod, 6, axis=-1)]
        h = x * (1.0 + sc1) + sh1
        if m == 0:
            y = _gmlp_mixer(h, gm_w_in[j], gm_b_in[j], gm_ln_g[j], gm_ln_b[j],
                            gm_w_s[j], gm_b_s[j], gm_w_out[j])
        elif m == 1:
            y = _fox_mixer(h, fox_w_in[j], fox_b_f[j], fox_w_out[j])
        elif m == 2:
            y = _stick_breaking_mixer(h, sb_w_in[j], sb_w_out[j])
        else:
            y = _conformer_conv_mixer(h, cv_w_in[j], cv_b_in[j], cv_dw[j], cv_dw_b[j],
                                      cv_ln_g[j], cv_ln_b[j], cv_w_out[j], cv_b_out[j])
        x = _layer_norm(DN_ALPHA * x + (1.0 + g1) * y, ln1_g[l], ln1_b[l])
        h = x * (1.0 + sc2) + sh2
        y = _swiglu(h, ffn_w_in[l], ffn_w_out[l])
        x = _layer_norm(DN_ALPHA * x + (1.0 + g2) * y, ln2_g[l], ln2_b[l])
    return x
```

```python
import numpy as np
from contextlib import ExitStack
import concourse.bass as bass
import concourse.mybir as mybir
from concourse.bass_utils import run_bass_kernel_spmd

F32 = mybir.dt.float32
BF16 = mybir.dt.bfloat16
AF = mybir.ActivationFunctionType
ALU = mybir.AluOpType

D = 1024
FH = 2816
NH = 16
DH = 64
DEPTH = 4
ALPHA = float((2.0 * DEPTH) ** 0.25)
EPS = 1e-5
CONVW = 31

ENGS = ["pe", "act", "dve", "pool", "sp"]

WEIGHT_SHAPES = {
    "mod_w": (4, 1024, 6144), "mod_b": (4, 6144), "ln1_g": (4, 1024), "ln1_b": (4, 1024),
    "ln2_g": (4, 1024), "ln2_b": (4, 1024), "ffn_w_in": (4, 1024, 5632), "ffn_w_out": (4, 2816, 1024),
    "gm_w_in": (1, 1024, 2048), "gm_b_in": (1, 2048), "gm_ln_g": (1, 1024), "gm_ln_b": (1, 1024),
    "gm_w_s": (1, 8, 128, 128), "gm_b_s": (1, 8, 128), "gm_w_out": (1, 1024, 1024),
    "fox_w_in": (1, 1024, 3088), "fox_b_f": (1, 16), "fox_w_out": (1, 1024, 1024),
    "sb_w_in": (1, 1024, 3072), "sb_w_out": (1, 1024, 1024),
    "cv_w_in": (1, 1024, 2048), "cv_b_in": (1, 2048), "cv_dw": (1, 31, 1024), "cv_dw_b": (1, 1024),
    "cv_ln_g": (1, 1024), "cv_ln_b": (1, 1024), "cv_w_out": (1, 1024, 1024), "cv_b_out": (1, 1024),
}


class Sched:
    def __init__(self, nc, same_engine_raw=True):
        self.nc = nc
        self.ops = {e: [] for e in ENGS}
        self.last_w = {}
        self.readers = {}
        self.seen = {e: {} for e in ENGS}
        self.dma_cnt = []
        self.keymap = {}
        self.same_engine_raw = same_engine_raw

    def _add_wait(self, eng, waits, ev, is_raw):
        if ev is None:
            return
        if ev[0] == "eng":
            _, e2, idx = ev
            if e2 == eng and (eng == "pe" or not self.same_engine_raw):
                return
            key, val = ("eng", e2), idx
        else:
            key, val = ("dma", ev[1]), ev[2]
        if self.seen[eng].get(key, -1) >= val:
            return
        if val > waits.get(key, -1):
            waits[key] = val

    def _deps(self, eng, reads, writes):
        waits = {}
        for r in reads:
            self._add_wait(eng, waits, self.last_w.get(r), True)
        for w in writes:
            self._add_wait(eng, waits, self.last_w.get(w), False)
            for ev in self.readers.get(w, ()):
                self._add_wait(eng, waits, ev, False)
        for k, v in waits.items():
            self.seen[eng][k] = v
        return waits

    def _commit(self, ev, reads, writes):
        for r in reads:
            self.readers.setdefault(r, []).append(ev)
        for w in writes:
            self.last_w[w] = ev
            self.readers[w] = []

    def op(self, eng, fn, reads=(), writes=()):
        waits = self._deps(eng, reads, writes)
        idx = len(self.ops[eng])
        self.ops[eng].append(dict(kind="op", fn=fn, waits=waits, needed=False))
        self._commit(("eng", eng, idx), reads, writes)

    def dma(self, eng, key, out, in_, reads=(), writes=(), **kw):
        waits = self._deps(eng, reads, writes)
        if key not in self.keymap:
            self.keymap[key] = len(self.keymap)
            if len(self.dma_cnt) < len(self.keymap):
                self.dma_cnt.append(0)
        ph = self.keymap[key]
        self.dma_cnt[ph] += 1
        self.ops[eng].append(dict(kind="dma", out=out, in_=in_, kw=kw, key=ph, waits=waits))
        self._commit(("dma", ph, 16 * self.dma_cnt[ph]), reads, writes)

    def _wait_everything(self, eng):
        waits = {}
        for ev in self.last_w.values():
            self._add_wait(eng, waits, ev, False)
        for evs in self.readers.values():
            for ev in evs:
                self._add_wait(eng, waits, ev, False)
        for k, v in waits.items():
            self.seen[eng][k] = v
        self.ops[eng].append(dict(kind="waitonly", waits=waits))

    def barrier(self):
        for e in ENGS:
            self._wait_everything(e)
        self.last_w = {}
        self.readers = {}
        self.keymap = {}

    def emit(self, stack):
        nc = self.nc
        for e in ENGS:
            for o in self.ops[e]:
                for k, v in o["waits"].items():
                    if k[0] == "eng":
                        self.ops[k[1]][v]["needed"] = True
        semval = {}
        for e in ENGS:
            c = 0
            for i, o in enumerate(self.ops[e]):
                if o["kind"] == "op" and o["needed"]:
                    c += 1
                    semval[(e, i)] = c
        esem = {e: stack.enter_context(nc.semaphore("s_" + e)) for e in ENGS}
        dsem = [stack.enter_context(nc.semaphore("d%d" % i)) for i in range(len(self.dma_cnt))]
        print("[sched] ops:", {e: len(self.ops[e]) for e in ENGS}, "sem max:", {e: max([0] + [v for (ee, _), v in semval.items() if ee == e]) for e in ENGS},
              "dma sems:", len(self.dma_cnt), "max dma cnt:", max(self.dma_cnt) * 16)
        block = stack.enter_context(nc.Block())

        def run(e, eng):
            for o in self.ops[e]:
                for k, v in o["waits"].items():
                    if k[0] == "eng":
                        eng.wait_ge(esem[k[1]], semval[(k[1], v)])
                    else:
                        eng.wait_ge(dsem[k[1]], v)
                if o["kind"] == "op":
                    ins = o["fn"](eng)
                    if o["needed"]:
                        ins.then_inc(esem[e], 1)
                elif o["kind"] == "dma":
                    eng.dma_start(out=o["out"], in_=o["in_"], **o["kw"]).then_inc(dsem[o["key"]], 16)

        @block.tensor
        def _(eng):
            run("pe", eng)

        @block.scalar
        def _(eng):
            run("act", eng)

        @block.vector
        def _(eng):
            run("dve", eng)

        @block.gpsimd
        def _(eng):
            run("pool", eng)

        @block.sync
        def _(eng):
            run("sp", eng)


class Ring:
    def __init__(self, items):
        self.items = list(items)
        self.i = 0

    def next(self):
        it = self.items[self.i % len(self.items)]
        self.i += 1
        return it


class Builder:
    def __init__(self, nseq, T, plan, same_engine_raw=True):
        self.nseq, self.T, self.plan = nseq, T, plan
        self.nc = nc = bass.Bass("TRN2", target_bir_lowering=False)
        self.S = Sched(nc, same_engine_raw)
        self._n = 0
        self.din = {}
        self.din["x"] = nc.dram_tensor("x", [nseq, T, D], F32, kind="ExternalInput").ap()
        self.din["c"] = nc.dram_tensor("c", [nseq, D], F32, kind="ExternalInput").ap()
        for k, shp in WEIGHT_SHAPES.items():
            self.din[k] = nc.dram_tensor(k, list(shp), F32, kind="ExternalInput").ap()
        self.dout = nc.dram_tensor("out", [nseq, T, D], F32, kind="ExternalOutput").ap()
        self.xa = nc.dram_tensor("xa_s", [nseq, T, D], F32).ap()
        self.xb = nc.dram_tensor("xb_s", [nseq, T, D], F32).ap()
        self.modrow = nc.dram_tensor("modrow_s", [DEPTH, nseq, 6 * D], F32).ap()
        self.qt_d = nc.dram_tensor("qt_s", [nseq, NH, DH, T], BF16).ap()
        self.kt_d = nc.dram_tensor("kt_s", [nseq, NH, DH, T], BF16).ap()
        self.v_d = nc.dram_tensor("v_s", [nseq, T, D], BF16).ap()
        self.fq_d = nc.dram_tensor("fq_s", [nseq, NH, 3, T], BF16).ap()
        self.fk_d = nc.dram_tensor("fk_s", [nseq, NH, 3, T], BF16).ap()
        self.o_d = nc.dram_tensor("o_s", [nseq, NH, DH, T], F32).ap()
        self.den_d = nc.dram_tensor("den_s", [nseq, NH, T], F32).ap()

    def sb(self, ph, name, shape, dt):
        self._n += 1
        return ph.enter_context(self.nc.sbuf_tensor("%s_%d" % (name, self._n), list(shape), dt))[:]

    def mm(self, out, lhsT, rhs, start, stop, reads, writes):
        self.S.op("pe", lambda e: e.matmul(out, lhsT, rhs, start=start, stop=stop), reads, writes)

    def tr(self, out, in_, ident, reads, writes):
        self.S.op("pe", lambda e: e.transpose(out, in_, ident), reads, writes)

    def act(self, out, in_, func, reads, writes, bias=None, scale=None, eng="act"):
        kw = {}
        if bias is not None:
            kw["bias"] = bias
        if scale is not None:
            kw["scale"] = scale
        self.S.op(eng, lambda e: e.activation(out, in_, func, **kw), reads, writes)

    def tt(self, eng, out, in0, in1, op, reads, writes):
        self.S.op(eng, lambda e: e.tensor_tensor(out, in0, in1, op), reads, writes)

    def ts(self, eng, out, in0, s1, s2, op0, op1, reads, writes):
        if op1 is None:
            self.S.op(eng, lambda e: e.tensor_scalar(out, in0, s1, None, op0), reads, writes)
        else:
            self.S.op(eng, lambda e: e.tensor_scalar(out, in0, s1, s2, op0, op1), reads, writes)

    def stt(self, out, in0, scalar, in1, op0, op1, reads, writes):
        self.S.op("dve", lambda e: e.scalar_tensor_tensor(out, in0, scalar, in1, op0, op1), reads, writes)

    def cp(self, eng, out, in_, reads, writes):
        if eng == "act":
            self.S.op("act", lambda e: e.copy(out, in_), reads, writes)
        else:
            self.S.op(eng, lambda e: e.tensor_copy(out, in_), reads, writes)

    def memset(self, eng, ap, val, writes):
        self.S.op(eng, lambda e: e.memset(ap, val), (), writes)

    def load_w_bf16(self, dst, src, key, res, maxcols=2048):
        n = dst.shape[-1]
        c0 = 0
        while c0 < n:
            c1 = min(n, c0 + maxcols)
            self.S.dma("pool", key, dst[:, c0:c1], src[:, c0:c1], writes=[res])
            c0 = c1

    def build(self):
        nc, S = self.nc, self.S
        with ExitStack() as st:
            self.ident = st.enter_context(nc.sbuf_tensor("ident", [128, 128], F32))[:]
            self.identb = st.enter_context(nc.sbuf_tensor("identb", [128, 128], BF16))[:]
            self.modT = st.enter_context(nc.sbuf_tensor("modT", [128, DEPTH, self.nseq, 48], F32))[:]
            self.mhalf = st.enter_context(nc.sbuf_tensor("mhalf", [128, 1], F32))[:]
            self.pd = []
            self.pb = []
            for i in range(4):
                t = st.enter_context(nc.psum_tensor("pd%d" % i, [128, 1024], F32))[:]
                self.pd.append((t, "pb%d" % (2 * i)))
                self.pb += [(t[:, 0:512], "pb%d" % (2 * i)), (t[:, 512:1024], "pb%d" % (2 * i + 1))]
            self.py = self.pd[2]
            self.py1 = self.pd[3]
            self.memset("pool", self.ident, 1.0, ["ident"])
            S.op("pool", lambda e: e.affine_select(self.ident, self.ident, [[1, 128]], ALU.is_equal, 0.0,
                                                    base=0, channel_multiplier=-1), ["ident"], ["ident"])
            self.cp("dve", self.identb, self.ident, ["ident"], ["identb"])
            self.memset("pool", self.mhalf, -0.5, ["mhalf"])
            cur = self.din["x"]
            nxt = [self.xa, self.xb]
            nph = len([p for p in self.plan if p[0] != "mod"])
            k = 0
            for p in self.plan:
                if p[0] == "mod":
                    self.phase_mod()
                else:
                    k += 1
                    dst = self.dout if k == nph else nxt[k % 2]
                    if p[0] == "ffn":
                        self.phase_ffn(p[1], cur, dst)
                    elif p[0] == "gmlp":
                        self.phase_gmlp(p[1], cur, dst)
                    elif p[0] == "conv":
                        self.phase_conv(p[1], cur, dst)
                    elif p[0] == "attn":
                        self.phase_attn(p[1], p[2], cur, dst)
                    cur = dst
                S.barrier()
            S.emit(st)
        return nc

    def phase_mod(self):
        nc, S, ns = self.nc, self.S, self.nseq
        with ExitStack() as ph:
            cs = self.sb(ph, "cs", [ns, D], F32)
            ca = self.sb(ph, "ca", [ns, D], F32)
            caT = self.sb(ph, "caT", [128, 8, ns], F32)
            CB = self.sb(ph, "CB", [128, ns, 8, 128], F32)
            wst = [self.sb(ph, "wst%d" % i, [128, 8, 512], F32) for i in range(2)]
            mbb = [self.sb(ph, "mbb%d" % i, [128, 512], F32) for i in range(2)]
            mrow = [self.sb(ph, "mrow%d" % i, [128, 512], F32) for i in range(2)]
            stg = [self.sb(ph, "stg%d" % i, [48, 128], F32) for i in range(2)]
            S.dma("sp", "cs", cs, self.din["c"][:, :], writes=["cs"])
            self.act(ca, cs, AF.Silu, ["cs"], ["ca"])
            pbt, pbr = self.pb[0]
            for kc in range(8):
                self.tr(pbt[:, kc * ns:(kc + 1) * ns], ca[:, kc * 128:(kc + 1) * 128], self.ident[0:ns, 0:ns],
                        ["ca", "ident"], [pbr])
            self.cp("dve", caT, pbt[:, 0:8 * ns].rearrange("p (k b) -> p k b", b=ns), [pbr], ["caT"])
            for b in range(ns):
                self.cp("dve", CB[:, b], caT[:, :, b:b + 1].to_broadcast([128, 8, 128]), ["caT"], ["CB"])
            it = 0
            mi = 0
            for l in range(DEPTH):
                for blk in range(12):
                    sl = it % 2
                    it += 1
                    cols = slice(blk * 512, (blk + 1) * 512)
                    S.dma("sp", "wst%d" % sl, wst[sl],
                          self.din["mod_w"][l, :, cols].rearrange("(k p) f -> p k f", p=128), writes=["wst%d" % sl])
                    S.dma("sp", "mbb%d" % sl, mbb[sl], self.din["mod_b"][l, cols].partition_broadcast(128),
                          writes=["mbb%d" % sl])
                    for b in range(ns):
                        pt, pr = self.pb[1 + (mi % 2)]
                        ms = mi % 2
                        mi += 1
                        for kc in range(8):
                            self.mm(pt, CB[:, b, kc, :], wst[sl][:, kc, :], kc == 0, kc == 7,
                                    ["CB", "wst%d" % sl], [pr])
                        self.tt("dve", mrow[ms], pt, mbb[sl], ALU.add, [pr, "mbb%d" % sl], ["mrow%d" % ms])
                        S.dma("sp", "mrow%d" % ms, self.modrow[l, b:b + 1, cols], mrow[ms][0:1, :],
                              reads=["mrow%d" % ms], writes=["modrow_%d_%d_%d" % (l, b, blk)])
            si = 0
            for l in range(DEPTH):
                for b in range(ns):
                    sl = si % 2
                    si += 1
                    S.dma("sp", "stg%d" % sl, stg[sl], self.modrow[l, b, :].rearrange("(c p) -> c p", p=128),
                          reads=["modrow_%d_%d_%d" % (l, b, blk) for blk in range(12)], writes=["stg%d" % sl])
                    pt, pr = self.pb[3 + sl]
                    self.tr(pt[:, 0:48], stg[sl], self.ident[0:48, 0:48], ["stg%d" % sl, "ident"], [pr])
                    self.cp("dve", self.modT[:, l, b, :], pt[:, 0:48], [pr], ["modT"])
            for c0 in (8, 32):
                self.ts("dve", self.modT[:, :, :, c0:c0 + 8], self.modT[:, :, :, c0:c0 + 8], 1.0, None, ALU.add, None,
                        ["modT"], ["modT"])

    def load_bcast(self, ph, name, src_row):
        n = src_row.shape[-1]
        t = self.sb(ph, name, [128, n], F32)
        self._n += 1
        res = "%s_%d" % (name, self._n)
        self.S.dma("sp", res, t, src_row.partition_broadcast(128), writes=[res])
        return t, res

    def load_pp(self, ph, name, src2d, pbi=0):
        n = src2d.shape[0]
        stg = self.sb(ph, name + "s", [n, 128], F32)
        t = self.sb(ph, name, [128, n], F32)
        self._n += 1
        res = "%s_%d" % (name, self._n)
        self.S.dma("sp", res + "s", stg, src2d, writes=[res + "s"])
        pt, pr = self.pb[pbi]
        self.tr(pt[:, 0:n], stg, self.ident[0:n, 0:n], [res + "s", "ident"], [pr])
        self.cp("dve", t, pt[:, 0:n], [pr], [res])
        return t, res

    def epi_setup(self, ph, l, which, single_gb=False):
        gcol = 2048 if which == 0 else 5120
        GB = []
        if single_gb:
            t = self.sb(ph, "GBs", [128, D], F32)
            GB = [(t, "GBs")] * self.nseq
        else:
            for b in range(self.nseq):
                t, r = self.load_bcast(ph, "GB%d" % b, self.modrow[l, b, gcol:gcol + D])
                self.ts("dve", t, t, 1.0, None, ALU.add, None, [r], [r])
                GB.append((t, r))
        lng = self.load_bcast(ph, "lng", self.din["ln1_g" if which == 0 else "ln2_g"][l, :])
        lnb = self.load_bcast(ph, "lnb", self.din["ln1_b" if which == 0 else "ln2_b"][l, :])
        eb = []
        for i in range(2):
            eb.append(dict(
                buf=self.sb(ph, "ebuf%d" % i, [128, D], F32), st=self.sb(ph, "est%d" % i, [128, 12], F32),
                mv=self.sb(ph, "emv%d" % i, [128, 2], F32), sm=self.sb(ph, "esm%d" % i, [128, 4], F32),
                res="ebuf%d_%d" % (i, self._n)))
        return dict(GB=GB, lng=lng, lnb=lnb, eb=eb, i=0, single=single_gb, cur=None, l=l, gcol=gcol)

    def epi_select(self, E, b):
        if not E["single"] or E["cur"] == b:
            return
        E["cur"] = b
        t, r = E["GB"][b]
        self.S.dma("sp", r, t, self.modrow[E["l"], b, E["gcol"]:E["gcol"] + D].partition_broadcast(128), writes=[r])
        self.ts("dve", t, t, 1.0, None, ALU.add, None, [r], [r])

    def epilogue(self, E, b, y, yres, x, xres, out_rows, ybias=None):
        S = self.S
        e = E["eb"][E["i"] % 2]
        E["i"] += 1
        buf, r = e["buf"], e["res"]
        GBt, GBr = E["GB"][b]
        if ybias is not None:
            self.tt("dve", buf, y, ybias[0], ALU.add, [yres, ybias[1]], [r])
            self.tt("dve", buf, buf, GBt, ALU.mult, [r, GBr], [r])
        else:
            self.tt("dve", buf, y, GBt, ALU.mult, [yres, GBr], [r])
        self.stt(buf, x, ALPHA, buf, ALU.mult, ALU.add, [xres, r], [r])
        self.ln_rows(buf, r, e)
        self.tt("dve", buf, buf, E["lng"][0], ALU.mult, [r, E["lng"][1]], [r])
        self.tt("pool", buf, buf, E["lnb"][0], ALU.add, [r, E["lnb"][1]], [r])
        S.dma("sp", r, out_rows, buf, reads=[r])

    def ln_rows(self, buf, r, e, out=None, eng_norm="act"):
        S = self.S
        st, mv, sm = e["st"], e["mv"], e["sm"]
        rs = r + "s"
        S.op("dve", lambda en: en.bn_stats(st[:, 0:6], buf[:, 0:512]), [r], [rs])
        S.op("dve", lambda en: en.bn_stats(st[:, 6:12], buf[:, 512:1024]), [r], [rs])
        S.op("dve", lambda en: en.bn_aggr(mv, st), [rs], [rs])
        self.ts("dve", sm[:, 0:1], mv[:, 1:2], EPS, None, ALU.add, None, [rs], [rs])
        self.tt("pool", sm[:, 1:2], sm[:, 0:1], self.mhalf, ALU.pow, [rs, "mhalf"], [rs])
        self.ts("dve", sm[:, 2:3], mv[:, 0:1], sm[:, 1:2], -1.0, ALU.mult, ALU.mult, [rs], [rs])
        o = buf if out is None else out[0]
        wr = [r] if out is None else [out[1]]
        self.act(o, buf, AF.Identity, [r, rs], wr, bias=sm[:, 2:3], scale=sm[:, 1:2])

    def xT_mod(self, xt, xres, nsub, hT, hres, l, b, which, tpr):
        sh0 = 0 if which == 0 else 24
        sc0 = 8 if which == 0 else 32
        for kc in range(8):
            pt, pr = tpr.next()
            for s in range(nsub):
                self.tr(pt[:, s * 128:(s + 1) * 128], xt[:, s, kc * 128:(kc + 1) * 128], self.ident,
                        [xres, "ident"], [pr])
            self.act(hT[:, kc, 0:nsub * 128], pt[:, 0:nsub * 128], AF.Identity, [pr, "modT"], [hres],
                     bias=self.modT[:, l, b, sh0 + kc:sh0 + kc + 1], scale=self.modT[:, l, b, sc0 + kc:sc0 + kc + 1])

    def x_tile(self, X, b, t0, tt):
        return X[b, t0:t0 + tt, :].rearrange("(s p) d -> p s d", p=128)

    def phase_ffn(self, l, Xin, Xout):
        nc, S, ns, T = self.nc, self.S, self.nseq, self.T
        TT = 256
        nsub = TT // 128
        NF = FH // 128
        with ExitStack() as ph:
            w1 = self.sb(ph, "w1", [128, 8, 2 * FH], BF16)
            w2 = self.sb(ph, "w2", [128, NF, D], BF16)
            for kc in range(8):
                self.load_w_bf16(w1[:, kc, :], self.din["ffn_w_in"][l, kc * 128:(kc + 1) * 128, :], "w1", "w1", 1408)
            for fc in range(NF):
                self.load_w_bf16(w2[:, fc, :], self.din["ffn_w_out"][l, fc * 128:(fc + 1) * 128, :], "w2", "w2")
            E = self.epi_setup(ph, l, 1)
            xt = [(self.sb(ph, "xt%d" % i, [128, nsub, D], F32), "xt%d" % i) for i in range(2)]
            hT = [(self.sb(ph, "hT%d" % i, [128, 8, TT], BF16), "hT%d" % i) for i in range(2)]
            aT = (self.sb(ph, "aT", [128, NF, TT], BF16), "aT")
            sg = [(self.sb(ph, "sg%d" % i, [128, TT], F32), "sg%d" % i) for i in range(2)]
            tpr = Ring(self.pb[0:2])
            gur = Ring([(self.pb[2], self.pb[3]), (self.pb[6], self.pb[7])])
            tiles = [(b, t0) for b in range(ns) for t0 in range(0, T, TT)]

            def loads(i):
                b_, t0_ = tiles[i]
                S.dma("sp", xt[i % 2][1], xt[i % 2][0], self.x_tile(Xin, b_, t0_, TT), writes=[xt[i % 2][1]])

            loads(0)
            print("[sbuf] ffn remaining", nc.sbuf_bytes_remaining)
            for it, (b, t0) in enumerate(tiles):
                if True:
                    if it + 1 < len(tiles):
                        loads(it + 1)
                    xs, xr = xt[it % 2]
                    hs, hr = hT[it % 2]
                    self.xT_mod(xs, xr, nsub, hs, hr, l, b, 1, tpr)
                    for fc in range(NF):
                        (pg, pgr), (pu, pur) = gur.next()
                        for kc in range(8):
                            self.mm(pg[:, 0:TT], w1[:, kc, fc * 128:(fc + 1) * 128], hs[:, kc, :], kc == 0, kc == 7,
                                    ["w1", hr], [pgr])
                        for kc in range(8):
                            self.mm(pu[:, 0:TT], w1[:, kc, FH + fc * 128:FH + (fc + 1) * 128], hs[:, kc, :], kc == 0,
                                    kc == 7, ["w1", hr], [pur])
                        sgt, sgr = sg[fc % 2]
                        self.act(sgt, pg[:, 0:TT], AF.Silu, [pgr], [sgr])
                        self.tt("dve", aT[0][:, fc, :], sgt, pu[:, 0:TT], ALU.mult, [sgr, pur], [aT[1]])
                    for s in range(nsub):
                        pyt, pyr = self.py
                        for hf in range(2):
                            for fc in range(NF):
                                self.mm(pyt[:, hf * 512:(hf + 1) * 512], aT[0][:, fc, s * 128:(s + 1) * 128],
                                        w2[:, fc, hf * 512:(hf + 1) * 512], fc == 0, fc == NF - 1, [aT[1], "w2"], [pyr])
                        self.epilogue(E, b, pyt, pyr, xs[:, s, :], xr, Xout[b, t0 + s * 128:t0 + (s + 1) * 128, :])

    def phase_gmlp(self, l, Xin, Xout):
        nc, S, ns, T = self.nc, self.S, self.nseq, self.T
        TT = 512
        nsub = TT // 128
        W = self.din
        with ExitStack() as ph:
            wi = self.sb(ph, "wi", [128, 8, 2 * D], BF16)
            wo = self.sb(ph, "wo", [128, 8, D], BF16)
            for kc in range(8):
                self.load_w_bf16(wi[:, kc, :], W["gm_w_in"][0, kc * 128:(kc + 1) * 128, :], "wi", "wi")
                self.load_w_bf16(wo[:, kc, :], W["gm_w_out"][0, kc * 128:(kc + 1) * 128, :], "wo", "wo")
            wsl = self.sb(ph, "wsl", [128, 8, 128], F32)
            wsm = self.sb(ph, "wsm", [128, 8, 128], F32)
            wmT = self.sb(ph, "wmT", [128, 8, 128], BF16)
            S.dma("sp", "wsl", wsl, W["gm_w_s"][0].rearrange("g t s -> t g s"), writes=["wsl"])
            for g in range(8):
                pt, pr = self.pb[g % 2]
                self.tr(pt[:, 0:128], wsl[:, g, :], self.ident, ["wsl", "ident"], [pr])
                self.cp("dve", wsm[:, g, :], pt[:, 0:128], [pr], ["wsm"])
                S.op("pool", lambda e, g=g: e.affine_select(wmT[:, g, :], wsm[:, g, :], [[1, 128]], ALU.is_ge, 0.0,
                                                             base=0, channel_multiplier=-1), ["wsm"], ["wmT"])
            binu, binu_r = self.load_pp(ph, "binu", W["gm_b_in"][0, 0:D].rearrange("(c p) -> c p", p=128), 2)
            binv, binv_r = self.load_bcast(ph, "binv", W["gm_b_in"][0, D:2 * D])
            glg, glg_r = self.load_bcast(ph, "glg", W["gm_ln_g"][0, :])
            glb, glb_r = self.load_bcast(ph, "glb", W["gm_ln_b"][0, :])
            bsb, bsb_r = self.load_bcast(ph, "bsb", W["gm_b_s"][0].rearrange("g t -> (g t)"))
            bsb3 = bsb.rearrange("p (g t) -> p g t", t=128)
            E = self.epi_setup(ph, l, 0, single_gb=True)
            xt = [(self.sb(ph, "xt%d" % i, [128, nsub, D], F32), "xt%d" % i) for i in range(3)]
            hT = [(self.sb(ph, "hT%d" % i, [128, 8, TT], BF16), "hT%d" % i) for i in range(2)]
            vz = [dict(buf=self.sb(ph, "vz%d" % i, [128, D], F32), st=self.sb(ph, "vst%d" % i, [128, 12], F32),
                       mv=self.sb(ph, "vmv%d" % i, [128, 2], F32), sm=self.sb(ph, "vsm%d" % i, [128, 4], F32),
                       res="vz%d" % i) for i in range(2)]
            vn = [[(self.sb(ph, "vn%d_%d" % (k, i), [128, D], BF16), "vn%d_%d" % (k, i)) for i in range(nsub)] for k in range(2)]
            uT = [(self.sb(ph, "uT%d" % i, [128, TT], F32), "uT%d" % i) for i in range(2)]
            tmp = [(self.sb(ph, "tmp%d" % i, [128, TT], F32), "tmp%d" % i) for i in range(2)]
            yT = (self.sb(ph, "yT", [128, 8, TT], BF16), "yT")
            print("[sbuf] gmlp remaining", nc.sbuf_bytes_remaining)
            tpr = Ring(self.pb[0:2])
            tiles = [(b, t0) for b in range(ns) for t0 in range(0, T, TT)]
            cnt = dict(vi=0)

            def loads(i):
                b_, t0_ = tiles[i]
                S.dma("sp", xt[i % 3][1], xt[i % 3][0], self.x_tile(Xin, b_, t0_, TT), writes=[xt[i % 3][1]])

            def stageA(i):
                b, t0 = tiles[i]
                xs, xr = xt[i % 3]
                hs, hr = hT[i % 2]
                self.xT_mod(xs, xr, nsub, hs, hr, l, b, 0, tpr)
                for s in range(nsub):
                    pv, pvr = self.py1
                    for hf in range(2):
                        for kc in range(8):
                            self.mm(pv[:, hf * 512:(hf + 1) * 512], hs[:, kc, s * 128:(s + 1) * 128],
                                    wi[:, kc, D + hf * 512:D + (hf + 1) * 512], kc == 0, kc == 7, [hr, "wi"], [pvr])
                    z = vz[cnt["vi"] % 2]
                    cnt["vi"] += 1
                    vt, vr = vn[i % 2][s]
                    self.tt("dve", z["buf"], pv, binv, ALU.add, [pvr, binv_r], [z["res"]])
                    self.act(z["buf"], z["buf"], AF.Gelu, [z["res"]], [z["res"]])
                    self.ln_rows(z["buf"], z["res"], z)
                    self.tt("dve", z["buf"], z["buf"], glg, ALU.mult, [z["res"], glg_r], [z["res"]])
                    self.tt("pool", vt, z["buf"], glb, ALU.add, [z["res"], glb_r], [vr])

            def stageB(i):
                b, t0 = tiles[i]
                xs, xr = xt[i % 3]
                hs, hr = hT[i % 2]
                self.epi_select(E, b)
                for g in range(8):
                    pu, pur = self.pb[2]
                    psv, psvr = self.pb[3]
                    for kc in range(8):
                        self.mm(pu, wi[:, kc, g * 128:(g + 1) * 128], hs[:, kc, :], kc == 0, kc == 7, ["wi", hr], [pur])
                    ut, utr = uT[g % 2]
                    self.act(ut, pu, AF.Gelu, [pur, binu_r], [utr], bias=binu[:, g:g + 1])
                    for s in range(nsub):
                        vt, vr = vn[i % 2][s]
                        self.mm(psv[:, s * 128:(s + 1) * 128], vt[:, g * 128:(g + 1) * 128], wmT[:, g, :],
                                True, True, [vr, "wmT"], [psvr])
                    tm, tmr = tmp[g % 2]
                    self.tt("dve", tm.rearrange("p (s t) -> p s t", t=128), psv.rearrange("p (s t) -> p s t", t=128),
                            bsb3[:, g:g + 1, :].to_broadcast([128, nsub, 128]), ALU.add, [psvr, bsb_r], [tmr])
                    self.tt("dve" if g % 2 == 0 else "pool", yT[0][:, g, :], tm, ut, ALU.mult, [tmr, utr], [yT[1]])
                for s in range(nsub):
                    pyt, pyr = self.py
                    for hf in range(2):
                        for g in range(8):
                            self.mm(pyt[:, hf * 512:(hf + 1) * 512], yT[0][:, g, s * 128:(s + 1) * 128],
                                    wo[:, g, hf * 512:(hf + 1) * 512], g == 0, g == 7, [yT[1], "wo"], [pyr])
                    self.epilogue(E, b, pyt, pyr, xs[:, s, :], xr, Xout[b, t0 + s * 128:t0 + (s + 1) * 128, :])

            n = len(tiles)
            loads(0)
            if n > 1:
                loads(1)
            for i in range(n):
                stageA(i)
                if i >= 1:
                    stageB(i - 1)
                if i + 2 < n:
                    loads(i + 2)
            stageB(n - 1)

    def phase_conv(self, l, Xin, Xout):
        nc, S, ns, T = self.nc, self.S, self.nseq, self.T
        TT = 256
        nsub = TT // 128
        W = self.din
        HW = CONVW - 1
        with ExitStack() as ph:
            wi = self.sb(ph, "wi", [128, 8, 2 * D], BF16)
            wo = self.sb(ph, "wo", [128, 8, D], BF16)
            for kc in range(8):
                self.load_w_bf16(wi[:, kc, :], W["cv_w_in"][0, kc * 128:(kc + 1) * 128, :], "wi", "wi")
                self.load_w_bf16(wo[:, kc, :], W["cv_w_out"][0, kc * 128:(kc + 1) * 128, :], "wo", "wo")
            bia, bia_r = self.load_pp(ph, "bia", W["cv_b_in"][0, :].rearrange("(c p) -> c p", p=128), 2)
            dwr = W["cv_dw"][0].rearrange("i (c p) -> (i c) p", p=128)
            dwa, dwa_r = self.load_pp(ph, "dwa", dwr[0:124, :], 2)
            dwb_, dwb_r = self.load_pp(ph, "dwb", dwr[124:248, :], 3)
            dwbias, dwbias_r = self.load_pp(ph, "dwbias", W["cv_dw_b"][0, :].rearrange("(c p) -> c p", p=128), 2)
            clg, clg_r = self.load_pp(ph, "clg", W["cv_ln_g"][0, :].rearrange("(c p) -> c p", p=128), 3)
            clb, clb_r = self.load_pp(ph, "clb", W["cv_ln_b"][0, :].rearrange("(c p) -> c p", p=128), 2)
            bo32 = self.sb(ph, "bo32", [1, D], F32)
            bohi = self.sb(ph, "bohi", [1, D], BF16)
            bolo = self.sb(ph, "bolo", [1, D], BF16)
            one1 = self.sb(ph, "one1", [1, 128], BF16)
            S.dma("sp", "bo32", bo32, W["cv_b_out"][0:1, :], writes=["bo"])
            self.cp("dve", bohi, bo32, ["bo"], ["bohi"])
            self.tt("dve", bo32, bo32, bohi, ALU.subtract, ["bo", "bohi"], ["bo"])
            self.cp("dve", bolo, bo32, ["bo"], ["bolo"])
            self.memset("pool", one1, 1.0, ["one1"])
            diag = self.sb(ph, "diag", [128, CONVW * 8, 128], BF16)
            for half, (dt_, dr_) in enumerate(((dwa, dwa_r), (dwb_, dwb_r))):
                self.tt("dve", diag[:, half * 124:(half + 1) * 124, :],
                        self.ident[:, None, :].to_broadcast([128, 124, 128]),
                        dt_[:, :, None].to_broadcast([128, 124, 128]), ALU.mult, ["ident", dr_], ["diag"])
            onesf = self.sb(ph, "onesf", [128, 128], F32)
            self.memset("pool", onesf, 1.0, ["onesf"])
            E = self.epi_setup(ph, l, 0, single_gb=True)
            xt = [(self.sb(ph, "xt%d" % i, [128, nsub, D], F32), "xt%d" % i) for i in range(3)]
            hT = (self.sb(ph, "hT", [128, 8, TT], BF16), "hT")
            ybuf = (self.sb(ph, "ybuf", [128, 8, HW + TT], BF16), "ybuf")
            sgm = [(self.sb(ph, "sgm%d" % i, [128, TT], F32), "sgm%d" % i) for i in range(2)]
            zT = [(self.sb(ph, "zT%d" % i, [128, 8, TT], F32), "zT%d" % i) for i in range(2)]
            zq = [(self.sb(ph, "zq%d" % i, [128, TT], F32), "zq%d" % i) for i in range(2)]
            mean_t = [(self.sb(ph, "mean_t%d" % i, [128, TT], F32), "mean_t%d" % i) for i in range(2)]
            rstd_t = [(self.sb(ph, "rstd_t%d" % i, [128, TT], F32), "rstd_t%d" % i) for i in range(2)]
            sT = (self.sb(ph, "sT", [128, 8, TT], BF16), "sT")
            print("[sbuf] conv remaining", nc.sbuf_bytes_remaining)
            tpr = Ring(self.pb[0:1])
            tiles = [(b, t0) for b in range(ns) for t0 in range(0, T, TT)]

            def loads(i):
                b_, t0_ = tiles[i]
                S.dma("sp", xt[i % 3][1], xt[i % 3][0], self.x_tile(Xin, b_, t0_, TT), writes=[xt[i % 3][1]])

            def stageA(i):
                b, t0 = tiles[i]
                xs, xr = xt[i % 3]
                hs, hr = hT
                self.xT_mod(xs, xr, nsub, hs, hr, l, b, 0, tpr)
                if t0 == 0:
                    self.memset("pool", ybuf[0][:, :, 0:HW], 0.0, [ybuf[1]])
                else:
                    self.cp("pool", ybuf[0][:, :, 0:HW], ybuf[0][:, :, TT:TT + HW], [ybuf[1]], [ybuf[1]])
                for kc in range(8):
                    pa, par = self.pb[2]
                    pg, pgr = self.pb[3]
                    for k in range(8):
                        self.mm(pa[:, 0:TT], wi[:, k, kc * 128:(kc + 1) * 128], hs[:, k, :], k == 0, k == 7, ["wi", hr], [par])
                    for k in range(8):
                        self.mm(pg[:, 0:TT], wi[:, k, D + kc * 128:D + (kc + 1) * 128], hs[:, k, :], k == 0, k == 7,
                                ["wi", hr], [pgr])
                    sg_, sgr = sgm[kc % 2]
                    self.act(sg_, pg[:, 0:TT], AF.Sigmoid, [pgr, bia_r], [sgr], bias=bia[:, 8 + kc:9 + kc])
                    self.stt(ybuf[0][:, kc, HW:HW + TT], pa[:, 0:TT], bia[:, kc:kc + 1], sg_, ALU.add, ALU.mult,
                             [par, sgr, bia_r], [ybuf[1]])
                z_, zr = zT[i % 2]
                p1, p1r = self.pb[7]
                p2, p2r = self.pb[1]
                for kc in range(8):
                    pz, pzr = self.pb[6]
                    for t in range(CONVW):
                        self.mm(pz[:, 0:TT], diag[:, t * 8 + kc, :], ybuf[0][:, kc, t:t + TT], t == 0, t == CONVW - 1,
                                ["diag", ybuf[1]], [pzr])
                    self.act(z_[:, kc, :], pz[:, 0:TT], AF.Identity, [pzr, dwbias_r], [zr], bias=dwbias[:, kc:kc + 1])
                    q_, qr = zq[kc % 2]
                    S.op("act", lambda e, q_=q_, kc=kc: e.activation(q_, z_[:, kc, :], AF.Square), [zr], [qr])
                    self.mm(p1[:, 0:TT], onesf, z_[:, kc, :], kc == 0, kc == 7, ["onesf", zr], [p1r])
                    self.mm(p2[:, 0:TT], onesf, q_, kc == 0, kc == 7, ["onesf", qr], [p2r])
                m_, mr = mean_t[i % 2]
                r_, rr = rstd_t[i % 2]
                q, qr_ = r_, rr
                self.ts("dve", m_, p1[:, 0:TT], 1.0 / D, None, ALU.mult, None, [p1r], [mr])
                self.tt("dve", q, m_, m_, ALU.mult, [mr], [qr_])
                self.stt(q, p2[:, 0:TT], 1.0 / D, q, ALU.mult, ALU.subtract, [p2r, qr_], [qr_])
                self.ts("dve", q, q, EPS, None, ALU.add, None, [qr_], [qr_])
                self.act(r_, q, AF.Sqrt, [qr_], [rr])
                S.op("dve", lambda e, r_=r_: e.reciprocal(r_, r_), [rr], [rr])

            def stageB(i):
                b, t0 = tiles[i]
                xs, xr = xt[i % 3]
                z_, zr = zT[i % 2]
                m_, mr = mean_t[i % 2]
                r_, rr = rstd_t[i % 2]
                self.epi_select(E, b)
                mb = m_[:, None, :].to_broadcast([128, 8, TT])
                rb = r_[:, None, :].to_broadcast([128, 8, TT])
                self.tt("dve", z_, z_, mb, ALU.subtract, [zr, mr], [zr])
                self.tt("dve", z_, z_, rb, ALU.mult, [zr, rr], [zr])
                for kc in range(8):
                    self.act(sT[0][:, kc, :], z_[:, kc, :], AF.Silu, [zr, clg_r, clb_r], [sT[1]], bias=clb[:, kc:kc + 1],
                             scale=clg[:, kc:kc + 1])
                for s in range(nsub):
                    pyt, pyr = self.py
                    for hf in range(2):
                        cs = slice(hf * 512, (hf + 1) * 512)
                        for kc in range(8):
                            self.mm(pyt[:, cs], sT[0][:, kc, s * 128:(s + 1) * 128], wo[:, kc, cs], kc == 0, False,
                                    [sT[1], "wo"], [pyr])
                        self.mm(pyt[:, cs], one1, bohi[:, cs], False, False, ["one1", "bohi"], [pyr])
                        self.mm(pyt[:, cs], one1, bolo[:, cs], False, True, ["one1", "bolo"], [pyr])
                    self.epilogue(E, b, pyt, pyr, xs[:, s, :], xr, Xout[b, t0 + s * 128:t0 + (s + 1) * 128, :])

            n = len(tiles)
            loads(0)
            if n > 1:
                loads(1)
            for i in range(n):
                stageA(i)
                if i >= 1:
                    stageB(i - 1)
                if i + 2 < n:
                    loads(i + 2)
            stageB(n - 1)

    def phase_attn(self, l, kind, Xin, Xout):
        self.attn_proj(l, kind, Xin)
        self.S.barrier()
        if kind == "fox":
            self.attn_core_fox()
        else:
            self.attn_core_sb()
        self.S.barrier()
        self.attn_out(l, kind, Xin, Xout)

    def attn_proj(self, l, kind, Xin):
        nc, S, ns, T = self.nc, self.S, self.nseq, self.T
        TT = 512
        nsub = TT // 128
        W = self.din
        fox = kind == "fox"
        NC = 3 * D + (NH if fox else 0)
        wname = "fox_w_in" if fox else "sb_w_in"
        with ExitStack() as ph:
            wi = self.sb(ph, "wi", [128, 8, NC], BF16)
            for kc in range(8):
                self.load_w_bf16(wi[:, kc, :], W[wname][0, kc * 128:(kc + 1) * 128, :], "wi", "wi", 1024 + (NH if fox else 0))
            xt = [(self.sb(ph, "xt%d" % i, [128, nsub, D], F32), "xt%d" % i) for i in range(2)]
            hT = [(self.sb(ph, "hT%d" % i, [128, 8, TT], BF16), "hT%d" % i) for i in range(2)]
            qst = [(self.sb(ph, "qst%d" % i, [128, 8, TT], BF16), "qst%d" % i) for i in range(2)]
            kst = [(self.sb(ph, "kst%d" % i, [128, 8, TT], BF16), "kst%d" % i) for i in range(2)]
            vst = [(self.sb(ph, "vst%d" % i, [128, nsub, D], BF16), "vst%d" % i) for i in range(2)]
            if fox:
                nbf = self.sb(ph, "nbf", [NH, 1], F32)
                S.dma("sp", "nbf", nbf, W["fox_b_f"][0, :].rearrange("(h o) -> h o", o=1), writes=["nbf"])
                self.ts("dve", nbf, nbf, -1.0, None, ALU.mult, None, ["nbf"], ["nbf"])
                ones16 = self.sb(ph, "ones16", [NH, TT], F32)
                self.memset("dve", ones16, 1.0, ["ones16"])
                ef = self.sb(ph, "ef", [NH, TT], F32)
                cum = [(self.sb(ph, "cum%d" % i, [NH, TT], F32), "cum%d" % i) for i in range(2)]
                s8 = self.sb(ph, "s8", [NH, TT], F32)
                r1 = self.sb(ph, "r1", [NH, TT], F32)
                fk3 = [(self.sb(ph, "fk3%d" % i, [NH, 3, TT], BF16), "fk3%d" % i) for i in range(2)]
                fq3 = [(self.sb(ph, "fq3%d" % i, [NH, 3, TT], BF16), "fq3%d" % i) for i in range(2)]
            tpr = Ring(self.pb[0:2])
            qkr = Ring(self.pb[2:4])
            ev = 0
            tiles = [(b, t0) for b in range(ns) for t0 in range(0, T, TT)]

            def loads(i):
                b_, t0_ = tiles[i]
                S.dma("sp", xt[i % 2][1], xt[i % 2][0], self.x_tile(Xin, b_, t0_, TT), writes=[xt[i % 2][1]])

            loads(0)
            for it, (b, t0) in enumerate(tiles):
                if True:
                    if it + 1 < len(tiles):
                        loads(it + 1)
                    sl = it % 2
                    xs, xr = xt[sl]
                    hs, hr = hT[sl]
                    self.xT_mod(xs, xr, nsub, hs, hr, l, b, 0, tpr)
                    for (stg, c0, dst) in ((qst[sl], 0, self.qt_d), (kst[sl], D, self.kt_d)):
                        for j in range(8):
                            pt, pr = qkr.next()
                            for kc in range(8):
                                self.mm(pt, wi[:, kc, c0 + j * 128:c0 + (j + 1) * 128], hs[:, kc, :], kc == 0, kc == 7,
                                        ["wi", hr], [pr])
                            self.cp("act" if ev % 2 == 0 else "dve", stg[0][:, j, :], pt, [pr], [stg[1]])
                            ev += 1
                        S.dma("sp", stg[1], dst[b].rearrange("(j hh) d t -> (hh d) j t", hh=2)[:, :, t0:t0 + TT], stg[0],
                              reads=[stg[1]])
                    for s in range(nsub):
                        pv, pvr = self.py
                        for hf in range(2):
                            for kc in range(8):
                                self.mm(pv[:, hf * 512:(hf + 1) * 512], hs[:, kc, s * 128:(s + 1) * 128],
                                        wi[:, kc, 2 * D + hf * 512:2 * D + (hf + 1) * 512], kc == 0, kc == 7, [hr, "wi"], [pvr])
                        self.cp("act" if ev % 2 == 0 else "dve", vst[sl][0][:, s, :], pv, [pvr], [vst[sl][1]])
                        ev += 1
                    S.dma("sp", vst[sl][1], self.v_d[b, t0:t0 + TT, :].rearrange("(s p) d -> p s d", p=128), vst[sl][0],
                          reads=[vst[sl][1]])
                    if fox:
                        pf, pfr = self.pb[6]
                        for kc in range(8):
                            self.mm(pf[0:NH, :], wi[:, kc, 3 * D:3 * D + NH], hs[:, kc, :], kc == 0, kc == 7, ["wi", hr], [pfr])
                        self.act(ef, pf[0:NH, :], AF.Exp, [pfr, "nbf"], ["ef"], bias=nbf, scale=-1.0)
                        self.act(ef, ef, AF.Ln, ["ef"], ["ef"], bias=1.0)
                        cm, cmr = cum[sl]
                        pc, pcr = cum[1 - sl]
                        init = 0.0 if t0 == 0 else pc[:, TT - 1:TT]
                        S.op("dve", lambda e, cm=cm, init=init: e.tensor_tensor_scan(cm, ones16, ef, init, ALU.mult, ALU.add),
                             ["ones16", "ef"] + ([] if t0 == 0 else [pcr]), [cmr])
                        fk, fkr = fk3[sl]
                        fq, fqr = fq3[sl]
                        self.ts("dve", s8, cm, 8.0, None, ALU.mult, None, [cmr], ["s8"])
                        self.cp("dve", fk[:, 0, :], s8, ["s8"], [fkr])
                        self.tt("dve", r1, s8, fk[:, 0, :], ALU.subtract, ["s8", fkr], ["r1"])
                        self.cp("dve", fk[:, 1, :], r1, ["r1"], [fkr])
                        self.tt("dve", r1, r1, fk[:, 1, :], ALU.subtract, ["r1", fkr], ["r1"])
                        self.cp("dve", fk[:, 2, :], r1, ["r1"], [fkr])
                        self.ts("dve", fq, fk, -1.0, None, ALU.mult, None, [fkr], [fqr])
                        S.dma("sp", fkr, self.fk_d[b, :, :, t0:t0 + TT], fk, reads=[fkr])
                        S.dma("sp", fqr, self.fq_d[b, :, :, t0:t0 + TT], fq, reads=[fqr])

    def attn_core_fox(self):
        nc, S, ns, T = self.nc, self.S, self.nseq, self.T
        NKB = T // 128
        QW = 1024
        NQ = T // QW
        KA = DH + 6
        LA = 2
        with ExitStack() as ph:
            kta = [(self.sb(ph, "kta%d" % i, [KA, T], BF16), "kta%d" % i) for i in range(3)]
            qta = [(self.sb(ph, "qta%d" % i, [KA, T], BF16), "qta%d" % i) for i in range(3)]
            va = [(self.sb(ph, "va%d" % i, [128, NKB, DH + 1], BF16), "va%d" % i) for i in range(3)]
            PT = [(self.sb(ph, "PT%d" % i, [128, QW], BF16), "PT%d" % i) for i in range(4)]
            ost = [(self.sb(ph, "ost%d" % i, [DH + 1, QW], F32), "ost%d" % i) for i in range(2)]
            for i in range(3):
                self.memset("dve", kta[i][0][DH:KA, :], 1.0, [kta[i][1]])
                self.memset("dve", qta[i][0][DH:KA, :], 1.0, [qta[i][1]])
                self.memset("pool", va[i][0][:, :, DH:DH + 1], 1.0, [va[i][1]])
            psr = Ring(self.pd[0:2])
            por = Ring(self.pd[2:4])
            ptr = Ring(PT)
            osr = Ring(ost)
            pend = []

            def halves(c0):
                return [(max(c0, hf * 512), (hf + 1) * 512) for hf in range(2) if max(c0, hf * 512) < (hf + 1) * 512]

            def stage2(tl):
                (b, h, j, kb, c0, nkb, sl, po, por_, pt, ptr_) = tl
                for (a0, a1) in halves(c0):
                    self.mm(po[0:DH + 1, a0:a1], va[sl][0][:, kb, :], pt[:, a0:a1], kb == 0, kb == nkb - 1,
                            [va[sl][1], ptr_], [por_])
                if kb == nkb - 1:
                    os_, osr_ = osr.next()
                    self.cp("dve", os_, po[0:DH + 1, :], [por_], [osr_])
                    S.dma("sp", osr_, self.o_d[b, h, :, j * QW:(j + 1) * QW], os_[0:DH, :], reads=[osr_])
                    S.dma("sp", osr_, self.den_d[b, h:h + 1, j * QW:(j + 1) * QW], os_[DH:DH + 1, :], reads=[osr_])

            heads = [(b, h) for b in range(ns) for h in range(NH)]

            def loads(i):
                b_, h_ = heads[i]
                sl_ = i % 3
                S.dma("sp", kta[sl_][1], kta[sl_][0][0:DH, :], self.kt_d[b_, h_], writes=[kta[sl_][1]])
                S.dma("sp", kta[sl_][1], kta[sl_][0][DH:DH + 3, :], self.fk_d[b_, h_], writes=[kta[sl_][1]])
                S.dma("sp", qta[sl_][1], qta[sl_][0][0:DH, :], self.qt_d[b_, h_], writes=[qta[sl_][1]])
                S.dma("sp", qta[sl_][1], qta[sl_][0][DH + 3:DH + 6, :], self.fq_d[b_, h_], writes=[qta[sl_][1]])
                S.dma("sp", va[sl_][1], va[sl_][0][:, :, 0:DH],
                      self.v_d[b_, :, h_ * DH:(h_ + 1) * DH].rearrange("(k p) d -> p k d", p=128), writes=[va[sl_][1]])

            loads(0)
            for hi, (b, h) in enumerate(heads):
                sl = hi % 3
                if hi + 1 < len(heads):
                    loads(hi + 1)
                for j in range(NQ):
                    nkb = 8 * j + 8
                    po, por_ = por.next()
                    for kb in range(nkb):
                        c0 = (kb - 8 * j) * 128 if kb >= 8 * j else 0
                        ps, psr_ = psr.next()
                        pt, ptr_ = ptr.next()
                        for (a0, a1) in halves(c0):
                            self.mm(ps[:, a0:a1], kta[sl][0][:, kb * 128:(kb + 1) * 128],
                                    qta[sl][0][:, j * QW + a0:j * QW + a1], True, True, [kta[sl][1], qta[sl][1]], [psr_])
                        self.act(pt[:, c0:QW], ps[:, c0:QW], AF.Exp, [psr_], [ptr_], scale=0.125)
                        if kb >= 8 * j:
                            S.op("pool", lambda e, pt=pt, c0=c0: e.affine_select(
                                pt[:, c0:c0 + 128], pt[:, c0:c0 + 128], [[1, 128]], ALU.is_ge, 0.0, base=0,
                                channel_multiplier=-1), [ptr_], [ptr_])
                        pend.append((b, h, j, kb, c0, nkb, sl, po, por_, pt, ptr_))
                        if len(pend) > LA:
                            stage2(pend.pop(0))
            while pend:
                stage2(pend.pop(0))

    def attn_core_sb(self):
        nc, S, ns, T = self.nc, self.S, self.nseq, self.T
        NKB = T // 128
        QW = 1024
        NQ = T // QW
        with ExitStack() as ph:
            kta = [(self.sb(ph, "kta%d" % i, [DH, T], BF16), "kta%d" % i) for i in range(3)]
            qta = [(self.sb(ph, "qta%d" % i, [DH, T], BF16), "qta%d" % i) for i in range(3)]
            va = [(self.sb(ph, "va%d" % i, [128, NKB, DH], BF16), "va%d" % i) for i in range(3)]
            Et = [(self.sb(ph, "Et%d" % i, [128, QW], F32), "Et%d" % i) for i in range(2)]
            SPt = [(self.sb(ph, "SPt%d" % i, [128, QW], BF16), "SPt%d" % i) for i in range(3)]
            X2 = [(self.sb(ph, "X2%d" % i, [128, QW], F32), "X2%d" % i) for i in range(2)]
            At = [(self.sb(ph, "At%d" % i, [128, QW], BF16), "At%d" % i) for i in range(3)]
            Ccs = [(self.sb(ph, "Cc%d" % i, [128, QW], F32), "Cc%d" % i) for i in range(2)]
            ccr = Ring(Ccs)
            ost = [(self.sb(ph, "ost%d" % i, [DH, QW], F32), "ost%d" % i) for i in range(2)]
            trin = self.sb(ph, "trin", [128, 128], BF16)
            onen = self.sb(ph, "onen", [128, 128], BF16)
            self.memset("pool", trin, -8.0, ["trin"])
            S.op("pool", lambda e: e.affine_select(trin, trin, [[-1, 128]], ALU.is_ge, 0.0, base=0, channel_multiplier=1),
                 ["trin"], ["trin"])
            self.memset("pool", onen, -8.0, ["onen"])
            psr = Ring(self.pd[0:2])
            pyy, pyyr = self.pd[2]
            po_, por_ = self.pd[3]
            etr, spr, x2r, atr, osr = Ring(Et), Ring(SPt), Ring(X2), Ring(At), Ring(ost)
            q1, q2 = [], []

            def halves(c0):
                return [(max(c0, hf * 512), (hf + 1) * 512) for hf in range(2) if max(c0, hf * 512) < (hf + 1) * 512]

            def mask(tile_ap, res, c0):
                S.op("pool", lambda e: e.affine_select(tile_ap[:, c0:c0 + 128], tile_ap[:, c0:c0 + 128], [[1, 128]], ALU.is_gt,
                                                        0.0, base=0, channel_multiplier=-1), [res], [res])

            def stageA1(tl):
                ps, psr_ = tl["ps"]
                c0, sl, kb, j = tl["c0"], tl["sl"], tl["kb"], tl["j"]
                for (a0, a1) in halves(c0):
                    self.mm(ps[:, a0:a1], kta[sl][0][:, kb * 128:(kb + 1) * 128], qta[sl][0][:, j * QW + a0:j * QW + a1],
                            True, True, [kta[sl][1], qta[sl][1]], [psr_])
                et, etr_ = etr.next()
                tl["et"] = (et, etr_)
                self.act(et[:, c0:QW], ps[:, c0:QW], AF.Exp, [psr_], [etr_], scale=0.125)

            def stageA2(tl):
                et, etr_ = tl["et"]
                c0 = tl["c0"]
                sp, spr_ = spr.next()
                tl["sp"] = (sp, spr_)
                self.act(sp[:, c0:QW], et[:, c0:QW], AF.Ln, [etr_], [spr_], bias=1.0)
                if tl["diag"]:
                    mask(sp, spr_, c0)

            def stageB(tl):
                ps, psr_ = tl["ps"]
                sp, spr_ = tl["sp"]
                c0 = tl["c0"]
                Cc = tl["Cc"]
                for (a0, a1) in halves(c0):
                    S.op("pe", lambda e, a0=a0, a1=a1: e.matmul(ps[:, a0:a1], trin, sp[:, a0:a1], start=False, stop=True,
                                                               skip_group_check=True), ["trin", spr_], [psr_])
                for (a0, a1) in halves(c0):
                    self.mm(pyy[:, a0:a1], onen, sp[:, a0:a1], True, True, ["onen", spr_], [pyyr])
                x2, x2r_ = x2r.next()
                at, atr_ = atr.next()
                tl["at"] = (at, atr_)
                self.tt("dve", x2[:, c0:QW], ps[:, c0:QW], Cc[0][:, c0:QW], ALU.add, [psr_, Cc[1]], [x2r_])
                self.tt("dve", Cc[0][:, c0:QW], pyy[:, c0:QW], Cc[0][:, c0:QW], ALU.add, [pyyr, Cc[1]], [Cc[1]])
                self.act(at[:, c0:QW], x2[:, c0:QW], AF.Exp, [x2r_], [atr_], scale=0.125)
                if tl["diag"]:
                    mask(at, atr_, c0)

            def stageC(tl):
                at, atr_ = tl["at"]
                c0, sl, kb = tl["c0"], tl["sl"], tl["kb"]
                st = tl["started"]
                for hf, (a0, a1) in [(a0 // 512, (a0, a1)) for (a0, a1) in halves(c0)]:
                    self.mm(po_[0:DH, a0:a1], va[sl][0][:, kb, :], at[:, a0:a1], not st[hf], tl["last"], [va[sl][1], atr_], [por_])
                    st[hf] = True
                if tl["last"]:
                    os_, osr_ = osr.next()
                    self.cp("act", os_, po_[0:DH, :], [por_], [osr_])
                    S.dma("sp", osr_, self.o_d[tl["b"], tl["h"], :, tl["j"] * QW:(tl["j"] + 1) * QW], os_, reads=[osr_])

            def push(tl):
                stageA1(tl)
                if q1:
                    t2 = q1.pop(0)
                    stageB(t2)
                    q2.append(t2)
                stageA2(tl)
                q1.append(tl)
                while len(q2) > 1:
                    stageC(q2.pop(0))

            heads = [(b, h) for b in range(ns) for h in range(NH)]

            def loads(i):
                b_, h_ = heads[i]
                sl_ = i % 3
                S.dma("sp", kta[sl_][1], kta[sl_][0], self.kt_d[b_, h_], writes=[kta[sl_][1]])
                S.dma("sp", qta[sl_][1], qta[sl_][0], self.qt_d[b_, h_], writes=[qta[sl_][1]])
                S.dma("sp", va[sl_][1], va[sl_][0],
                      self.v_d[b_, :, h_ * DH:(h_ + 1) * DH].rearrange("(k p) d -> p k d", p=128), writes=[va[sl_][1]])

            loads(0)
            for hi, (b, h) in enumerate(heads):
                sl = hi % 3
                if hi + 1 < len(heads):
                    loads(hi + 1)
                for j in range(NQ):
                    nkb = 8 * j + 8
                    Cc = ccr.next()
                    self.memset("pool", Cc[0], 0.0, [Cc[1]])
                    started = [False, False]
                    for n, kb in enumerate(range(nkb - 1, -1, -1)):
                        diag = kb >= 8 * j
                        c0 = (kb - 8 * j) * 128 if diag else 0
                        tl = dict(b=b, h=h, j=j, kb=kb, c0=c0, sl=sl, diag=diag, started=started,
                                  last=(kb == 0), ps=psr.next(), Cc=Cc)
                        push(tl)
            while q1:
                t2 = q1.pop(0)
                stageB(t2)
                q2.append(t2)
            while q2:
                stageC(q2.pop(0))

    def attn_out(self, l, kind, Xin, Xout):
        nc, S, ns, T = self.nc, self.S, self.nseq, self.T
        TT = 512
        nsub = TT // 128
        W = self.din
        fox = kind == "fox"
        with ExitStack() as ph:
            wo = self.sb(ph, "wo", [128, 8, D], BF16)
            for kc in range(8):
                self.load_w_bf16(wo[:, kc, :], W["fox_w_out" if fox else "sb_w_out"][0, kc * 128:(kc + 1) * 128, :], "wo", "wo")
            E = self.epi_setup(ph, l, 0)
            xt = [(self.sb(ph, "xt%d" % i, [128, nsub, D], F32), "xt%d" % i) for i in range(2)]
            oT = [(self.sb(ph, "oT%d" % i, [128, 8, TT], F32), "oT%d" % i) for i in range(2)]
            oTb = [(self.sb(ph, "oTb%d" % i, [128, 8, TT], BF16), "oTb%d" % i) for i in range(2)]
            if fox:
                dnb = [(self.sb(ph, "dnb%d" % i, [128, 8, TT], F32), "dnb%d" % i) for i in range(2)]
            tiles = [(b, t0) for b in range(ns) for t0 in range(0, T, TT)]

            def loads(i):
                b_, t0_ = tiles[i]
                sl_ = i % 2
                S.dma("sp", xt[sl_][1], xt[sl_][0], self.x_tile(Xin, b_, t0_, TT), writes=[xt[sl_][1]])
                S.dma("sp", oT[sl_][1], oT[sl_][0],
                      self.o_d[b_].rearrange("(j hh) d t -> (hh d) j t", hh=2)[:, :, t0_:t0_ + TT], writes=[oT[sl_][1]])
                if fox:
                    dv = self.den_d[b_].rearrange("(j hh) t -> hh j t", hh=2)
                    for hh in range(2):
                        S.dma("sp", dnb[sl_][1], dnb[sl_][0][hh * DH:(hh + 1) * DH],
                              dv[hh, :, t0_:t0_ + TT].partition_broadcast(DH), writes=[dnb[sl_][1]])

            loads(0)
            for it, (b, t0) in enumerate(tiles):
                if True:
                    if it + 1 < len(tiles):
                        loads(it + 1)
                    sl = it % 2
                    xs, xr = xt[sl]
                    o_, or_ = oT[sl]
                    ob, obr = oTb[sl]
                    if fox:
                        dn, dnr = dnb[sl]
                        S.op("dve", lambda e, dn=dn: e.reciprocal(dn, dn), [dnr], [dnr])
                        self.tt("dve", ob, o_, dn, ALU.mult, [or_, dnr], [obr])
                    else:
                        self.cp("pool", ob, o_, [or_], [obr])
                    for s in range(nsub):
                        pyt, pyr = self.py
                        for hf in range(2):
                            for j in range(8):
                                self.mm(pyt[:, hf * 512:(hf + 1) * 512], ob[:, j, s * 128:(s + 1) * 128],
                                        wo[:, j, hf * 512:(hf + 1) * 512], j == 0, j == 7, [obr, "wo"], [pyr])
                        self.epilogue(E, b, pyt, pyr, xs[:, s, :], xr, Xout[b, t0 + s * 128:t0 + (s + 1) * 128, :])


def _plan_full():
    plan = [("mod",)]
    plan += [("gmlp", 0), ("ffn", 0), ("attn", 1, "fox"), ("ffn", 1), ("attn", 2, "sb"), ("ffn", 2), ("conv", 3), ("ffn", 3)]
    return plan


_CACHE = {}


def kernel(**inputs):
    ncores = 8
    nseq = 2
    T = 4096
    if "nc" not in _CACHE:
        _CACHE["nc"] = Builder(nseq, T, _plan_full()).build()
    nc = _CACHE["nc"]
    x = np.ascontiguousarray(inputs["x"], dtype=np.float32)
    c = np.ascontiguousarray(inputs["c"], dtype=np.float32)
    in_maps = []
    for i in range(ncores):
        m = {k: np.ascontiguousarray(inputs[k], dtype=np.float32) for k in WEIGHT_SHAPES}
        m["x"] = x[i * nseq:(i + 1) * nseq]
        m["c"] = c[i * nseq:(i + 1) * nseq]
        in_maps.append(m)
    res = run_bass_kernel_spmd(nc, in_maps, core_ids=list(range(ncores)))
    return np.concatenate([r["out"] for r in res.results], axis=0)
```

```python
import numpy as np
from contextlib import ExitStack
import concourse.bass as bass
import concourse.mybir as mybir
from concourse.bass_utils import run_bass_kernel_spmd

F32 = mybir.dt.float32
BF16 = mybir.dt.bfloat16
AF = mybir.ActivationFunctionType
ALU = mybir.AluOpType

D = 1024
FH = 2816
NH = 16
DH = 64
DEPTH = 4
ALPHA = float((2.0 * DEPTH) ** 0.25)
EPS = 1e-5
CONVW = 31

ENGS = ["pe", "act", "dve", "pool", "sp"]

WEIGHT_SHAPES = {
    "mod_w": (4, 1024, 6144), "mod_b": (4, 6144), "ln1_g": (4, 1024), "ln1_b": (4, 1024),
    "ln2_g": (4, 1024), "ln2_b": (4, 1024), "ffn_w_in": (4, 1024, 5632), "ffn_w_out": (4, 2816, 1024),
    "gm_w_in": (1, 1024, 2048), "gm_b_in": (1, 2048), "gm_ln_g": (1, 1024), "gm_ln_b": (1, 1024),
    "gm_w_s": (1, 8, 128, 128), "gm_b_s": (1, 8, 128), "gm_w_out": (1, 1024, 1024),
    "fox_w_in": (1, 1024, 3088), "fox_b_f": (1, 16), "fox_w_out": (1, 1024, 1024),
    "sb_w_in": (1, 1024, 3072), "sb_w_out": (1, 1024, 1024),
    "cv_w_in": (1, 1024, 2048), "cv_b_in": (1, 2048), "cv_dw": (1, 31, 1024), "cv_dw_b": (1, 1024),
    "cv_ln_g": (1, 1024), "cv_ln_b": (1, 1024), "cv_w_out": (1, 1024, 1024), "cv_b_out": (1, 1024),
}


class Sched:
    def __init__(self, nc, same_engine_raw=True):
        self.nc = nc
        self.ops = {e: [] for e in ENGS}
        self.last_w = {}
        self.readers = {}
        self.seen = {e: {} for e in ENGS}
        self.dma_cnt = []
        self.keymap = {}
        self.same_engine_raw = same_engine_raw

    def _add_wait(self, eng, waits, ev, is_raw):
        if ev is None:
            return
        if ev[0] == "eng":
            _, e2, idx = ev
            if e2 == eng and (eng == "pe" or not self.same_engine_raw):
                return
            key, val = ("eng", e2), idx
        else:
            key, val = ("dma", ev[1]), ev[2]
        if self.seen[eng].get(key, -1) >= val:
            return
        if val > waits.get(key, -1):
            waits[key] = val

    def _deps(self, eng, reads, writes):
        waits = {}
        for r in reads:
            self._add_wait(eng, waits, self.last_w.get(r), True)
        for w in writes:
            self._add_wait(eng, waits, self.last_w.get(w), False)
            for ev in self.readers.get(w, ()):
                self._add_wait(eng, waits, ev, False)
        for k, v in waits.items():
            self.seen[eng][k] = v
        return waits

    def _commit(self, ev, reads, writes):
        for r in reads:
            self.readers.setdefault(r, []).append(ev)
        for w in writes:
            self.last_w[w] = ev
            self.readers[w] = []

    def op(self, eng, fn, reads=(), writes=()):
        waits = self._deps(eng, reads, writes)
        idx = len(self.ops[eng])
        self.ops[eng].append(dict(kind="op", fn=fn, waits=waits, needed=False))
        self._commit(("eng", eng, idx), reads, writes)

    def dma(self, eng, key, out, in_, reads=(), writes=(), **kw):
        waits = self._deps(eng, reads, writes)
        if key not in self.keymap:
            self.keymap[key] = len(self.keymap)
            if len(self.dma_cnt) < len(self.keymap):
                self.dma_cnt.append(0)
        ph = self.keymap[key]
        self.dma_cnt[ph] += 1
        self.ops[eng].append(dict(kind="dma", out=out, in_=in_, kw=kw, key=ph, waits=waits))
        self._commit(("dma", ph, 16 * self.dma_cnt[ph]), reads, writes)

    def _wait_everything(self, eng):
        waits = {}
        for ev in self.last_w.values():
            self._add_wait(eng, waits, ev, False)
        for evs in self.readers.values():
            for ev in evs:
                self._add_wait(eng, waits, ev, False)
        for k, v in waits.items():
            self.seen[eng][k] = v
        self.ops[eng].append(dict(kind="waitonly", waits=waits))

    def barrier(self):
        for e in ENGS:
            self._wait_everything(e)
        self.last_w = {}
        self.readers = {}
        self.keymap = {}

    def emit(self, stack):
        nc = self.nc
        for e in ENGS:
            for o in self.ops[e]:
                for k, v in o["waits"].items():
                    if k[0] == "eng":
                        self.ops[k[1]][v]["needed"] = True
        semval = {}
        for e in ENGS:
            c = 0
            for i, o in enumerate(self.ops[e]):
                if o["kind"] == "op" and o["needed"]:
                    c += 1
                    semval[(e, i)] = c
        esem = {e: stack.enter_context(nc.semaphore("s_" + e)) for e in ENGS}
        dsem = [stack.enter_context(nc.semaphore("d%d" % i)) for i in range(len(self.dma_cnt))]
        print("[sched] ops:", {e: len(self.ops[e]) for e in ENGS}, "sem max:", {e: max([0] + [v for (ee, _), v in semval.items() if ee == e]) for e in ENGS},
              "dma sems:", len(self.dma_cnt), "max dma cnt:", max(self.dma_cnt) * 16)
        block = stack.enter_context(nc.Block())

        def run(e, eng):
            for o in self.ops[e]:
                for k, v in o["waits"].items():
                    if k[0] == "eng":
                        eng.wait_ge(esem[k[1]], semval[(k[1], v)])
                    else:
                        eng.wait_ge(dsem[k[1]], v)
                if o["kind"] == "op":
                    ins = o["fn"](eng)
                    if o["needed"]:
                        ins.then_inc(esem[e], 1)
                elif o["kind"] == "dma":
                    eng.dma_start(out=o["out"], in_=o["in_"], **o["kw"]).then_inc(dsem[o["key"]], 16)

        @block.tensor
        def _(eng):
            run("pe", eng)

        @block.scalar
        def _(eng):
            run("act", eng)

        @block.vector
        def _(eng):
            run("dve", eng)

        @block.gpsimd
        def _(eng):
            run("pool", eng)

        @block.sync
        def _(eng):
            run("sp", eng)


class Ring:
    def __init__(self, items):
        self.items = list(items)
        self.i = 0

    def next(self):
        it = self.items[self.i % len(self.items)]
        self.i += 1
        return it


class Builder:
    def __init__(self, nseq, T, plan, same_engine_raw=True):
        self.nseq, self.T, self.plan = nseq, T, plan
        self.nc = nc = bass.Bass("TRN2", target_bir_lowering=False)
        self.S = Sched(nc, same_engine_raw)
        self._n = 0
        self.din = {}
        self.din["x"] = nc.dram_tensor("x", [nseq, T, D], F32, kind="ExternalInput").ap()
        self.din["c"] = nc.dram_tensor("c", [nseq, D], F32, kind="ExternalInput").ap()
        for k, shp in WEIGHT_SHAPES.items():
            self.din[k] = nc.dram_tensor(k, list(shp), F32, kind="ExternalInput").ap()
        self.dout = nc.dram_tensor("out", [nseq, T, D], F32, kind="ExternalOutput").ap()
        self.xa = nc.dram_tensor("xa_s", [nseq, T, D], F32).ap()
        self.xb = nc.dram_tensor("xb_s", [nseq, T, D], F32).ap()
        self.modrow = nc.dram_tensor("modrow_s", [DEPTH, nseq, 6 * D], F32).ap()
        self.qt_d = nc.dram_tensor("qt_s", [nseq, NH, DH, T], BF16).ap()
        self.kt_d = nc.dram_tensor("kt_s", [nseq, NH, DH, T], BF16).ap()
        self.v_d = nc.dram_tensor("v_s", [nseq, T, D], BF16).ap()
        self.fq_d = nc.dram_tensor("fq_s", [nseq, NH, 3, T], BF16).ap()
        self.fk_d = nc.dram_tensor("fk_s", [nseq, NH, 3, T], BF16).ap()
        self.o_d = nc.dram_tensor("o_s", [nseq, NH, DH, T], F32).ap()
        self.den_d = nc.dram_tensor("den_s", [nseq, NH, T], F32).ap()

    def sb(self, ph, name, shape, dt):
        self._n += 1
        return ph.enter_context(self.nc.sbuf_tensor("%s_%d" % (name, self._n), list(shape), dt))[:]

    def mm(self, out, lhsT, rhs, start, stop, reads, writes):
        self.S.op("pe", lambda e: e.matmul(out, lhsT, rhs, start=start, stop=stop), reads, writes)

    def tr(self, out, in_, ident, reads, writes):
        self.S.op("pe", lambda e: e.transpose(out, in_, ident), reads, writes)

    def act(self, out, in_, func, reads, writes, bias=None, scale=None, eng="act"):
        kw = {}
        if bias is not None:
            kw["bias"] = bias
        if scale is not None:
            kw["scale"] = scale
        self.S.op(eng, lambda e: e.activation(out, in_, func, **kw), reads, writes)

    def tt(self, eng, out, in0, in1, op, reads, writes):
        self.S.op(eng, lambda e: e.tensor_tensor(out, in0, in1, op), reads, writes)

    def ts(self, eng, out, in0, s1, s2, op0, op1, reads, writes):
        if op1 is None:
            self.S.op(eng, lambda e: e.tensor_scalar(out, in0, s1, None, op0), reads, writes)
        else:
            self.S.op(eng, lambda e: e.tensor_scalar(out, in0, s1, s2, op0, op1), reads, writes)

    def stt(self, out, in0, scalar, in1, op0, op1, reads, writes):
        self.S.op("dve", lambda e: e.scalar_tensor_tensor(out, in0, scalar, in1, op0, op1), reads, writes)

    def cp(self, eng, out, in_, reads, writes):
        if eng == "act":
            self.S.op("act", lambda e: e.copy(out, in_), reads, writes)
        else:
            self.S.op(eng, lambda e: e.tensor_copy(out, in_), reads, writes)

    def memset(self, eng, ap, val, writes):
        self.S.op(eng, lambda e: e.memset(ap, val), (), writes)

    def load_w_bf16(self, dst, src, key, res, maxcols=2048):
        n = dst.shape[-1]
        c0 = 0
        while c0 < n:
            c1 = min(n, c0 + maxcols)
            self.S.dma("pool", key, dst[:, c0:c1], src[:, c0:c1], writes=[res])
            c0 = c1

    def build(self):
        nc, S = self.nc, self.S
        with ExitStack() as st:
            self.ident = st.enter_context(nc.sbuf_tensor("ident", [128, 128], F32))[:]
            self.identb = st.enter_context(nc.sbuf_tensor("identb", [128, 128], BF16))[:]
            self.modT = st.enter_context(nc.sbuf_tensor("modT", [128, DEPTH, self.nseq, 48], F32))[:]
            self.mhalf = st.enter_context(nc.sbuf_tensor("mhalf", [128, 1], F32))[:]
            self.pd = []
            self.pb = []
            for i in range(4):
                t = st.enter_context(nc.psum_tensor("pd%d" % i, [128, 1024], F32))[:]
                self.pd.append((t, "pb%d" % (2 * i)))
                self.pb += [(t[:, 0:512], "pb%d" % (2 * i)), (t[:, 512:1024], "pb%d" % (2 * i + 1))]
            self.py = self.pd[2]
            self.py1 = self.pd[3]
            self.memset("pool", self.ident, 1.0, ["ident"])
            S.op("pool", lambda e: e.affine_select(self.ident, self.ident, [[1, 128]], ALU.is_equal, 0.0,
                                                    base=0, channel_multiplier=-1), ["ident"], ["ident"])
            self.cp("dve", self.identb, self.ident, ["ident"], ["identb"])
            self.memset("pool", self.mhalf, -0.5, ["mhalf"])
            cur = self.din["x"]
            nxt = [self.xa, self.xb]
            nph = len([p for p in self.plan if p[0] != "mod"])
            k = 0
            for p in self.plan:
                if p[0] == "mod":
                    self.phase_mod()
                else:
                    k += 1
                    dst = self.dout if k == nph else nxt[k % 2]
                    if p[0] == "ffn":
                        self.phase_ffn(p[1], cur, dst)
                    elif p[0] == "gmlp":
                        self.phase_gmlp(p[1], cur, dst)
                    elif p[0] == "conv":
                        self.phase_conv(p[1], cur, dst)
                    elif p[0] == "attn":
                        self.phase_attn(p[1], p[2], cur, dst)
                    cur = dst
                S.barrier()
            S.emit(st)
        return nc

    def phase_mod(self):
        nc, S, ns = self.nc, self.S, self.nseq
        with ExitStack() as ph:
            cs = self.sb(ph, "cs", [ns, D], F32)
            ca = self.sb(ph, "ca", [ns, D], F32)
            caT = self.sb(ph, "caT", [128, 8, ns], F32)
            CB = self.sb(ph, "CB", [128, ns, 8, 128], F32)
            wst = [self.sb(ph, "wst%d" % i, [128, 8, 512], F32) for i in range(2)]
            mbb = [self.sb(ph, "mbb%d" % i, [128, 512], F32) for i in range(2)]
            mrow = [self.sb(ph, "mrow%d" % i, [128, 512], F32) for i in range(2)]
            stg = [self.sb(ph, "stg%d" % i, [48, 128], F32) for i in range(2)]
            S.dma("sp", "cs", cs, self.din["c"][:, :], writes=["cs"])
            self.act(ca, cs, AF.Silu, ["cs"], ["ca"])
            pbt, pbr = self.pb[0]
            for kc in range(8):
                self.tr(pbt[:, kc * ns:(kc + 1) * ns], ca[:, kc * 128:(kc + 1) * 128], self.ident[0:ns, 0:ns],
                        ["ca", "ident"], [pbr])
            self.cp("dve", caT, pbt[:, 0:8 * ns].rearrange("p (k b) -> p k b", b=ns), [pbr], ["caT"])
            for b in range(ns):
                self.cp("dve", CB[:, b], caT[:, :, b:b + 1].to_broadcast([128, 8, 128]), ["caT"], ["CB"])
            it = 0
            mi = 0
            for l in range(DEPTH):
                for blk in range(12):
                    sl = it % 2
                    it += 1
                    cols = slice(blk * 512, (blk + 1) * 512)
                    S.dma("sp", "wst%d" % sl, wst[sl],
                          self.din["mod_w"][l, :, cols].rearrange("(k p) f -> p k f", p=128), writes=["wst%d" % sl])
                    S.dma("sp", "mbb%d" % sl, mbb[sl], self.din["mod_b"][l, cols].partition_broadcast(128),
                          writes=["mbb%d" % sl])
                    for b in range(ns):
                        pt, pr = self.pb[1 + (mi % 2)]
                        ms = mi % 2
                        mi += 1
                        for kc in range(8):
                            self.mm(pt, CB[:, b, kc, :], wst[sl][:, kc, :], kc == 0, kc == 7,
                                    ["CB", "wst%d" % sl], [pr])
                        self.tt("dve", mrow[ms], pt, mbb[sl], ALU.add, [pr, "mbb%d" % sl], ["mrow%d" % ms])
                        S.dma("sp", "mrow%d" % ms, self.modrow[l, b:b + 1, cols], mrow[ms][0:1, :],
                              reads=["mrow%d" % ms], writes=["modrow_%d_%d_%d" % (l, b, blk)])
            si = 0
            for l in range(DEPTH):
                for b in range(ns):
                    sl = si % 2
                    si += 1
                    S.dma("sp", "stg%d" % sl, stg[sl], self.modrow[l, b, :].rearrange("(c p) -> c p", p=128),
                          reads=["modrow_%d_%d_%d" % (l, b, blk) for blk in range(12)], writes=["stg%d" % sl])
                    pt, pr = self.pb[3 + sl]
                    self.tr(pt[:, 0:48], stg[sl], self.ident[0:48, 0:48], ["stg%d" % sl, "ident"], [pr])
                    self.cp("dve", self.modT[:, l, b, :], pt[:, 0:48], [pr], ["modT"])
            for c0 in (8, 32):
                self.ts("dve", self.modT[:, :, :, c0:c0 + 8], self.modT[:, :, :, c0:c0 + 8], 1.0, None, ALU.add, None,
                        ["modT"], ["modT"])

    def load_bcast(self, ph, name, src_row):
        n = src_row.shape[-1]
        t = self.sb(ph, name, [128, n], F32)
        self._n += 1
        res = "%s_%d" % (name, self._n)
        self.S.dma("sp", res, t, src_row.partition_broadcast(128), writes=[res])
        return t, res

    def load_pp(self, ph, name, src2d, pbi=0):
        n = src2d.shape[0]
        stg = self.sb(ph, name + "s", [n, 128], F32)
        t = self.sb(ph, name, [128, n], F32)
        self._n += 1
        res = "%s_%d" % (name, self._n)
        self.S.dma("sp", res + "s", stg, src2d, writes=[res + "s"])
        pt, pr = self.pb[pbi]
        self.tr(pt[:, 0:n], stg, self.ident[0:n, 0:n], [res + "s", "ident"], [pr])
        self.cp("dve", t, pt[:, 0:n], [pr], [res])
        return t, res

    def epi_setup(self, ph, l, which, single_gb=False):
        gcol = 2048 if which == 0 else 5120
        GB = []
        if single_gb:
            t = self.sb(ph, "GBs", [128, D], F32)
            GB = [(t, "GBs")] * self.nseq
        else:
            for b in range(self.nseq):
                t, r = self.load_bcast(ph, "GB%d" % b, self.modrow[l, b, gcol:gcol + D])
                self.ts("dve", t, t, 1.0, None, ALU.add, None, [r], [r])
                GB.append((t, r))
        lng = self.load_bcast(ph, "lng", self.din["ln1_g" if which == 0 else "ln2_g"][l, :])
        lnb = self.load_bcast(ph, "lnb", self.din["ln1_b" if which == 0 else "ln2_b"][l, :])
        eb = []
        for i in range(2):
            eb.append(dict(
                buf=self.sb(ph, "ebuf%d" % i, [128, D], F32), st=self.sb(ph, "est%d" % i, [128, 12], F32),
                mv=self.sb(ph, "emv%d" % i, [128, 2], F32), sm=self.sb(ph, "esm%d" % i, [128, 4], F32),
                res="ebuf%d_%d" % (i, self._n)))
        return dict(GB=GB, lng=lng, lnb=lnb, eb=eb, i=0, single=single_gb, cur=None, l=l, gcol=gcol)

    def epi_select(self, E, b):
        if not E["single"] or E["cur"] == b:
            return
        E["cur"] = b
        t, r = E["GB"][b]
        self.S.dma("sp", r, t, self.modrow[E["l"], b, E["gcol"]:E["gcol"] + D].partition_broadcast(128), writes=[r])
        self.ts("dve", t, t, 1.0, None, ALU.add, None, [r], [r])

    def epilogue(self, E, b, y, yres, x, xres, out_rows, ybias=None):
        S = self.S
        e = E["eb"][E["i"] % 2]
        E["i"] += 1
        buf, r = e["buf"], e["res"]
        GBt, GBr = E["GB"][b]
        if ybias is not None:
            self.tt("dve", buf, y, ybias[0], ALU.add, [yres, ybias[1]], [r])
            self.tt("dve", buf, buf, GBt, ALU.mult, [r, GBr], [r])
        else:
            self.tt("dve", buf, y, GBt, ALU.mult, [yres, GBr], [r])
        self.stt(buf, x, ALPHA, buf, ALU.mult, ALU.add, [xres, r], [r])
        self.ln_rows(buf, r, e)
        self.tt("dve", buf, buf, E["lng"][0], ALU.mult, [r, E["lng"][1]], [r])
        self.tt("pool", buf, buf, E["lnb"][0], ALU.add, [r, E["lnb"][1]], [r])
        S.dma("sp", r, out_rows, buf, reads=[r])

    def ln_rows(self, buf, r, e, out=None, eng_norm="act"):
        S = self.S
        st, mv, sm = e["st"], e["mv"], e["sm"]
        rs = r + "s"
        S.op("dve", lambda en: en.bn_stats(st[:, 0:6], buf[:, 0:512]), [r], [rs])
        S.op("dve", lambda en: en.bn_stats(st[:, 6:12], buf[:, 512:1024]), [r], [rs])
        S.op("dve", lambda en: en.bn_aggr(mv, st), [rs], [rs])
        self.ts("dve", sm[:, 0:1], mv[:, 1:2], EPS, None, ALU.add, None, [rs], [rs])
        self.tt("pool", sm[:, 1:2], sm[:, 0:1], self.mhalf, ALU.pow, [rs, "mhalf"], [rs])
        self.ts("dve", sm[:, 2:3], mv[:, 0:1], sm[:, 1:2], -1.0, ALU.mult, ALU.mult, [rs], [rs])
        o = buf if out is None else out[0]
        wr = [r] if out is None else [out[1]]
        self.act(o, buf, AF.Identity, [r, rs], wr, bias=sm[:, 2:3], scale=sm[:, 1:2])

    def xT_mod(self, xt, xres, nsub, hT, hres, l, b, which, tpr):
        sh0 = 0 if which == 0 else 24
        sc0 = 8 if which == 0 else 32
        for kc in range(8):
            pt, pr = tpr.next()
            for s in range(nsub):
                self.tr(pt[:, s * 128:(s + 1) * 128], xt[:, s, kc * 128:(kc + 1) * 128], self.ident,
                        [xres, "ident"], [pr])
            self.act(hT[:, kc, 0:nsub * 128], pt[:, 0:nsub * 128], AF.Identity, [pr, "modT"], [hres],
                     bias=self.modT[:, l, b, sh0 + kc:sh0 + kc + 1], scale=self.modT[:, l, b, sc0 + kc:sc0 + kc + 1])

    def x_tile(self, X, b, t0, tt):
        return X[b, t0:t0 + tt, :].rearrange("(s p) d -> p s d", p=128)

    def phase_ffn(self, l, Xin, Xout):
        nc, S, ns, T = self.nc, self.S, self.nseq, self.T
        TT = 256
        nsub = TT // 128
        NF = FH // 128
        with ExitStack() as ph:
            w1 = self.sb(ph, "w1", [128, 8, 2 * FH], BF16)
            w2 = self.sb(ph, "w2", [128, NF, D], BF16)
            for kc in range(8):
                self.load_w_bf16(w1[:, kc, :], self.din["ffn_w_in"][l, kc * 128:(kc + 1) * 128, :], "w1", "w1", 1408)
            for fc in range(NF):
                self.load_w_bf16(w2[:, fc, :], self.din["ffn_w_out"][l, fc * 128:(fc + 1) * 128, :], "w2", "w2")
            E = self.epi_setup(ph, l, 1)
            xt = [(self.sb(ph, "xt%d" % i, [128, nsub, D], F32), "xt%d" % i) for i in range(2)]
            hT = [(self.sb(ph, "hT%d" % i, [128, 8, TT], BF16), "hT%d" % i) for i in range(2)]
            aT = (self.sb(ph, "aT", [128, NF, TT], BF16), "aT")
            sg = [(self.sb(ph, "sg%d" % i, [128, TT], F32), "sg%d" % i) for i in range(2)]
            tpr = Ring(self.pb[0:2])
            gur = Ring([(self.pb[2], self.pb[3]), (self.pb[6], self.pb[7])])
            tiles = [(b, t0) for b in range(ns) for t0 in range(0, T, TT)]

            def loads(i):
                b_, t0_ = tiles[i]
                S.dma("sp", xt[i % 2][1], xt[i % 2][0], self.x_tile(Xin, b_, t0_, TT), writes=[xt[i % 2][1]])

            loads(0)
            print("[sbuf] ffn remaining", nc.sbuf_bytes_remaining)
            for it, (b, t0) in enumerate(tiles):
                if True:
                    if it + 1 < len(tiles):
                        loads(it + 1)
                    xs, xr = xt[it % 2]
                    hs, hr = hT[it % 2]
                    self.xT_mod(xs, xr, nsub, hs, hr, l, b, 1, tpr)
                    for fc in range(NF):
                        (pg, pgr), (pu, pur) = gur.next()
                        for kc in range(8):
                            self.mm(pg[:, 0:TT], w1[:, kc, fc * 128:(fc + 1) * 128], hs[:, kc, :], kc == 0, kc == 7,
                                    ["w1", hr], [pgr])
                        for kc in range(8):
                            self.mm(pu[:, 0:TT], w1[:, kc, FH + fc * 128:FH + (fc + 1) * 128], hs[:, kc, :], kc == 0,
                                    kc == 7, ["w1", hr], [pur])
                        sgt, sgr = sg[fc % 2]
                        self.act(sgt, pg[:, 0:TT], AF.Silu, [pgr], [sgr])
                        self.tt("dve", aT[0][:, fc, :], sgt, pu[:, 0:TT], ALU.mult, [sgr, pur], [aT[1]])
                    for s in range(nsub):
                        pyt, pyr = self.py
                        for hf in range(2):
                            for fc in range(NF):
                                self.mm(pyt[:, hf * 512:(hf + 1) * 512], aT[0][:, fc, s * 128:(s + 1) * 128],
                                        w2[:, fc, hf * 512:(hf + 1) * 512], fc == 0, fc == NF - 1, [aT[1], "w2"], [pyr])
                        self.epilogue(E, b, pyt, pyr, xs[:, s, :], xr, Xout[b, t0 + s * 128:t0 + (s + 1) * 128, :])

    def phase_gmlp(self, l, Xin, Xout):
        nc, S, ns, T = self.nc, self.S, self.nseq, self.T
        TT = 512
        nsub = TT // 128
        W = self.din
        with ExitStack() as ph:
            wi = self.sb(ph, "wi", [128, 8, 2 * D], BF16)
            wo = self.sb(ph, "wo", [128, 8, D], BF16)
            for kc in range(8):
                self.load_w_bf16(wi[:, kc, :], W["gm_w_in"][0, kc * 128:(kc + 1) * 128, :], "wi", "wi")
                self.load_w_bf16(wo[:, kc, :], W["gm_w_out"][0, kc * 128:(kc + 1) * 128, :], "wo", "wo")
            wsl = self.sb(ph, "wsl", [128, 8, 128], F32)
            wsm = self.sb(ph, "wsm", [128, 8, 128], F32)
            wmT = self.sb(ph, "wmT", [128, 8, 128], BF16)
            S.dma("sp", "wsl", wsl, W["gm_w_s"][0].rearrange("g t s -> t g s"), writes=["wsl"])
            for g in range(8):
                pt, pr = self.pb[g % 2]
                self.tr(pt[:, 0:128], wsl[:, g, :], self.ident, ["wsl", "ident"], [pr])
                self.cp("dve", wsm[:, g, :], pt[:, 0:128], [pr], ["wsm"])
                S.op("pool", lambda e, g=g: e.affine_select(wmT[:, g, :], wsm[:, g, :], [[1, 128]], ALU.is_ge, 0.0,
                                                             base=0, channel_multiplier=-1), ["wsm"], ["wmT"])
            binu, binu_r = self.load_pp(ph, "binu", W["gm_b_in"][0, 0:D].rearrange("(c p) -> c p", p=128), 2)
            binv, binv_r = self.load_bcast(ph, "binv", W["gm_b_in"][0, D:2 * D])
            glg, glg_r = self.load_bcast(ph, "glg", W["gm_ln_g"][0, :])
            glb, glb_r = self.load_bcast(ph, "glb", W["gm_ln_b"][0, :])
            bsb, bsb_r = self.load_bcast(ph, "bsb", W["gm_b_s"][0].rearrange("g t -> (g t)"))
            bsb3 = bsb.rearrange("p (g t) -> p g t", t=128)
            E = self.epi_setup(ph, l, 0, single_gb=True)
            xt = [(self.sb(ph, "xt%d" % i, [128, nsub, D], F32), "xt%d" % i) for i in range(3)]
            hT = [(self.sb(ph, "hT%d" % i, [128, 8, TT], BF16), "hT%d" % i) for i in range(2)]
            vz = [dict(buf=self.sb(ph, "vz%d" % i, [128, D], F32), st=self.sb(ph, "vst%d" % i, [128, 12], F32),
                       mv=self.sb(ph, "vmv%d" % i, [128, 2], F32), sm=self.sb(ph, "vsm%d" % i, [128, 4], F32),
                       res="vz%d" % i) for i in range(2)]
            vn = [[(self.sb(ph, "vn%d_%d" % (k, i), [128, D], BF16), "vn%d_%d" % (k, i)) for i in range(nsub)] for k in range(2)]
            uT = [(self.sb(ph, "uT%d" % i, [128, TT], F32), "uT%d" % i) for i in range(2)]
            tmp = [(self.sb(ph, "tmp%d" % i, [128, TT], F32), "tmp%d" % i) for i in range(2)]
            yT = (self.sb(ph, "yT", [128, 8, TT], BF16), "yT")
            print("[sbuf] gmlp remaining", nc.sbuf_bytes_remaining)
            tpr = Ring(self.pb[0:2])
            tiles = [(b, t0) for b in range(ns) for t0 in range(0, T, TT)]
            cnt = dict(vi=0)

            def loads(i):
                b_, t0_ = tiles[i]
                S.dma("sp", xt[i % 3][1], xt[i % 3][0], self.x_tile(Xin, b_, t0_, TT), writes=[xt[i % 3][1]])

            def stageA(i):
                b, t0 = tiles[i]
                xs, xr = xt[i % 3]
                hs, hr = hT[i % 2]
                self.xT_mod(xs, xr, nsub, hs, hr, l, b, 0, tpr)
                for s in range(nsub):
                    pv, pvr = self.py1
                    for hf in range(2):
                        for kc in range(8):
                            self.mm(pv[:, hf * 512:(hf + 1) * 512], hs[:, kc, s * 128:(s + 1) * 128],
                                    wi[:, kc, D + hf * 512:D + (hf + 1) * 512], kc == 0, kc == 7, [hr, "wi"], [pvr])
                    z = vz[cnt["vi"] % 2]
                    cnt["vi"] += 1
                    vt, vr = vn[i % 2][s]
                    self.tt("dve", z["buf"], pv, binv, ALU.add, [pvr, binv_r], [z["res"]])
                    self.act(z["buf"], z["buf"], AF.Gelu, [z["res"]], [z["res"]])
                    self.ln_rows(z["buf"], z["res"], z)
                    self.tt("dve", z["buf"], z["buf"], glg, ALU.mult, [z["res"], glg_r], [z["res"]])
                    self.tt("pool", vt, z["buf"], glb, ALU.add, [z["res"], glb_r], [vr])

            def stageB(i):
                b, t0 = tiles[i]
                xs, xr = xt[i % 3]
                hs, hr = hT[i % 2]
                self.epi_select(E, b)
                for g in range(8):
                    pu, pur = self.pb[2]
                    psv, psvr = self.pb[3]
                    for kc in range(8):
                        self.mm(pu, wi[:, kc, g * 128:(g + 1) * 128], hs[:, kc, :], kc == 0, kc == 7, ["wi", hr], [pur])
                    ut, utr = uT[g % 2]
                    self.act(ut, pu, AF.Gelu, [pur, binu_r], [utr], bias=binu[:, g:g + 1])
                    for s in range(nsub):
                        vt, vr = vn[i % 2][s]
                        self.mm(psv[:, s * 128:(s + 1) * 128], vt[:, g * 128:(g + 1) * 128], wmT[:, g, :],
                                True, True, [vr, "wmT"], [psvr])
                    tm, tmr = tmp[g % 2]
                    self.tt("dve", tm.rearrange("p (s t) -> p s t", t=128), psv.rearrange("p (s t) -> p s t", t=128),
                            bsb3[:, g:g + 1, :].to_broadcast([128, nsub, 128]), ALU.add, [psvr, bsb_r], [tmr])
                    self.tt("dve" if g % 2 == 0 else "pool", yT[0][:, g, :], tm, ut, ALU.mult, [tmr, utr], [yT[1]])
                for s in range(nsub):
                    pyt, pyr = self.py
                    for hf in range(2):
                        for g in range(8):
                            self.mm(pyt[:, hf * 512:(hf + 1) * 512], yT[0][:, g, s * 128:(s + 1) * 128],
                                    wo[:, g, hf * 512:(hf + 1) * 512], g == 0, g == 7, [yT[1], "wo"], [pyr])
                    self.epilogue(E, b, pyt, pyr, xs[:, s, :], xr, Xout[b, t0 + s * 128:t0 + (s + 1) * 128, :])

            n = len(tiles)
            loads(0)
            if n > 1:
                loads(1)
            for i in range(n):
                stageA(i)
                if i >= 1:
                    stageB(i - 1)
                if i + 2 < n:
                    loads(i + 2)
            stageB(n - 1)

    def phase_conv(self, l, Xin, Xout):
        nc, S, ns, T = self.nc, self.S, self.nseq, self.T
        TT = 256
        nsub = TT // 128
        W = self.din
        HW = CONVW - 1
        with ExitStack() as ph:
            wi = self.sb(ph, "wi", [128, 8, 2 * D], BF16)
            wo = self.sb(ph, "wo", [128, 8, D], BF16)
            for kc in range(8):
                self.load_w_bf16(wi[:, kc, :], W["cv_w_in"][0, kc * 128:(kc + 1) * 128, :], "wi", "wi")
                self.load_w_bf16(wo[:, kc, :], W["cv_w_out"][0, kc * 128:(kc + 1) * 128, :], "wo", "wo")
            bia, bia_r = self.load_pp(ph, "bia", W["cv_b_in"][0, :].rearrange("(c p) -> c p", p=128), 2)
            dwr = W["cv_dw"][0].rearrange("i (c p) -> (i c) p", p=128)
            dwa, dwa_r = self.load_pp(ph, "dwa", dwr[0:124, :], 2)
            dwb_, dwb_r = self.load_pp(ph, "dwb", dwr[124:248, :], 3)
            dwbias, dwbias_r = self.load_pp(ph, "dwbias", W["cv_dw_b"][0, :].rearrange("(c p) -> c p", p=128), 2)
            clg, clg_r = self.load_pp(ph, "clg", W["cv_ln_g"][0, :].rearrange("(c p) -> c p", p=128), 3)
            clb, clb_r = self.load_pp(ph, "clb", W["cv_ln_b"][0, :].rearrange("(c p) -> c p", p=128), 2)
            bo32 = self.sb(ph, "bo32", [1, D], F32)
            bohi = self.sb(ph, "bohi", [1, D], BF16)
            bolo = self.sb(ph, "bolo", [1, D], BF16)
            one1 = self.sb(ph, "one1", [1, 128], BF16)
            S.dma("sp", "bo32", bo32, W["cv_b_out"][0:1, :], writes=["bo"])
            self.cp("dve", bohi, bo32, ["bo"], ["bohi"])
            self.tt("dve", bo32, bo32, bohi, ALU.subtract, ["bo", "bohi"], ["bo"])
            self.cp("dve", bolo, bo32, ["bo"], ["bolo"])
            self.memset("pool", one1, 1.0, ["one1"])
            diag = self.sb(ph, "diag", [128, CONVW * 8, 128], BF16)
            for half, (dt_, dr_) in enumerate(((dwa, dwa_r), (dwb_, dwb_r))):
                self.tt("dve", diag[:, half * 124:(half + 1) * 124, :],
                        self.ident[:, None, :].to_broadcast([128, 124, 128]),
                        dt_[:, :, None].to_broadcast([128, 124, 128]), ALU.mult, ["ident", dr_], ["diag"])
            onesf = self.sb(ph, "onesf", [128, 128], F32)
            self.memset("pool", onesf, 1.0, ["onesf"])
            E = self.epi_setup(ph, l, 0, single_gb=True)
            xt = [(self.sb(ph, "xt%d" % i, [128, nsub, D], F32), "xt%d" % i) for i in range(3)]
            hT = (self.sb(ph, "hT", [128, 8, TT], BF16), "hT")
            ybuf = (self.sb(ph, "ybuf", [128, 8, HW + TT], BF16), "ybuf")
            sgm = [(self.sb(ph, "sgm%d" % i, [128, TT], F32), "sgm%d" % i) for i in range(2)]
            zT = [(self.sb(ph, "zT%d" % i, [128, 8, TT], F32), "zT%d" % i) for i in range(2)]
            zq = [(self.sb(ph, "zq%d" % i, [128, TT], F32), "zq%d" % i) for i in range(2)]
            mean_t = [(self.sb(ph, "mean_t%d" % i, [128, TT], F32), "mean_t%d" % i) for i in range(2)]
            rstd_t = [(self.sb(ph, "rstd_t%d" % i, [128, TT], F32), "rstd_t%d" % i) for i in range(2)]
            sT = (self.sb(ph, "sT", [128, 8, TT], BF16), "sT")
            print("[sbuf] conv remaining", nc.sbuf_bytes_remaining)
            tpr = Ring(self.pb[0:1])
            tiles = [(b, t0) for b in range(ns) for t0 in range(0, T, TT)]

            def loads(i):
                b_, t0_ = tiles[i]
                S.dma("sp", xt[i % 3][1], xt[i % 3][0], self.x_tile(Xin, b_, t0_, TT), writes=[xt[i % 3][1]])

            def stageA(i):
                b, t0 = tiles[i]
                xs, xr = xt[i % 3]
                hs, hr = hT
                self.xT_mod(xs, xr, nsub, hs, hr, l, b, 0, tpr)
                if t0 == 0:
                    self.memset("pool", ybuf[0][:, :, 0:HW], 0.0, [ybuf[1]])
                else:
                    self.cp("pool", ybuf[0][:, :, 0:HW], ybuf[0][:, :, TT:TT + HW], [ybuf[1]], [ybuf[1]])
                for kc in range(8):
                    pa, par = self.pb[2]
                    pg, pgr = self.pb[3]
                    for k in range(8):
                        self.mm(pa[:, 0:TT], wi[:, k, kc * 128:(kc + 1) * 128], hs[:, k, :], k == 0, k == 7, ["wi", hr], [par])
                    for k in range(8):
                        self.mm(pg[:, 0:TT], wi[:, k, D + kc * 128:D + (kc + 1) * 128], hs[:, k, :], k == 0, k == 7,
                                ["wi", hr], [pgr])
                    sg_, sgr = sgm[kc % 2]
                    self.act(sg_, pg[:, 0:TT], AF.Sigmoid, [pgr, bia_r], [sgr], bias=bia[:, 8 + kc:9 + kc])
                    self.stt(ybuf[0][:, kc, HW:HW + TT], pa[:, 0:TT], bia[:, kc:kc + 1], sg_, ALU.add, ALU.mult,
                             [par, sgr, bia_r], [ybuf[1]])
                z_, zr = zT[i % 2]
                p1, p1r = self.pb[7]
                p2, p2r = self.pb[1]
                for kc in range(8):
                    pz, pzr = self.pb[6]
                    for t in range(CONVW):
                        self.mm(pz[:, 0:TT], diag[:, t * 8 + kc, :], ybuf[0][:, kc, t:t + TT], t == 0, t == CONVW - 1,
                                ["diag", ybuf[1]], [pzr])
                    self.act(z_[:, kc, :], pz[:, 0:TT], AF.Identity, [pzr, dwbias_r], [zr], bias=dwbias[:, kc:kc + 1])
                    q_, qr = zq[kc % 2]
                    S.op("act", lambda e, q_=q_, kc=kc: e.activation(q_, z_[:, kc, :], AF.Square), [zr], [qr])
                    self.mm(p1[:, 0:TT], onesf, z_[:, kc, :], kc == 0, kc == 7, ["onesf", zr], [p1r])
                    self.mm(p2[:, 0:TT], onesf, q_, kc == 0, kc == 7, ["onesf", qr], [p2r])
                m_, mr = mean_t[i % 2]
                r_, rr = rstd_t[i % 2]
                q, qr_ = r_, rr
                self.ts("dve", m_, p1[:, 0:TT], 1.0 / D, None, ALU.mult, None, [p1r], [mr])
                self.tt("dve", q, m_, m_, ALU.mult, [mr], [qr_])
                self.stt(q, p2[:, 0:TT], 1.0 / D, q, ALU.mult, ALU.subtract, [p2r, qr_], [qr_])
                self.ts("dve", q, q, EPS, None, ALU.add, None, [qr_], [qr_])
                self.act(r_, q, AF.Sqrt, [qr_], [rr])
                S.op("dve", lambda e, r_=r_: e.reciprocal(r_, r_), [rr], [rr])

            def stageB(i):
                b, t0 = tiles[i]
                xs, xr = xt[i % 3]
                z_, zr = zT[i % 2]
                m_, mr = mean_t[i % 2]
                r_, rr = rstd_t[i % 2]
                self.epi_select(E, b)
                mb = m_[:, None, :].to_broadcast([128, 8, TT])
                rb = r_[:, None, :].to_broadcast([128, 8, TT])
                self.tt("dve", z_, z_, mb, ALU.subtract, [zr, mr], [zr])
                self.tt("dve", z_, z_, rb, ALU.mult, [zr, rr], [zr])
                for kc in range(8):
                    self.act(sT[0][:, kc, :], z_[:, kc, :], AF.Silu, [zr, clg_r, clb_r], [sT[1]], bias=clb[:, kc:kc + 1],
                             scale=clg[:, kc:kc + 1])
                for s in range(nsub):
                    pyt, pyr = self.py
                    for hf in range(2):
                        cs = slice(hf * 512, (hf + 1) * 512)
                        for kc in range(8):
                            self.mm(pyt[:, cs], sT[0][:, kc, s * 128:(s + 1) * 128], wo[:, kc, cs], kc == 0, False,
                                    [sT[1], "wo"], [pyr])
                        self.mm(pyt[:, cs], one1, bohi[:, cs], False, False, ["one1", "bohi"], [pyr])
                        self.mm(pyt[:, cs], one1, bolo[:, cs], False, True, ["one1", "bolo"], [pyr])
                    self.epilogue(E, b, pyt, pyr, xs[:, s, :], xr, Xout[b, t0 + s * 128:t0 + (s + 1) * 128, :])

            n = len(tiles)
            loads(0)
            if n > 1:
                loads(1)
            for i in range(n):
                stageA(i)
                if i >= 1:
                    stageB(i - 1)
                if i + 2 < n:
                    loads(i + 2)
            stageB(n - 1)

    def phase_attn(self, l, kind, Xin, Xout):
        self.attn_proj(l, kind, Xin)
        self.S.barrier()
        if kind == "fox":
            self.attn_core_fox()
        else:
            self.attn_core_sb()
        self.S.barrier()
        self.attn_out(l, kind, Xin, Xout)

    def attn_proj(self, l, kind, Xin):
        nc, S, ns, T = self.nc, self.S, self.nseq, self.T
        TT = 512
        nsub = TT // 128
        W = self.din
        fox = kind == "fox"
        NC = 3 * D + (NH if fox else 0)
        wname = "fox_w_in" if fox else "sb_w_in"
        with ExitStack() as ph:
            wi = self.sb(ph, "wi", [128, 8, NC], BF16)
            for kc in range(8):
                self.load_w_bf16(wi[:, kc, :], W[wname][0, kc * 128:(kc + 1) * 128, :], "wi", "wi", 1024 + (NH if fox else 0))
            xt = [(self.sb(ph, "xt%d" % i, [128, nsub, D], F32), "xt%d" % i) for i in range(2)]
            hT = [(self.sb(ph, "hT%d" % i, [128, 8, TT], BF16), "hT%d" % i) for i in range(2)]
            qst = [(self.sb(ph, "qst%d" % i, [128, 8, TT], BF16), "qst%d" % i) for i in range(2)]
            kst = [(self.sb(ph, "kst%d" % i, [128, 8, TT], BF16), "kst%d" % i) for i in range(2)]
            vst = [(self.sb(ph, "vst%d" % i, [128, nsub, D], BF16), "vst%d" % i) for i in range(2)]
            if fox:
                nbf = self.sb(ph, "nbf", [NH, 1], F32)
                S.dma("sp", "nbf", nbf, W["fox_b_f"][0, :].rearrange("(h o) -> h o", o=1), writes=["nbf"])
                self.ts("dve", nbf, nbf, -1.0, None, ALU.mult, None, ["nbf"], ["nbf"])
                ones16 = self.sb(ph, "ones16", [NH, TT], F32)
                self.memset("dve", ones16, 1.0, ["ones16"])
                ef = self.sb(ph, "ef", [NH, TT], F32)
                cum = [(self.sb(ph, "cum%d" % i, [NH, TT], F32), "cum%d" % i) for i in range(2)]
                s8 = self.sb(ph, "s8", [NH, TT], F32)
                r1 = self.sb(ph, "r1", [NH, TT], F32)
                fk3 = [(self.sb(ph, "fk3%d" % i, [NH, 3, TT], BF16), "fk3%d" % i) for i in range(2)]
                fq3 = [(self.sb(ph, "fq3%d" % i, [NH, 3, TT], BF16), "fq3%d" % i) for i in range(2)]
            tpr = Ring(self.pb[0:2])
            qkr = Ring(self.pb[2:4])
            ev = 0
            tiles = [(b, t0) for b in range(ns) for t0 in range(0, T, TT)]

            def loads(i):
                b_, t0_ = tiles[i]
                S.dma("sp", xt[i % 2][1], xt[i % 2][0], self.x_tile(Xin, b_, t0_, TT), writes=[xt[i % 2][1]])

            loads(0)
            for it, (b, t0) in enumerate(tiles):
                if True:
                    if it + 1 < len(tiles):
                        loads(it + 1)
                    sl = it % 2
                    xs, xr = xt[sl]
                    hs, hr = hT[sl]
                    self.xT_mod(xs, xr, nsub, hs, hr, l, b, 0, tpr)
                    for (stg, c0, dst) in ((qst[sl], 0, self.qt_d), (kst[sl], D, self.kt_d)):
                        for j in range(8):
                            pt, pr = qkr.next()
                            for kc in range(8):
                                self.mm(pt, wi[:, kc, c0 + j * 128:c0 + (j + 1) * 128], hs[:, kc, :], kc == 0, kc == 7,
                                        ["wi", hr], [pr])
                            self.cp("act" if ev % 2 == 0 else "dve", stg[0][:, j, :], pt, [pr], [stg[1]])
                            ev += 1
                        S.dma("sp", stg[1], dst[b].rearrange("(j hh) d t -> (hh d) j t", hh=2)[:, :, t0:t0 + TT], stg[0],
                              reads=[stg[1]])
                    for s in range(nsub):
                        pv, pvr = self.py
                        for hf in range(2):
                            for kc in range(8):
                                self.mm(pv[:, hf * 512:(hf + 1) * 512], hs[:, kc, s * 128:(s + 1) * 128],
                                        wi[:, kc, 2 * D + hf * 512:2 * D + (hf + 1) * 512], kc == 0, kc == 7, [hr, "wi"], [pvr])
                        self.cp("act" if ev % 2 == 0 else "dve", vst[sl][0][:, s, :], pv, [pvr], [vst[sl][1]])
                        ev += 1
                    S.dma("sp", vst[sl][1], self.v_d[b, t0:t0 + TT, :].rearrange("(s p) d -> p s d", p=128), vst[sl][0],
                          reads=[vst[sl][1]])
                    if fox:
                        pf, pfr = self.pb[6]
                        for kc in range(8):
                            self.mm(pf[0:NH, :], wi[:, kc, 3 * D:3 * D + NH], hs[:, kc, :], kc == 0, kc == 7, ["wi", hr], [pfr])
                        self.act(ef, pf[0:NH, :], AF.Exp, [pfr, "nbf"], ["ef"], bias=nbf, scale=-1.0)
                        self.act(ef, ef, AF.Ln, ["ef"], ["ef"], bias=1.0)
                        cm, cmr = cum[sl]
                        pc, pcr = cum[1 - sl]
                        init = 0.0 if t0 == 0 else pc[:, TT - 1:TT]
                        S.op("dve", lambda e, cm=cm, init=init: e.tensor_tensor_scan(cm, ones16, ef, init, ALU.mult, ALU.add),
                             ["ones16", "ef"] + ([] if t0 == 0 else [pcr]), [cmr])
                        fk, fkr = fk3[sl]
                        fq, fqr = fq3[sl]
                        self.ts("dve", s8, cm, 8.0, None, ALU.mult, None, [cmr], ["s8"])
                        self.cp("dve", fk[:, 0, :], s8, ["s8"], [fkr])
                        self.tt("dve", r1, s8, fk[:, 0, :], ALU.subtract, ["s8", fkr], ["r1"])
                        self.cp("dve", fk[:, 1, :], r1, ["r1"], [fkr])
                        self.tt("dve", r1, r1, fk[:, 1, :], ALU.subtract, ["r1", fkr], ["r1"])
                        self.cp("dve", fk[:, 2, :], r1, ["r1"], [fkr])
                        self.ts("dve", fq, fk, -1.0, None, ALU.mult, None, [fkr], [fqr])
                        S.dma("sp", fkr, self.fk_d[b, :, :, t0:t0 + TT], fk, reads=[fkr])
                        S.dma("sp", fqr, self.fq_d[b, :, :, t0:t0 + TT], fq, reads=[fqr])

    def attn_core_fox(self):
        nc, S, ns, T = self.nc, self.S, self.nseq, self.T
        NKB = T // 128
        QW = 1024
        NQ = T // QW
        KA = DH + 6
        LA = 2
        with ExitStack() as ph:
            kta = [(self.sb(ph, "kta%d" % i, [KA, T], BF16), "kta%d" % i) for i in range(3)]
            qta = [(self.sb(ph, "qta%d" % i, [KA, T], BF16), "qta%d" % i) for i in range(3)]
            va = [(self.sb(ph, "va%d" % i, [128, NKB, DH + 1], BF16), "va%d" % i) for i in range(3)]
            PT = [(self.sb(ph, "PT%d" % i, [128, QW], BF16), "PT%d" % i) for i in range(4)]
            ost = [(self.sb(ph, "ost%d" % i, [DH + 1, QW], F32), "ost%d" % i) for i in range(2)]
            for i in range(3):
                self.memset("dve", kta[i][0][DH:KA, :], 1.0, [kta[i][1]])
                self.memset("dve", qta[i][0][DH:KA, :], 1.0, [qta[i][1]])
                self.memset("pool", va[i][0][:, :, DH:DH + 1], 1.0, [va[i][1]])
            psr = Ring(self.pd[0:2])
            por = Ring(self.pd[2:4])
            ptr = Ring(PT)
            osr = Ring(ost)
            pend = []

            def halves(c0):
                return [(max(c0, hf * 512), (hf + 1) * 512) for hf in range(2) if max(c0, hf * 512) < (hf + 1) * 512]

            def stage2(tl):
                (b, h, j, kb, c0, nkb, sl, po, por_, pt, ptr_) = tl
                for (a0, a1) in halves(c0):
                    self.mm(po[0:DH + 1, a0:a1], va[sl][0][:, kb, :], pt[:, a0:a1], kb == 0, kb == nkb - 1,
                            [va[sl][1], ptr_], [por_])
                if kb == nkb - 1:
                    os_, osr_ = osr.next()
                    self.cp("dve", os_, po[0:DH + 1, :], [por_], [osr_])
                    S.dma("sp", osr_, self.o_d[b, h, :, j * QW:(j + 1) * QW], os_[0:DH, :], reads=[osr_])
                    S.dma("sp", osr_, self.den_d[b, h:h + 1, j * QW:(j + 1) * QW], os_[DH:DH + 1, :], reads=[osr_])

            heads = [(b, h) for b in range(ns) for h in range(NH)]

            def loads(i):
                b_, h_ = heads[i]
                sl_ = i % 3
                S.dma("sp", kta[sl_][1], kta[sl_][0][0:DH, :], self.kt_d[b_, h_], writes=[kta[sl_][1]])
                S.dma("sp", kta[sl_][1], kta[sl_][0][DH:DH + 3, :], self.fk_d[b_, h_], writes=[kta[sl_][1]])
                S.dma("sp", qta[sl_][1], qta[sl_][0][0:DH, :], self.qt_d[b_, h_], writes=[qta[sl_][1]])
                S.dma("sp", qta[sl_][1], qta[sl_][0][DH + 3:DH + 6, :], self.fq_d[b_, h_], writes=[qta[sl_][1]])
                S.dma("sp", va[sl_][1], va[sl_][0][:, :, 0:DH],
                      self.v_d[b_, :, h_ * DH:(h_ + 1) * DH].rearrange("(k p) d -> p k d", p=128), writes=[va[sl_][1]])

            loads(0)
            for hi, (b, h) in enumerate(heads):
                sl = hi % 3
                if hi + 1 < len(heads):
                    loads(hi + 1)
                for j in range(NQ):
                    nkb = 8 * j + 8
                    po, por_ = por.next()
                    for kb in range(nkb):
                        c0 = (kb - 8 * j) * 128 if kb >= 8 * j else 0
                        ps, psr_ = psr.next()
                        pt, ptr_ = ptr.next()
                        for (a0, a1) in halves(c0):
                            self.mm(ps[:, a0:a1], kta[sl][0][:, kb * 128:(kb + 1) * 128],
                                    qta[sl][0][:, j * QW + a0:j * QW + a1], True, True, [kta[sl][1], qta[sl][1]], [psr_])
                        self.act(pt[:, c0:QW], ps[:, c0:QW], AF.Exp, [psr_], [ptr_], scale=0.125)
                        if kb >= 8 * j:
                            S.op("pool", lambda e, pt=pt, c0=c0: e.affine_select(
                                pt[:, c0:c0 + 128], pt[:, c0:c0 + 128], [[1, 128]], ALU.is_ge, 0.0, base=0,
                                channel_multiplier=-1), [ptr_], [ptr_])
                        pend.append((b, h, j, kb, c0, nkb, sl, po, por_, pt, ptr_))
                        if len(pend) > LA:
                            stage2(pend.pop(0))
            while pend:
                stage2(pend.pop(0))

    def attn_core_sb(self):
        nc, S, ns, T = self.nc, self.S, self.nseq, self.T
        NKB = T // 128
        QW = 1024
        NQ = T // QW
        with ExitStack() as ph:
            kta = [(self.sb(ph, "kta%d" % i, [DH, T], BF16), "kta%d" % i) for i in range(3)]
            qta = [(self.sb(ph, "qta%d" % i, [DH, T], BF16), "qta%d" % i) for i in range(3)]
            va = [(self.sb(ph, "va%d" % i, [128, NKB, DH], BF16), "va%d" % i) for i in range(3)]
            Et = [(self.sb(ph, "Et%d" % i, [128, QW], F32), "Et%d" % i) for i in range(2)]
            SPt = [(self.sb(ph, "SPt%d" % i, [128, QW], BF16), "SPt%d" % i) for i in range(3)]
            X2 = [(self.sb(ph, "X2%d" % i, [128, QW], F32), "X2%d" % i) for i in range(2)]
            At = [(self.sb(ph, "At%d" % i, [128, QW], BF16), "At%d" % i) for i in range(3)]
            Ccs = [(self.sb(ph, "Cc%d" % i, [128, QW], F32), "Cc%d" % i) for i in range(2)]
            ccr = Ring(Ccs)
            ost = [(self.sb(ph, "ost%d" % i, [DH, QW], F32), "ost%d" % i) for i in range(2)]
            trin = self.sb(ph, "trin", [128, 128], BF16)
            onen = self.sb(ph, "onen", [128, 128], BF16)
            self.memset("pool", trin, -8.0, ["trin"])
            S.op("pool", lambda e: e.affine_select(trin, trin, [[-1, 128]], ALU.is_ge, 0.0, base=0, channel_multiplier=1),
                 ["trin"], ["trin"])
            self.memset("pool", onen, -8.0, ["onen"])
            psr = Ring(self.pd[0:2])
            pyy, pyyr = self.pd[2]
            po_, por_ = self.pd[3]
            etr, spr, x2r, atr, osr = Ring(Et), Ring(SPt), Ring(X2), Ring(At), Ring(ost)
            q1, q2 = [], []

            def halves(c0):
                return [(max(c0, hf * 512), (hf + 1) * 512) for hf in range(2) if max(c0, hf * 512) < (hf + 1) * 512]

            def mask(tile_ap, res, c0):
                S.op("pool", lambda e: e.affine_select(tile_ap[:, c0:c0 + 128], tile_ap[:, c0:c0 + 128], [[1, 128]], ALU.is_gt,
                                                        0.0, base=0, channel_multiplier=-1), [res], [res])

            def stageA1(tl):
                ps, psr_ = tl["ps"]
                c0, sl, kb, j = tl["c0"], tl["sl"], tl["kb"], tl["j"]
                for (a0, a1) in halves(c0):
                    self.mm(ps[:, a0:a1], kta[sl][0][:, kb * 128:(kb + 1) * 128], qta[sl][0][:, j * QW + a0:j * QW + a1],
                            True, True, [kta[sl][1], qta[sl][1]], [psr_])
                et, etr_ = etr.next()
                tl["et"] = (et, etr_)
                self.act(et[:, c0:QW], ps[:, c0:QW], AF.Exp, [psr_], [etr_], scale=0.125)

            def stageA2(tl):
                et, etr_ = tl["et"]
                c0 = tl["c0"]
                sp, spr_ = spr.next()
                tl["sp"] = (sp, spr_)
                self.act(sp[:, c0:QW], et[:, c0:QW], AF.Ln, [etr_], [spr_], bias=1.0)
                if tl["diag"]:
                    mask(sp, spr_, c0)

            def stageB(tl):
                ps, psr_ = tl["ps"]
                sp, spr_ = tl["sp"]
                c0 = tl["c0"]
                Cc = tl["Cc"]
                for (a0, a1) in halves(c0):
                    S.op("pe", lambda e, a0=a0, a1=a1: e.matmul(ps[:, a0:a1], trin, sp[:, a0:a1], start=False, stop=True,
                                                               skip_group_check=True), ["trin", spr_], [psr_])
                for (a0, a1) in halves(c0):
                    self.mm(pyy[:, a0:a1], onen, sp[:, a0:a1], True, True, ["onen", spr_], [pyyr])
                x2, x2r_ = x2r.next()
                at, atr_ = atr.next()
                tl["at"] = (at, atr_)
                self.tt("dve", x2[:, c0:QW], ps[:, c0:QW], Cc[0][:, c0:QW], ALU.add, [psr_, Cc[1]], [x2r_])
                self.tt("dve", Cc[0][:, c0:QW], pyy[:, c0:QW], Cc[0][:, c0:QW], ALU.add, [pyyr, Cc[1]], [Cc[1]])
                self.act(at[:, c0:QW], x2[:, c0:QW], AF.Exp, [x2r_], [atr_], scale=0.125)
                if tl["diag"]:
                    mask(at, atr_, c0)

            def stageC(tl):
                at, atr_ = tl["at"]
                c0, sl, kb = tl["c0"], tl["sl"], tl["kb"]
                st = tl["started"]
                for hf, (a0, a1) in [(a0 // 512, (a0, a1)) for (a0, a1) in halves(c0)]:
                    self.mm(po_[0:DH, a0:a1], va[sl][0][:, kb, :], at[:, a0:a1], not st[hf], tl["last"], [va[sl][1], atr_], [por_])
                    st[hf] = True
                if tl["last"]:
                    os_, osr_ = osr.next()
                    self.cp("act", os_, po_[0:DH, :], [por_], [osr_])
                    S.dma("sp", osr_, self.o_d[tl["b"], tl["h"], :, tl["j"] * QW:(tl["j"] + 1) * QW], os_, reads=[osr_])

            def push(tl):
                stageA1(tl)
                stageA2(tl)
                if q1:
                    t2 = q1.pop(0)
                    stageB(t2)
                    q2.append(t2)
                q1.append(tl)
                while len(q2) > 1:
                    stageC(q2.pop(0))

            heads = [(b, h) for b in range(ns) for h in range(NH)]

            def loads(i):
                b_, h_ = heads[i]
                sl_ = i % 3
                S.dma("sp", kta[sl_][1], kta[sl_][0], self.kt_d[b_, h_], writes=[kta[sl_][1]])
                S.dma("sp", qta[sl_][1], qta[sl_][0], self.qt_d[b_, h_], writes=[qta[sl_][1]])
                S.dma("sp", va[sl_][1], va[sl_][0],
                      self.v_d[b_, :, h_ * DH:(h_ + 1) * DH].rearrange("(k p) d -> p k d", p=128), writes=[va[sl_][1]])

            loads(0)
            for hi, (b, h) in enumerate(heads):
                sl = hi % 3
                if hi + 1 < len(heads):
                    loads(hi + 1)
                for j in range(NQ):
                    nkb = 8 * j + 8
                    Cc = ccr.next()
                    self.memset("pool", Cc[0], 0.0, [Cc[1]])
                    started = [False, False]
                    for n, kb in enumerate(range(nkb - 1, -1, -1)):
                        diag = kb >= 8 * j
                        c0 = (kb - 8 * j) * 128 if diag else 0
                        tl = dict(b=b, h=h, j=j, kb=kb, c0=c0, sl=sl, diag=diag, started=started,
                                  last=(kb == 0), ps=psr.next(), Cc=Cc)
                        push(tl)
            while q1:
                t2 = q1.pop(0)
                stageB(t2)
                q2.append(t2)
            while q2:
                stageC(q2.pop(0))

    def attn_out(self, l, kind, Xin, Xout):
        nc, S, ns, T = self.nc, self.S, self.nseq, self.T
        TT = 512
        nsub = TT // 128
        W = self.din
        fox = kind == "fox"
        with ExitStack() as ph:
            wo = self.sb(ph, "wo", [128, 8, D], BF16)
            for kc in range(8):
                self.load_w_bf16(wo[:, kc, :], W["fox_w_out" if fox else "sb_w_out"][0, kc * 128:(kc + 1) * 128, :], "wo", "wo")
            E = self.epi_setup(ph, l, 0)
            xt = [(self.sb(ph, "xt%d" % i, [128, nsub, D], F32), "xt%d" % i) for i in range(2)]
            oT = [(self.sb(ph, "oT%d" % i, [128, 8, TT], F32), "oT%d" % i) for i in range(2)]
            oTb = [(self.sb(ph, "oTb%d" % i, [128, 8, TT], BF16), "oTb%d" % i) for i in range(2)]
            if fox:
                dnb = [(self.sb(ph, "dnb%d" % i, [128, 8, TT], F32), "dnb%d" % i) for i in range(2)]
            tiles = [(b, t0) for b in range(ns) for t0 in range(0, T, TT)]

            def loads(i):
                b_, t0_ = tiles[i]
                sl_ = i % 2
                S.dma("sp", xt[sl_][1], xt[sl_][0], self.x_tile(Xin, b_, t0_, TT), writes=[xt[sl_][1]])
                S.dma("sp", oT[sl_][1], oT[sl_][0],
                      self.o_d[b_].rearrange("(j hh) d t -> (hh d) j t", hh=2)[:, :, t0_:t0_ + TT], writes=[oT[sl_][1]])
                if fox:
                    dv = self.den_d[b_].rearrange("(j hh) t -> hh j t", hh=2)
                    for hh in range(2):
                        S.dma("sp", dnb[sl_][1], dnb[sl_][0][hh * DH:(hh + 1) * DH],
                              dv[hh, :, t0_:t0_ + TT].partition_broadcast(DH), writes=[dnb[sl_][1]])

            loads(0)
            for it, (b, t0) in enumerate(tiles):
                if True:
                    if it + 1 < len(tiles):
                        loads(it + 1)
                    sl = it % 2
                    xs, xr = xt[sl]
                    o_, or_ = oT[sl]
                    ob, obr = oTb[sl]
                    if fox:
                        dn, dnr = dnb[sl]
                        S.op("dve", lambda e, dn=dn: e.reciprocal(dn, dn), [dnr], [dnr])
                        self.tt("dve", ob, o_, dn, ALU.mult, [or_, dnr], [obr])
                    else:
                        self.cp("pool", ob, o_, [or_], [obr])
                    for s in range(nsub):
                        pyt, pyr = self.py
                        for hf in range(2):
                            for j in range(8):
                                self.mm(pyt[:, hf * 512:(hf + 1) * 512], ob[:, j, s * 128:(s + 1) * 128],
                                        wo[:, j, hf * 512:(hf + 1) * 512], j == 0, j == 7, [obr, "wo"], [pyr])
                        self.epilogue(E, b, pyt, pyr, xs[:, s, :], xr, Xout[b, t0 + s * 128:t0 + (s + 1) * 128, :])


def _plan_full():
    plan = [("mod",)]
    plan += [("gmlp", 0), ("ffn", 0), ("attn", 1, "fox"), ("ffn", 1), ("attn", 2, "sb"), ("ffn", 2), ("conv", 3), ("ffn", 3)]
    return plan


_CACHE = {}


def kernel(**inputs):
    ncores = 8
    nseq = 2
    T = 4096
    if "nc" not in _CACHE:
        _CACHE["nc"] = Builder(nseq, T, _plan_full()).build()
    nc = _CACHE["nc"]
    x = np.ascontiguousarray(inputs["x"], dtype=np.float32)
    c = np.ascontiguousarray(inputs["c"], dtype=np.float32)
    in_maps = []
    for i in range(ncores):
        m = {k: np.ascontiguousarray(inputs[k], dtype=np.float32) for k in WEIGHT_SHAPES}
        m["x"] = x[i * nseq:(i + 1) * nseq]
        m["c"] = c[i * nseq:(i + 1) * nseq]
        in_maps.append(m)
    res = run_bass_kernel_spmd(nc, in_maps, core_ids=list(range(ncores)))
    return np.concatenate([r["out"] for r in res.results], axis=0)
```

```python
import numpy as np
from contextlib import ExitStack
import concourse.bass as bass
import concourse.mybir as mybir
from concourse.bass_utils import run_bass_kernel_spmd

F32 = mybir.dt.float32
BF16 = mybir.dt.bfloat16
AF = mybir.ActivationFunctionType
ALU = mybir.AluOpType

D = 1024
FH = 2816
NH = 16
DH = 64
DEPTH = 4
ALPHA = float((2.0 * DEPTH) ** 0.25)
EPS = 1e-5
CONVW = 31

ENGS = ["pe", "act", "dve", "pool", "sp"]

WEIGHT_SHAPES = {
    "mod_w": (4, 1024, 6144), "mod_b": (4, 6144), "ln1_g": (4, 1024), "ln1_b": (4, 1024),
    "ln2_g": (4, 1024), "ln2_b": (4, 1024), "ffn_w_in": (4, 1024, 5632), "ffn_w_out": (4, 2816, 1024),
    "gm_w_in": (1, 1024, 2048), "gm_b_in": (1, 2048), "gm_ln_g": (1, 1024), "gm_ln_b": (1, 1024),
    "gm_w_s": (1, 8, 128, 128), "gm_b_s": (1, 8, 128), "gm_w_out": (1, 1024, 1024),
    "fox_w_in": (1, 1024, 3088), "fox_b_f": (1, 16), "fox_w_out": (1, 1024, 1024),
    "sb_w_in": (1, 1024, 3072), "sb_w_out": (1, 1024, 1024),
    "cv_w_in": (1, 1024, 2048), "cv_b_in": (1, 2048), "cv_dw": (1, 31, 1024), "cv_dw_b": (1, 1024),
    "cv_ln_g": (1, 1024), "cv_ln_b": (1, 1024), "cv_w_out": (1, 1024, 1024), "cv_b_out": (1, 1024),
}


class Sched:
    def __init__(self, nc, same_engine_raw=True):
        self.nc = nc
        self.ops = {e: [] for e in ENGS}
        self.last_w = {}
        self.readers = {}
        self.seen = {e: {} for e in ENGS}
        self.dma_cnt = []
        self.keymap = {}
        self.same_engine_raw = same_engine_raw

    def _add_wait(self, eng, waits, ev, is_raw):
        if ev is None:
            return
        if ev[0] == "eng":
            _, e2, idx = ev
            if e2 == eng and (eng == "pe" or not self.same_engine_raw):
                return
            key, val = ("eng", e2), idx
        else:
            key, val = ("dma", ev[1]), ev[2]
        if self.seen[eng].get(key, -1) >= val:
            return
        if val > waits.get(key, -1):
            waits[key] = val

    def _deps(self, eng, reads, writes):
        waits = {}
        for r in reads:
            self._add_wait(eng, waits, self.last_w.get(r), True)
        for w in writes:
            self._add_wait(eng, waits, self.last_w.get(w), False)
            for ev in self.readers.get(w, ()):
                self._add_wait(eng, waits, ev, False)
        for k, v in waits.items():
            self.seen[eng][k] = v
        return waits

    def _commit(self, ev, reads, writes):
        for r in reads:
            self.readers.setdefault(r, []).append(ev)
        for w in writes:
            self.last_w[w] = ev
            self.readers[w] = []

    def op(self, eng, fn, reads=(), writes=()):
        waits = self._deps(eng, reads, writes)
        idx = len(self.ops[eng])
        self.ops[eng].append(dict(kind="op", fn=fn, waits=waits, needed=False))
        self._commit(("eng", eng, idx), reads, writes)

    def dma(self, eng, key, out, in_, reads=(), writes=(), **kw):
        waits = self._deps(eng, reads, writes)
        if key not in self.keymap:
            self.keymap[key] = len(self.keymap)
            if len(self.dma_cnt) < len(self.keymap):
                self.dma_cnt.append(0)
        ph = self.keymap[key]
        self.dma_cnt[ph] += 1
        self.ops[eng].append(dict(kind="dma", out=out, in_=in_, kw=kw, key=ph, waits=waits))
        self._commit(("dma", ph, 16 * self.dma_cnt[ph]), reads, writes)

    def _wait_everything(self, eng):
        waits = {}
        for ev in self.last_w.values():
            self._add_wait(eng, waits, ev, False)
        for evs in self.readers.values():
            for ev in evs:
                self._add_wait(eng, waits, ev, False)
        for k, v in waits.items():
            self.seen[eng][k] = v
        self.ops[eng].append(dict(kind="waitonly", waits=waits))

    def barrier(self):
        for e in ENGS:
            self._wait_everything(e)
        self.last_w = {}
        self.readers = {}
        self.keymap = {}

    def emit(self, stack):
        nc = self.nc
        for e in ENGS:
            for o in self.ops[e]:
                for k, v in o["waits"].items():
                    if k[0] == "eng":
                        self.ops[k[1]][v]["needed"] = True
        semval = {}
        for e in ENGS:
            c = 0
            for i, o in enumerate(self.ops[e]):
                if o["kind"] == "op" and o["needed"]:
                    c += 1
                    semval[(e, i)] = c
        esem = {e: stack.enter_context(nc.semaphore("s_" + e)) for e in ENGS}
        dsem = [stack.enter_context(nc.semaphore("d%d" % i)) for i in range(len(self.dma_cnt))]
        print("[sched] ops:", {e: len(self.ops[e]) for e in ENGS}, "sem max:", {e: max([0] + [v for (ee, _), v in semval.items() if ee == e]) for e in ENGS},
              "dma sems:", len(self.dma_cnt), "max dma cnt:", max(self.dma_cnt) * 16)
        block = stack.enter_context(nc.Block())

        def run(e, eng):
            for o in self.ops[e]:
                for k, v in o["waits"].items():
                    if k[0] == "eng":
                        eng.wait_ge(esem[k[1]], semval[(k[1], v)])
                    else:
                        eng.wait_ge(dsem[k[1]], v)
                if o["kind"] == "op":
                    ins = o["fn"](eng)
                    if o["needed"]:
                        ins.then_inc(esem[e], 1)
                elif o["kind"] == "dma":
                    eng.dma_start(out=o["out"], in_=o["in_"], **o["kw"]).then_inc(dsem[o["key"]], 16)

        @block.tensor
        def _(eng):
            run("pe", eng)

        @block.scalar
        def _(eng):
            run("act", eng)

        @block.vector
        def _(eng):
            run("dve", eng)

        @block.gpsimd
        def _(eng):
            run("pool", eng)

        @block.sync
        def _(eng):
            run("sp", eng)


class Ring:
    def __init__(self, items):
        self.items = list(items)
        self.i = 0

    def next(self):
        it = self.items[self.i % len(self.items)]
        self.i += 1
        return it


class Builder:
    def __init__(self, nseq, T, plan, same_engine_raw=True):
        self.nseq, self.T, self.plan = nseq, T, plan
        self.nc = nc = bass.Bass("TRN2", target_bir_lowering=False)
        self.S = Sched(nc, same_engine_raw)
        self._n = 0
        self.din = {}
        self.din["x"] = nc.dram_tensor("x", [nseq, T, D], F32, kind="ExternalInput").ap()
        self.din["c"] = nc.dram_tensor("c", [nseq, D], F32, kind="ExternalInput").ap()
        for k, shp in WEIGHT_SHAPES.items():
            self.din[k] = nc.dram_tensor(k, list(shp), F32, kind="ExternalInput").ap()
        self.dout = nc.dram_tensor("out", [nseq, T, D], F32, kind="ExternalOutput").ap()
        self.xa = nc.dram_tensor("xa_s", [nseq, T, D], F32).ap()
        self.xb = nc.dram_tensor("xb_s", [nseq, T, D], F32).ap()
        self.modrow = nc.dram_tensor("modrow_s", [DEPTH, nseq, 6 * D], F32).ap()
        self.qt_d = nc.dram_tensor("qt_s", [nseq, NH, DH, T], BF16).ap()
        self.kt_d = nc.dram_tensor("kt_s", [nseq, NH, DH, T], BF16).ap()
        self.v_d = nc.dram_tensor("v_s", [nseq, T, D], BF16).ap()
        self.fq_d = nc.dram_tensor("fq_s", [nseq, NH, 3, T], BF16).ap()
        self.fk_d = nc.dram_tensor("fk_s", [nseq, NH, 3, T], BF16).ap()
        self.o_d = nc.dram_tensor("o_s", [nseq, NH, DH, T], F32).ap()
        self.den_d = nc.dram_tensor("den_s", [nseq, NH, T], F32).ap()

    def sb(self, ph, name, shape, dt):
        self._n += 1
        return ph.enter_context(self.nc.sbuf_tensor("%s_%d" % (name, self._n), list(shape), dt))[:]

    def mm(self, out, lhsT, rhs, start, stop, reads, writes):
        self.S.op("pe", lambda e: e.matmul(out, lhsT, rhs, start=start, stop=stop), reads, writes)

    def tr(self, out, in_, ident, reads, writes):
        self.S.op("pe", lambda e: e.transpose(out, in_, ident), reads, writes)

    def act(self, out, in_, func, reads, writes, bias=None, scale=None, eng="act"):
        kw = {}
        if bias is not None:
            kw["bias"] = bias
        if scale is not None:
            kw["scale"] = scale
        self.S.op(eng, lambda e: e.activation(out, in_, func, **kw), reads, writes)

    def tt(self, eng, out, in0, in1, op, reads, writes):
        self.S.op(eng, lambda e: e.tensor_tensor(out, in0, in1, op), reads, writes)

    def ts(self, eng, out, in0, s1, s2, op0, op1, reads, writes):
        if op1 is None:
            self.S.op(eng, lambda e: e.tensor_scalar(out, in0, s1, None, op0), reads, writes)
        else:
            self.S.op(eng, lambda e: e.tensor_scalar(out, in0, s1, s2, op0, op1), reads, writes)

    def stt(self, out, in0, scalar, in1, op0, op1, reads, writes):
        self.S.op("dve", lambda e: e.scalar_tensor_tensor(out, in0, scalar, in1, op0, op1), reads, writes)

    def cp(self, eng, out, in_, reads, writes):
        if eng == "act":
            self.S.op("act", lambda e: e.copy(out, in_), reads, writes)
        else:
            self.S.op(eng, lambda e: e.tensor_copy(out, in_), reads, writes)

    def memset(self, eng, ap, val, writes):
        self.S.op(eng, lambda e: e.memset(ap, val), (), writes)

    def load_w_bf16(self, dst, src, key, res, maxcols=2048):
        n = dst.shape[-1]
        c0 = 0
        while c0 < n:
            c1 = min(n, c0 + maxcols)
            self.S.dma("pool", key, dst[:, c0:c1], src[:, c0:c1], writes=[res])
            c0 = c1

    def build(self):
        nc, S = self.nc, self.S
        with ExitStack() as st:
            self.ident = st.enter_context(nc.sbuf_tensor("ident", [128, 128], F32))[:]
            self.identb = st.enter_context(nc.sbuf_tensor("identb", [128, 128], BF16))[:]
            self.modT = st.enter_context(nc.sbuf_tensor("modT", [128, DEPTH, self.nseq, 48], F32))[:]
            self.mhalf = st.enter_context(nc.sbuf_tensor("mhalf", [128, 1], F32))[:]
            self.pd = []
            self.pb = []
            for i in range(4):
                t = st.enter_context(nc.psum_tensor("pd%d" % i, [128, 1024], F32))[:]
                self.pd.append((t, "pb%d" % (2 * i)))
                self.pb += [(t[:, 0:512], "pb%d" % (2 * i)), (t[:, 512:1024], "pb%d" % (2 * i + 1))]
            self.py = self.pd[2]
            self.py1 = self.pd[3]
            self.memset("pool", self.ident, 1.0, ["ident"])
            S.op("pool", lambda e: e.affine_select(self.ident, self.ident, [[1, 128]], ALU.is_equal, 0.0,
                                                    base=0, channel_multiplier=-1), ["ident"], ["ident"])
            self.cp("dve", self.identb, self.ident, ["ident"], ["identb"])
            self.memset("pool", self.mhalf, -0.5, ["mhalf"])
            cur = self.din["x"]
            nxt = [self.xa, self.xb]
            nph = len([p for p in self.plan if p[0] != "mod"])
            k = 0
            for p in self.plan:
                if p[0] == "mod":
                    self.phase_mod()
                else:
                    k += 1
                    dst = self.dout if k == nph else nxt[k % 2]
                    if p[0] == "ffn":
                        self.phase_ffn(p[1], cur, dst)
                    elif p[0] == "gmlp":
                        self.phase_gmlp(p[1], cur, dst)
                    elif p[0] == "conv":
                        self.phase_conv(p[1], cur, dst)
                    elif p[0] == "attn":
                        self.phase_attn(p[1], p[2], cur, dst)
                    cur = dst
                S.barrier()
            S.emit(st)
        return nc

    def phase_mod(self):
        nc, S, ns = self.nc, self.S, self.nseq
        with ExitStack() as ph:
            cs = self.sb(ph, "cs", [ns, D], F32)
            ca = self.sb(ph, "ca", [ns, D], F32)
            caT = self.sb(ph, "caT", [128, 8, ns], F32)
            CB = self.sb(ph, "CB", [128, ns, 8, 128], F32)
            wst = [self.sb(ph, "wst%d" % i, [128, 8, 512], F32) for i in range(2)]
            mbb = [self.sb(ph, "mbb%d" % i, [128, 512], F32) for i in range(2)]
            mrow = [self.sb(ph, "mrow%d" % i, [128, 512], F32) for i in range(2)]
            stg = [self.sb(ph, "stg%d" % i, [48, 128], F32) for i in range(2)]
            S.dma("sp", "cs", cs, self.din["c"][:, :], writes=["cs"])
            self.act(ca, cs, AF.Silu, ["cs"], ["ca"])
            pbt, pbr = self.pb[0]
            for kc in range(8):
                self.tr(pbt[:, kc * ns:(kc + 1) * ns], ca[:, kc * 128:(kc + 1) * 128], self.ident[0:ns, 0:ns],
                        ["ca", "ident"], [pbr])
            self.cp("dve", caT, pbt[:, 0:8 * ns].rearrange("p (k b) -> p k b", b=ns), [pbr], ["caT"])
            for b in range(ns):
                self.cp("dve", CB[:, b], caT[:, :, b:b + 1].to_broadcast([128, 8, 128]), ["caT"], ["CB"])
            it = 0
            mi = 0
            for l in range(DEPTH):
                for blk in range(12):
                    sl = it % 2
                    it += 1
                    cols = slice(blk * 512, (blk + 1) * 512)
                    S.dma("sp", "wst%d" % sl, wst[sl],
                          self.din["mod_w"][l, :, cols].rearrange("(k p) f -> p k f", p=128), writes=["wst%d" % sl])
                    S.dma("sp", "mbb%d" % sl, mbb[sl], self.din["mod_b"][l, cols].partition_broadcast(128),
                          writes=["mbb%d" % sl])
                    for b in range(ns):
                        pt, pr = self.pb[1 + (mi % 2)]
                        ms = mi % 2
                        mi += 1
                        for kc in range(8):
                            self.mm(pt, CB[:, b, kc, :], wst[sl][:, kc, :], kc == 0, kc == 7,
                                    ["CB", "wst%d" % sl], [pr])
                        self.tt("dve", mrow[ms], pt, mbb[sl], ALU.add, [pr, "mbb%d" % sl], ["mrow%d" % ms])
                        S.dma("sp", "mrow%d" % ms, self.modrow[l, b:b + 1, cols], mrow[ms][0:1, :],
                              reads=["mrow%d" % ms], writes=["modrow_%d_%d_%d" % (l, b, blk)])
            si = 0
            for l in range(DEPTH):
                for b in range(ns):
                    sl = si % 2
                    si += 1
                    S.dma("sp", "stg%d" % sl, stg[sl], self.modrow[l, b, :].rearrange("(c p) -> c p", p=128),
                          reads=["modrow_%d_%d_%d" % (l, b, blk) for blk in range(12)], writes=["stg%d" % sl])
                    pt, pr = self.pb[3 + sl]
                    self.tr(pt[:, 0:48], stg[sl], self.ident[0:48, 0:48], ["stg%d" % sl, "ident"], [pr])
                    self.cp("dve", self.modT[:, l, b, :], pt[:, 0:48], [pr], ["modT"])
            for c0 in (8, 32):
                self.ts("dve", self.modT[:, :, :, c0:c0 + 8], self.modT[:, :, :, c0:c0 + 8], 1.0, None, ALU.add, None,
                        ["modT"], ["modT"])

    def load_bcast(self, ph, name, src_row):
        n = src_row.shape[-1]
        t = self.sb(ph, name, [128, n], F32)
        self._n += 1
        res = "%s_%d" % (name, self._n)
        self.S.dma("sp", res, t, src_row.partition_broadcast(128), writes=[res])
        return t, res

    def load_pp(self, ph, name, src2d, pbi=0):
        n = src2d.shape[0]
        stg = self.sb(ph, name + "s", [n, 128], F32)
        t = self.sb(ph, name, [128, n], F32)
        self._n += 1
        res = "%s_%d" % (name, self._n)
        self.S.dma("sp", res + "s", stg, src2d, writes=[res + "s"])
        pt, pr = self.pb[pbi]
        self.tr(pt[:, 0:n], stg, self.ident[0:n, 0:n], [res + "s", "ident"], [pr])
        self.cp("dve", t, pt[:, 0:n], [pr], [res])
        return t, res

    def epi_setup(self, ph, l, which, single_gb=False):
        gcol = 2048 if which == 0 else 5120
        GB = []
        if single_gb:
            t = self.sb(ph, "GBs", [128, D], F32)
            GB = [(t, "GBs")] * self.nseq
        else:
            for b in range(self.nseq):
                t, r = self.load_bcast(ph, "GB%d" % b, self.modrow[l, b, gcol:gcol + D])
                self.ts("dve", t, t, 1.0, None, ALU.add, None, [r], [r])
                GB.append((t, r))
        lng = self.load_bcast(ph, "lng", self.din["ln1_g" if which == 0 else "ln2_g"][l, :])
        lnb = self.load_bcast(ph, "lnb", self.din["ln1_b" if which == 0 else "ln2_b"][l, :])
        eb = []
        for i in range(2):
            eb.append(dict(
                buf=self.sb(ph, "ebuf%d" % i, [128, D], F32), st=self.sb(ph, "est%d" % i, [128, 12], F32),
                mv=self.sb(ph, "emv%d" % i, [128, 2], F32), sm=self.sb(ph, "esm%d" % i, [128, 4], F32),
                res="ebuf%d_%d" % (i, self._n)))
        return dict(GB=GB, lng=lng, lnb=lnb, eb=eb, i=0, single=single_gb, cur=None, l=l, gcol=gcol)

    def epi_select(self, E, b):
        if not E["single"] or E["cur"] == b:
            return
        E["cur"] = b
        t, r = E["GB"][b]
        self.S.dma("sp", r, t, self.modrow[E["l"], b, E["gcol"]:E["gcol"] + D].partition_broadcast(128), writes=[r])
        self.ts("dve", t, t, 1.0, None, ALU.add, None, [r], [r])

    def epilogue(self, E, b, y, yres, x, xres, out_rows, ybias=None):
        S = self.S
        e = E["eb"][E["i"] % 2]
        E["i"] += 1
        buf, r = e["buf"], e["res"]
        GBt, GBr = E["GB"][b]
        if ybias is not None:
            self.tt("dve", buf, y, ybias[0], ALU.add, [yres, ybias[1]], [r])
            self.tt("dve", buf, buf, GBt, ALU.mult, [r, GBr], [r])
        else:
            self.tt("dve", buf, y, GBt, ALU.mult, [yres, GBr], [r])
        self.stt(buf, x, ALPHA, buf, ALU.mult, ALU.add, [xres, r], [r])
        self.ln_rows(buf, r, e)
        self.tt("dve", buf, buf, E["lng"][0], ALU.mult, [r, E["lng"][1]], [r])
        self.tt("pool", buf, buf, E["lnb"][0], ALU.add, [r, E["lnb"][1]], [r])
        S.dma("sp", r, out_rows, buf, reads=[r])

    def ln_rows(self, buf, r, e, out=None, eng_norm="act"):
        S = self.S
        st, mv, sm = e["st"], e["mv"], e["sm"]
        rs = r + "s"
        S.op("dve", lambda en: en.bn_stats(st[:, 0:6], buf[:, 0:512]), [r], [rs])
        S.op("dve", lambda en: en.bn_stats(st[:, 6:12], buf[:, 512:1024]), [r], [rs])
        S.op("dve", lambda en: en.bn_aggr(mv, st), [rs], [rs])
        self.ts("dve", sm[:, 0:1], mv[:, 1:2], EPS, None, ALU.add, None, [rs], [rs])
        self.tt("pool", sm[:, 1:2], sm[:, 0:1], self.mhalf, ALU.pow, [rs, "mhalf"], [rs])
        self.ts("dve", sm[:, 2:3], mv[:, 0:1], sm[:, 1:2], -1.0, ALU.mult, ALU.mult, [rs], [rs])
        o = buf if out is None else out[0]
        wr = [r] if out is None else [out[1]]
        self.act(o, buf, AF.Identity, [r, rs], wr, bias=sm[:, 2:3], scale=sm[:, 1:2])

    def xT_mod(self, xt, xres, nsub, hT, hres, l, b, which, tpr):
        sh0 = 0 if which == 0 else 24
        sc0 = 8 if which == 0 else 32
        for kc in range(8):
            pt, pr = tpr.next()
            for s in range(nsub):
                self.tr(pt[:, s * 128:(s + 1) * 128], xt[:, s, kc * 128:(kc + 1) * 128], self.ident,
                        [xres, "ident"], [pr])
            self.act(hT[:, kc, 0:nsub * 128], pt[:, 0:nsub * 128], AF.Identity, [pr, "modT"], [hres],
                     bias=self.modT[:, l, b, sh0 + kc:sh0 + kc + 1], scale=self.modT[:, l, b, sc0 + kc:sc0 + kc + 1])

    def x_tile(self, X, b, t0, tt):
        return X[b, t0:t0 + tt, :].rearrange("(s p) d -> p s d", p=128)

    def phase_ffn(self, l, Xin, Xout):
        nc, S, ns, T = self.nc, self.S, self.nseq, self.T
        TT = 256
        nsub = TT // 128
        NF = FH // 128
        with ExitStack() as ph:
            w1 = self.sb(ph, "w1", [128, 8, 2 * FH], BF16)
            w2 = self.sb(ph, "w2", [128, NF, D], BF16)
            for kc in range(8):
                self.load_w_bf16(w1[:, kc, :], self.din["ffn_w_in"][l, kc * 128:(kc + 1) * 128, :], "w1", "w1", 1408)
            for fc in range(NF):
                self.load_w_bf16(w2[:, fc, :], self.din["ffn_w_out"][l, fc * 128:(fc + 1) * 128, :], "w2", "w2")
            E = self.epi_setup(ph, l, 1)
            xt = [(self.sb(ph, "xt%d" % i, [128, nsub, D], F32), "xt%d" % i) for i in range(2)]
            hT = [(self.sb(ph, "hT%d" % i, [128, 8, TT], BF16), "hT%d" % i) for i in range(2)]
            aT = (self.sb(ph, "aT", [128, NF, TT], BF16), "aT")
            sg = [(self.sb(ph, "sg%d" % i, [128, TT], F32), "sg%d" % i) for i in range(2)]
            tpr = Ring(self.pb[0:2])
            gur = Ring([(self.pb[2], self.pb[3]), (self.pb[6], self.pb[7])])
            tiles = [(b, t0) for b in range(ns) for t0 in range(0, T, TT)]
            n = len(tiles)

            def loads(i):
                b_, t0_ = tiles[i]
                S.dma("sp", xt[i % 2][1], xt[i % 2][0], self.x_tile(Xin, b_, t0_, TT), writes=[xt[i % 2][1]])

            def prologue(i):
                self.xT_mod(xt[i % 2][0], xt[i % 2][1], nsub, hT[i % 2][0], hT[i % 2][1], l, tiles[i][0], 1, tpr)

            def inproj(i):
                hs, hr = hT[i % 2]
                for fc in range(NF):
                    (pg, pgr), (pu, pur) = gur.next()
                    for kc in range(8):
                        self.mm(pg[:, 0:TT], w1[:, kc, fc * 128:(fc + 1) * 128], hs[:, kc, :], kc == 0, kc == 7,
                                ["w1", hr], [pgr])
                    for kc in range(8):
                        self.mm(pu[:, 0:TT], w1[:, kc, FH + fc * 128:FH + (fc + 1) * 128], hs[:, kc, :], kc == 0,
                                kc == 7, ["w1", hr], [pur])
                    sgt, sgr = sg[fc % 2]
                    self.act(sgt, pg[:, 0:TT], AF.Silu, [pgr], [sgr])
                    self.tt("dve", aT[0][:, fc, :], sgt, pu[:, 0:TT], ALU.mult, [sgr, pur], [aT[1]])

            def outproj(i):
                b, t0 = tiles[i]
                xs, xr = xt[i % 2]
                for s in range(nsub):
                    pyt, pyr = self.py
                    for hf in range(2):
                        for fc in range(NF):
                            self.mm(pyt[:, hf * 512:(hf + 1) * 512], aT[0][:, fc, s * 128:(s + 1) * 128],
                                    w2[:, fc, hf * 512:(hf + 1) * 512], fc == 0, fc == NF - 1, [aT[1], "w2"], [pyr])
                    self.epilogue(E, b, pyt, pyr, xs[:, s, :], xr, Xout[b, t0 + s * 128:t0 + (s + 1) * 128, :])

            loads(0)
            if n > 1:
                loads(1)
            print("[sbuf] ffn remaining", nc.sbuf_bytes_remaining)
            prologue(0)
            for i in range(n):
                inproj(i)
                if i + 1 < n:
                    prologue(i + 1)
                outproj(i)
                if i + 2 < n:
                    loads(i + 2)

    def phase_gmlp(self, l, Xin, Xout):
        nc, S, ns, T = self.nc, self.S, self.nseq, self.T
        TT = 512
        nsub = TT // 128
        W = self.din
        with ExitStack() as ph:
            wi = self.sb(ph, "wi", [128, 8, 2 * D], BF16)
            wo = self.sb(ph, "wo", [128, 8, D], BF16)
            for kc in range(8):
                self.load_w_bf16(wi[:, kc, :], W["gm_w_in"][0, kc * 128:(kc + 1) * 128, :], "wi", "wi")
                self.load_w_bf16(wo[:, kc, :], W["gm_w_out"][0, kc * 128:(kc + 1) * 128, :], "wo", "wo")
            wsl = self.sb(ph, "wsl", [128, 8, 128], F32)
            wsm = self.sb(ph, "wsm", [128, 8, 128], F32)
            wmT = self.sb(ph, "wmT", [128, 8, 128], BF16)
            S.dma("sp", "wsl", wsl, W["gm_w_s"][0].rearrange("g t s -> t g s"), writes=["wsl"])
            for g in range(8):
                pt, pr = self.pb[g % 2]
                self.tr(pt[:, 0:128], wsl[:, g, :], self.ident, ["wsl", "ident"], [pr])
                self.cp("dve", wsm[:, g, :], pt[:, 0:128], [pr], ["wsm"])
                S.op("pool", lambda e, g=g: e.affine_select(wmT[:, g, :], wsm[:, g, :], [[1, 128]], ALU.is_ge, 0.0,
                                                             base=0, channel_multiplier=-1), ["wsm"], ["wmT"])
            binu, binu_r = self.load_pp(ph, "binu", W["gm_b_in"][0, 0:D].rearrange("(c p) -> c p", p=128), 2)
            binv, binv_r = self.load_bcast(ph, "binv", W["gm_b_in"][0, D:2 * D])
            glg, glg_r = self.load_bcast(ph, "glg", W["gm_ln_g"][0, :])
            glb, glb_r = self.load_bcast(ph, "glb", W["gm_ln_b"][0, :])
            bsb, bsb_r = self.load_bcast(ph, "bsb", W["gm_b_s"][0].rearrange("g t -> (g t)"))
            bsb3 = bsb.rearrange("p (g t) -> p g t", t=128)
            E = self.epi_setup(ph, l, 0, single_gb=True)
            xt = [(self.sb(ph, "xt%d" % i, [128, nsub, D], F32), "xt%d" % i) for i in range(3)]
            hT = [(self.sb(ph, "hT%d" % i, [128, 8, TT], BF16), "hT%d" % i) for i in range(2)]
            vz = [dict(buf=self.sb(ph, "vz%d" % i, [128, D], F32), st=self.sb(ph, "vst%d" % i, [128, 12], F32),
                       mv=self.sb(ph, "vmv%d" % i, [128, 2], F32), sm=self.sb(ph, "vsm%d" % i, [128, 4], F32),
                       res="vz%d" % i) for i in range(2)]
            vn = [[(self.sb(ph, "vn%d_%d" % (k, i), [128, D], BF16), "vn%d_%d" % (k, i)) for i in range(nsub)] for k in range(2)]
            uT = [(self.sb(ph, "uT%d" % i, [128, TT], F32), "uT%d" % i) for i in range(2)]
            tmp = [(self.sb(ph, "tmp%d" % i, [128, TT], F32), "tmp%d" % i) for i in range(2)]
            yT = (self.sb(ph, "yT", [128, 8, TT], BF16), "yT")
            print("[sbuf] gmlp remaining", nc.sbuf_bytes_remaining)
            tpr = Ring(self.pb[0:2])
            tiles = [(b, t0) for b in range(ns) for t0 in range(0, T, TT)]
            cnt = dict(vi=0)

            def loads(i):
                b_, t0_ = tiles[i]
                S.dma("sp", xt[i % 3][1], xt[i % 3][0], self.x_tile(Xin, b_, t0_, TT), writes=[xt[i % 3][1]])

            def stageA(i):
                b, t0 = tiles[i]
                xs, xr = xt[i % 3]
                hs, hr = hT[i % 2]
                self.xT_mod(xs, xr, nsub, hs, hr, l, b, 0, tpr)
                for s in range(nsub):
                    pv, pvr = self.py1
                    for hf in range(2):
                        for kc in range(8):
                            self.mm(pv[:, hf * 512:(hf + 1) * 512], hs[:, kc, s * 128:(s + 1) * 128],
                                    wi[:, kc, D + hf * 512:D + (hf + 1) * 512], kc == 0, kc == 7, [hr, "wi"], [pvr])
                    z = vz[cnt["vi"] % 2]
                    cnt["vi"] += 1
                    vt, vr = vn[i % 2][s]
                    self.tt("dve", z["buf"], pv, binv, ALU.add, [pvr, binv_r], [z["res"]])
                    self.act(z["buf"], z["buf"], AF.Gelu, [z["res"]], [z["res"]])
                    self.ln_rows(z["buf"], z["res"], z)
                    self.tt("dve", z["buf"], z["buf"], glg, ALU.mult, [z["res"], glg_r], [z["res"]])
                    self.tt("pool", vt, z["buf"], glb, ALU.add, [z["res"], glb_r], [vr])

            def stageB(i):
                b, t0 = tiles[i]
                xs, xr = xt[i % 3]
                hs, hr = hT[i % 2]
                self.epi_select(E, b)
                for g in range(8):
                    pu, pur = self.pb[2]
                    psv, psvr = self.pb[3]
                    for kc in range(8):
                        self.mm(pu, wi[:, kc, g * 128:(g + 1) * 128], hs[:, kc, :], kc == 0, kc == 7, ["wi", hr], [pur])
                    ut, utr = uT[g % 2]
                    self.act(ut, pu, AF.Gelu, [pur, binu_r], [utr], bias=binu[:, g:g + 1])
                    for s in range(nsub):
                        vt, vr = vn[i % 2][s]
                        self.mm(psv[:, s * 128:(s + 1) * 128], vt[:, g * 128:(g + 1) * 128], wmT[:, g, :],
                                True, True, [vr, "wmT"], [psvr])
                    tm, tmr = tmp[g % 2]
                    self.tt("dve", tm.rearrange("p (s t) -> p s t", t=128), psv.rearrange("p (s t) -> p s t", t=128),
                            bsb3[:, g:g + 1, :].to_broadcast([128, nsub, 128]), ALU.add, [psvr, bsb_r], [tmr])
                    self.tt("dve" if g % 2 == 0 else "pool", yT[0][:, g, :], tm, ut, ALU.mult, [tmr, utr], [yT[1]])
                for s in range(nsub):
                    pyt, pyr = self.py
                    for hf in range(2):
                        for g in range(8):
                            self.mm(pyt[:, hf * 512:(hf + 1) * 512], yT[0][:, g, s * 128:(s + 1) * 128],
                                    wo[:, g, hf * 512:(hf + 1) * 512], g == 0, g == 7, [yT[1], "wo"], [pyr])
                    self.epilogue(E, b, pyt, pyr, xs[:, s, :], xr, Xout[b, t0 + s * 128:t0 + (s + 1) * 128, :])

            n = len(tiles)
            loads(0)
            if n > 1:
                loads(1)
            for i in range(n):
                stageA(i)
                if i >= 1:
                    stageB(i - 1)
                if i + 2 < n:
                    loads(i + 2)
            stageB(n - 1)

    def phase_conv(self, l, Xin, Xout):
        nc, S, ns, T = self.nc, self.S, self.nseq, self.T
        TT = 256
        nsub = TT // 128
        W = self.din
        HW = CONVW - 1
        with ExitStack() as ph:
            wi = self.sb(ph, "wi", [128, 8, 2 * D], BF16)
            wo = self.sb(ph, "wo", [128, 8, D], BF16)
            for kc in range(8):
                self.load_w_bf16(wi[:, kc, :], W["cv_w_in"][0, kc * 128:(kc + 1) * 128, :], "wi", "wi")
                self.load_w_bf16(wo[:, kc, :], W["cv_w_out"][0, kc * 128:(kc + 1) * 128, :], "wo", "wo")
            bia, bia_r = self.load_pp(ph, "bia", W["cv_b_in"][0, :].rearrange("(c p) -> c p", p=128), 2)
            dwr = W["cv_dw"][0].rearrange("i (c p) -> (i c) p", p=128)
            dwa, dwa_r = self.load_pp(ph, "dwa", dwr[0:124, :], 2)
            dwb_, dwb_r = self.load_pp(ph, "dwb", dwr[124:248, :], 3)
            dwbias, dwbias_r = self.load_pp(ph, "dwbias", W["cv_dw_b"][0, :].rearrange("(c p) -> c p", p=128), 2)
            clg, clg_r = self.load_pp(ph, "clg", W["cv_ln_g"][0, :].rearrange("(c p) -> c p", p=128), 3)
            clb, clb_r = self.load_pp(ph, "clb", W["cv_ln_b"][0, :].rearrange("(c p) -> c p", p=128), 2)
            bo32 = self.sb(ph, "bo32", [1, D], F32)
            bohi = self.sb(ph, "bohi", [1, D], BF16)
            bolo = self.sb(ph, "bolo", [1, D], BF16)
            one1 = self.sb(ph, "one1", [1, 128], BF16)
            S.dma("sp", "bo32", bo32, W["cv_b_out"][0:1, :], writes=["bo"])
            self.cp("dve", bohi, bo32, ["bo"], ["bohi"])
            self.tt("dve", bo32, bo32, bohi, ALU.subtract, ["bo", "bohi"], ["bo"])
            self.cp("dve", bolo, bo32, ["bo"], ["bolo"])
            self.memset("pool", one1, 1.0, ["one1"])
            diag = self.sb(ph, "diag", [128, CONVW * 8, 128], BF16)
            for half, (dt_, dr_) in enumerate(((dwa, dwa_r), (dwb_, dwb_r))):
                self.tt("dve", diag[:, half * 124:(half + 1) * 124, :],
                        self.ident[:, None, :].to_broadcast([128, 124, 128]),
                        dt_[:, :, None].to_broadcast([128, 124, 128]), ALU.mult, ["ident", dr_], ["diag"])
            onesf = self.sb(ph, "onesf", [128, 128], F32)
            self.memset("pool", onesf, 1.0, ["onesf"])
            E = self.epi_setup(ph, l, 0, single_gb=True)
            xt = [(self.sb(ph, "xt%d" % i, [128, nsub, D], F32), "xt%d" % i) for i in range(3)]
            hT = (self.sb(ph, "hT", [128, 8, TT], BF16), "hT")
            ybuf = (self.sb(ph, "ybuf", [128, 8, HW + TT], BF16), "ybuf")
            sgm = [(self.sb(ph, "sgm%d" % i, [128, TT], F32), "sgm%d" % i) for i in range(2)]
            zT = [(self.sb(ph, "zT%d" % i, [128, 8, TT], F32), "zT%d" % i) for i in range(2)]
            zq = [(self.sb(ph, "zq%d" % i, [128, TT], F32), "zq%d" % i) for i in range(2)]
            mean_t = [(self.sb(ph, "mean_t%d" % i, [128, TT], F32), "mean_t%d" % i) for i in range(2)]
            rstd_t = [(self.sb(ph, "rstd_t%d" % i, [128, TT], F32), "rstd_t%d" % i) for i in range(2)]
            sT = (self.sb(ph, "sT", [128, 8, TT], BF16), "sT")
            print("[sbuf] conv remaining", nc.sbuf_bytes_remaining)
            tpr = Ring(self.pb[0:1])
            tiles = [(b, t0) for b in range(ns) for t0 in range(0, T, TT)]

            def loads(i):
                b_, t0_ = tiles[i]
                S.dma("sp", xt[i % 3][1], xt[i % 3][0], self.x_tile(Xin, b_, t0_, TT), writes=[xt[i % 3][1]])

            def stageA(i):
                b, t0 = tiles[i]
                xs, xr = xt[i % 3]
                hs, hr = hT
                self.xT_mod(xs, xr, nsub, hs, hr, l, b, 0, tpr)
                if t0 == 0:
                    self.memset("pool", ybuf[0][:, :, 0:HW], 0.0, [ybuf[1]])
                else:
                    self.cp("pool", ybuf[0][:, :, 0:HW], ybuf[0][:, :, TT:TT + HW], [ybuf[1]], [ybuf[1]])
                for kc in range(8):
                    pa, par = self.pb[2]
                    pg, pgr = self.pb[3]
                    for k in range(8):
                        self.mm(pa[:, 0:TT], wi[:, k, kc * 128:(kc + 1) * 128], hs[:, k, :], k == 0, k == 7, ["wi", hr], [par])
                    for k in range(8):
                        self.mm(pg[:, 0:TT], wi[:, k, D + kc * 128:D + (kc + 1) * 128], hs[:, k, :], k == 0, k == 7,
                                ["wi", hr], [pgr])
                    sg_, sgr = sgm[kc % 2]
                    self.act(sg_, pg[:, 0:TT], AF.Sigmoid, [pgr, bia_r], [sgr], bias=bia[:, 8 + kc:9 + kc])
                    self.stt(ybuf[0][:, kc, HW:HW + TT], pa[:, 0:TT], bia[:, kc:kc + 1], sg_, ALU.add, ALU.mult,
                             [par, sgr, bia_r], [ybuf[1]])
                z_, zr = zT[i % 2]
                p1, p1r = self.pb[7]
                p2, p2r = self.pb[1]
                for kc in range(8):
                    pz, pzr = self.pb[6]
                    for t in range(CONVW):
                        self.mm(pz[:, 0:TT], diag[:, t * 8 + kc, :], ybuf[0][:, kc, t:t + TT], t == 0, t == CONVW - 1,
                                ["diag", ybuf[1]], [pzr])
                    self.act(z_[:, kc, :], pz[:, 0:TT], AF.Identity, [pzr, dwbias_r], [zr], bias=dwbias[:, kc:kc + 1])
                    q_, qr = zq[kc % 2]
                    S.op("act", lambda e, q_=q_, kc=kc: e.activation(q_, z_[:, kc, :], AF.Square), [zr], [qr])
                    self.mm(p1[:, 0:TT], onesf, z_[:, kc, :], kc == 0, kc == 7, ["onesf", zr], [p1r])
                    self.mm(p2[:, 0:TT], onesf, q_, kc == 0, kc == 7, ["onesf", qr], [p2r])
                m_, mr = mean_t[i % 2]
                r_, rr = rstd_t[i % 2]
                q, qr_ = r_, rr
                self.ts("dve", m_, p1[:, 0:TT], 1.0 / D, None, ALU.mult, None, [p1r], [mr])
                self.tt("dve", q, m_, m_, ALU.mult, [mr], [qr_])
                self.stt(q, p2[:, 0:TT], 1.0 / D, q, ALU.mult, ALU.subtract, [p2r, qr_], [qr_])
                self.ts("dve", q, q, EPS, None, ALU.add, None, [qr_], [qr_])
                self.act(r_, q, AF.Sqrt, [qr_], [rr])
                S.op("dve", lambda e, r_=r_: e.reciprocal(r_, r_), [rr], [rr])

            def stageB(i):
                b, t0 = tiles[i]
                xs, xr = xt[i % 3]
                z_, zr = zT[i % 2]
                m_, mr = mean_t[i % 2]
                r_, rr = rstd_t[i % 2]
                self.epi_select(E, b)
                mb = m_[:, None, :].to_broadcast([128, 8, TT])
                rb = r_[:, None, :].to_broadcast([128, 8, TT])
                self.tt("dve", z_, z_, mb, ALU.subtract, [zr, mr], [zr])
                self.tt("dve", z_, z_, rb, ALU.mult, [zr, rr], [zr])
                for kc in range(8):
                    self.act(sT[0][:, kc, :], z_[:, kc, :], AF.Silu, [zr, clg_r, clb_r], [sT[1]], bias=clb[:, kc:kc + 1],
                             scale=clg[:, kc:kc + 1])
                for s in range(nsub):
                    pyt, pyr = self.py
                    for hf in range(2):
                        cs = slice(hf * 512, (hf + 1) * 512)
                        for kc in range(8):
                            self.mm(pyt[:, cs], sT[0][:, kc, s * 128:(s + 1) * 128], wo[:, kc, cs], kc == 0, False,
                                    [sT[1], "wo"], [pyr])
                        self.mm(pyt[:, cs], one1, bohi[:, cs], False, False, ["one1", "bohi"], [pyr])
                        self.mm(pyt[:, cs], one1, bolo[:, cs], False, True, ["one1", "bolo"], [pyr])
                    self.epilogue(E, b, pyt, pyr, xs[:, s, :], xr, Xout[b, t0 + s * 128:t0 + (s + 1) * 128, :])

            n = len(tiles)
            loads(0)
            if n > 1:
                loads(1)
            for i in range(n):
                stageA(i)
                if i >= 1:
                    stageB(i - 1)
                if i + 2 < n:
                    loads(i + 2)
            stageB(n - 1)

    def phase_attn(self, l, kind, Xin, Xout):
        self.attn_proj(l, kind, Xin)
        self.S.barrier()
        if kind == "fox":
            self.attn_core_fox()
        else:
            self.attn_core_sb()
        self.S.barrier()
        self.attn_out(l, kind, Xin, Xout)

    def attn_proj(self, l, kind, Xin):
        nc, S, ns, T = self.nc, self.S, self.nseq, self.T
        TT = 512
        nsub = TT // 128
        W = self.din
        fox = kind == "fox"
        NC = 3 * D + (NH if fox else 0)
        wname = "fox_w_in" if fox else "sb_w_in"
        with ExitStack() as ph:
            wi = self.sb(ph, "wi", [128, 8, NC], BF16)
            for kc in range(8):
                self.load_w_bf16(wi[:, kc, :], W[wname][0, kc * 128:(kc + 1) * 128, :], "wi", "wi", 1024 + (NH if fox else 0))
            xt = [(self.sb(ph, "xt%d" % i, [128, nsub, D], F32), "xt%d" % i) for i in range(2)]
            hT = [(self.sb(ph, "hT%d" % i, [128, 8, TT], BF16), "hT%d" % i) for i in range(2)]
            qst = [(self.sb(ph, "qst%d" % i, [128, 8, TT], BF16), "qst%d" % i) for i in range(2)]
            kst = [(self.sb(ph, "kst%d" % i, [128, 8, TT], BF16), "kst%d" % i) for i in range(2)]
            vst = [(self.sb(ph, "vst%d" % i, [128, nsub, D], BF16), "vst%d" % i) for i in range(2)]
            if fox:
                nbf = self.sb(ph, "nbf", [NH, 1], F32)
                S.dma("sp", "nbf", nbf, W["fox_b_f"][0, :].rearrange("(h o) -> h o", o=1), writes=["nbf"])
                self.ts("dve", nbf, nbf, -1.0, None, ALU.mult, None, ["nbf"], ["nbf"])
                ones16 = self.sb(ph, "ones16", [NH, TT], F32)
                self.memset("dve", ones16, 1.0, ["ones16"])
                ef = self.sb(ph, "ef", [NH, TT], F32)
                cum = [(self.sb(ph, "cum%d" % i, [NH, TT], F32), "cum%d" % i) for i in range(2)]
                s8 = self.sb(ph, "s8", [NH, TT], F32)
                r1 = self.sb(ph, "r1", [NH, TT], F32)
                fk3 = [(self.sb(ph, "fk3%d" % i, [NH, 3, TT], BF16), "fk3%d" % i) for i in range(2)]
                fq3 = [(self.sb(ph, "fq3%d" % i, [NH, 3, TT], BF16), "fq3%d" % i) for i in range(2)]
            tpr = Ring(self.pb[0:2])
            qkr = Ring(self.pb[2:4])
            ev = 0
            tiles = [(b, t0) for b in range(ns) for t0 in range(0, T, TT)]

            def loads(i):
                b_, t0_ = tiles[i]
                S.dma("sp", xt[i % 2][1], xt[i % 2][0], self.x_tile(Xin, b_, t0_, TT), writes=[xt[i % 2][1]])

            loads(0)
            for it, (b, t0) in enumerate(tiles):
                if True:
                    if it + 1 < len(tiles):
                        loads(it + 1)
                    sl = it % 2
                    xs, xr = xt[sl]
                    hs, hr = hT[sl]
                    self.xT_mod(xs, xr, nsub, hs, hr, l, b, 0, tpr)
                    for (stg, c0, dst) in ((qst[sl], 0, self.qt_d), (kst[sl], D, self.kt_d)):
                        for j in range(8):
                            pt, pr = qkr.next()
                            for kc in range(8):
                                self.mm(pt, wi[:, kc, c0 + j * 128:c0 + (j + 1) * 128], hs[:, kc, :], kc == 0, kc == 7,
                                        ["wi", hr], [pr])
                            self.cp("act" if ev % 2 == 0 else "dve", stg[0][:, j, :], pt, [pr], [stg[1]])
                            ev += 1
                        S.dma("sp", stg[1], dst[b].rearrange("(j hh) d t -> (hh d) j t", hh=2)[:, :, t0:t0 + TT], stg[0],
                              reads=[stg[1]])
                    for s in range(nsub):
                        pv, pvr = self.py
                        for hf in range(2):
                            for kc in range(8):
                                self.mm(pv[:, hf * 512:(hf + 1) * 512], hs[:, kc, s * 128:(s + 1) * 128],
                                        wi[:, kc, 2 * D + hf * 512:2 * D + (hf + 1) * 512], kc == 0, kc == 7, [hr, "wi"], [pvr])
                        self.cp("act" if ev % 2 == 0 else "dve", vst[sl][0][:, s, :], pv, [pvr], [vst[sl][1]])
                        ev += 1
                    S.dma("sp", vst[sl][1], self.v_d[b, t0:t0 + TT, :].rearrange("(s p) d -> p s d", p=128), vst[sl][0],
                          reads=[vst[sl][1]])
                    if fox:
                        pf, pfr = self.pb[6]
                        for kc in range(8):
                            self.mm(pf[0:NH, :], wi[:, kc, 3 * D:3 * D + NH], hs[:, kc, :], kc == 0, kc == 7, ["wi", hr], [pfr])
                        self.act(ef, pf[0:NH, :], AF.Exp, [pfr, "nbf"], ["ef"], bias=nbf, scale=-1.0)
                        self.act(ef, ef, AF.Ln, ["ef"], ["ef"], bias=1.0)
                        cm, cmr = cum[sl]
                        pc, pcr = cum[1 - sl]
                        init = 0.0 if t0 == 0 else pc[:, TT - 1:TT]
                        S.op("dve", lambda e, cm=cm, init=init: e.tensor_tensor_scan(cm, ones16, ef, init, ALU.mult, ALU.add),
                             ["ones16", "ef"] + ([] if t0 == 0 else [pcr]), [cmr])
                        fk, fkr = fk3[sl]
                        fq, fqr = fq3[sl]
                        self.ts("dve", s8, cm, 8.0, None, ALU.mult, None, [cmr], ["s8"])
                        self.cp("dve", fk[:, 0, :], s8, ["s8"], [fkr])
                        self.tt("dve", r1, s8, fk[:, 0, :], ALU.subtract, ["s8", fkr], ["r1"])
                        self.cp("dve", fk[:, 1, :], r1, ["r1"], [fkr])
                        self.tt("dve", r1, r1, fk[:, 1, :], ALU.subtract, ["r1", fkr], ["r1"])
                        self.cp("dve", fk[:, 2, :], r1, ["r1"], [fkr])
                        self.ts("dve", fq, fk, -1.0, None, ALU.mult, None, [fkr], [fqr])
                        S.dma("sp", fkr, self.fk_d[b, :, :, t0:t0 + TT], fk, reads=[fkr])
                        S.dma("sp", fqr, self.fq_d[b, :, :, t0:t0 + TT], fq, reads=[fqr])

    def attn_core_fox(self):
        nc, S, ns, T = self.nc, self.S, self.nseq, self.T
        NKB = T // 128
        QW = 1024
        NQ = T // QW
        KA = DH + 6
        LA = 2
        with ExitStack() as ph:
            kta = [(self.sb(ph, "kta%d" % i, [KA, T], BF16), "kta%d" % i) for i in range(3)]
            qta = [(self.sb(ph, "qta%d" % i, [KA, T], BF16), "qta%d" % i) for i in range(3)]
            va = [(self.sb(ph, "va%d" % i, [128, NKB, DH + 1], BF16), "va%d" % i) for i in range(3)]
            PT = [(self.sb(ph, "PT%d" % i, [128, QW], BF16), "PT%d" % i) for i in range(4)]
            ost = [(self.sb(ph, "ost%d" % i, [DH + 1, QW], F32), "ost%d" % i) for i in range(2)]
            for i in range(3):
                self.memset("dve", kta[i][0][DH:KA, :], 1.0, [kta[i][1]])
                self.memset("dve", qta[i][0][DH:KA, :], 1.0, [qta[i][1]])
                self.memset("pool", va[i][0][:, :, DH:DH + 1], 1.0, [va[i][1]])
            psr = Ring(self.pd[0:2])
            por = Ring(self.pd[2:4])
            ptr = Ring(PT)
            osr = Ring(ost)
            pend = []

            def halves(c0):
                return [(max(c0, hf * 512), (hf + 1) * 512) for hf in range(2) if max(c0, hf * 512) < (hf + 1) * 512]

            def stage2(tl):
                (b, h, j, kb, c0, nkb, sl, po, por_, pt, ptr_) = tl
                for (a0, a1) in halves(c0):
                    self.mm(po[0:DH + 1, a0:a1], va[sl][0][:, kb, :], pt[:, a0:a1], kb == 0, kb == nkb - 1,
                            [va[sl][1], ptr_], [por_])
                if kb == nkb - 1:
                    os_, osr_ = osr.next()
                    self.cp("dve", os_, po[0:DH + 1, :], [por_], [osr_])
                    S.dma("sp", osr_, self.o_d[b, h, :, j * QW:(j + 1) * QW], os_[0:DH, :], reads=[osr_])
                    S.dma("sp", osr_, self.den_d[b, h:h + 1, j * QW:(j + 1) * QW], os_[DH:DH + 1, :], reads=[osr_])

            heads = [(b, h) for b in range(ns) for h in range(NH)]

            def loads(i):
                b_, h_ = heads[i]
                sl_ = i % 3
                S.dma("sp", kta[sl_][1], kta[sl_][0][0:DH, :], self.kt_d[b_, h_], writes=[kta[sl_][1]])
                S.dma("sp", kta[sl_][1], kta[sl_][0][DH:DH + 3, :], self.fk_d[b_, h_], writes=[kta[sl_][1]])
                S.dma("sp", qta[sl_][1], qta[sl_][0][0:DH, :], self.qt_d[b_, h_], writes=[qta[sl_][1]])
                S.dma("sp", qta[sl_][1], qta[sl_][0][DH + 3:DH + 6, :], self.fq_d[b_, h_], writes=[qta[sl_][1]])
                S.dma("sp", va[sl_][1], va[sl_][0][:, :, 0:DH],
                      self.v_d[b_, :, h_ * DH:(h_ + 1) * DH].rearrange("(k p) d -> p k d", p=128), writes=[va[sl_][1]])

            loads(0)
            for hi, (b, h) in enumerate(heads):
                sl = hi % 3
                if hi + 1 < len(heads):
                    loads(hi + 1)
                for j in range(NQ):
                    nkb = 8 * j + 8
                    po, por_ = por.next()
                    for kb in range(nkb):
                        c0 = (kb - 8 * j) * 128 if kb >= 8 * j else 0
                        ps, psr_ = psr.next()
                        pt, ptr_ = ptr.next()
                        for (a0, a1) in halves(c0):
                            self.mm(ps[:, a0:a1], kta[sl][0][:, kb * 128:(kb + 1) * 128],
                                    qta[sl][0][:, j * QW + a0:j * QW + a1], True, True, [kta[sl][1], qta[sl][1]], [psr_])
                        self.act(pt[:, c0:QW], ps[:, c0:QW], AF.Exp, [psr_], [ptr_], scale=0.125)
                        if kb >= 8 * j:
                            S.op("pool", lambda e, pt=pt, c0=c0: e.affine_select(
                                pt[:, c0:c0 + 128], pt[:, c0:c0 + 128], [[1, 128]], ALU.is_ge, 0.0, base=0,
                                channel_multiplier=-1), [ptr_], [ptr_])
                        pend.append((b, h, j, kb, c0, nkb, sl, po, por_, pt, ptr_))
                        if len(pend) > LA:
                            stage2(pend.pop(0))
            while pend:
                stage2(pend.pop(0))

    def attn_core_sb(self):
        nc, S, ns, T = self.nc, self.S, self.nseq, self.T
        NKB = T // 128
        QW = 1024
        NQ = T // QW
        with ExitStack() as ph:
            kta = [(self.sb(ph, "kta%d" % i, [DH, T], BF16), "kta%d" % i) for i in range(3)]
            qta = [(self.sb(ph, "qta%d" % i, [DH, T], BF16), "qta%d" % i) for i in range(3)]
            va = [(self.sb(ph, "va%d" % i, [128, NKB, DH], BF16), "va%d" % i) for i in range(3)]
            Et = [(self.sb(ph, "Et%d" % i, [128, QW], F32), "Et%d" % i) for i in range(2)]
            SPt = [(self.sb(ph, "SPt%d" % i, [128, QW], BF16), "SPt%d" % i) for i in range(3)]
            X2 = [(self.sb(ph, "X2%d" % i, [128, QW], F32), "X2%d" % i) for i in range(2)]
            At = [(self.sb(ph, "At%d" % i, [128, QW], BF16), "At%d" % i) for i in range(3)]
            Ccs = [(self.sb(ph, "Cc%d" % i, [128, QW], F32), "Cc%d" % i) for i in range(2)]
            ccr = Ring(Ccs)
            ost = [(self.sb(ph, "ost%d" % i, [DH, QW], F32), "ost%d" % i) for i in range(2)]
            trin = self.sb(ph, "trin", [128, 128], BF16)
            onen = self.sb(ph, "onen", [128, 128], BF16)
            self.memset("pool", trin, -8.0, ["trin"])
            S.op("pool", lambda e: e.affine_select(trin, trin, [[-1, 128]], ALU.is_ge, 0.0, base=0, channel_multiplier=1),
                 ["trin"], ["trin"])
            self.memset("pool", onen, -8.0, ["onen"])
            psr = Ring(self.pd[0:2])
            pyy, pyyr = self.pd[2]
            po_, por_ = self.pd[3]
            etr, spr, x2r, atr, osr = Ring(Et), Ring(SPt), Ring(X2), Ring(At), Ring(ost)
            q1, q2 = [], []

            def halves(c0):
                return [(max(c0, hf * 512), (hf + 1) * 512) for hf in range(2) if max(c0, hf * 512) < (hf + 1) * 512]

            def mask(tile_ap, res, c0):
                S.op("pool", lambda e: e.affine_select(tile_ap[:, c0:c0 + 128], tile_ap[:, c0:c0 + 128], [[1, 128]], ALU.is_gt,
                                                        0.0, base=0, channel_multiplier=-1), [res], [res])

            def stageA1(tl):
                ps, psr_ = tl["ps"]
                c0, sl, kb, j = tl["c0"], tl["sl"], tl["kb"], tl["j"]
                for (a0, a1) in halves(c0):
                    self.mm(ps[:, a0:a1], kta[sl][0][:, kb * 128:(kb + 1) * 128], qta[sl][0][:, j * QW + a0:j * QW + a1],
                            True, True, [kta[sl][1], qta[sl][1]], [psr_])
                et, etr_ = etr.next()
                tl["et"] = (et, etr_)
                self.act(et[:, c0:QW], ps[:, c0:QW], AF.Exp, [psr_], [etr_], scale=0.125)

            def stageA2(tl):
                et, etr_ = tl["et"]
                c0 = tl["c0"]
                sp, spr_ = spr.next()
                tl["sp"] = (sp, spr_)
                self.act(sp[:, c0:QW], et[:, c0:QW], AF.Ln, [etr_], [spr_], bias=1.0)
                if tl["diag"]:
                    mask(sp, spr_, c0)

            def stageB(tl):
                ps, psr_ = tl["ps"]
                sp, spr_ = tl["sp"]
                c0 = tl["c0"]
                Cc = tl["Cc"]
                for (a0, a1) in halves(c0):
                    S.op("pe", lambda e, a0=a0, a1=a1: e.matmul(ps[:, a0:a1], trin, sp[:, a0:a1], start=False, stop=True,
                                                               skip_group_check=True), ["trin", spr_], [psr_])
                for (a0, a1) in halves(c0):
                    self.mm(pyy[:, a0:a1], onen, sp[:, a0:a1], True, True, ["onen", spr_], [pyyr])
                x2, x2r_ = x2r.next()
                at, atr_ = atr.next()
                tl["at"] = (at, atr_)
                self.tt("dve", x2[:, c0:QW], ps[:, c0:QW], Cc[0][:, c0:QW], ALU.add, [psr_, Cc[1]], [x2r_])
                self.tt("dve", Cc[0][:, c0:QW], pyy[:, c0:QW], Cc[0][:, c0:QW], ALU.add, [pyyr, Cc[1]], [Cc[1]])
                self.act(at[:, c0:QW], x2[:, c0:QW], AF.Exp, [x2r_], [atr_], scale=0.125)
                if tl["diag"]:
                    mask(at, atr_, c0)

            def stageC(tl):
                at, atr_ = tl["at"]
                c0, sl, kb = tl["c0"], tl["sl"], tl["kb"]
                st = tl["started"]
                for hf, (a0, a1) in [(a0 // 512, (a0, a1)) for (a0, a1) in halves(c0)]:
                    self.mm(po_[0:DH, a0:a1], va[sl][0][:, kb, :], at[:, a0:a1], not st[hf], tl["last"], [va[sl][1], atr_], [por_])
                    st[hf] = True
                if tl["last"]:
                    os_, osr_ = osr.next()
                    self.cp("act", os_, po_[0:DH, :], [por_], [osr_])
                    S.dma("sp", osr_, self.o_d[tl["b"], tl["h"], :, tl["j"] * QW:(tl["j"] + 1) * QW], os_, reads=[osr_])

            def push(tl):
                stageA1(tl)
                stageA2(tl)
                if q1:
                    t2 = q1.pop(0)
                    stageB(t2)
                    q2.append(t2)
                q1.append(tl)
                while len(q2) > 1:
                    stageC(q2.pop(0))

            heads = [(b, h) for b in range(ns) for h in range(NH)]

            def loads(i):
                b_, h_ = heads[i]
                sl_ = i % 3
                S.dma("sp", kta[sl_][1], kta[sl_][0], self.kt_d[b_, h_], writes=[kta[sl_][1]])
                S.dma("sp", qta[sl_][1], qta[sl_][0], self.qt_d[b_, h_], writes=[qta[sl_][1]])
                S.dma("sp", va[sl_][1], va[sl_][0],
                      self.v_d[b_, :, h_ * DH:(h_ + 1) * DH].rearrange("(k p) d -> p k d", p=128), writes=[va[sl_][1]])

            loads(0)
            for hi, (b, h) in enumerate(heads):
                sl = hi % 3
                if hi + 1 < len(heads):
                    loads(hi + 1)
                for j in range(NQ):
                    nkb = 8 * j + 8
                    Cc = ccr.next()
                    self.memset("pool", Cc[0], 0.0, [Cc[1]])
                    started = [False, False]
                    for n, kb in enumerate(range(nkb - 1, -1, -1)):
                        diag = kb >= 8 * j
                        c0 = (kb - 8 * j) * 128 if diag else 0
                        tl = dict(b=b, h=h, j=j, kb=kb, c0=c0, sl=sl, diag=diag, started=started,
                                  last=(kb == 0), ps=psr.next(), Cc=Cc)
                        push(tl)
            while q1:
                t2 = q1.pop(0)
                stageB(t2)
                q2.append(t2)
            while q2:
                stageC(q2.pop(0))

    def attn_out(self, l, kind, Xin, Xout):
        nc, S, ns, T = self.nc, self.S, self.nseq, self.T
        TT = 512
        nsub = TT // 128
        W = self.din
        fox = kind == "fox"
        with ExitStack() as ph:
            wo = self.sb(ph, "wo", [128, 8, D], BF16)
            for kc in range(8):
                self.load_w_bf16(wo[:, kc, :], W["fox_w_out" if fox else "sb_w_out"][0, kc * 128:(kc + 1) * 128, :], "wo", "wo")
            E = self.epi_setup(ph, l, 0)
            xt = [(self.sb(ph, "xt%d" % i, [128, nsub, D], F32), "xt%d" % i) for i in range(3)]
            oT = [(self.sb(ph, "oT%d" % i, [128, 8, TT], F32), "oT%d" % i) for i in range(2)]
            oTb = [(self.sb(ph, "oTb%d" % i, [128, 8, TT], BF16), "oTb%d" % i) for i in range(2)]
            if fox:
                dnb = [(self.sb(ph, "dnb%d" % i, [128, 8, TT], F32), "dnb%d" % i) for i in range(2)]
            tiles = [(b, t0) for b in range(ns) for t0 in range(0, T, TT)]
            n = len(tiles)

            def loads(i):
                b_, t0_ = tiles[i]
                sl_ = i % 2
                S.dma("sp", xt[i % 3][1], xt[i % 3][0], self.x_tile(Xin, b_, t0_, TT), writes=[xt[i % 3][1]])
                S.dma("sp", oT[sl_][1], oT[sl_][0],
                      self.o_d[b_].rearrange("(j hh) d t -> (hh d) j t", hh=2)[:, :, t0_:t0_ + TT], writes=[oT[sl_][1]])
                if fox:
                    dv = self.den_d[b_].rearrange("(j hh) t -> hh j t", hh=2)
                    for hh in range(2):
                        S.dma("sp", dnb[sl_][1], dnb[sl_][0][hh * DH:(hh + 1) * DH],
                              dv[hh, :, t0_:t0_ + TT].partition_broadcast(DH), writes=[dnb[sl_][1]])

            def norm(i):
                sl = i % 2
                o_, or_ = oT[sl]
                ob, obr = oTb[sl]
                if fox:
                    dn, dnr = dnb[sl]
                    S.op("dve", lambda e, dn=dn: e.reciprocal(dn, dn), [dnr], [dnr])
                    self.tt("dve", ob, o_, dn, ALU.mult, [or_, dnr], [obr])
                else:
                    self.cp("pool", ob, o_, [or_], [obr])

            def main(i):
                b, t0 = tiles[i]
                xs, xr = xt[i % 3]
                ob, obr = oTb[i % 2]
                for s in range(nsub):
                    pyt, pyr = self.py
                    for hf in range(2):
                        for j in range(8):
                            self.mm(pyt[:, hf * 512:(hf + 1) * 512], ob[:, j, s * 128:(s + 1) * 128],
                                    wo[:, j, hf * 512:(hf + 1) * 512], j == 0, j == 7, [obr, "wo"], [pyr])
                    self.epilogue(E, b, pyt, pyr, xs[:, s, :], xr, Xout[b, t0 + s * 128:t0 + (s + 1) * 128, :])

            loads(0)
            if n > 1:
                loads(1)
            norm(0)
            for i in range(n):
                if i + 2 < n:
                    loads(i + 2)
                if i + 1 < n:
                    norm(i + 1)
                main(i)


def _plan_full():
    plan = [("mod",)]
    plan += [("gmlp", 0), ("ffn", 0), ("attn", 1, "fox"), ("ffn", 1), ("attn", 2, "sb"), ("ffn", 2), ("conv", 3), ("ffn", 3)]
    return plan


_CACHE = {}


def kernel(**inputs):
    ncores = 8
    nseq = 2
    T = 4096
    if "nc" not in _CACHE:
        _CACHE["nc"] = Builder(nseq, T, _plan_full()).build()
    nc = _CACHE["nc"]
    x = np.ascontiguousarray(inputs["x"], dtype=np.float32)
    c = np.ascontiguousarray(inputs["c"], dtype=np.float32)
    in_maps = []
    for i in range(ncores):
        m = {k: np.ascontiguousarray(inputs[k], dtype=np.float32) for k in WEIGHT_SHAPES}
        m["x"] = x[i * nseq:(i + 1) * nseq]
        m["c"] = c[i * nseq:(i + 1) * nseq]
        in_maps.append(m)
    res = run_bass_kernel_spmd(nc, in_maps, core_ids=list(range(ncores)))
    return np.concatenate([r["out"] for r in res.results], axis=0)
```

```python
import numpy as np
from contextlib import ExitStack
import concourse.bass as bass
import concourse.mybir as mybir
from concourse.bass_utils import run_bass_kernel_spmd

F32 = mybir.dt.float32
BF16 = mybir.dt.bfloat16
AF = mybir.ActivationFunctionType
ALU = mybir.AluOpType

D = 1024
FH = 2816
NH = 16
DH = 64
DEPTH = 4
ALPHA = float((2.0 * DEPTH) ** 0.25)
EPS = 1e-5
CONVW = 31

ENGS = ["pe", "act", "dve", "pool", "sp"]

WEIGHT_SHAPES = {
    "mod_w": (4, 1024, 6144), "mod_b": (4, 6144), "ln1_g": (4, 1024), "ln1_b": (4, 1024),
    "ln2_g": (4, 1024), "ln2_b": (4, 1024), "ffn_w_in": (4, 1024, 5632), "ffn_w_out": (4, 2816, 1024),
    "gm_w_in": (1, 1024, 2048), "gm_b_in": (1, 2048), "gm_ln_g": (1, 1024), "gm_ln_b": (1, 1024),
    "gm_w_s": (1, 8, 128, 128), "gm_b_s": (1, 8, 128), "gm_w_out": (1, 1024, 1024),
    "fox_w_in": (1, 1024, 3088), "fox_b_f": (1, 16), "fox_w_out": (1, 1024, 1024),
    "sb_w_in": (1, 1024, 3072), "sb_w_out": (1, 1024, 1024),
    "cv_w_in": (1, 1024, 2048), "cv_b_in": (1, 2048), "cv_dw": (1, 31, 1024), "cv_dw_b": (1, 1024),
    "cv_ln_g": (1, 1024), "cv_ln_b": (1, 1024), "cv_w_out": (1, 1024, 1024), "cv_b_out": (1, 1024),
}


class Sched:
    def __init__(self, nc, same_engine_raw=True):
        self.nc = nc
        self.ops = {e: [] for e in ENGS}
        self.last_w = {}
        self.readers = {}
        self.seen = {e: {} for e in ENGS}
        self.dma_cnt = []
        self.keymap = {}
        self.same_engine_raw = same_engine_raw

    def _add_wait(self, eng, waits, ev, is_raw):
        if ev is None:
            return
        if ev[0] == "eng":
            _, e2, idx = ev
            if e2 == eng and (eng == "pe" or not self.same_engine_raw):
                return
            key, val = ("eng", e2), idx
        else:
            key, val = ("dma", ev[1]), ev[2]
        if self.seen[eng].get(key, -1) >= val:
            return
        if val > waits.get(key, -1):
            waits[key] = val

    def _deps(self, eng, reads, writes):
        waits = {}
        for r in reads:
            self._add_wait(eng, waits, self.last_w.get(r), True)
        for w in writes:
            self._add_wait(eng, waits, self.last_w.get(w), False)
            for ev in self.readers.get(w, ()):
                self._add_wait(eng, waits, ev, False)
        for k, v in waits.items():
            self.seen[eng][k] = v
        return waits

    def _commit(self, ev, reads, writes):
        for r in reads:
            self.readers.setdefault(r, []).append(ev)
        for w in writes:
            self.last_w[w] = ev
            self.readers[w] = []

    def op(self, eng, fn, reads=(), writes=()):
        waits = self._deps(eng, reads, writes)
        idx = len(self.ops[eng])
        self.ops[eng].append(dict(kind="op", fn=fn, waits=waits, needed=False))
        self._commit(("eng", eng, idx), reads, writes)

    def dma(self, eng, key, out, in_, reads=(), writes=(), **kw):
        waits = self._deps(eng, reads, writes)
        if key not in self.keymap:
            self.keymap[key] = len(self.keymap)
            if len(self.dma_cnt) < len(self.keymap):
                self.dma_cnt.append(0)
        ph = self.keymap[key]
        self.dma_cnt[ph] += 1
        self.ops[eng].append(dict(kind="dma", out=out, in_=in_, kw=kw, key=ph, waits=waits))
        self._commit(("dma", ph, 16 * self.dma_cnt[ph]), reads, writes)

    def _wait_everything(self, eng):
        waits = {}
        for ev in self.last_w.values():
            self._add_wait(eng, waits, ev, False)
        for evs in self.readers.values():
            for ev in evs:
                self._add_wait(eng, waits, ev, False)
        for k, v in waits.items():
            self.seen[eng][k] = v
        self.ops[eng].append(dict(kind="waitonly", waits=waits))

    def barrier(self):
        for e in ENGS:
            self._wait_everything(e)
        self.last_w = {}
        self.readers = {}
        self.keymap = {}

    def emit(self, stack):
        nc = self.nc
        for e in ENGS:
            for o in self.ops[e]:
                for k, v in o["waits"].items():
                    if k[0] == "eng":
                        self.ops[k[1]][v]["needed"] = True
        semval = {}
        for e in ENGS:
            c = 0
            for i, o in enumerate(self.ops[e]):
                if o["kind"] == "op" and o["needed"]:
                    c += 1
                    semval[(e, i)] = c
        esem = {e: stack.enter_context(nc.semaphore("s_" + e)) for e in ENGS}
        dsem = [stack.enter_context(nc.semaphore("d%d" % i)) for i in range(len(self.dma_cnt))]
        print("[sched] ops:", {e: len(self.ops[e]) for e in ENGS}, "sem max:", {e: max([0] + [v for (ee, _), v in semval.items() if ee == e]) for e in ENGS},
              "dma sems:", len(self.dma_cnt), "max dma cnt:", max(self.dma_cnt) * 16)
        block = stack.enter_context(nc.Block())

        def run(e, eng):
            for o in self.ops[e]:
                for k, v in o["waits"].items():
                    if k[0] == "eng":
                        eng.wait_ge(esem[k[1]], semval[(k[1], v)])
                    else:
                        eng.wait_ge(dsem[k[1]], v)
                if o["kind"] == "op":
                    ins = o["fn"](eng)
                    if o["needed"]:
                        ins.then_inc(esem[e], 1)
                elif o["kind"] == "dma":
                    eng.dma_start(out=o["out"], in_=o["in_"], **o["kw"]).then_inc(dsem[o["key"]], 16)

        @block.tensor
        def _(eng):
            run("pe", eng)

        @block.scalar
        def _(eng):
            run("act", eng)

        @block.vector
        def _(eng):
            run("dve", eng)

        @block.gpsimd
        def _(eng):
            run("pool", eng)

        @block.sync
        def _(eng):
            run("sp", eng)


class Ring:
    def __init__(self, items):
        self.items = list(items)
        self.i = 0

    def next(self):
        it = self.items[self.i % len(self.items)]
        self.i += 1
        return it


class Builder:
    def __init__(self, nseq, T, plan, same_engine_raw=True):
        self.nseq, self.T, self.plan = nseq, T, plan
        self.nc = nc = bass.Bass("TRN2", target_bir_lowering=False)
        self.S = Sched(nc, same_engine_raw)
        self._n = 0
        self.din = {}
        self.din["x"] = nc.dram_tensor("x", [nseq, T, D], F32, kind="ExternalInput").ap()
        self.din["c"] = nc.dram_tensor("c", [nseq, D], F32, kind="ExternalInput").ap()
        for k, shp in WEIGHT_SHAPES.items():
            self.din[k] = nc.dram_tensor(k, list(shp), F32, kind="ExternalInput").ap()
        self.dout = nc.dram_tensor("out", [nseq, T, D], F32, kind="ExternalOutput").ap()
        self.xa = nc.dram_tensor("xa_s", [nseq, T, D], F32).ap()
        self.xb = nc.dram_tensor("xb_s", [nseq, T, D], F32).ap()
        self.modrow = nc.dram_tensor("modrow_s", [DEPTH, nseq, 6 * D], F32).ap()
        self.qt_d = nc.dram_tensor("qt_s", [nseq, NH, DH, T], BF16).ap()
        self.kt_d = nc.dram_tensor("kt_s", [nseq, NH, DH, T], BF16).ap()
        self.v_d = nc.dram_tensor("v_s", [nseq, T, D], BF16).ap()
        self.fq_d = nc.dram_tensor("fq_s", [nseq, NH, 3, T], BF16).ap()
        self.fk_d = nc.dram_tensor("fk_s", [nseq, NH, 3, T], BF16).ap()
        self.o_d = nc.dram_tensor("o_s", [nseq, NH, DH, T], F32).ap()
        self.den_d = nc.dram_tensor("den_s", [nseq, NH, T], F32).ap()

    def sb(self, ph, name, shape, dt):
        self._n += 1
        return ph.enter_context(self.nc.sbuf_tensor("%s_%d" % (name, self._n), list(shape), dt))[:]

    def mm(self, out, lhsT, rhs, start, stop, reads, writes):
        self.S.op("pe", lambda e: e.matmul(out, lhsT, rhs, start=start, stop=stop), reads, writes)

    def tr(self, out, in_, ident, reads, writes):
        self.S.op("pe", lambda e: e.transpose(out, in_, ident), reads, writes)

    def act(self, out, in_, func, reads, writes, bias=None, scale=None, eng="act"):
        kw = {}
        if bias is not None:
            kw["bias"] = bias
        if scale is not None:
            kw["scale"] = scale
        self.S.op(eng, lambda e: e.activation(out, in_, func, **kw), reads, writes)

    def tt(self, eng, out, in0, in1, op, reads, writes):
        self.S.op(eng, lambda e: e.tensor_tensor(out, in0, in1, op), reads, writes)

    def ts(self, eng, out, in0, s1, s2, op0, op1, reads, writes):
        if op1 is None:
            self.S.op(eng, lambda e: e.tensor_scalar(out, in0, s1, None, op0), reads, writes)
        else:
            self.S.op(eng, lambda e: e.tensor_scalar(out, in0, s1, s2, op0, op1), reads, writes)

    def stt(self, out, in0, scalar, in1, op0, op1, reads, writes):
        self.S.op("dve", lambda e: e.scalar_tensor_tensor(out, in0, scalar, in1, op0, op1), reads, writes)

    def cp(self, eng, out, in_, reads, writes):
        if eng == "act":
            self.S.op("act", lambda e: e.copy(out, in_), reads, writes)
        else:
            self.S.op(eng, lambda e: e.tensor_copy(out, in_), reads, writes)

    def memset(self, eng, ap, val, writes):
        self.S.op(eng, lambda e: e.memset(ap, val), (), writes)

    def load_w_bf16(self, dst, src, key, res, maxcols=2048):
        n = dst.shape[-1]
        c0 = 0
        while c0 < n:
            c1 = min(n, c0 + maxcols)
            self.S.dma("pool", key, dst[:, c0:c1], src[:, c0:c1], writes=[res])
            c0 = c1

    def build(self):
        nc, S = self.nc, self.S
        with ExitStack() as st:
            self.ident = st.enter_context(nc.sbuf_tensor("ident", [128, 128], F32))[:]
            self.identb = st.enter_context(nc.sbuf_tensor("identb", [128, 128], BF16))[:]
            self.modT = st.enter_context(nc.sbuf_tensor("modT", [128, DEPTH, self.nseq, 48], F32))[:]
            self.mhalf = st.enter_context(nc.sbuf_tensor("mhalf", [128, 1], F32))[:]
            self.pd = []
            self.pb = []
            for i in range(4):
                t = st.enter_context(nc.psum_tensor("pd%d" % i, [128, 1024], F32))[:]
                self.pd.append((t, "pb%d" % (2 * i)))
                self.pb += [(t[:, 0:512], "pb%d" % (2 * i)), (t[:, 512:1024], "pb%d" % (2 * i + 1))]
            self.py = self.pd[2]
            self.py1 = self.pd[3]
            self.memset("pool", self.ident, 1.0, ["ident"])
            S.op("pool", lambda e: e.affine_select(self.ident, self.ident, [[1, 128]], ALU.is_equal, 0.0,
                                                    base=0, channel_multiplier=-1), ["ident"], ["ident"])
            self.cp("dve", self.identb, self.ident, ["ident"], ["identb"])
            self.memset("pool", self.mhalf, -0.5, ["mhalf"])
            cur = self.din["x"]
            nxt = [self.xa, self.xb]
            nph = len([p for p in self.plan if p[0] != "mod"])
            k = 0
            for p in self.plan:
                if p[0] == "mod":
                    self.phase_mod()
                else:
                    k += 1
                    dst = self.dout if k == nph else nxt[k % 2]
                    if p[0] == "ffn":
                        self.phase_ffn(p[1], cur, dst)
                    elif p[0] == "gmlp":
                        self.phase_gmlp(p[1], cur, dst)
                    elif p[0] == "conv":
                        self.phase_conv(p[1], cur, dst)
                    elif p[0] == "attn":
                        self.phase_attn(p[1], p[2], cur, dst)
                    cur = dst
                S.barrier()
            S.emit(st)
        return nc

    def phase_mod(self):
        nc, S, ns = self.nc, self.S, self.nseq
        with ExitStack() as ph:
            cs = self.sb(ph, "cs", [ns, D], F32)
            ca = self.sb(ph, "ca", [ns, D], F32)
            caT = self.sb(ph, "caT", [128, 8, ns], F32)
            CB = self.sb(ph, "CB", [128, ns, 8, 128], F32)
            wst = [self.sb(ph, "wst%d" % i, [128, 8, 512], F32) for i in range(2)]
            mbb = [self.sb(ph, "mbb%d" % i, [128, 512], F32) for i in range(2)]
            mrow = [self.sb(ph, "mrow%d" % i, [128, 512], F32) for i in range(2)]
            stg = [self.sb(ph, "stg%d" % i, [48, 128], F32) for i in range(2)]
            S.dma("sp", "cs", cs, self.din["c"][:, :], writes=["cs"])
            self.act(ca, cs, AF.Silu, ["cs"], ["ca"])
            pbt, pbr = self.pb[0]
            for kc in range(8):
                self.tr(pbt[:, kc * ns:(kc + 1) * ns], ca[:, kc * 128:(kc + 1) * 128], self.ident[0:ns, 0:ns],
                        ["ca", "ident"], [pbr])
            self.cp("dve", caT, pbt[:, 0:8 * ns].rearrange("p (k b) -> p k b", b=ns), [pbr], ["caT"])
            for b in range(ns):
                self.cp("dve", CB[:, b], caT[:, :, b:b + 1].to_broadcast([128, 8, 128]), ["caT"], ["CB"])
            mi = 0
            blocks = [(l, blk) for l in range(DEPTH) for blk in range(12)]

            def wload(i):
                l_, blk_ = blocks[i]
                sl_ = i % 2
                cols_ = slice(blk_ * 512, (blk_ + 1) * 512)
                S.dma("sp", "wst%d" % sl_, wst[sl_],
                      self.din["mod_w"][l_, :, cols_].rearrange("(k p) f -> p k f", p=128), writes=["wst%d" % sl_])
                S.dma("sp", "mbb%d" % sl_, mbb[sl_], self.din["mod_b"][l_, cols_].partition_broadcast(128),
                      writes=["mbb%d" % sl_])

            wload(0)
            for it, (l, blk) in enumerate(blocks):
                if True:
                    sl = it % 2
                    cols = slice(blk * 512, (blk + 1) * 512)
                    if it + 1 < len(blocks):
                        wload(it + 1)
                    for b in range(ns):
                        pt, pr = self.pb[1 + (mi % 2)]
                        ms = mi % 2
                        mi += 1
                        for kc in range(8):
                            self.mm(pt, CB[:, b, kc, :], wst[sl][:, kc, :], kc == 0, kc == 7,
                                    ["CB", "wst%d" % sl], [pr])
                        self.tt("dve", mrow[ms], pt, mbb[sl], ALU.add, [pr, "mbb%d" % sl], ["mrow%d" % ms])
                        S.dma("sp", "mrow%d" % ms, self.modrow[l, b:b + 1, cols], mrow[ms][0:1, :],
                              reads=["mrow%d" % ms], writes=["modrow_%d_%d_%d" % (l, b, blk)])
            si = 0
            for l in range(DEPTH):
                for b in range(ns):
                    sl = si % 2
                    si += 1
                    S.dma("sp", "stg%d" % sl, stg[sl], self.modrow[l, b, :].rearrange("(c p) -> c p", p=128),
                          reads=["modrow_%d_%d_%d" % (l, b, blk) for blk in range(12)], writes=["stg%d" % sl])
                    pt, pr = self.pb[3 + sl]
                    self.tr(pt[:, 0:48], stg[sl], self.ident[0:48, 0:48], ["stg%d" % sl, "ident"], [pr])
                    self.cp("dve", self.modT[:, l, b, :], pt[:, 0:48], [pr], ["modT"])
            for c0 in (8, 32):
                self.ts("dve", self.modT[:, :, :, c0:c0 + 8], self.modT[:, :, :, c0:c0 + 8], 1.0, None, ALU.add, None,
                        ["modT"], ["modT"])

    def load_bcast(self, ph, name, src_row):
        n = src_row.shape[-1]
        t = self.sb(ph, name, [128, n], F32)
        self._n += 1
        res = "%s_%d" % (name, self._n)
        self.S.dma("sp", res, t, src_row.partition_broadcast(128), writes=[res])
        return t, res

    def load_pp(self, ph, name, src2d, pbi=0):
        n = src2d.shape[0]
        stg = self.sb(ph, name + "s", [n, 128], F32)
        t = self.sb(ph, name, [128, n], F32)
        self._n += 1
        res = "%s_%d" % (name, self._n)
        self.S.dma("sp", res + "s", stg, src2d, writes=[res + "s"])
        pt, pr = self.pb[pbi]
        self.tr(pt[:, 0:n], stg, self.ident[0:n, 0:n], [res + "s", "ident"], [pr])
        self.cp("dve", t, pt[:, 0:n], [pr], [res])
        return t, res

    def epi_setup(self, ph, l, which, single_gb=False):
        gcol = 2048 if which == 0 else 5120
        GB = []
        if single_gb:
            t = self.sb(ph, "GBs", [128, D], F32)
            GB = [(t, "GBs")] * self.nseq
        else:
            for b in range(self.nseq):
                t, r = self.load_bcast(ph, "GB%d" % b, self.modrow[l, b, gcol:gcol + D])
                self.ts("dve", t, t, 1.0, None, ALU.add, None, [r], [r])
                GB.append((t, r))
        lng = self.load_bcast(ph, "lng", self.din["ln1_g" if which == 0 else "ln2_g"][l, :])
        lnb = self.load_bcast(ph, "lnb", self.din["ln1_b" if which == 0 else "ln2_b"][l, :])
        eb = []
        for i in range(2):
            eb.append(dict(
                buf=self.sb(ph, "ebuf%d" % i, [128, D], F32), st=self.sb(ph, "est%d" % i, [128, 12], F32),
                mv=self.sb(ph, "emv%d" % i, [128, 2], F32), sm=self.sb(ph, "esm%d" % i, [128, 4], F32),
                res="ebuf%d_%d" % (i, self._n)))
        return dict(GB=GB, lng=lng, lnb=lnb, eb=eb, i=0, single=single_gb, cur=None, l=l, gcol=gcol)

    def epi_select(self, E, b):
        if not E["single"] or E["cur"] == b:
            return
        E["cur"] = b
        t, r = E["GB"][b]
        self.S.dma("sp", r, t, self.modrow[E["l"], b, E["gcol"]:E["gcol"] + D].partition_broadcast(128), writes=[r])
        self.ts("dve", t, t, 1.0, None, ALU.add, None, [r], [r])

    def epilogue(self, E, b, y, yres, x, xres, out_rows, ybias=None):
        S = self.S
        e = E["eb"][E["i"] % 2]
        E["i"] += 1
        buf, r = e["buf"], e["res"]
        GBt, GBr = E["GB"][b]
        if ybias is not None:
            self.tt("dve", buf, y, ybias[0], ALU.add, [yres, ybias[1]], [r])
            self.tt("dve", buf, buf, GBt, ALU.mult, [r, GBr], [r])
        else:
            self.tt("dve", buf, y, GBt, ALU.mult, [yres, GBr], [r])
        self.stt(buf, x, ALPHA, buf, ALU.mult, ALU.add, [xres, r], [r])
        self.ln_rows(buf, r, e)
        self.tt("dve", buf, buf, E["lng"][0], ALU.mult, [r, E["lng"][1]], [r])
        self.tt("pool", buf, buf, E["lnb"][0], ALU.add, [r, E["lnb"][1]], [r])
        S.dma("sp", r, out_rows, buf, reads=[r])

    def ln_rows(self, buf, r, e, out=None, eng_norm="act"):
        S = self.S
        st, mv, sm = e["st"], e["mv"], e["sm"]
        rs = r + "s"
        S.op("dve", lambda en: en.bn_stats(st[:, 0:6], buf[:, 0:512]), [r], [rs])
        S.op("dve", lambda en: en.bn_stats(st[:, 6:12], buf[:, 512:1024]), [r], [rs])
        S.op("dve", lambda en: en.bn_aggr(mv, st), [rs], [rs])
        self.ts("dve", sm[:, 0:1], mv[:, 1:2], EPS, None, ALU.add, None, [rs], [rs])
        self.tt("pool", sm[:, 1:2], sm[:, 0:1], self.mhalf, ALU.pow, [rs, "mhalf"], [rs])
        self.ts("dve", sm[:, 2:3], mv[:, 0:1], sm[:, 1:2], -1.0, ALU.mult, ALU.mult, [rs], [rs])
        o = buf if out is None else out[0]
        wr = [r] if out is None else [out[1]]
        self.act(o, buf, AF.Identity, [r, rs], wr, bias=sm[:, 2:3], scale=sm[:, 1:2])

    def xT_mod(self, xt, xres, nsub, hT, hres, l, b, which, tpr):
        sh0 = 0 if which == 0 else 24
        sc0 = 8 if which == 0 else 32
        for kc in range(8):
            pt, pr = tpr.next()
            for s in range(nsub):
                self.tr(pt[:, s * 128:(s + 1) * 128], xt[:, s, kc * 128:(kc + 1) * 128], self.ident,
                        [xres, "ident"], [pr])
            self.act(hT[:, kc, 0:nsub * 128], pt[:, 0:nsub * 128], AF.Identity, [pr, "modT"], [hres],
                     bias=self.modT[:, l, b, sh0 + kc:sh0 + kc + 1], scale=self.modT[:, l, b, sc0 + kc:sc0 + kc + 1])

    def x_tile(self, X, b, t0, tt):
        return X[b, t0:t0 + tt, :].rearrange("(s p) d -> p s d", p=128)

    def phase_ffn(self, l, Xin, Xout):
        nc, S, ns, T = self.nc, self.S, self.nseq, self.T
        TT = 256
        nsub = TT // 128
        NF = FH // 128
        with ExitStack() as ph:
            w1 = self.sb(ph, "w1", [128, 8, 2 * FH], BF16)
            w2 = self.sb(ph, "w2", [128, NF, D], BF16)
            for kc in range(8):
                self.load_w_bf16(w1[:, kc, :], self.din["ffn_w_in"][l, kc * 128:(kc + 1) * 128, :], "w1", "w1", 1408)
            for fc in range(NF):
                self.load_w_bf16(w2[:, fc, :], self.din["ffn_w_out"][l, fc * 128:(fc + 1) * 128, :], "w2", "w2")
            E = self.epi_setup(ph, l, 1)
            xt = [(self.sb(ph, "xt%d" % i, [128, nsub, D], F32), "xt%d" % i) for i in range(2)]
            hT = [(self.sb(ph, "hT%d" % i, [128, 8, TT], BF16), "hT%d" % i) for i in range(2)]
            aT = (self.sb(ph, "aT", [128, NF, TT], BF16), "aT")
            sg = [(self.sb(ph, "sg%d" % i, [128, TT], F32), "sg%d" % i) for i in range(2)]
            tpr = Ring(self.pb[0:2])
            gur = Ring([(self.pb[2], self.pb[3]), (self.pb[6], self.pb[7])])
            tiles = [(b, t0) for b in range(ns) for t0 in range(0, T, TT)]
            n = len(tiles)

            def loads(i):
                b_, t0_ = tiles[i]
                S.dma("sp", xt[i % 2][1], xt[i % 2][0], self.x_tile(Xin, b_, t0_, TT), writes=[xt[i % 2][1]])

            def prologue(i):
                self.xT_mod(xt[i % 2][0], xt[i % 2][1], nsub, hT[i % 2][0], hT[i % 2][1], l, tiles[i][0], 1, tpr)

            def inproj(i):
                hs, hr = hT[i % 2]
                for fc in range(NF):
                    (pg, pgr), (pu, pur) = gur.next()
                    for kc in range(8):
                        self.mm(pg[:, 0:TT], w1[:, kc, fc * 128:(fc + 1) * 128], hs[:, kc, :], kc == 0, kc == 7,
                                ["w1", hr], [pgr])
                    for kc in range(8):
                        self.mm(pu[:, 0:TT], w1[:, kc, FH + fc * 128:FH + (fc + 1) * 128], hs[:, kc, :], kc == 0,
                                kc == 7, ["w1", hr], [pur])
                    sgt, sgr = sg[fc % 2]
                    self.act(sgt, pg[:, 0:TT], AF.Silu, [pgr], [sgr])
                    self.tt("dve", aT[0][:, fc, :], sgt, pu[:, 0:TT], ALU.mult, [sgr, pur], [aT[1]])

            def outproj(i):
                b, t0 = tiles[i]
                xs, xr = xt[i % 2]
                for s in range(nsub):
                    pyt, pyr = self.py
                    for hf in range(2):
                        for fc in range(NF):
                            self.mm(pyt[:, hf * 512:(hf + 1) * 512], aT[0][:, fc, s * 128:(s + 1) * 128],
                                    w2[:, fc, hf * 512:(hf + 1) * 512], fc == 0, fc == NF - 1, [aT[1], "w2"], [pyr])
                    self.epilogue(E, b, pyt, pyr, xs[:, s, :], xr, Xout[b, t0 + s * 128:t0 + (s + 1) * 128, :])

            loads(0)
            if n > 1:
                loads(1)
            print("[sbuf] ffn remaining", nc.sbuf_bytes_remaining)
            prologue(0)
            for i in range(n):
                inproj(i)
                if i + 1 < n:
                    prologue(i + 1)
                outproj(i)
                if i + 2 < n:
                    loads(i + 2)

    def phase_gmlp(self, l, Xin, Xout):
        nc, S, ns, T = self.nc, self.S, self.nseq, self.T
        TT = 512
        nsub = TT // 128
        W = self.din
        with ExitStack() as ph:
            wi = self.sb(ph, "wi", [128, 8, 2 * D], BF16)
            wo = self.sb(ph, "wo", [128, 8, D], BF16)
            for kc in range(8):
                self.load_w_bf16(wi[:, kc, :], W["gm_w_in"][0, kc * 128:(kc + 1) * 128, :], "wi", "wi")
                self.load_w_bf16(wo[:, kc, :], W["gm_w_out"][0, kc * 128:(kc + 1) * 128, :], "wo", "wo")
            wsl = self.sb(ph, "wsl", [128, 8, 128], F32)
            wsm = self.sb(ph, "wsm", [128, 8, 128], F32)
            wmT = self.sb(ph, "wmT", [128, 8, 128], BF16)
            S.dma("sp", "wsl", wsl, W["gm_w_s"][0].rearrange("g t s -> t g s"), writes=["wsl"])
            for g in range(8):
                pt, pr = self.pb[g % 2]
                self.tr(pt[:, 0:128], wsl[:, g, :], self.ident, ["wsl", "ident"], [pr])
                self.cp("dve", wsm[:, g, :], pt[:, 0:128], [pr], ["wsm"])
                S.op("pool", lambda e, g=g: e.affine_select(wmT[:, g, :], wsm[:, g, :], [[1, 128]], ALU.is_ge, 0.0,
                                                             base=0, channel_multiplier=-1), ["wsm"], ["wmT"])
            binu, binu_r = self.load_pp(ph, "binu", W["gm_b_in"][0, 0:D].rearrange("(c p) -> c p", p=128), 2)
            binv, binv_r = self.load_bcast(ph, "binv", W["gm_b_in"][0, D:2 * D])
            glg, glg_r = self.load_bcast(ph, "glg", W["gm_ln_g"][0, :])
            glb, glb_r = self.load_bcast(ph, "glb", W["gm_ln_b"][0, :])
            bsb, bsb_r = self.load_bcast(ph, "bsb", W["gm_b_s"][0].rearrange("g t -> (g t)"))
            bsb3 = bsb.rearrange("p (g t) -> p g t", t=128)
            E = self.epi_setup(ph, l, 0, single_gb=True)
            xt = [(self.sb(ph, "xt%d" % i, [128, nsub, D], F32), "xt%d" % i) for i in range(3)]
            hT = [(self.sb(ph, "hT%d" % i, [128, 8, TT], BF16), "hT%d" % i) for i in range(2)]
            vz = [dict(buf=self.sb(ph, "vz%d" % i, [128, D], F32), st=self.sb(ph, "vst%d" % i, [128, 12], F32),
                       mv=self.sb(ph, "vmv%d" % i, [128, 2], F32), sm=self.sb(ph, "vsm%d" % i, [128, 4], F32),
                       res="vz%d" % i) for i in range(2)]
            vn = [[(self.sb(ph, "vn%d_%d" % (k, i), [128, D], BF16), "vn%d_%d" % (k, i)) for i in range(nsub)] for k in range(2)]
            uT = [(self.sb(ph, "uT%d" % i, [128, TT], F32), "uT%d" % i) for i in range(2)]
            tmp = [(self.sb(ph, "tmp%d" % i, [128, TT], F32), "tmp%d" % i) for i in range(2)]
            yT = (self.sb(ph, "yT", [128, 8, TT], BF16), "yT")
            print("[sbuf] gmlp remaining", nc.sbuf_bytes_remaining)
            tpr = Ring(self.pb[0:2])
            tiles = [(b, t0) for b in range(ns) for t0 in range(0, T, TT)]
            cnt = dict(vi=0)

            def loads(i):
                b_, t0_ = tiles[i]
                S.dma("sp", xt[i % 3][1], xt[i % 3][0], self.x_tile(Xin, b_, t0_, TT), writes=[xt[i % 3][1]])

            def stageA(i):
                b, t0 = tiles[i]
                xs, xr = xt[i % 3]
                hs, hr = hT[i % 2]
                self.xT_mod(xs, xr, nsub, hs, hr, l, b, 0, tpr)
                for s in range(nsub):
                    pv, pvr = self.py1
                    for hf in range(2):
                        for kc in range(8):
                            self.mm(pv[:, hf * 512:(hf + 1) * 512], hs[:, kc, s * 128:(s + 1) * 128],
                                    wi[:, kc, D + hf * 512:D + (hf + 1) * 512], kc == 0, kc == 7, [hr, "wi"], [pvr])
                    z = vz[cnt["vi"] % 2]
                    cnt["vi"] += 1
                    vt, vr = vn[i % 2][s]
                    self.tt("dve", z["buf"], pv, binv, ALU.add, [pvr, binv_r], [z["res"]])
                    self.act(z["buf"], z["buf"], AF.Gelu, [z["res"]], [z["res"]])
                    self.ln_rows(z["buf"], z["res"], z)
                    self.tt("dve", z["buf"], z["buf"], glg, ALU.mult, [z["res"], glg_r], [z["res"]])
                    self.tt("pool", vt, z["buf"], glb, ALU.add, [z["res"], glb_r], [vr])

            def stageB(i):
                b, t0 = tiles[i]
                xs, xr = xt[i % 3]
                hs, hr = hT[i % 2]
                self.epi_select(E, b)
                for g in range(8):
                    pu, pur = self.pb[2]
                    psv, psvr = self.pb[3]
                    for kc in range(8):
                        self.mm(pu, wi[:, kc, g * 128:(g + 1) * 128], hs[:, kc, :], kc == 0, kc == 7, ["wi", hr], [pur])
                    ut, utr = uT[g % 2]
                    self.act(ut, pu, AF.Gelu, [pur, binu_r], [utr], bias=binu[:, g:g + 1])
                    for s in range(nsub):
                        vt, vr = vn[i % 2][s]
                        self.mm(psv[:, s * 128:(s + 1) * 128], vt[:, g * 128:(g + 1) * 128], wmT[:, g, :],
                                True, True, [vr, "wmT"], [psvr])
                    tm, tmr = tmp[g % 2]
                    self.tt("dve", tm.rearrange("p (s t) -> p s t", t=128), psv.rearrange("p (s t) -> p s t", t=128),
                            bsb3[:, g:g + 1, :].to_broadcast([128, nsub, 128]), ALU.add, [psvr, bsb_r], [tmr])
                    self.tt("dve" if g % 2 == 0 else "pool", yT[0][:, g, :], tm, ut, ALU.mult, [tmr, utr], [yT[1]])
                for s in range(nsub):
                    pyt, pyr = self.py
                    for hf in range(2):
                        for g in range(8):
                            self.mm(pyt[:, hf * 512:(hf + 1) * 512], yT[0][:, g, s * 128:(s + 1) * 128],
                                    wo[:, g, hf * 512:(hf + 1) * 512], g == 0, g == 7, [yT[1], "wo"], [pyr])
                    self.epilogue(E, b, pyt, pyr, xs[:, s, :], xr, Xout[b, t0 + s * 128:t0 + (s + 1) * 128, :])

            n = len(tiles)
            loads(0)
            if n > 1:
                loads(1)
            for i in range(n):
                stageA(i)
                if i >= 1:
                    stageB(i - 1)
                if i + 2 < n:
                    loads(i + 2)
            stageB(n - 1)

    def phase_conv(self, l, Xin, Xout):
        nc, S, ns, T = self.nc, self.S, self.nseq, self.T
        TT = 256
        nsub = TT // 128
        W = self.din
        HW = CONVW - 1
        with ExitStack() as ph:
            wi = self.sb(ph, "wi", [128, 8, 2 * D], BF16)
            wo = self.sb(ph, "wo", [128, 8, D], BF16)
            for kc in range(8):
                self.load_w_bf16(wi[:, kc, :], W["cv_w_in"][0, kc * 128:(kc + 1) * 128, :], "wi", "wi")
                self.load_w_bf16(wo[:, kc, :], W["cv_w_out"][0, kc * 128:(kc + 1) * 128, :], "wo", "wo")
            bia, bia_r = self.load_pp(ph, "bia", W["cv_b_in"][0, :].rearrange("(c p) -> c p", p=128), 2)
            dwr = W["cv_dw"][0].rearrange("i (c p) -> (i c) p", p=128)
            dwa, dwa_r = self.load_pp(ph, "dwa", dwr[0:124, :], 2)
            dwb_, dwb_r = self.load_pp(ph, "dwb", dwr[124:248, :], 3)
            dwbias, dwbias_r = self.load_pp(ph, "dwbias", W["cv_dw_b"][0, :].rearrange("(c p) -> c p", p=128), 2)
            clg, clg_r = self.load_pp(ph, "clg", W["cv_ln_g"][0, :].rearrange("(c p) -> c p", p=128), 3)
            clb, clb_r = self.load_pp(ph, "clb", W["cv_ln_b"][0, :].rearrange("(c p) -> c p", p=128), 2)
            bo32 = self.sb(ph, "bo32", [1, D], F32)
            bohi = self.sb(ph, "bohi", [1, D], BF16)
            bolo = self.sb(ph, "bolo", [1, D], BF16)
            one1 = self.sb(ph, "one1", [1, 128], BF16)
            S.dma("sp", "bo32", bo32, W["cv_b_out"][0:1, :], writes=["bo"])
            self.cp("dve", bohi, bo32, ["bo"], ["bohi"])
            self.tt("dve", bo32, bo32, bohi, ALU.subtract, ["bo", "bohi"], ["bo"])
            self.cp("dve", bolo, bo32, ["bo"], ["bolo"])
            self.memset("pool", one1, 1.0, ["one1"])
            diag = self.sb(ph, "diag", [128, CONVW * 8, 128], BF16)
            for half, (dt_, dr_) in enumerate(((dwa, dwa_r), (dwb_, dwb_r))):
                self.tt("dve", diag[:, half * 124:(half + 1) * 124, :],
                        self.ident[:, None, :].to_broadcast([128, 124, 128]),
                        dt_[:, :, None].to_broadcast([128, 124, 128]), ALU.mult, ["ident", dr_], ["diag"])
            onesf = self.sb(ph, "onesf", [128, 128], F32)
            self.memset("pool", onesf, 1.0, ["onesf"])
            E = self.epi_setup(ph, l, 0, single_gb=True)
            xt = [(self.sb(ph, "xt%d" % i, [128, nsub, D], F32), "xt%d" % i) for i in range(3)]
            hT = (self.sb(ph, "hT", [128, 8, TT], BF16), "hT")
            ybuf = (self.sb(ph, "ybuf", [128, 8, HW + TT], BF16), "ybuf")
            sgm = [(self.sb(ph, "sgm%d" % i, [128, TT], F32), "sgm%d" % i) for i in range(2)]
            zT = [(self.sb(ph, "zT%d" % i, [128, 8, TT], F32), "zT%d" % i) for i in range(2)]
            zq = [(self.sb(ph, "zq%d" % i, [128, TT], F32), "zq%d" % i) for i in range(2)]
            mean_t = [(self.sb(ph, "mean_t%d" % i, [128, TT], F32), "mean_t%d" % i) for i in range(2)]
            rstd_t = [(self.sb(ph, "rstd_t%d" % i, [128, TT], F32), "rstd_t%d" % i) for i in range(2)]
            sT = (self.sb(ph, "sT", [128, 8, TT], BF16), "sT")
            print("[sbuf] conv remaining", nc.sbuf_bytes_remaining)
            tpr = Ring(self.pb[0:1])
            tiles = [(b, t0) for b in range(ns) for t0 in range(0, T, TT)]

            def loads(i):
                b_, t0_ = tiles[i]
                S.dma("sp", xt[i % 3][1], xt[i % 3][0], self.x_tile(Xin, b_, t0_, TT), writes=[xt[i % 3][1]])

            def stageA(i):
                b, t0 = tiles[i]
                xs, xr = xt[i % 3]
                hs, hr = hT
                self.xT_mod(xs, xr, nsub, hs, hr, l, b, 0, tpr)
                if t0 == 0:
                    self.memset("pool", ybuf[0][:, :, 0:HW], 0.0, [ybuf[1]])
                else:
                    self.cp("pool", ybuf[0][:, :, 0:HW], ybuf[0][:, :, TT:TT + HW], [ybuf[1]], [ybuf[1]])
                for kc in range(8):
                    pa, par = self.pb[2]
                    pg, pgr = self.pb[3]
                    for k in range(8):
                        self.mm(pa[:, 0:TT], wi[:, k, kc * 128:(kc + 1) * 128], hs[:, k, :], k == 0, k == 7, ["wi", hr], [par])
                    for k in range(8):
                        self.mm(pg[:, 0:TT], wi[:, k, D + kc * 128:D + (kc + 1) * 128], hs[:, k, :], k == 0, k == 7,
                                ["wi", hr], [pgr])
                    sg_, sgr = sgm[kc % 2]
                    self.act(sg_, pg[:, 0:TT], AF.Sigmoid, [pgr, bia_r], [sgr], bias=bia[:, 8 + kc:9 + kc])
                    self.stt(ybuf[0][:, kc, HW:HW + TT], pa[:, 0:TT], bia[:, kc:kc + 1], sg_, ALU.add, ALU.mult,
                             [par, sgr, bia_r], [ybuf[1]])
                z_, zr = zT[i % 2]
                p1, p1r = self.pb[7]
                p2, p2r = self.pb[1]
                for kc in range(8):
                    pz, pzr = self.pb[6]
                    for t in range(CONVW):
                        self.mm(pz[:, 0:TT], diag[:, t * 8 + kc, :], ybuf[0][:, kc, t:t + TT], t == 0, t == CONVW - 1,
                                ["diag", ybuf[1]], [pzr])
                    self.act(z_[:, kc, :], pz[:, 0:TT], AF.Identity, [pzr, dwbias_r], [zr], bias=dwbias[:, kc:kc + 1])
                    q_, qr = zq[kc % 2]
                    S.op("act", lambda e, q_=q_, kc=kc: e.activation(q_, z_[:, kc, :], AF.Square), [zr], [qr])
                    self.mm(p1[:, 0:TT], onesf, z_[:, kc, :], kc == 0, kc == 7, ["onesf", zr], [p1r])
                    self.mm(p2[:, 0:TT], onesf, q_, kc == 0, kc == 7, ["onesf", qr], [p2r])
                m_, mr = mean_t[i % 2]
                r_, rr = rstd_t[i % 2]
                q, qr_ = r_, rr
                self.ts("dve", m_, p1[:, 0:TT], 1.0 / D, None, ALU.mult, None, [p1r], [mr])
                self.tt("dve", q, m_, m_, ALU.mult, [mr], [qr_])
                self.stt(q, p2[:, 0:TT], 1.0 / D, q, ALU.mult, ALU.subtract, [p2r, qr_], [qr_])
                self.ts("dve", q, q, EPS, None, ALU.add, None, [qr_], [qr_])
                self.act(r_, q, AF.Sqrt, [qr_], [rr])
                S.op("dve", lambda e, r_=r_: e.reciprocal(r_, r_), [rr], [rr])

            def stageB(i):
                b, t0 = tiles[i]
                xs, xr = xt[i % 3]
                z_, zr = zT[i % 2]
                m_, mr = mean_t[i % 2]
                r_, rr = rstd_t[i % 2]
                self.epi_select(E, b)
                mb = m_[:, None, :].to_broadcast([128, 8, TT])
                rb = r_[:, None, :].to_broadcast([128, 8, TT])
                self.tt("dve", z_, z_, mb, ALU.subtract, [zr, mr], [zr])
                self.tt("dve", z_, z_, rb, ALU.mult, [zr, rr], [zr])
                for kc in range(8):
                    self.act(sT[0][:, kc, :], z_[:, kc, :], AF.Silu, [zr, clg_r, clb_r], [sT[1]], bias=clb[:, kc:kc + 1],
                             scale=clg[:, kc:kc + 1])
                for s in range(nsub):
                    pyt, pyr = self.py
                    for hf in range(2):
                        cs = slice(hf * 512, (hf + 1) * 512)
                        for kc in range(8):
                            self.mm(pyt[:, cs], sT[0][:, kc, s * 128:(s + 1) * 128], wo[:, kc, cs], kc == 0, False,
                                    [sT[1], "wo"], [pyr])
                        self.mm(pyt[:, cs], one1, bohi[:, cs], False, False, ["one1", "bohi"], [pyr])
                        self.mm(pyt[:, cs], one1, bolo[:, cs], False, True, ["one1", "bolo"], [pyr])
                    self.epilogue(E, b, pyt, pyr, xs[:, s, :], xr, Xout[b, t0 + s * 128:t0 + (s + 1) * 128, :])

            n = len(tiles)
            loads(0)
            if n > 1:
                loads(1)
            for i in range(n):
                stageA(i)
                if i >= 1:
                    stageB(i - 1)
                if i + 2 < n:
                    loads(i + 2)
            stageB(n - 1)

    def phase_attn(self, l, kind, Xin, Xout):
        self.attn_proj(l, kind, Xin)
        self.S.barrier()
        if kind == "fox":
            self.attn_core_fox()
        else:
            self.attn_core_sb()
        self.S.barrier()
        self.attn_out(l, kind, Xin, Xout)

    def attn_proj(self, l, kind, Xin):
        nc, S, ns, T = self.nc, self.S, self.nseq, self.T
        TT = 512
        nsub = TT // 128
        W = self.din
        fox = kind == "fox"
        NC = 3 * D + (NH if fox else 0)
        wname = "fox_w_in" if fox else "sb_w_in"
        with ExitStack() as ph:
            wi = self.sb(ph, "wi", [128, 8, NC], BF16)
            for kc in range(8):
                self.load_w_bf16(wi[:, kc, :], W[wname][0, kc * 128:(kc + 1) * 128, :], "wi", "wi", 1024 + (NH if fox else 0))
            xt = [(self.sb(ph, "xt%d" % i, [128, nsub, D], F32), "xt%d" % i) for i in range(2)]
            hT = [(self.sb(ph, "hT%d" % i, [128, 8, TT], BF16), "hT%d" % i) for i in range(2)]
            qst = [(self.sb(ph, "qst%d" % i, [128, 8, TT], BF16), "qst%d" % i) for i in range(2)]
            kst = [(self.sb(ph, "kst%d" % i, [128, 8, TT], BF16), "kst%d" % i) for i in range(2)]
            vst = [(self.sb(ph, "vst%d" % i, [128, nsub, D], BF16), "vst%d" % i) for i in range(2)]
            if fox:
                nbf = self.sb(ph, "nbf", [NH, 1], F32)
                S.dma("sp", "nbf", nbf, W["fox_b_f"][0, :].rearrange("(h o) -> h o", o=1), writes=["nbf"])
                self.ts("dve", nbf, nbf, -1.0, None, ALU.mult, None, ["nbf"], ["nbf"])
                ones16 = self.sb(ph, "ones16", [NH, TT], F32)
                self.memset("dve", ones16, 1.0, ["ones16"])
                ef = self.sb(ph, "ef", [NH, TT], F32)
                cum = [(self.sb(ph, "cum%d" % i, [NH, TT], F32), "cum%d" % i) for i in range(2)]
                s8 = self.sb(ph, "s8", [NH, TT], F32)
                r1 = self.sb(ph, "r1", [NH, TT], F32)
                fk3 = [(self.sb(ph, "fk3%d" % i, [NH, 3, TT], BF16), "fk3%d" % i) for i in range(2)]
                fq3 = [(self.sb(ph, "fq3%d" % i, [NH, 3, TT], BF16), "fq3%d" % i) for i in range(2)]
            tpr = Ring(self.pb[0:2])
            qkr = Ring(self.pb[2:4])
            ev = 0
            tiles = [(b, t0) for b in range(ns) for t0 in range(0, T, TT)]

            def loads(i):
                b_, t0_ = tiles[i]
                S.dma("sp", xt[i % 2][1], xt[i % 2][0], self.x_tile(Xin, b_, t0_, TT), writes=[xt[i % 2][1]])

            loads(0)
            for it, (b, t0) in enumerate(tiles):
                if True:
                    if it + 1 < len(tiles):
                        loads(it + 1)
                    sl = it % 2
                    xs, xr = xt[sl]
                    hs, hr = hT[sl]
                    self.xT_mod(xs, xr, nsub, hs, hr, l, b, 0, tpr)
                    for (stg, c0, dst) in ((qst[sl], 0, self.qt_d), (kst[sl], D, self.kt_d)):
                        for j in range(8):
                            pt, pr = qkr.next()
                            for kc in range(8):
                                self.mm(pt, wi[:, kc, c0 + j * 128:c0 + (j + 1) * 128], hs[:, kc, :], kc == 0, kc == 7,
                                        ["wi", hr], [pr])
                            self.cp("act" if ev % 2 == 0 else "dve", stg[0][:, j, :], pt, [pr], [stg[1]])
                            ev += 1
                        S.dma("sp", stg[1], dst[b].rearrange("(j hh) d t -> (hh d) j t", hh=2)[:, :, t0:t0 + TT], stg[0],
                              reads=[stg[1]])
                    for s in range(nsub):
                        pv, pvr = self.py
                        for hf in range(2):
                            for kc in range(8):
                                self.mm(pv[:, hf * 512:(hf + 1) * 512], hs[:, kc, s * 128:(s + 1) * 128],
                                        wi[:, kc, 2 * D + hf * 512:2 * D + (hf + 1) * 512], kc == 0, kc == 7, [hr, "wi"], [pvr])
                        self.cp("act" if ev % 2 == 0 else "dve", vst[sl][0][:, s, :], pv, [pvr], [vst[sl][1]])
                        ev += 1
                    S.dma("sp", vst[sl][1], self.v_d[b, t0:t0 + TT, :].rearrange("(s p) d -> p s d", p=128), vst[sl][0],
                          reads=[vst[sl][1]])
                    if fox:
                        pf, pfr = self.pb[6]
                        for kc in range(8):
                            self.mm(pf[0:NH, :], wi[:, kc, 3 * D:3 * D + NH], hs[:, kc, :], kc == 0, kc == 7, ["wi", hr], [pfr])
                        self.act(ef, pf[0:NH, :], AF.Exp, [pfr, "nbf"], ["ef"], bias=nbf, scale=-1.0)
                        self.act(ef, ef, AF.Ln, ["ef"], ["ef"], bias=1.0)
                        cm, cmr = cum[sl]
                        pc, pcr = cum[1 - sl]
                        init = 0.0 if t0 == 0 else pc[:, TT - 1:TT]
                        S.op("dve", lambda e, cm=cm, init=init: e.tensor_tensor_scan(cm, ones16, ef, init, ALU.mult, ALU.add),
                             ["ones16", "ef"] + ([] if t0 == 0 else [pcr]), [cmr])
                        fk, fkr = fk3[sl]
                        fq, fqr = fq3[sl]
                        self.ts("dve", s8, cm, 8.0, None, ALU.mult, None, [cmr], ["s8"])
                        self.cp("dve", fk[:, 0, :], s8, ["s8"], [fkr])
                        self.tt("dve", r1, s8, fk[:, 0, :], ALU.subtract, ["s8", fkr], ["r1"])
                        self.cp("dve", fk[:, 1, :], r1, ["r1"], [fkr])
                        self.tt("dve", r1, r1, fk[:, 1, :], ALU.subtract, ["r1", fkr], ["r1"])
                        self.cp("dve", fk[:, 2, :], r1, ["r1"], [fkr])
                        self.ts("dve", fq, fk, -1.0, None, ALU.mult, None, [fkr], [fqr])
                        S.dma("sp", fkr, self.fk_d[b, :, :, t0:t0 + TT], fk, reads=[fkr])
                        S.dma("sp", fqr, self.fq_d[b, :, :, t0:t0 + TT], fq, reads=[fqr])

    def attn_core_fox(self):
        nc, S, ns, T = self.nc, self.S, self.nseq, self.T
        NKB = T // 128
        QW = 1024
        NQ = T // QW
        KA = DH + 6
        LA = 2
        with ExitStack() as ph:
            kta = [(self.sb(ph, "kta%d" % i, [KA, T], BF16), "kta%d" % i) for i in range(3)]
            qta = [(self.sb(ph, "qta%d" % i, [KA, T], BF16), "qta%d" % i) for i in range(3)]
            va = [(self.sb(ph, "va%d" % i, [128, NKB, DH + 1], BF16), "va%d" % i) for i in range(3)]
            PT = [(self.sb(ph, "PT%d" % i, [128, QW], BF16), "PT%d" % i) for i in range(4)]
            ost = [(self.sb(ph, "ost%d" % i, [DH + 1, QW], F32), "ost%d" % i) for i in range(2)]
            for i in range(3):
                self.memset("dve", kta[i][0][DH:KA, :], 1.0, [kta[i][1]])
                self.memset("dve", qta[i][0][DH:KA, :], 1.0, [qta[i][1]])
                self.memset("pool", va[i][0][:, :, DH:DH + 1], 1.0, [va[i][1]])
            psr = Ring(self.pd[0:2])
            por = Ring(self.pd[2:4])
            ptr = Ring(PT)
            osr = Ring(ost)
            pend = []

            def halves(c0):
                return [(max(c0, hf * 512), (hf + 1) * 512) for hf in range(2) if max(c0, hf * 512) < (hf + 1) * 512]

            def stage2(tl):
                (b, h, j, kb, c0, nkb, sl, po, por_, pt, ptr_) = tl
                for (a0, a1) in halves(c0):
                    self.mm(po[0:DH + 1, a0:a1], va[sl][0][:, kb, :], pt[:, a0:a1], kb == 0, kb == nkb - 1,
                            [va[sl][1], ptr_], [por_])
                if kb == nkb - 1:
                    os_, osr_ = osr.next()
                    self.cp("dve", os_, po[0:DH + 1, :], [por_], [osr_])
                    S.dma("sp", osr_, self.o_d[b, h, :, j * QW:(j + 1) * QW], os_[0:DH, :], reads=[osr_])
                    S.dma("sp", osr_, self.den_d[b, h:h + 1, j * QW:(j + 1) * QW], os_[DH:DH + 1, :], reads=[osr_])

            heads = [(b, h) for b in range(ns) for h in range(NH)]

            def loads(i):
                b_, h_ = heads[i]
                sl_ = i % 3
                S.dma("sp", kta[sl_][1], kta[sl_][0][0:DH, :], self.kt_d[b_, h_], writes=[kta[sl_][1]])
                S.dma("sp", kta[sl_][1], kta[sl_][0][DH:DH + 3, :], self.fk_d[b_, h_], writes=[kta[sl_][1]])
                S.dma("sp", qta[sl_][1], qta[sl_][0][0:DH, :], self.qt_d[b_, h_], writes=[qta[sl_][1]])
                S.dma("sp", qta[sl_][1], qta[sl_][0][DH + 3:DH + 6, :], self.fq_d[b_, h_], writes=[qta[sl_][1]])
                S.dma("sp", va[sl_][1], va[sl_][0][:, :, 0:DH],
                      self.v_d[b_, :, h_ * DH:(h_ + 1) * DH].rearrange("(k p) d -> p k d", p=128), writes=[va[sl_][1]])

            loads(0)
            for hi, (b, h) in enumerate(heads):
                sl = hi % 3
                if hi + 1 < len(heads):
                    loads(hi + 1)
                for j in range(NQ):
                    nkb = 8 * j + 8
                    po, por_ = por.next()
                    for kb in range(nkb):
                        c0 = (kb - 8 * j) * 128 if kb >= 8 * j else 0
                        ps, psr_ = psr.next()
                        pt, ptr_ = ptr.next()
                        for (a0, a1) in halves(c0):
                            self.mm(ps[:, a0:a1], kta[sl][0][:, kb * 128:(kb + 1) * 128],
                                    qta[sl][0][:, j * QW + a0:j * QW + a1], True, True, [kta[sl][1], qta[sl][1]], [psr_])
                        self.act(pt[:, c0:QW], ps[:, c0:QW], AF.Exp, [psr_], [ptr_], scale=0.125)
                        if kb >= 8 * j:
                            S.op("pool", lambda e, pt=pt, c0=c0: e.affine_select(
                                pt[:, c0:c0 + 128], pt[:, c0:c0 + 128], [[1, 128]], ALU.is_ge, 0.0, base=0,
                                channel_multiplier=-1), [ptr_], [ptr_])
                        pend.append((b, h, j, kb, c0, nkb, sl, po, por_, pt, ptr_))
                        if len(pend) > LA:
                            stage2(pend.pop(0))
            while pend:
                stage2(pend.pop(0))

    def attn_core_sb(self):
        nc, S, ns, T = self.nc, self.S, self.nseq, self.T
        NKB = T // 128
        QW = 1024
        NQ = T // QW
        with ExitStack() as ph:
            kta = [(self.sb(ph, "kta%d" % i, [DH, T], BF16), "kta%d" % i) for i in range(3)]
            qta = [(self.sb(ph, "qta%d" % i, [DH, T], BF16), "qta%d" % i) for i in range(3)]
            va = [(self.sb(ph, "va%d" % i, [128, NKB, DH], BF16), "va%d" % i) for i in range(3)]
            Et = [(self.sb(ph, "Et%d" % i, [128, QW], F32), "Et%d" % i) for i in range(2)]
            SPt = [(self.sb(ph, "SPt%d" % i, [128, QW], BF16), "SPt%d" % i) for i in range(3)]
            X2 = [(self.sb(ph, "X2%d" % i, [128, QW], F32), "X2%d" % i) for i in range(2)]
            At = [(self.sb(ph, "At%d" % i, [128, QW], BF16), "At%d" % i) for i in range(3)]
            Ccs = [(self.sb(ph, "Cc%d" % i, [128, QW], F32), "Cc%d" % i) for i in range(2)]
            ccr = Ring(Ccs)
            ost = [(self.sb(ph, "ost%d" % i, [DH, QW], F32), "ost%d" % i) for i in range(2)]
            trin = self.sb(ph, "trin", [128, 128], BF16)
            onen = self.sb(ph, "onen", [128, 128], BF16)
            self.memset("pool", trin, -8.0, ["trin"])
            S.op("pool", lambda e: e.affine_select(trin, trin, [[-1, 128]], ALU.is_ge, 0.0, base=0, channel_multiplier=1),
                 ["trin"], ["trin"])
            self.memset("pool", onen, -8.0, ["onen"])
            psr = Ring(self.pd[0:2])
            pyy, pyyr = self.pd[2]
            po_, por_ = self.pd[3]
            etr, spr, x2r, atr, osr = Ring(Et), Ring(SPt), Ring(X2), Ring(At), Ring(ost)
            q1, q2 = [], []

            def halves(c0):
                return [(max(c0, hf * 512), (hf + 1) * 512) for hf in range(2) if max(c0, hf * 512) < (hf + 1) * 512]

            def mask(tile_ap, res, c0):
                S.op("pool", lambda e: e.affine_select(tile_ap[:, c0:c0 + 128], tile_ap[:, c0:c0 + 128], [[1, 128]], ALU.is_gt,
                                                        0.0, base=0, channel_multiplier=-1), [res], [res])

            def stageA1(tl):
                ps, psr_ = tl["ps"]
                c0, sl, kb, j = tl["c0"], tl["sl"], tl["kb"], tl["j"]
                for (a0, a1) in halves(c0):
                    self.mm(ps[:, a0:a1], kta[sl][0][:, kb * 128:(kb + 1) * 128], qta[sl][0][:, j * QW + a0:j * QW + a1],
                            True, True, [kta[sl][1], qta[sl][1]], [psr_])
                et, etr_ = etr.next()
                tl["et"] = (et, etr_)
                self.act(et[:, c0:QW], ps[:, c0:QW], AF.Exp, [psr_], [etr_], scale=0.125)

            def stageA2(tl):
                et, etr_ = tl["et"]
                c0 = tl["c0"]
                sp, spr_ = spr.next()
                tl["sp"] = (sp, spr_)
                self.act(sp[:, c0:QW], et[:, c0:QW], AF.Ln, [etr_], [spr_], bias=1.0)
                if tl["diag"]:
                    mask(sp, spr_, c0)

            def stageB(tl):
                ps, psr_ = tl["ps"]
                sp, spr_ = tl["sp"]
                c0 = tl["c0"]
                Cc = tl["Cc"]
                for (a0, a1) in halves(c0):
                    S.op("pe", lambda e, a0=a0, a1=a1: e.matmul(ps[:, a0:a1], trin, sp[:, a0:a1], start=False, stop=True,
                                                               skip_group_check=True), ["trin", spr_], [psr_])
                for (a0, a1) in halves(c0):
                    self.mm(pyy[:, a0:a1], onen, sp[:, a0:a1], True, True, ["onen", spr_], [pyyr])
                x2, x2r_ = x2r.next()
                at, atr_ = atr.next()
                tl["at"] = (at, atr_)
                self.tt("dve", x2[:, c0:QW], ps[:, c0:QW], Cc[0][:, c0:QW], ALU.add, [psr_, Cc[1]], [x2r_])
                self.tt("dve", Cc[0][:, c0:QW], pyy[:, c0:QW], Cc[0][:, c0:QW], ALU.add, [pyyr, Cc[1]], [Cc[1]])
                self.act(at[:, c0:QW], x2[:, c0:QW], AF.Exp, [x2r_], [atr_], scale=0.125)
                if tl["diag"]:
                    mask(at, atr_, c0)

            def stageC(tl):
                at, atr_ = tl["at"]
                c0, sl, kb = tl["c0"], tl["sl"], tl["kb"]
                st = tl["started"]
                for hf, (a0, a1) in [(a0 // 512, (a0, a1)) for (a0, a1) in halves(c0)]:
                    self.mm(po_[0:DH, a0:a1], va[sl][0][:, kb, :], at[:, a0:a1], not st[hf], tl["last"], [va[sl][1], atr_], [por_])
                    st[hf] = True
                if tl["last"]:
                    os_, osr_ = osr.next()
                    self.cp("act", os_, po_[0:DH, :], [por_], [osr_])
                    S.dma("sp", osr_, self.o_d[tl["b"], tl["h"], :, tl["j"] * QW:(tl["j"] + 1) * QW], os_, reads=[osr_])

            def push(tl):
                stageA1(tl)
                stageA2(tl)
                if q1:
                    t2 = q1.pop(0)
                    stageB(t2)
                    q2.append(t2)
                q1.append(tl)
                while len(q2) > 1:
                    stageC(q2.pop(0))

            heads = [(b, h) for b in range(ns) for h in range(NH)]

            def loads(i):
                b_, h_ = heads[i]
                sl_ = i % 3
                S.dma("sp", kta[sl_][1], kta[sl_][0], self.kt_d[b_, h_], writes=[kta[sl_][1]])
                S.dma("sp", qta[sl_][1], qta[sl_][0], self.qt_d[b_, h_], writes=[qta[sl_][1]])
                S.dma("sp", va[sl_][1], va[sl_][0],
                      self.v_d[b_, :, h_ * DH:(h_ + 1) * DH].rearrange("(k p) d -> p k d", p=128), writes=[va[sl_][1]])

            loads(0)
            for hi, (b, h) in enumerate(heads):
                sl = hi % 3
                if hi + 1 < len(heads):
                    loads(hi + 1)
                for j in range(NQ):
                    nkb = 8 * j + 8
                    Cc = ccr.next()
                    self.memset("pool", Cc[0], 0.0, [Cc[1]])
                    started = [False, False]
                    for n, kb in enumerate(range(nkb - 1, -1, -1)):
                        diag = kb >= 8 * j
                        c0 = (kb - 8 * j) * 128 if diag else 0
                        tl = dict(b=b, h=h, j=j, kb=kb, c0=c0, sl=sl, diag=diag, started=started,
                                  last=(kb == 0), ps=psr.next(), Cc=Cc)
                        push(tl)
            while q1:
                t2 = q1.pop(0)
                stageB(t2)
                q2.append(t2)
            while q2:
                stageC(q2.pop(0))

    def attn_out(self, l, kind, Xin, Xout):
        nc, S, ns, T = self.nc, self.S, self.nseq, self.T
        TT = 512
        nsub = TT // 128
        W = self.din
        fox = kind == "fox"
        with ExitStack() as ph:
            wo = self.sb(ph, "wo", [128, 8, D], BF16)
            for kc in range(8):
                self.load_w_bf16(wo[:, kc, :], W["fox_w_out" if fox else "sb_w_out"][0, kc * 128:(kc + 1) * 128, :], "wo", "wo")
            E = self.epi_setup(ph, l, 0)
            xt = [(self.sb(ph, "xt%d" % i, [128, nsub, D], F32), "xt%d" % i) for i in range(3)]
            oT = [(self.sb(ph, "oT%d" % i, [128, 8, TT], F32), "oT%d" % i) for i in range(2)]
            oTb = [(self.sb(ph, "oTb%d" % i, [128, 8, TT], BF16), "oTb%d" % i) for i in range(2)]
            if fox:
                dnb = [(self.sb(ph, "dnb%d" % i, [128, 8, TT], F32), "dnb%d" % i) for i in range(2)]
            tiles = [(b, t0) for b in range(ns) for t0 in range(0, T, TT)]
            n = len(tiles)

            def loads(i):
                b_, t0_ = tiles[i]
                sl_ = i % 2
                S.dma("sp", xt[i % 3][1], xt[i % 3][0], self.x_tile(Xin, b_, t0_, TT), writes=[xt[i % 3][1]])
                S.dma("sp", oT[sl_][1], oT[sl_][0],
                      self.o_d[b_].rearrange("(j hh) d t -> (hh d) j t", hh=2)[:, :, t0_:t0_ + TT], writes=[oT[sl_][1]])
                if fox:
                    dv = self.den_d[b_].rearrange("(j hh) t -> hh j t", hh=2)
                    for hh in range(2):
                        S.dma("sp", dnb[sl_][1], dnb[sl_][0][hh * DH:(hh + 1) * DH],
                              dv[hh, :, t0_:t0_ + TT].partition_broadcast(DH), writes=[dnb[sl_][1]])

            def norm(i):
                sl = i % 2
                o_, or_ = oT[sl]
                ob, obr = oTb[sl]
                if fox:
                    dn, dnr = dnb[sl]
                    S.op("dve", lambda e, dn=dn: e.reciprocal(dn, dn), [dnr], [dnr])
                    self.tt("dve", ob, o_, dn, ALU.mult, [or_, dnr], [obr])
                else:
                    self.cp("pool", ob, o_, [or_], [obr])

            def main(i):
                b, t0 = tiles[i]
                xs, xr = xt[i % 3]
                ob, obr = oTb[i % 2]
                for s in range(nsub):
                    pyt, pyr = self.py
                    for hf in range(2):
                        for j in range(8):
                            self.mm(pyt[:, hf * 512:(hf + 1) * 512], ob[:, j, s * 128:(s + 1) * 128],
                                    wo[:, j, hf * 512:(hf + 1) * 512], j == 0, j == 7, [obr, "wo"], [pyr])
                    self.epilogue(E, b, pyt, pyr, xs[:, s, :], xr, Xout[b, t0 + s * 128:t0 + (s + 1) * 128, :])

            loads(0)
            if n > 1:
                loads(1)
            norm(0)
            for i in range(n):
                if i + 2 < n:
                    loads(i + 2)
                if i + 1 < n:
                    norm(i + 1)
                main(i)


def _plan_full():
    plan = [("mod",)]
    plan += [("gmlp", 0), ("ffn", 0), ("attn", 1, "fox"), ("ffn", 1), ("attn", 2, "sb"), ("ffn", 2), ("conv", 3), ("ffn", 3)]
    return plan


_CACHE = {}


def kernel(**inputs):
    ncores = 8
    nseq = 2
    T = 4096
    if "nc" not in _CACHE:
        _CACHE["nc"] = Builder(nseq, T, _plan_full()).build()
    nc = _CACHE["nc"]
    x = np.ascontiguousarray(inputs["x"], dtype=np.float32)
    c = np.ascontiguousarray(inputs["c"], dtype=np.float32)
    in_maps = []
    for i in range(ncores):
        m = {k: np.ascontiguousarray(inputs[k], dtype=np.float32) for k in WEIGHT_SHAPES}
        m["x"] = x[i * nseq:(i + 1) * nseq]
        m["c"] = c[i * nseq:(i + 1) * nseq]
        in_maps.append(m)
    res = run_bass_kernel_spmd(nc, in_maps, core_ids=list(range(ncores)))
    return np.concatenate([r["out"] for r in res.results], axis=0)
```

```python
import numpy as np
from contextlib import ExitStack
import concourse.bass as bass
import concourse.mybir as mybir
from concourse.bass_utils import run_bass_kernel_spmd

F32 = mybir.dt.float32
BF16 = mybir.dt.bfloat16
AF = mybir.ActivationFunctionType
ALU = mybir.AluOpType

D = 1024
FH = 2816
NH = 16
DH = 64
DEPTH = 4
ALPHA = float((2.0 * DEPTH) ** 0.25)
EPS = 1e-5
CONVW = 31

ENGS = ["pe", "act", "dve", "pool", "sp"]

WEIGHT_SHAPES = {
    "mod_w": (4, 1024, 6144), "mod_b": (4, 6144), "ln1_g": (4, 1024), "ln1_b": (4, 1024),
    "ln2_g": (4, 1024), "ln2_b": (4, 1024), "ffn_w_in": (4, 1024, 5632), "ffn_w_out": (4, 2816, 1024),
    "gm_w_in": (1, 1024, 2048), "gm_b_in": (1, 2048), "gm_ln_g": (1, 1024), "gm_ln_b": (1, 1024),
    "gm_w_s": (1, 8, 128, 128), "gm_b_s": (1, 8, 128), "gm_w_out": (1, 1024, 1024),
    "fox_w_in": (1, 1024, 3088), "fox_b_f": (1, 16), "fox_w_out": (1, 1024, 1024),
    "sb_w_in": (1, 1024, 3072), "sb_w_out": (1, 1024, 1024),
    "cv_w_in": (1, 1024, 2048), "cv_b_in": (1, 2048), "cv_dw": (1, 31, 1024), "cv_dw_b": (1, 1024),
    "cv_ln_g": (1, 1024), "cv_ln_b": (1, 1024), "cv_w_out": (1, 1024, 1024), "cv_b_out": (1, 1024),
}


class Sched:
    def __init__(self, nc, same_engine_raw=True):
        self.nc = nc
        self.ops = {e: [] for e in ENGS}
        self.last_w = {}
        self.readers = {}
        self.seen = {e: {} for e in ENGS}
        self.dma_cnt = []
        self.dma_sw = []
        self.keymap = {}
        self.same_engine_raw = same_engine_raw

    def _add_wait(self, eng, waits, ev, is_raw):
        if ev is None:
            return
        if ev[0] == "eng":
            _, e2, idx = ev
            if e2 == eng and (eng == "pe" or not self.same_engine_raw):
                return
            key, val = ("eng", e2), idx
        else:
            key, val = ("dma", ev[1]), ev[2]
        if self.seen[eng].get(key, -1) >= val:
            return
        if val > waits.get(key, -1):
            waits[key] = val

    def _deps(self, eng, reads, writes):
        waits = {}
        for r in reads:
            self._add_wait(eng, waits, self.last_w.get(r), True)
        for w in writes:
            self._add_wait(eng, waits, self.last_w.get(w), False)
            for ev in self.readers.get(w, ()):
                self._add_wait(eng, waits, ev, False)
        for k, v in waits.items():
            self.seen[eng][k] = v
        return waits

    def _commit(self, ev, reads, writes):
        for r in reads:
            self.readers.setdefault(r, []).append(ev)
        for w in writes:
            self.last_w[w] = ev
            self.readers[w] = []

    def op(self, eng, fn, reads=(), writes=()):
        waits = self._deps(eng, reads, writes)
        idx = len(self.ops[eng])
        self.ops[eng].append(dict(kind="op", fn=fn, waits=waits, needed=False))
        self._commit(("eng", eng, idx), reads, writes)

    def dma(self, eng, key, out, in_, reads=(), writes=(), **kw):
        waits = self._deps(eng, reads, writes)
        key = (eng == "pool", key)
        if key not in self.keymap:
            used = {v for k, v in self.keymap.items()}
            cand = [i for i, sw in enumerate(self.dma_sw) if sw == key[0] and i not in used]
            if cand:
                self.keymap[key] = cand[0]
            else:
                self.keymap[key] = len(self.dma_cnt)
                self.dma_cnt.append(0)
                self.dma_sw.append(key[0])
        ph = self.keymap[key]
        self.dma_cnt[ph] += 1
        self.ops[eng].append(dict(kind="dma", out=out, in_=in_, kw=kw, key=ph, waits=waits))
        self._commit(("dma", ph, 16 * self.dma_cnt[ph]), reads, writes)

    def _wait_everything(self, eng):
        waits = {}
        for ev in self.last_w.values():
            self._add_wait(eng, waits, ev, False)
        for evs in self.readers.values():
            for ev in evs:
                self._add_wait(eng, waits, ev, False)
        for k, v in waits.items():
            self.seen[eng][k] = v
        self.ops[eng].append(dict(kind="waitonly", waits=waits))

    def barrier(self):
        for e in ENGS:
            self._wait_everything(e)
        self.last_w = {}
        self.readers = {}
        self.keymap = {}

    def emit(self, stack):
        nc = self.nc
        for e in ENGS:
            for o in self.ops[e]:
                for k, v in o["waits"].items():
                    if k[0] == "eng":
                        self.ops[k[1]][v]["needed"] = True
        semval = {}
        for e in ENGS:
            c = 0
            for i, o in enumerate(self.ops[e]):
                if o["kind"] == "op" and o["needed"]:
                    c += 1
                    semval[(e, i)] = c
        esem = {e: stack.enter_context(nc.semaphore("s_" + e)) for e in ENGS}
        dsem = [stack.enter_context(nc.semaphore("d%d" % i)) for i in range(len(self.dma_cnt))]
        print("[sched] ops:", {e: len(self.ops[e]) for e in ENGS}, "sem max:", {e: max([0] + [v for (ee, _), v in semval.items() if ee == e]) for e in ENGS},
              "dma sems:", len(self.dma_cnt), "max dma cnt:", max(self.dma_cnt) * 16)
        block = stack.enter_context(nc.Block())

        def run(e, eng):
            for o in self.ops[e]:
                for k, v in o["waits"].items():
                    if k[0] == "eng":
                        eng.wait_ge(esem[k[1]], semval[(k[1], v)])
                    else:
                        eng.wait_ge(dsem[k[1]], v)
                if o["kind"] == "op":
                    ins = o["fn"](eng)
                    if o["needed"]:
                        ins.then_inc(esem[e], 1)
                elif o["kind"] == "dma":
                    eng.dma_start(out=o["out"], in_=o["in_"], **o["kw"]).then_inc(dsem[o["key"]], 16)

        @block.tensor
        def _(eng):
            run("pe", eng)

        @block.scalar
        def _(eng):
            run("act", eng)

        @block.vector
        def _(eng):
            run("dve", eng)

        @block.gpsimd
        def _(eng):
            run("pool", eng)

        @block.sync
        def _(eng):
            run("sp", eng)


class Ring:
    def __init__(self, items):
        self.items = list(items)
        self.i = 0

    def next(self):
        it = self.items[self.i % len(self.items)]
        self.i += 1
        return it


class Builder:
    def __init__(self, nseq, T, plan, same_engine_raw=True):
        self.nseq, self.T, self.plan = nseq, T, plan
        self.nc = nc = bass.Bass("TRN2", target_bir_lowering=False)
        self.S = Sched(nc, same_engine_raw)
        self._n = 0
        self.din = {}
        self.din["x"] = nc.dram_tensor("x", [nseq, T, D], F32, kind="ExternalInput").ap()
        self.din["c"] = nc.dram_tensor("c", [nseq, D], F32, kind="ExternalInput").ap()
        for k, shp in WEIGHT_SHAPES.items():
            self.din[k] = nc.dram_tensor(k, list(shp), F32, kind="ExternalInput").ap()
        self.dout = nc.dram_tensor("out", [nseq, T, D], F32, kind="ExternalOutput").ap()
        self.xa = nc.dram_tensor("xa_s", [nseq, T, D], F32).ap()
        self.xb = nc.dram_tensor("xb_s", [nseq, T, D], F32).ap()
        self.modrow = nc.dram_tensor("modrow_s", [DEPTH, nseq, 6 * D], F32).ap()
        self.qt_d = nc.dram_tensor("qt_s", [nseq, NH, DH, T], BF16).ap()
        self.kt_d = nc.dram_tensor("kt_s", [nseq, NH, DH, T], BF16).ap()
        self.v_d = nc.dram_tensor("v_s", [nseq, T, D], BF16).ap()
        self.fq_d = nc.dram_tensor("fq_s", [nseq, NH, 3, T], BF16).ap()
        self.fk_d = nc.dram_tensor("fk_s", [nseq, NH, 3, T], BF16).ap()
        self.o_d = nc.dram_tensor("o_s", [nseq, NH, DH, T], F32).ap()
        self.den_d = nc.dram_tensor("den_s", [nseq, NH, T], F32).ap()

    def sb(self, ph, name, shape, dt):
        self._n += 1
        return ph.enter_context(self.nc.sbuf_tensor("%s_%d" % (name, self._n), list(shape), dt))[:]

    def mm(self, out, lhsT, rhs, start, stop, reads, writes):
        self.S.op("pe", lambda e: e.matmul(out, lhsT, rhs, start=start, stop=stop), reads, writes)

    def tr(self, out, in_, ident, reads, writes):
        self.S.op("pe", lambda e: e.transpose(out, in_, ident), reads, writes)

    def act(self, out, in_, func, reads, writes, bias=None, scale=None, eng="act"):
        kw = {}
        if bias is not None:
            kw["bias"] = bias
        if scale is not None:
            kw["scale"] = scale
        self.S.op(eng, lambda e: e.activation(out, in_, func, **kw), reads, writes)

    def tt(self, eng, out, in0, in1, op, reads, writes):
        self.S.op(eng, lambda e: e.tensor_tensor(out, in0, in1, op), reads, writes)

    def ts(self, eng, out, in0, s1, s2, op0, op1, reads, writes):
        if op1 is None:
            self.S.op(eng, lambda e: e.tensor_scalar(out, in0, s1, None, op0), reads, writes)
        else:
            self.S.op(eng, lambda e: e.tensor_scalar(out, in0, s1, s2, op0, op1), reads, writes)

    def stt(self, out, in0, scalar, in1, op0, op1, reads, writes):
        self.S.op("dve", lambda e: e.scalar_tensor_tensor(out, in0, scalar, in1, op0, op1), reads, writes)

    def cp(self, eng, out, in_, reads, writes):
        if eng == "act":
            self.S.op("act", lambda e: e.copy(out, in_), reads, writes)
        else:
            self.S.op(eng, lambda e: e.tensor_copy(out, in_), reads, writes)

    def memset(self, eng, ap, val, writes):
        self.S.op(eng, lambda e: e.memset(ap, val), (), writes)

    def load_w_bf16(self, dst, src, key, res, maxcols=2048):
        n = dst.shape[-1]
        c0 = 0
        while c0 < n:
            c1 = min(n, c0 + maxcols)
            self.S.dma("pool", key, dst[:, c0:c1], src[:, c0:c1], writes=[res])
            c0 = c1

    def build(self):
        nc, S = self.nc, self.S
        with ExitStack() as st:
            self.ident = st.enter_context(nc.sbuf_tensor("ident", [128, 128], F32))[:]
            self.identb = st.enter_context(nc.sbuf_tensor("identb", [128, 128], BF16))[:]
            self.modT = st.enter_context(nc.sbuf_tensor("modT", [128, DEPTH, self.nseq, 48], F32))[:]
            self.mhalf = st.enter_context(nc.sbuf_tensor("mhalf", [128, 1], F32))[:]
            self.pd = []
            self.pb = []
            for i in range(4):
                t = st.enter_context(nc.psum_tensor("pd%d" % i, [128, 1024], F32))[:]
                self.pd.append((t, "pb%d" % (2 * i)))
                self.pb += [(t[:, 0:512], "pb%d" % (2 * i)), (t[:, 512:1024], "pb%d" % (2 * i + 1))]
            self.py = self.pd[2]
            self.py1 = self.pd[3]
            self.memset("pool", self.ident, 1.0, ["ident"])
            S.op("pool", lambda e: e.affine_select(self.ident, self.ident, [[1, 128]], ALU.is_equal, 0.0,
                                                    base=0, channel_multiplier=-1), ["ident"], ["ident"])
            self.cp("dve", self.identb, self.ident, ["ident"], ["identb"])
            self.memset("pool", self.mhalf, -0.5, ["mhalf"])
            cur = self.din["x"]
            nxt = [self.xa, self.xb]
            nph = len([p for p in self.plan if p[0] != "mod"])
            k = 0
            for p in self.plan:
                if p[0] == "mod":
                    self.phase_mod()
                else:
                    k += 1
                    dst = self.dout if k == nph else nxt[k % 2]
                    if p[0] == "ffn":
                        self.phase_ffn(p[1], cur, dst)
                    elif p[0] == "gmlp":
                        self.phase_gmlp(p[1], cur, dst)
                    elif p[0] == "conv":
                        self.phase_conv(p[1], cur, dst)
                    elif p[0] == "attn":
                        self.phase_attn(p[1], p[2], cur, dst)
                    cur = dst
                S.barrier()
            S.emit(st)
        return nc

    def phase_mod(self):
        nc, S, ns = self.nc, self.S, self.nseq
        with ExitStack() as ph:
            cs = self.sb(ph, "cs", [ns, D], F32)
            ca = self.sb(ph, "ca", [ns, D], F32)
            caT = self.sb(ph, "caT", [128, 8, ns], F32)
            CB = self.sb(ph, "CB", [128, ns, 8, 128], F32)
            wst = [self.sb(ph, "wst%d" % i, [128, 8, 512], F32) for i in range(2)]
            mbb = [self.sb(ph, "mbb%d" % i, [128, 512], F32) for i in range(2)]
            mrow = [self.sb(ph, "mrow%d" % i, [128, 512], F32) for i in range(2)]
            stg = [self.sb(ph, "stg%d" % i, [48, 128], F32) for i in range(2)]
            S.dma("sp", "cs", cs, self.din["c"][:, :], writes=["cs"])
            self.act(ca, cs, AF.Silu, ["cs"], ["ca"])
            pbt, pbr = self.pb[0]
            for kc in range(8):
                self.tr(pbt[:, kc * ns:(kc + 1) * ns], ca[:, kc * 128:(kc + 1) * 128], self.ident[0:ns, 0:ns],
                        ["ca", "ident"], [pbr])
            self.cp("dve", caT, pbt[:, 0:8 * ns].rearrange("p (k b) -> p k b", b=ns), [pbr], ["caT"])
            for b in range(ns):
                self.cp("dve", CB[:, b], caT[:, :, b:b + 1].to_broadcast([128, 8, 128]), ["caT"], ["CB"])
            mi = 0
            blocks = [(l, blk) for l in range(DEPTH) for blk in range(12)]

            def wload(i):
                l_, blk_ = blocks[i]
                sl_ = i % 2
                cols_ = slice(blk_ * 512, (blk_ + 1) * 512)
                S.dma("sp", "wst%d" % sl_, wst[sl_],
                      self.din["mod_w"][l_, :, cols_].rearrange("(k p) f -> p k f", p=128), writes=["wst%d" % sl_])
                S.dma("sp", "mbb%d" % sl_, mbb[sl_], self.din["mod_b"][l_, cols_].partition_broadcast(128),
                      writes=["mbb%d" % sl_])

            wload(0)
            for it, (l, blk) in enumerate(blocks):
                if True:
                    sl = it % 2
                    cols = slice(blk * 512, (blk + 1) * 512)
                    if it + 1 < len(blocks):
                        wload(it + 1)
                    for b in range(ns):
                        pt, pr = self.pb[1 + (mi % 2)]
                        ms = mi % 2
                        mi += 1
                        for kc in range(8):
                            self.mm(pt, CB[:, b, kc, :], wst[sl][:, kc, :], kc == 0, kc == 7,
                                    ["CB", "wst%d" % sl], [pr])
                        self.tt("dve", mrow[ms], pt, mbb[sl], ALU.add, [pr, "mbb%d" % sl], ["mrow%d" % ms])
                        S.dma("sp", "mrow%d" % ms, self.modrow[l, b:b + 1, cols], mrow[ms][0:1, :],
                              reads=["mrow%d" % ms], writes=["modrow_%d_%d_%d" % (l, b, blk)])
            si = 0
            for l in range(DEPTH):
                for b in range(ns):
                    sl = si % 2
                    si += 1
                    S.dma("sp", "stg%d" % sl, stg[sl], self.modrow[l, b, :].rearrange("(c p) -> c p", p=128),
                          reads=["modrow_%d_%d_%d" % (l, b, blk) for blk in range(12)], writes=["stg%d" % sl])
                    pt, pr = self.pb[3 + sl]
                    self.tr(pt[:, 0:48], stg[sl], self.ident[0:48, 0:48], ["stg%d" % sl, "ident"], [pr])
                    self.cp("dve", self.modT[:, l, b, :], pt[:, 0:48], [pr], ["modT"])
            for c0 in (8, 32):
                self.ts("dve", self.modT[:, :, :, c0:c0 + 8], self.modT[:, :, :, c0:c0 + 8], 1.0, None, ALU.add, None,
                        ["modT"], ["modT"])

    def load_bcast(self, ph, name, src_row):
        n = src_row.shape[-1]
        t = self.sb(ph, name, [128, n], F32)
        self._n += 1
        res = "%s_%d" % (name, self._n)
        self.S.dma("sp", res, t, src_row.partition_broadcast(128), writes=[res])
        return t, res

    def load_pp(self, ph, name, src2d, pbi=0):
        n = src2d.shape[0]
        stg = self.sb(ph, name + "s", [n, 128], F32)
        t = self.sb(ph, name, [128, n], F32)
        self._n += 1
        res = "%s_%d" % (name, self._n)
        self.S.dma("sp", res + "s", stg, src2d, writes=[res + "s"])
        pt, pr = self.pb[pbi]
        self.tr(pt[:, 0:n], stg, self.ident[0:n, 0:n], [res + "s", "ident"], [pr])
        self.cp("dve", t, pt[:, 0:n], [pr], [res])
        return t, res

    def epi_setup(self, ph, l, which, single_gb=False):
        gcol = 2048 if which == 0 else 5120
        GB = []
        if single_gb:
            t = self.sb(ph, "GBs", [128, D], F32)
            GB = [(t, "GBs")] * self.nseq
        else:
            for b in range(self.nseq):
                t, r = self.load_bcast(ph, "GB%d" % b, self.modrow[l, b, gcol:gcol + D])
                self.ts("dve", t, t, 1.0, None, ALU.add, None, [r], [r])
                GB.append((t, r))
        lng = self.load_bcast(ph, "lng", self.din["ln1_g" if which == 0 else "ln2_g"][l, :])
        lnb = self.load_bcast(ph, "lnb", self.din["ln1_b" if which == 0 else "ln2_b"][l, :])
        eb = []
        for i in range(2):
            eb.append(dict(
                buf=self.sb(ph, "ebuf%d" % i, [128, D], F32), st=self.sb(ph, "est%d" % i, [128, 12], F32),
                mv=self.sb(ph, "emv%d" % i, [128, 2], F32), sm=self.sb(ph, "esm%d" % i, [128, 4], F32),
                res="ebuf%d_%d" % (i, self._n)))
        return dict(GB=GB, lng=lng, lnb=lnb, eb=eb, i=0, single=single_gb, cur=None, l=l, gcol=gcol)

    def epi_select(self, E, b):
        if not E["single"] or E["cur"] == b:
            return
        E["cur"] = b
        t, r = E["GB"][b]
        self.S.dma("sp", r, t, self.modrow[E["l"], b, E["gcol"]:E["gcol"] + D].partition_broadcast(128), writes=[r])
        self.ts("dve", t, t, 1.0, None, ALU.add, None, [r], [r])

    def epilogue(self, E, b, y, yres, x, xres, out_rows, ybias=None):
        S = self.S
        e = E["eb"][E["i"] % 2]
        E["i"] += 1
        buf, r = e["buf"], e["res"]
        GBt, GBr = E["GB"][b]
        if ybias is not None:
            self.tt("dve", buf, y, ybias[0], ALU.add, [yres, ybias[1]], [r])
            self.tt("dve", buf, buf, GBt, ALU.mult, [r, GBr], [r])
        else:
            self.tt("dve", buf, y, GBt, ALU.mult, [yres, GBr], [r])
        self.stt(buf, x, ALPHA, buf, ALU.mult, ALU.add, [xres, r], [r])
        self.ln_rows(buf, r, e)
        self.tt("dve", buf, buf, E["lng"][0], ALU.mult, [r, E["lng"][1]], [r])
        self.tt("pool", buf, buf, E["lnb"][0], ALU.add, [r, E["lnb"][1]], [r])
        S.dma("sp", r, out_rows, buf, reads=[r])

    def ln_rows(self, buf, r, e, out=None, eng_norm="act"):
        S = self.S
        st, mv, sm = e["st"], e["mv"], e["sm"]
        rs = r + "s"
        S.op("dve", lambda en: en.bn_stats(st[:, 0:6], buf[:, 0:512]), [r], [rs])
        S.op("dve", lambda en: en.bn_stats(st[:, 6:12], buf[:, 512:1024]), [r], [rs])
        S.op("dve", lambda en: en.bn_aggr(mv, st), [rs], [rs])
        self.ts("dve", sm[:, 0:1], mv[:, 1:2], EPS, None, ALU.add, None, [rs], [rs])
        self.tt("pool", sm[:, 1:2], sm[:, 0:1], self.mhalf, ALU.pow, [rs, "mhalf"], [rs])
        self.ts("dve", sm[:, 2:3], mv[:, 0:1], sm[:, 1:2], -1.0, ALU.mult, ALU.mult, [rs], [rs])
        o = buf if out is None else out[0]
        wr = [r] if out is None else [out[1]]
        self.act(o, buf, AF.Identity, [r, rs], wr, bias=sm[:, 2:3], scale=sm[:, 1:2])

    def xT_mod(self, xt, xres, nsub, hT, hres, l, b, which, tpr):
        sh0 = 0 if which == 0 else 24
        sc0 = 8 if which == 0 else 32
        for kc in range(8):
            pt, pr = tpr.next()
            for s in range(nsub):
                self.tr(pt[:, s * 128:(s + 1) * 128], xt[:, s, kc * 128:(kc + 1) * 128], self.ident,
                        [xres, "ident"], [pr])
            self.act(hT[:, kc, 0:nsub * 128], pt[:, 0:nsub * 128], AF.Identity, [pr, "modT"], [hres],
                     bias=self.modT[:, l, b, sh0 + kc:sh0 + kc + 1], scale=self.modT[:, l, b, sc0 + kc:sc0 + kc + 1])

    def x_tile(self, X, b, t0, tt):
        return X[b, t0:t0 + tt, :].rearrange("(s p) d -> p s d", p=128)

    def phase_ffn(self, l, Xin, Xout):
        nc, S, ns, T = self.nc, self.S, self.nseq, self.T
        TT = 256
        nsub = TT // 128
        NF = FH // 128
        with ExitStack() as ph:
            w1 = self.sb(ph, "w1", [128, 8, 2 * FH], BF16)
            w2 = self.sb(ph, "w2", [128, NF, D], BF16)
            for kc in range(8):
                self.load_w_bf16(w1[:, kc, :], self.din["ffn_w_in"][l, kc * 128:(kc + 1) * 128, :], "w1", "w1", 1408)
            for fc in range(NF):
                self.load_w_bf16(w2[:, fc, :], self.din["ffn_w_out"][l, fc * 128:(fc + 1) * 128, :], "w2", "w2")
            E = self.epi_setup(ph, l, 1)
            xt = [(self.sb(ph, "xt%d" % i, [128, nsub, D], F32), "xt%d" % i) for i in range(2)]
            hT = [(self.sb(ph, "hT%d" % i, [128, 8, TT], BF16), "hT%d" % i) for i in range(2)]
            aT = (self.sb(ph, "aT", [128, NF, TT], BF16), "aT")
            sg = [(self.sb(ph, "sg%d" % i, [128, TT], F32), "sg%d" % i) for i in range(2)]
            tpr = Ring(self.pb[0:2])
            gur = Ring([(self.pb[2], self.pb[3]), (self.pb[6], self.pb[7])])
            tiles = [(b, t0) for b in range(ns) for t0 in range(0, T, TT)]
            n = len(tiles)

            def loads(i):
                b_, t0_ = tiles[i]
                S.dma("sp", xt[i % 2][1], xt[i % 2][0], self.x_tile(Xin, b_, t0_, TT), writes=[xt[i % 2][1]])

            def prologue(i):
                self.xT_mod(xt[i % 2][0], xt[i % 2][1], nsub, hT[i % 2][0], hT[i % 2][1], l, tiles[i][0], 1, tpr)

            def inproj(i):
                hs, hr = hT[i % 2]
                for fc in range(NF):
                    (pg, pgr), (pu, pur) = gur.next()
                    for kc in range(8):
                        self.mm(pg[:, 0:TT], w1[:, kc, fc * 128:(fc + 1) * 128], hs[:, kc, :], kc == 0, kc == 7,
                                ["w1", hr], [pgr])
                    for kc in range(8):
                        self.mm(pu[:, 0:TT], w1[:, kc, FH + fc * 128:FH + (fc + 1) * 128], hs[:, kc, :], kc == 0,
                                kc == 7, ["w1", hr], [pur])
                    sgt, sgr = sg[fc % 2]
                    self.act(sgt, pg[:, 0:TT], AF.Silu, [pgr], [sgr])
                    self.tt("dve", aT[0][:, fc, :], sgt, pu[:, 0:TT], ALU.mult, [sgr, pur], [aT[1]])

            def outproj(i):
                b, t0 = tiles[i]
                xs, xr = xt[i % 2]
                for s in range(nsub):
                    pyt, pyr = self.py
                    for hf in range(2):
                        for fc in range(NF):
                            self.mm(pyt[:, hf * 512:(hf + 1) * 512], aT[0][:, fc, s * 128:(s + 1) * 128],
                                    w2[:, fc, hf * 512:(hf + 1) * 512], fc == 0, fc == NF - 1, [aT[1], "w2"], [pyr])
                    self.epilogue(E, b, pyt, pyr, xs[:, s, :], xr, Xout[b, t0 + s * 128:t0 + (s + 1) * 128, :])

            loads(0)
            if n > 1:
                loads(1)
            print("[sbuf] ffn remaining", nc.sbuf_bytes_remaining)
            prologue(0)
            for i in range(n):
                inproj(i)
                if i + 1 < n:
                    prologue(i + 1)
                outproj(i)
                if i + 2 < n:
                    loads(i + 2)

    def phase_gmlp(self, l, Xin, Xout):
        nc, S, ns, T = self.nc, self.S, self.nseq, self.T
        TT = 512
        nsub = TT // 128
        W = self.din
        with ExitStack() as ph:
            wi = self.sb(ph, "wi", [128, 8, 2 * D], BF16)
            wo = self.sb(ph, "wo", [128, 8, D], BF16)
            for kc in range(8):
                self.load_w_bf16(wi[:, kc, :], W["gm_w_in"][0, kc * 128:(kc + 1) * 128, :], "wi", "wi")
                self.load_w_bf16(wo[:, kc, :], W["gm_w_out"][0, kc * 128:(kc + 1) * 128, :], "wo", "wo")
            wsl = self.sb(ph, "wsl", [128, 8, 128], F32)
            wsm = self.sb(ph, "wsm", [128, 8, 128], F32)
            wmT = self.sb(ph, "wmT", [128, 8, 128], BF16)
            S.dma("sp", "wsl", wsl, W["gm_w_s"][0].rearrange("g t s -> t g s"), writes=["wsl"])
            for g in range(8):
                pt, pr = self.pb[g % 2]
                self.tr(pt[:, 0:128], wsl[:, g, :], self.ident, ["wsl", "ident"], [pr])
                self.cp("dve", wsm[:, g, :], pt[:, 0:128], [pr], ["wsm"])
                S.op("pool", lambda e, g=g: e.affine_select(wmT[:, g, :], wsm[:, g, :], [[1, 128]], ALU.is_ge, 0.0,
                                                             base=0, channel_multiplier=-1), ["wsm"], ["wmT"])
            binu, binu_r = self.load_pp(ph, "binu", W["gm_b_in"][0, 0:D].rearrange("(c p) -> c p", p=128), 2)
            binv, binv_r = self.load_bcast(ph, "binv", W["gm_b_in"][0, D:2 * D])
            glg, glg_r = self.load_bcast(ph, "glg", W["gm_ln_g"][0, :])
            glb, glb_r = self.load_bcast(ph, "glb", W["gm_ln_b"][0, :])
            bsb, bsb_r = self.load_bcast(ph, "bsb", W["gm_b_s"][0].rearrange("g t -> (g t)"))
            bsb3 = bsb.rearrange("p (g t) -> p g t", t=128)
            E = self.epi_setup(ph, l, 0, single_gb=True)
            xt = [(self.sb(ph, "xt%d" % i, [128, nsub, D], F32), "xt%d" % i) for i in range(3)]
            hT = [(self.sb(ph, "hT%d" % i, [128, 8, TT], BF16), "hT%d" % i) for i in range(2)]
            vz = [dict(buf=self.sb(ph, "vz%d" % i, [128, D], F32), st=self.sb(ph, "vst%d" % i, [128, 12], F32),
                       mv=self.sb(ph, "vmv%d" % i, [128, 2], F32), sm=self.sb(ph, "vsm%d" % i, [128, 4], F32),
                       res="vz%d" % i) for i in range(2)]
            vn = [[(self.sb(ph, "vn%d_%d" % (k, i), [128, D], BF16), "vn%d_%d" % (k, i)) for i in range(nsub)] for k in range(2)]
            uT = [(self.sb(ph, "uT%d" % i, [128, TT], F32), "uT%d" % i) for i in range(2)]
            tmp = [(self.sb(ph, "tmp%d" % i, [128, TT], F32), "tmp%d" % i) for i in range(2)]
            yT = (self.sb(ph, "yT", [128, 8, TT], BF16), "yT")
            print("[sbuf] gmlp remaining", nc.sbuf_bytes_remaining)
            tpr = Ring(self.pb[0:2])
            tiles = [(b, t0) for b in range(ns) for t0 in range(0, T, TT)]
            cnt = dict(vi=0)

            def loads(i):
                b_, t0_ = tiles[i]
                S.dma("sp", xt[i % 3][1], xt[i % 3][0], self.x_tile(Xin, b_, t0_, TT), writes=[xt[i % 3][1]])

            def stageA(i):
                b, t0 = tiles[i]
                xs, xr = xt[i % 3]
                hs, hr = hT[i % 2]
                self.xT_mod(xs, xr, nsub, hs, hr, l, b, 0, tpr)
                for s in range(nsub):
                    pv, pvr = self.py1
                    for hf in range(2):
                        for kc in range(8):
                            self.mm(pv[:, hf * 512:(hf + 1) * 512], hs[:, kc, s * 128:(s + 1) * 128],
                                    wi[:, kc, D + hf * 512:D + (hf + 1) * 512], kc == 0, kc == 7, [hr, "wi"], [pvr])
                    z = vz[cnt["vi"] % 2]
                    cnt["vi"] += 1
                    vt, vr = vn[i % 2][s]
                    self.tt("dve", z["buf"], pv, binv, ALU.add, [pvr, binv_r], [z["res"]])
                    self.act(z["buf"], z["buf"], AF.Gelu, [z["res"]], [z["res"]])
                    self.ln_rows(z["buf"], z["res"], z)
                    self.tt("dve", z["buf"], z["buf"], glg, ALU.mult, [z["res"], glg_r], [z["res"]])
                    self.tt("pool", vt, z["buf"], glb, ALU.add, [z["res"], glb_r], [vr])

            def stageB(i):
                b, t0 = tiles[i]
                xs, xr = xt[i % 3]
                hs, hr = hT[i % 2]
                self.epi_select(E, b)
                for g in range(8):
                    pu, pur = self.pb[2]
                    psv, psvr = self.pb[3]
                    for kc in range(8):
                        self.mm(pu, wi[:, kc, g * 128:(g + 1) * 128], hs[:, kc, :], kc == 0, kc == 7, ["wi", hr], [pur])
                    ut, utr = uT[g % 2]
                    self.act(ut, pu, AF.Gelu, [pur, binu_r], [utr], bias=binu[:, g:g + 1])
                    for s in range(nsub):
                        vt, vr = vn[i % 2][s]
                        self.mm(psv[:, s * 128:(s + 1) * 128], vt[:, g * 128:(g + 1) * 128], wmT[:, g, :],
                                True, True, [vr, "wmT"], [psvr])
                    tm, tmr = tmp[g % 2]
                    self.tt("dve", tm.rearrange("p (s t) -> p s t", t=128), psv.rearrange("p (s t) -> p s t", t=128),
                            bsb3[:, g:g + 1, :].to_broadcast([128, nsub, 128]), ALU.add, [psvr, bsb_r], [tmr])
                    self.tt("dve" if g % 2 == 0 else "pool", yT[0][:, g, :], tm, ut, ALU.mult, [tmr, utr], [yT[1]])
                for s in range(nsub):
                    pyt, pyr = self.py
                    for hf in range(2):
                        for g in range(8):
                            self.mm(pyt[:, hf * 512:(hf + 1) * 512], yT[0][:, g, s * 128:(s + 1) * 128],
                                    wo[:, g, hf * 512:(hf + 1) * 512], g == 0, g == 7, [yT[1], "wo"], [pyr])
                    self.epilogue(E, b, pyt, pyr, xs[:, s, :], xr, Xout[b, t0 + s * 128:t0 + (s + 1) * 128, :])

            n = len(tiles)
            loads(0)
            if n > 1:
                loads(1)
            for i in range(n):
                stageA(i)
                if i >= 1:
                    stageB(i - 1)
                if i + 2 < n:
                    loads(i + 2)
            stageB(n - 1)

    def phase_conv(self, l, Xin, Xout):
        nc, S, ns, T = self.nc, self.S, self.nseq, self.T
        TT = 256
        nsub = TT // 128
        W = self.din
        HW = CONVW - 1
        with ExitStack() as ph:
            wi = self.sb(ph, "wi", [128, 8, 2 * D], BF16)
            wo = self.sb(ph, "wo", [128, 8, D], BF16)
            for kc in range(8):
                self.load_w_bf16(wi[:, kc, :], W["cv_w_in"][0, kc * 128:(kc + 1) * 128, :], "wi", "wi")
                self.load_w_bf16(wo[:, kc, :], W["cv_w_out"][0, kc * 128:(kc + 1) * 128, :], "wo", "wo")
            bia, bia_r = self.load_pp(ph, "bia", W["cv_b_in"][0, :].rearrange("(c p) -> c p", p=128), 2)
            dwr = W["cv_dw"][0].rearrange("i (c p) -> (i c) p", p=128)
            dwa, dwa_r = self.load_pp(ph, "dwa", dwr[0:124, :], 2)
            dwb_, dwb_r = self.load_pp(ph, "dwb", dwr[124:248, :], 3)
            dwbias, dwbias_r = self.load_pp(ph, "dwbias", W["cv_dw_b"][0, :].rearrange("(c p) -> c p", p=128), 2)
            clg, clg_r = self.load_pp(ph, "clg", W["cv_ln_g"][0, :].rearrange("(c p) -> c p", p=128), 3)
            clb, clb_r = self.load_pp(ph, "clb", W["cv_ln_b"][0, :].rearrange("(c p) -> c p", p=128), 2)
            bo32 = self.sb(ph, "bo32", [1, D], F32)
            bohi = self.sb(ph, "bohi", [1, D], BF16)
            bolo = self.sb(ph, "bolo", [1, D], BF16)
            one1 = self.sb(ph, "one1", [1, 128], BF16)
            S.dma("sp", "bo32", bo32, W["cv_b_out"][0:1, :], writes=["bo"])
            self.cp("dve", bohi, bo32, ["bo"], ["bohi"])
            self.tt("dve", bo32, bo32, bohi, ALU.subtract, ["bo", "bohi"], ["bo"])
            self.cp("dve", bolo, bo32, ["bo"], ["bolo"])
            self.memset("pool", one1, 1.0, ["one1"])
            diag = self.sb(ph, "diag", [128, CONVW * 8, 128], BF16)
            for half, (dt_, dr_) in enumerate(((dwa, dwa_r), (dwb_, dwb_r))):
                self.tt("dve", diag[:, half * 124:(half + 1) * 124, :],
                        self.ident[:, None, :].to_broadcast([128, 124, 128]),
                        dt_[:, :, None].to_broadcast([128, 124, 128]), ALU.mult, ["ident", dr_], ["diag"])
            onesf = self.sb(ph, "onesf", [128, 128], F32)
            self.memset("pool", onesf, 1.0, ["onesf"])
            E = self.epi_setup(ph, l, 0, single_gb=True)
            xt = [(self.sb(ph, "xt%d" % i, [128, nsub, D], F32), "xt%d" % i) for i in range(3)]
            hT = (self.sb(ph, "hT", [128, 8, TT], BF16), "hT")
            ybuf = (self.sb(ph, "ybuf", [128, 8, HW + TT], BF16), "ybuf")
            sgm = [(self.sb(ph, "sgm%d" % i, [128, TT], F32), "sgm%d" % i) for i in range(2)]
            zT = [(self.sb(ph, "zT%d" % i, [128, 8, TT], F32), "zT%d" % i) for i in range(2)]
            zq = [(self.sb(ph, "zq%d" % i, [128, TT], F32), "zq%d" % i) for i in range(2)]
            mean_t = [(self.sb(ph, "mean_t%d" % i, [128, TT], F32), "mean_t%d" % i) for i in range(2)]
            rstd_t = [(self.sb(ph, "rstd_t%d" % i, [128, TT], F32), "rstd_t%d" % i) for i in range(2)]
            sT = (self.sb(ph, "sT", [128, 8, TT], BF16), "sT")
            print("[sbuf] conv remaining", nc.sbuf_bytes_remaining)
            tpr = Ring(self.pb[0:1])
            tiles = [(b, t0) for b in range(ns) for t0 in range(0, T, TT)]

            def loads(i):
                b_, t0_ = tiles[i]
                S.dma("sp", xt[i % 3][1], xt[i % 3][0], self.x_tile(Xin, b_, t0_, TT), writes=[xt[i % 3][1]])

            def stageA(i):
                b, t0 = tiles[i]
                xs, xr = xt[i % 3]
                hs, hr = hT
                self.xT_mod(xs, xr, nsub, hs, hr, l, b, 0, tpr)
                if t0 == 0:
                    self.memset("pool", ybuf[0][:, :, 0:HW], 0.0, [ybuf[1]])
                else:
                    self.cp("pool", ybuf[0][:, :, 0:HW], ybuf[0][:, :, TT:TT + HW], [ybuf[1]], [ybuf[1]])
                for kc in range(8):
                    pa, par = self.pb[2]
                    pg, pgr = self.pb[3]
                    for k in range(8):
                        self.mm(pa[:, 0:TT], wi[:, k, kc * 128:(kc + 1) * 128], hs[:, k, :], k == 0, k == 7, ["wi", hr], [par])
                    for k in range(8):
                        self.mm(pg[:, 0:TT], wi[:, k, D + kc * 128:D + (kc + 1) * 128], hs[:, k, :], k == 0, k == 7,
                                ["wi", hr], [pgr])
                    sg_, sgr = sgm[kc % 2]
                    self.act(sg_, pg[:, 0:TT], AF.Sigmoid, [pgr, bia_r], [sgr], bias=bia[:, 8 + kc:9 + kc])
                    self.stt(ybuf[0][:, kc, HW:HW + TT], pa[:, 0:TT], bia[:, kc:kc + 1], sg_, ALU.add, ALU.mult,
                             [par, sgr, bia_r], [ybuf[1]])
                z_, zr = zT[i % 2]
                p1, p1r = self.pb[7]
                p2, p2r = self.pb[1]
                for kc in range(8):
                    pz, pzr = self.pb[6]
                    for t in range(CONVW):
                        self.mm(pz[:, 0:TT], diag[:, t * 8 + kc, :], ybuf[0][:, kc, t:t + TT], t == 0, t == CONVW - 1,
                                ["diag", ybuf[1]], [pzr])
                    self.act(z_[:, kc, :], pz[:, 0:TT], AF.Identity, [pzr, dwbias_r], [zr], bias=dwbias[:, kc:kc + 1])
                    q_, qr = zq[kc % 2]
                    S.op("act", lambda e, q_=q_, kc=kc: e.activation(q_, z_[:, kc, :], AF.Square), [zr], [qr])
                    self.mm(p1[:, 0:TT], onesf, z_[:, kc, :], kc == 0, kc == 7, ["onesf", zr], [p1r])
                    self.mm(p2[:, 0:TT], onesf, q_, kc == 0, kc == 7, ["onesf", qr], [p2r])
                m_, mr = mean_t[i % 2]
                r_, rr = rstd_t[i % 2]
                q, qr_ = r_, rr
                self.ts("dve", m_, p1[:, 0:TT], 1.0 / D, None, ALU.mult, None, [p1r], [mr])
                self.tt("dve", q, m_, m_, ALU.mult, [mr], [qr_])
                self.stt(q, p2[:, 0:TT], 1.0 / D, q, ALU.mult, ALU.subtract, [p2r, qr_], [qr_])
                self.ts("dve", q, q, EPS, None, ALU.add, None, [qr_], [qr_])
                self.act(r_, q, AF.Sqrt, [qr_], [rr])
                S.op("dve", lambda e, r_=r_: e.reciprocal(r_, r_), [rr], [rr])

            def stageB(i):
                b, t0 = tiles[i]
                xs, xr = xt[i % 3]
                z_, zr = zT[i % 2]
                m_, mr = mean_t[i % 2]
                r_, rr = rstd_t[i % 2]
                self.epi_select(E, b)
                mb = m_[:, None, :].to_broadcast([128, 8, TT])
                rb = r_[:, None, :].to_broadcast([128, 8, TT])
                self.tt("dve", z_, z_, mb, ALU.subtract, [zr, mr], [zr])
                self.tt("dve", z_, z_, rb, ALU.mult, [zr, rr], [zr])
                for kc in range(8):
                    self.act(sT[0][:, kc, :], z_[:, kc, :], AF.Silu, [zr, clg_r, clb_r], [sT[1]], bias=clb[:, kc:kc + 1],
                             scale=clg[:, kc:kc + 1])
                for s in range(nsub):
                    pyt, pyr = self.py
                    for hf in range(2):
                        cs = slice(hf * 512, (hf + 1) * 512)
                        for kc in range(8):
                            self.mm(pyt[:, cs], sT[0][:, kc, s * 128:(s + 1) * 128], wo[:, kc, cs], kc == 0, False,
                                    [sT[1], "wo"], [pyr])
                        self.mm(pyt[:, cs], one1, bohi[:, cs], False, False, ["one1", "bohi"], [pyr])
                        self.mm(pyt[:, cs], one1, bolo[:, cs], False, True, ["one1", "bolo"], [pyr])
                    self.epilogue(E, b, pyt, pyr, xs[:, s, :], xr, Xout[b, t0 + s * 128:t0 + (s + 1) * 128, :])

            n = len(tiles)
            loads(0)
            if n > 1:
                loads(1)
            for i in range(n):
                stageA(i)
                if i >= 1:
                    stageB(i - 1)
                if i + 2 < n:
                    loads(i + 2)
            stageB(n - 1)

    def phase_attn(self, l, kind, Xin, Xout):
        self.attn_proj(l, kind, Xin)
        self.S.barrier()
        if kind == "fox":
            self.attn_core_fox()
        else:
            self.attn_core_sb()
        self.S.barrier()
        self.attn_out(l, kind, Xin, Xout)

    def attn_proj(self, l, kind, Xin):
        nc, S, ns, T = self.nc, self.S, self.nseq, self.T
        TT = 512
        nsub = TT // 128
        W = self.din
        fox = kind == "fox"
        NC = 3 * D + (NH if fox else 0)
        wname = "fox_w_in" if fox else "sb_w_in"
        with ExitStack() as ph:
            wi = self.sb(ph, "wi", [128, 8, NC], BF16)
            for kc in range(8):
                self.load_w_bf16(wi[:, kc, :], W[wname][0, kc * 128:(kc + 1) * 128, :], "wi", "wi", 1024 + (NH if fox else 0))
            xt = [(self.sb(ph, "xt%d" % i, [128, nsub, D], F32), "xt%d" % i) for i in range(2)]
            hT = [(self.sb(ph, "hT%d" % i, [128, 8, TT], BF16), "hT%d" % i) for i in range(2)]
            qst = [(self.sb(ph, "qst%d" % i, [128, 8, TT], BF16), "qst%d" % i) for i in range(2)]
            kst = [(self.sb(ph, "kst%d" % i, [128, 8, TT], BF16), "kst%d" % i) for i in range(2)]
            vst = [(self.sb(ph, "vst%d" % i, [128, nsub, D], BF16), "vst%d" % i) for i in range(2)]
            if fox:
                nbf = self.sb(ph, "nbf", [NH, 1], F32)
                S.dma("sp", "nbf", nbf, W["fox_b_f"][0, :].rearrange("(h o) -> h o", o=1), writes=["nbf"])
                self.ts("dve", nbf, nbf, -1.0, None, ALU.mult, None, ["nbf"], ["nbf"])
                ones16 = self.sb(ph, "ones16", [NH, TT], F32)
                self.memset("dve", ones16, 1.0, ["ones16"])
                ef = self.sb(ph, "ef", [NH, TT], F32)
                cum = [(self.sb(ph, "cum%d" % i, [NH, TT], F32), "cum%d" % i) for i in range(2)]
                s8 = self.sb(ph, "s8", [NH, TT], F32)
                r1 = self.sb(ph, "r1", [NH, TT], F32)
                fk3 = [(self.sb(ph, "fk3%d" % i, [NH, 3, TT], BF16), "fk3%d" % i) for i in range(2)]
                fq3 = [(self.sb(ph, "fq3%d" % i, [NH, 3, TT], BF16), "fq3%d" % i) for i in range(2)]
            tpr = Ring(self.pb[0:2])
            qkr = Ring(self.pb[2:4])
            ev = 0
            tiles = [(b, t0) for b in range(ns) for t0 in range(0, T, TT)]

            def loads(i):
                b_, t0_ = tiles[i]
                S.dma("sp", xt[i % 2][1], xt[i % 2][0], self.x_tile(Xin, b_, t0_, TT), writes=[xt[i % 2][1]])

            loads(0)
            for it, (b, t0) in enumerate(tiles):
                if True:
                    if it + 1 < len(tiles):
                        loads(it + 1)
                    sl = it % 2
                    xs, xr = xt[sl]
                    hs, hr = hT[sl]
                    self.xT_mod(xs, xr, nsub, hs, hr, l, b, 0, tpr)
                    for (stg, c0, dst) in ((qst[sl], 0, self.qt_d), (kst[sl], D, self.kt_d)):
                        for j in range(8):
                            pt, pr = qkr.next()
                            for kc in range(8):
                                self.mm(pt, wi[:, kc, c0 + j * 128:c0 + (j + 1) * 128], hs[:, kc, :], kc == 0, kc == 7,
                                        ["wi", hr], [pr])
                            self.cp("act" if ev % 2 == 0 else "dve", stg[0][:, j, :], pt, [pr], [stg[1]])
                            ev += 1
                        S.dma("sp", stg[1], dst[b].rearrange("(j hh) d t -> (hh d) j t", hh=2)[:, :, t0:t0 + TT], stg[0],
                              reads=[stg[1]])
                    for s in range(nsub):
                        pv, pvr = self.py
                        for hf in range(2):
                            for kc in range(8):
                                self.mm(pv[:, hf * 512:(hf + 1) * 512], hs[:, kc, s * 128:(s + 1) * 128],
                                        wi[:, kc, 2 * D + hf * 512:2 * D + (hf + 1) * 512], kc == 0, kc == 7, [hr, "wi"], [pvr])
                        self.cp("act" if ev % 2 == 0 else "dve", vst[sl][0][:, s, :], pv, [pvr], [vst[sl][1]])
                        ev += 1
                    S.dma("sp", vst[sl][1], self.v_d[b, t0:t0 + TT, :].rearrange("(s p) d -> p s d", p=128), vst[sl][0],
                          reads=[vst[sl][1]])
                    if fox:
                        pf, pfr = self.pb[6]
                        for kc in range(8):
                            self.mm(pf[0:NH, :], wi[:, kc, 3 * D:3 * D + NH], hs[:, kc, :], kc == 0, kc == 7, ["wi", hr], [pfr])
                        self.act(ef, pf[0:NH, :], AF.Exp, [pfr, "nbf"], ["ef"], bias=nbf, scale=-1.0)
                        self.act(ef, ef, AF.Ln, ["ef"], ["ef"], bias=1.0)
                        cm, cmr = cum[sl]
                        pc, pcr = cum[1 - sl]
                        init = 0.0 if t0 == 0 else pc[:, TT - 1:TT]
                        S.op("dve", lambda e, cm=cm, init=init: e.tensor_tensor_scan(cm, ones16, ef, init, ALU.mult, ALU.add),
                             ["ones16", "ef"] + ([] if t0 == 0 else [pcr]), [cmr])
                        fk, fkr = fk3[sl]
                        fq, fqr = fq3[sl]
                        self.ts("dve", s8, cm, 8.0, None, ALU.mult, None, [cmr], ["s8"])
                        self.cp("dve", fk[:, 0, :], s8, ["s8"], [fkr])
                        self.tt("dve", r1, s8, fk[:, 0, :], ALU.subtract, ["s8", fkr], ["r1"])
                        self.cp("dve", fk[:, 1, :], r1, ["r1"], [fkr])
                        self.tt("dve", r1, r1, fk[:, 1, :], ALU.subtract, ["r1", fkr], ["r1"])
                        self.cp("dve", fk[:, 2, :], r1, ["r1"], [fkr])
                        self.ts("dve", fq, fk, -1.0, None, ALU.mult, None, [fkr], [fqr])
                        S.dma("sp", fkr, self.fk_d[b, :, :, t0:t0 + TT], fk, reads=[fkr])
                        S.dma("sp", fqr, self.fq_d[b, :, :, t0:t0 + TT], fq, reads=[fqr])

    def attn_core_fox(self):
        nc, S, ns, T = self.nc, self.S, self.nseq, self.T
        NKB = T // 128
        QW = 1024
        NQ = T // QW
        KA = DH + 6
        LA = 2
        with ExitStack() as ph:
            kta = [(self.sb(ph, "kta%d" % i, [KA, T], BF16), "kta%d" % i) for i in range(3)]
            qta = [(self.sb(ph, "qta%d" % i, [KA, T], BF16), "qta%d" % i) for i in range(3)]
            va = [(self.sb(ph, "va%d" % i, [128, NKB, DH + 1], BF16), "va%d" % i) for i in range(3)]
            PT = [(self.sb(ph, "PT%d" % i, [128, QW], BF16), "PT%d" % i) for i in range(4)]
            ost = [(self.sb(ph, "ost%d" % i, [DH + 1, QW], F32), "ost%d" % i) for i in range(2)]
            for i in range(3):
                self.memset("dve", kta[i][0][DH:KA, :], 1.0, [kta[i][1]])
                self.memset("dve", qta[i][0][DH:KA, :], 1.0, [qta[i][1]])
                self.memset("pool", va[i][0][:, :, DH:DH + 1], 1.0, [va[i][1]])
            psr = Ring(self.pd[0:2])
            por = Ring(self.pd[2:4])
            ptr = Ring(PT)
            osr = Ring(ost)
            pend = []

            def halves(c0):
                return [(max(c0, hf * 512), (hf + 1) * 512) for hf in range(2) if max(c0, hf * 512) < (hf + 1) * 512]

            def stage2(tl):
                (b, h, j, kb, c0, nkb, sl, po, por_, pt, ptr_) = tl
                for (a0, a1) in halves(c0):
                    self.mm(po[0:DH + 1, a0:a1], va[sl][0][:, kb, :], pt[:, a0:a1], kb == 0, kb == nkb - 1,
                            [va[sl][1], ptr_], [por_])
                if kb == nkb - 1:
                    os_, osr_ = osr.next()
                    self.cp("dve", os_, po[0:DH + 1, :], [por_], [osr_])
                    S.dma("sp", osr_, self.o_d[b, h, :, j * QW:(j + 1) * QW], os_[0:DH, :], reads=[osr_])
                    S.dma("sp", osr_, self.den_d[b, h:h + 1, j * QW:(j + 1) * QW], os_[DH:DH + 1, :], reads=[osr_])

            heads = [(b, h) for b in range(ns) for h in range(NH)]

            def loads(i):
                b_, h_ = heads[i]
                sl_ = i % 3
                S.dma("sp", kta[sl_][1], kta[sl_][0][0:DH, :], self.kt_d[b_, h_], writes=[kta[sl_][1]])
                S.dma("sp", kta[sl_][1], kta[sl_][0][DH:DH + 3, :], self.fk_d[b_, h_], writes=[kta[sl_][1]])
                S.dma("sp", qta[sl_][1], qta[sl_][0][0:DH, :], self.qt_d[b_, h_], writes=[qta[sl_][1]])
                S.dma("sp", qta[sl_][1], qta[sl_][0][DH + 3:DH + 6, :], self.fq_d[b_, h_], writes=[qta[sl_][1]])
                S.dma("sp", va[sl_][1], va[sl_][0][:, :, 0:DH],
                      self.v_d[b_, :, h_ * DH:(h_ + 1) * DH].rearrange("(k p) d -> p k d", p=128), writes=[va[sl_][1]])

            loads(0)
            for hi, (b, h) in enumerate(heads):
                sl = hi % 3
                if hi + 1 < len(heads):
                    loads(hi + 1)
                for j in range(NQ):
                    nkb = 8 * j + 8
                    po, por_ = por.next()
                    for kb in range(nkb):
                        c0 = (kb - 8 * j) * 128 if kb >= 8 * j else 0
                        ps, psr_ = psr.next()
                        pt, ptr_ = ptr.next()
                        for (a0, a1) in halves(c0):
                            self.mm(ps[:, a0:a1], kta[sl][0][:, kb * 128:(kb + 1) * 128],
                                    qta[sl][0][:, j * QW + a0:j * QW + a1], True, True, [kta[sl][1], qta[sl][1]], [psr_])
                        self.act(pt[:, c0:QW], ps[:, c0:QW], AF.Exp, [psr_], [ptr_], scale=0.125)
                        if kb >= 8 * j:
                            S.op("pool", lambda e, pt=pt, c0=c0: e.affine_select(
                                pt[:, c0:c0 + 128], pt[:, c0:c0 + 128], [[1, 128]], ALU.is_ge, 0.0, base=0,
                                channel_multiplier=-1), [ptr_], [ptr_])
                        pend.append((b, h, j, kb, c0, nkb, sl, po, por_, pt, ptr_))
                        if len(pend) > LA:
                            stage2(pend.pop(0))
            while pend:
                stage2(pend.pop(0))

    def attn_core_sb(self):
        nc, S, ns, T = self.nc, self.S, self.nseq, self.T
        NKB = T // 128
        QW = 1024
        NQ = T // QW
        with ExitStack() as ph:
            kta = [(self.sb(ph, "kta%d" % i, [DH, T], BF16), "kta%d" % i) for i in range(3)]
            qta = [(self.sb(ph, "qta%d" % i, [DH, T], BF16), "qta%d" % i) for i in range(3)]
            va = [(self.sb(ph, "va%d" % i, [128, NKB, DH], BF16), "va%d" % i) for i in range(3)]
            Et = [(self.sb(ph, "Et%d" % i, [128, QW], F32), "Et%d" % i) for i in range(2)]
            SPt = [(self.sb(ph, "SPt%d" % i, [128, QW], BF16), "SPt%d" % i) for i in range(3)]
            X2 = [(self.sb(ph, "X2%d" % i, [128, QW], F32), "X2%d" % i) for i in range(2)]
            At = [(self.sb(ph, "At%d" % i, [128, QW], BF16), "At%d" % i) for i in range(3)]
            Ccs = [(self.sb(ph, "Cc%d" % i, [128, QW], F32), "Cc%d" % i) for i in range(2)]
            ccr = Ring(Ccs)
            ost = [(self.sb(ph, "ost%d" % i, [DH, QW], F32), "ost%d" % i) for i in range(2)]
            trin = self.sb(ph, "trin", [128, 128], BF16)
            onen = self.sb(ph, "onen", [128, 128], BF16)
            self.memset("pool", trin, -8.0, ["trin"])
            S.op("pool", lambda e: e.affine_select(trin, trin, [[-1, 128]], ALU.is_ge, 0.0, base=0, channel_multiplier=1),
                 ["trin"], ["trin"])
            self.memset("pool", onen, -8.0, ["onen"])
            psr = Ring(self.pd[0:2])
            pyy, pyyr = self.pd[2]
            po_, por_ = self.pd[3]
            etr, spr, x2r, atr, osr = Ring(Et), Ring(SPt), Ring(X2), Ring(At), Ring(ost)
            q1, q2 = [], []

            def halves(c0):
                return [(max(c0, hf * 512), (hf + 1) * 512) for hf in range(2) if max(c0, hf * 512) < (hf + 1) * 512]

            def mask(tile_ap, res, c0):
                S.op("pool", lambda e: e.affine_select(tile_ap[:, c0:c0 + 128], tile_ap[:, c0:c0 + 128], [[1, 128]], ALU.is_gt,
                                                        0.0, base=0, channel_multiplier=-1), [res], [res])

            def stageA1(tl):
                ps, psr_ = tl["ps"]
                c0, sl, kb, j = tl["c0"], tl["sl"], tl["kb"], tl["j"]
                for (a0, a1) in halves(c0):
                    self.mm(ps[:, a0:a1], kta[sl][0][:, kb * 128:(kb + 1) * 128], qta[sl][0][:, j * QW + a0:j * QW + a1],
                            True, True, [kta[sl][1], qta[sl][1]], [psr_])
                et, etr_ = etr.next()
                tl["et"] = (et, etr_)
                self.act(et[:, c0:QW], ps[:, c0:QW], AF.Exp, [psr_], [etr_], scale=0.125)

            def stageA2(tl):
                et, etr_ = tl["et"]
                c0 = tl["c0"]
                sp, spr_ = spr.next()
                tl["sp"] = (sp, spr_)
                self.act(sp[:, c0:QW], et[:, c0:QW], AF.Ln, [etr_], [spr_], bias=1.0)
                if tl["diag"]:
                    mask(sp, spr_, c0)

            def stageB(tl):
                ps, psr_ = tl["ps"]
                sp, spr_ = tl["sp"]
                c0 = tl["c0"]
                Cc = tl["Cc"]
                for (a0, a1) in halves(c0):
                    S.op("pe", lambda e, a0=a0, a1=a1: e.matmul(ps[:, a0:a1], trin, sp[:, a0:a1], start=False, stop=True,
                                                               skip_group_check=True), ["trin", spr_], [psr_])
                for (a0, a1) in halves(c0):
                    self.mm(pyy[:, a0:a1], onen, sp[:, a0:a1], True, True, ["onen", spr_], [pyyr])
                x2, x2r_ = x2r.next()
                at, atr_ = atr.next()
                tl["at"] = (at, atr_)
                self.tt("dve", x2[:, c0:QW], ps[:, c0:QW], Cc[0][:, c0:QW], ALU.add, [psr_, Cc[1]], [x2r_])
                self.tt("dve", Cc[0][:, c0:QW], pyy[:, c0:QW], Cc[0][:, c0:QW], ALU.add, [pyyr, Cc[1]], [Cc[1]])
                self.act(at[:, c0:QW], x2[:, c0:QW], AF.Exp, [x2r_], [atr_], scale=0.125)
                if tl["diag"]:
                    mask(at, atr_, c0)

            def stageC(tl):
                at, atr_ = tl["at"]
                c0, sl, kb = tl["c0"], tl["sl"], tl["kb"]
                st = tl["started"]
                for hf, (a0, a1) in [(a0 // 512, (a0, a1)) for (a0, a1) in halves(c0)]:
                    self.mm(po_[0:DH, a0:a1], va[sl][0][:, kb, :], at[:, a0:a1], not st[hf], tl["last"], [va[sl][1], atr_], [por_])
                    st[hf] = True
                if tl["last"]:
                    os_, osr_ = osr.next()
                    self.cp("act", os_, po_[0:DH, :], [por_], [osr_])
                    S.dma("sp", osr_, self.o_d[tl["b"], tl["h"], :, tl["j"] * QW:(tl["j"] + 1) * QW], os_, reads=[osr_])

            def push(tl):
                stageA1(tl)
                stageA2(tl)
                if q1:
                    t2 = q1.pop(0)
                    stageB(t2)
                    q2.append(t2)
                q1.append(tl)
                while len(q2) > 1:
                    stageC(q2.pop(0))

            heads = [(b, h) for b in range(ns) for h in range(NH)]

            def loads(i):
                b_, h_ = heads[i]
                sl_ = i % 3
                S.dma("sp", kta[sl_][1], kta[sl_][0], self.kt_d[b_, h_], writes=[kta[sl_][1]])
                S.dma("sp", qta[sl_][1], qta[sl_][0], self.qt_d[b_, h_], writes=[qta[sl_][1]])
                S.dma("sp", va[sl_][1], va[sl_][0],
                      self.v_d[b_, :, h_ * DH:(h_ + 1) * DH].rearrange("(k p) d -> p k d", p=128), writes=[va[sl_][1]])

            loads(0)
            for hi, (b, h) in enumerate(heads):
                sl = hi % 3
                if hi + 1 < len(heads):
                    loads(hi + 1)
                for j in range(NQ):
                    nkb = 8 * j + 8
                    Cc = ccr.next()
                    self.memset("pool", Cc[0], 0.0, [Cc[1]])
                    started = [False, False]
                    for n, kb in enumerate(range(nkb - 1, -1, -1)):
                        diag = kb >= 8 * j
                        c0 = (kb - 8 * j) * 128 if diag else 0
                        tl = dict(b=b, h=h, j=j, kb=kb, c0=c0, sl=sl, diag=diag, started=started,
                                  last=(kb == 0), ps=psr.next(), Cc=Cc)
                        push(tl)
            while q1:
                t2 = q1.pop(0)
                stageB(t2)
                q2.append(t2)
            while q2:
                stageC(q2.pop(0))

    def attn_out(self, l, kind, Xin, Xout):
        nc, S, ns, T = self.nc, self.S, self.nseq, self.T
        TT = 512
        nsub = TT // 128
        W = self.din
        fox = kind == "fox"
        with ExitStack() as ph:
            wo = self.sb(ph, "wo", [128, 8, D], BF16)
            for kc in range(8):
                self.load_w_bf16(wo[:, kc, :], W["fox_w_out" if fox else "sb_w_out"][0, kc * 128:(kc + 1) * 128, :], "wo", "wo")
            E = self.epi_setup(ph, l, 0)
            xt = [(self.sb(ph, "xt%d" % i, [128, nsub, D], F32), "xt%d" % i) for i in range(3)]
            oT = [(self.sb(ph, "oT%d" % i, [128, 8, TT], F32), "oT%d" % i) for i in range(2)]
            oTb = [(self.sb(ph, "oTb%d" % i, [128, 8, TT], BF16), "oTb%d" % i) for i in range(2)]
            if fox:
                dnb = [(self.sb(ph, "dnb%d" % i, [128, 8, TT], F32), "dnb%d" % i) for i in range(2)]
            tiles = [(b, t0) for b in range(ns) for t0 in range(0, T, TT)]
            n = len(tiles)

            def loads(i):
                b_, t0_ = tiles[i]
                sl_ = i % 2
                S.dma("sp", xt[i % 3][1], xt[i % 3][0], self.x_tile(Xin, b_, t0_, TT), writes=[xt[i % 3][1]])
                S.dma("sp", oT[sl_][1], oT[sl_][0],
                      self.o_d[b_].rearrange("(j hh) d t -> (hh d) j t", hh=2)[:, :, t0_:t0_ + TT], writes=[oT[sl_][1]])
                if fox:
                    dv = self.den_d[b_].rearrange("(j hh) t -> hh j t", hh=2)
                    for hh in range(2):
                        S.dma("sp", dnb[sl_][1], dnb[sl_][0][hh * DH:(hh + 1) * DH],
                              dv[hh, :, t0_:t0_ + TT].partition_broadcast(DH), writes=[dnb[sl_][1]])

            def norm(i):
                sl = i % 2
                o_, or_ = oT[sl]
                ob, obr = oTb[sl]
                if fox:
                    dn, dnr = dnb[sl]
                    S.op("dve", lambda e, dn=dn: e.reciprocal(dn, dn), [dnr], [dnr])
                    self.tt("dve", ob, o_, dn, ALU.mult, [or_, dnr], [obr])
                else:
                    self.cp("pool", ob, o_, [or_], [obr])

            def main(i):
                b, t0 = tiles[i]
                xs, xr = xt[i % 3]
                ob, obr = oTb[i % 2]
                for s in range(nsub):
                    pyt, pyr = self.py
                    for hf in range(2):
                        for j in range(8):
                            self.mm(pyt[:, hf * 512:(hf + 1) * 512], ob[:, j, s * 128:(s + 1) * 128],
                                    wo[:, j, hf * 512:(hf + 1) * 512], j == 0, j == 7, [obr, "wo"], [pyr])
                    self.epilogue(E, b, pyt, pyr, xs[:, s, :], xr, Xout[b, t0 + s * 128:t0 + (s + 1) * 128, :])

            loads(0)
            if n > 1:
                loads(1)
            norm(0)
            for i in range(n):
                if i + 2 < n:
                    loads(i + 2)
                if i + 1 < n:
                    norm(i + 1)
                main(i)


def _plan_full():
    plan = [("mod",)]
    plan += [("gmlp", 0), ("ffn", 0), ("attn", 1, "fox"), ("ffn", 1), ("attn", 2, "sb"), ("ffn", 2), ("conv", 3), ("ffn", 3)]
    return plan


_CACHE = {}


def kernel(**inputs):
    ncores = 8
    nseq = 2
    T = 4096
    if "nc" not in _CACHE:
        _CACHE["nc"] = Builder(nseq, T, _plan_full()).build()
    nc = _CACHE["nc"]
    x = np.ascontiguousarray(inputs["x"], dtype=np.float32)
    c = np.ascontiguousarray(inputs["c"], dtype=np.float32)
    in_maps = []
    for i in range(ncores):
        m = {k: np.ascontiguousarray(inputs[k], dtype=np.float32) for k in WEIGHT_SHAPES}
        m["x"] = x[i * nseq:(i + 1) * nseq]
        m["c"] = c[i * nseq:(i + 1) * nseq]
        in_maps.append(m)
    res = run_bass_kernel_spmd(nc, in_maps, core_ids=list(range(ncores)))
    return np.concatenate([r["out"] for r in res.results], axis=0)
```

```python
import numpy as np
from contextlib import ExitStack
import concourse.bass as bass
import concourse.mybir as mybir
from concourse.bass_utils import run_bass_kernel_spmd

F32 = mybir.dt.float32
BF16 = mybir.dt.bfloat16
AF = mybir.ActivationFunctionType
ALU = mybir.AluOpType

D = 1024
FH = 2816
NH = 16
DH = 64
DEPTH = 4
ALPHA = float((2.0 * DEPTH) ** 0.25)
EPS = 1e-5
CONVW = 31

ENGS = ["pe", "act", "dve", "pool", "sp"]

WEIGHT_SHAPES = {
    "mod_w": (4, 1024, 6144), "mod_b": (4, 6144), "ln1_g": (4, 1024), "ln1_b": (4, 1024),
    "ln2_g": (4, 1024), "ln2_b": (4, 1024), "ffn_w_in": (4, 1024, 5632), "ffn_w_out": (4, 2816, 1024),
    "gm_w_in": (1, 1024, 2048), "gm_b_in": (1, 2048), "gm_ln_g": (1, 1024), "gm_ln_b": (1, 1024),
    "gm_w_s": (1, 8, 128, 128), "gm_b_s": (1, 8, 128), "gm_w_out": (1, 1024, 1024),
    "fox_w_in": (1, 1024, 3088), "fox_b_f": (1, 16), "fox_w_out": (1, 1024, 1024),
    "sb_w_in": (1, 1024, 3072), "sb_w_out": (1, 1024, 1024),
    "cv_w_in": (1, 1024, 2048), "cv_b_in": (1, 2048), "cv_dw": (1, 31, 1024), "cv_dw_b": (1, 1024),
    "cv_ln_g": (1, 1024), "cv_ln_b": (1, 1024), "cv_w_out": (1, 1024, 1024), "cv_b_out": (1, 1024),
}


class Sched:
    def __init__(self, nc, same_engine_raw=True):
        self.nc = nc
        self.ops = {e: [] for e in ENGS}
        self.last_w = {}
        self.readers = {}
        self.seen = {e: {} for e in ENGS}
        self.dma_cnt = []
        self.dma_sw = []
        self.keymap = {}
        self.same_engine_raw = same_engine_raw

    def _add_wait(self, eng, waits, ev, is_raw):
        if ev is None:
            return
        if ev[0] == "eng":
            _, e2, idx = ev
            if e2 == eng and (eng == "pe" or not self.same_engine_raw):
                return
            key, val = ("eng", e2), idx
        else:
            key, val = ("dma", ev[1]), ev[2]
        if self.seen[eng].get(key, -1) >= val:
            return
        if val > waits.get(key, -1):
            waits[key] = val

    def _deps(self, eng, reads, writes):
        waits = {}
        for r in reads:
            self._add_wait(eng, waits, self.last_w.get(r), True)
        for w in writes:
            self._add_wait(eng, waits, self.last_w.get(w), False)
            for ev in self.readers.get(w, ()):
                self._add_wait(eng, waits, ev, False)
        for k, v in waits.items():
            self.seen[eng][k] = v
        return waits

    def _commit(self, ev, reads, writes):
        for r in reads:
            self.readers.setdefault(r, []).append(ev)
        for w in writes:
            self.last_w[w] = ev
            self.readers[w] = []

    def op(self, eng, fn, reads=(), writes=()):
        waits = self._deps(eng, reads, writes)
        idx = len(self.ops[eng])
        self.ops[eng].append(dict(kind="op", fn=fn, waits=waits, needed=False))
        self._commit(("eng", eng, idx), reads, writes)

    def dma(self, eng, key, out, in_, reads=(), writes=(), **kw):
        waits = self._deps(eng, reads, writes)
        key = (eng == "pool", key)
        if key not in self.keymap:
            used = {v for k, v in self.keymap.items()}
            cand = [i for i, sw in enumerate(self.dma_sw) if sw == key[0] and i not in used]
            if cand:
                self.keymap[key] = cand[0]
            else:
                self.keymap[key] = len(self.dma_cnt)
                self.dma_cnt.append(0)
                self.dma_sw.append(key[0])
        ph = self.keymap[key]
        self.dma_cnt[ph] += 1
        self.ops[eng].append(dict(kind="dma", out=out, in_=in_, kw=kw, key=ph, waits=waits))
        self._commit(("dma", ph, 16 * self.dma_cnt[ph]), reads, writes)

    def _wait_everything(self, eng):
        waits = {}
        for ev in self.last_w.values():
            self._add_wait(eng, waits, ev, False)
        for evs in self.readers.values():
            for ev in evs:
                self._add_wait(eng, waits, ev, False)
        for k, v in waits.items():
            self.seen[eng][k] = v
        self.ops[eng].append(dict(kind="waitonly", waits=waits))

    def barrier(self):
        for e in ENGS:
            self._wait_everything(e)
        self.last_w = {}
        self.readers = {}
        self.keymap = {}

    def emit(self, stack):
        nc = self.nc
        for e in ENGS:
            for o in self.ops[e]:
                for k, v in o["waits"].items():
                    if k[0] == "eng":
                        self.ops[k[1]][v]["needed"] = True
        semval = {}
        for e in ENGS:
            c = 0
            for i, o in enumerate(self.ops[e]):
                if o["kind"] == "op" and o["needed"]:
                    c += 1
                    semval[(e, i)] = c
        esem = {e: stack.enter_context(nc.semaphore("s_" + e)) for e in ENGS}
        dsem = [stack.enter_context(nc.semaphore("d%d" % i)) for i in range(len(self.dma_cnt))]
        print("[sched] ops:", {e: len(self.ops[e]) for e in ENGS}, "sem max:", {e: max([0] + [v for (ee, _), v in semval.items() if ee == e]) for e in ENGS},
              "dma sems:", len(self.dma_cnt), "max dma cnt:", max(self.dma_cnt) * 16)
        block = stack.enter_context(nc.Block())

        def run(e, eng):
            for o in self.ops[e]:
                for k, v in o["waits"].items():
                    if k[0] == "eng":
                        eng.wait_ge(esem[k[1]], semval[(k[1], v)])
                    else:
                        eng.wait_ge(dsem[k[1]], v)
                if o["kind"] == "op":
                    ins = o["fn"](eng)
                    if o["needed"]:
                        ins.then_inc(esem[e], 1)
                elif o["kind"] == "dma":
                    eng.dma_start(out=o["out"], in_=o["in_"], **o["kw"]).then_inc(dsem[o["key"]], 16)

        @block.tensor
        def _(eng):
            run("pe", eng)

        @block.scalar
        def _(eng):
            run("act", eng)

        @block.vector
        def _(eng):
            run("dve", eng)

        @block.gpsimd
        def _(eng):
            run("pool", eng)

        @block.sync
        def _(eng):
            run("sp", eng)


class Ring:
    def __init__(self, items):
        self.items = list(items)
        self.i = 0

    def next(self):
        it = self.items[self.i % len(self.items)]
        self.i += 1
        return it


class Builder:
    def __init__(self, nseq, T, plan, same_engine_raw=True):
        self.nseq, self.T, self.plan = nseq, T, plan
        self.nc = nc = bass.Bass("TRN2", target_bir_lowering=False)
        self.S = Sched(nc, same_engine_raw)
        self._n = 0
        self.din = {}
        self.din["x"] = nc.dram_tensor("x", [nseq, T, D], F32, kind="ExternalInput").ap()
        self.din["c"] = nc.dram_tensor("c", [nseq, D], F32, kind="ExternalInput").ap()
        for k, shp in WEIGHT_SHAPES.items():
            self.din[k] = nc.dram_tensor(k, list(shp), F32, kind="ExternalInput").ap()
        self.dout = nc.dram_tensor("out", [nseq, T, D], F32, kind="ExternalOutput").ap()
        self.xa = nc.dram_tensor("xa_s", [nseq, T, D], F32).ap()
        self.xb = nc.dram_tensor("xb_s", [nseq, T, D], F32).ap()
        self.modrow = nc.dram_tensor("modrow_s", [DEPTH, nseq, 6 * D], F32).ap()
        self.qt_d = nc.dram_tensor("qt_s", [nseq, NH, DH, T], BF16).ap()
        self.kt_d = nc.dram_tensor("kt_s", [nseq, NH, DH, T], BF16).ap()
        self.v_d = nc.dram_tensor("v_s", [nseq, T, D], BF16).ap()
        self.fq_d = nc.dram_tensor("fq_s", [nseq, NH, 3, T], BF16).ap()
        self.fk_d = nc.dram_tensor("fk_s", [nseq, NH, 3, T], BF16).ap()
        self.o_d = nc.dram_tensor("o_s", [nseq, NH, DH, T], F32).ap()
        self.den_d = nc.dram_tensor("den_s", [nseq, NH, T], F32).ap()

    def sb(self, ph, name, shape, dt):
        self._n += 1
        return ph.enter_context(self.nc.sbuf_tensor("%s_%d" % (name, self._n), list(shape), dt))[:]

    def mm(self, out, lhsT, rhs, start, stop, reads, writes):
        self.S.op("pe", lambda e: e.matmul(out, lhsT, rhs, start=start, stop=stop), reads, writes)

    def tr(self, out, in_, ident, reads, writes):
        self.S.op("pe", lambda e: e.transpose(out, in_, ident), reads, writes)

    def act(self, out, in_, func, reads, writes, bias=None, scale=None, eng="act"):
        kw = {}
        if bias is not None:
            kw["bias"] = bias
        if scale is not None:
            kw["scale"] = scale
        self.S.op(eng, lambda e: e.activation(out, in_, func, **kw), reads, writes)

    def tt(self, eng, out, in0, in1, op, reads, writes):
        self.S.op(eng, lambda e: e.tensor_tensor(out, in0, in1, op), reads, writes)

    def ts(self, eng, out, in0, s1, s2, op0, op1, reads, writes):
        if op1 is None:
            self.S.op(eng, lambda e: e.tensor_scalar(out, in0, s1, None, op0), reads, writes)
        else:
            self.S.op(eng, lambda e: e.tensor_scalar(out, in0, s1, s2, op0, op1), reads, writes)

    def stt(self, out, in0, scalar, in1, op0, op1, reads, writes):
        self.S.op("dve", lambda e: e.scalar_tensor_tensor(out, in0, scalar, in1, op0, op1), reads, writes)

    def cp(self, eng, out, in_, reads, writes):
        if eng == "act":
            self.S.op("act", lambda e: e.copy(out, in_), reads, writes)
        else:
            self.S.op(eng, lambda e: e.tensor_copy(out, in_), reads, writes)

    def memset(self, eng, ap, val, writes):
        self.S.op(eng, lambda e: e.memset(ap, val), (), writes)

    def load_w_bf16(self, dst, src, key, res, maxcols=2048):
        n = dst.shape[-1]
        c0 = 0
        while c0 < n:
            c1 = min(n, c0 + maxcols)
            self.S.dma("pool", key, dst[:, c0:c1], src[:, c0:c1], writes=[res])
            c0 = c1

    def build(self):
        nc, S = self.nc, self.S
        with ExitStack() as st:
            self.ident = st.enter_context(nc.sbuf_tensor("ident", [128, 128], F32))[:]
            self.identb = st.enter_context(nc.sbuf_tensor("identb", [128, 128], BF16))[:]
            self.modT = st.enter_context(nc.sbuf_tensor("modT", [128, DEPTH, self.nseq, 48], F32))[:]
            self.mhalf = st.enter_context(nc.sbuf_tensor("mhalf", [128, 1], F32))[:]
            self.pd = []
            self.pb = []
            for i in range(4):
                t = st.enter_context(nc.psum_tensor("pd%d" % i, [128, 1024], F32))[:]
                self.pd.append((t, "pb%d" % (2 * i)))
                self.pb += [(t[:, 0:512], "pb%d" % (2 * i)), (t[:, 512:1024], "pb%d" % (2 * i + 1))]
            self.py = self.pd[2]
            self.py1 = self.pd[3]
            self.memset("pool", self.ident, 1.0, ["ident"])
            S.op("pool", lambda e: e.affine_select(self.ident, self.ident, [[1, 128]], ALU.is_equal, 0.0,
                                                    base=0, channel_multiplier=-1), ["ident"], ["ident"])
            self.cp("dve", self.identb, self.ident, ["ident"], ["identb"])
            self.memset("pool", self.mhalf, -0.5, ["mhalf"])
            cur = self.din["x"]
            nxt = [self.xa, self.xb]
            nph = len([p for p in self.plan if p[0] != "mod"])
            k = 0
            for p in self.plan:
                if p[0] == "mod":
                    self.phase_mod()
                else:
                    k += 1
                    dst = self.dout if k == nph else nxt[k % 2]
                    if p[0] == "ffn":
                        self.phase_ffn(p[1], cur, dst)
                    elif p[0] == "gmlp":
                        self.phase_gmlp(p[1], cur, dst)
                    elif p[0] == "conv":
                        self.phase_conv(p[1], cur, dst)
                    elif p[0] == "attn":
                        self.phase_attn(p[1], p[2], cur, dst)
                    cur = dst
                S.barrier()
            S.emit(st)
        return nc

    def phase_mod(self):
        nc, S, ns = self.nc, self.S, self.nseq
        with ExitStack() as ph:
            cs = self.sb(ph, "cs", [ns, D], F32)
            ca = self.sb(ph, "ca", [ns, D], F32)
            caT = self.sb(ph, "caT", [128, 8, ns], F32)
            CB = self.sb(ph, "CB", [128, ns, 8, 128], F32)
            wst = [self.sb(ph, "wst%d" % i, [128, 8, 512], F32) for i in range(2)]
            mbb = [self.sb(ph, "mbb%d" % i, [128, 512], F32) for i in range(2)]
            mrow = [self.sb(ph, "mrow%d" % i, [128, 512], F32) for i in range(2)]
            stg = [self.sb(ph, "stg%d" % i, [48, 128], F32) for i in range(2)]
            S.dma("sp", "cs", cs, self.din["c"][:, :], writes=["cs"])
            self.act(ca, cs, AF.Silu, ["cs"], ["ca"])
            pbt, pbr = self.pb[0]
            for kc in range(8):
                self.tr(pbt[:, kc * ns:(kc + 1) * ns], ca[:, kc * 128:(kc + 1) * 128], self.ident[0:ns, 0:ns],
                        ["ca", "ident"], [pbr])
            self.cp("dve", caT, pbt[:, 0:8 * ns].rearrange("p (k b) -> p k b", b=ns), [pbr], ["caT"])
            for b in range(ns):
                self.cp("dve", CB[:, b], caT[:, :, b:b + 1].to_broadcast([128, 8, 128]), ["caT"], ["CB"])
            mi = 0
            blocks = [(l, blk) for l in range(DEPTH) for blk in range(12)]

            def wload(i):
                l_, blk_ = blocks[i]
                sl_ = i % 2
                cols_ = slice(blk_ * 512, (blk_ + 1) * 512)
                S.dma("sp", "wst%d" % sl_, wst[sl_],
                      self.din["mod_w"][l_, :, cols_].rearrange("(k p) f -> p k f", p=128), writes=["wst%d" % sl_])
                S.dma("sp", "mbb%d" % sl_, mbb[sl_], self.din["mod_b"][l_, cols_].partition_broadcast(128),
                      writes=["mbb%d" % sl_])

            wload(0)
            for it, (l, blk) in enumerate(blocks):
                if True:
                    sl = it % 2
                    cols = slice(blk * 512, (blk + 1) * 512)
                    if it + 1 < len(blocks):
                        wload(it + 1)
                    for b in range(ns):
                        pt, pr = self.pb[1 + (mi % 2)]
                        ms = mi % 2
                        mi += 1
                        for kc in range(8):
                            self.mm(pt, CB[:, b, kc, :], wst[sl][:, kc, :], kc == 0, kc == 7,
                                    ["CB", "wst%d" % sl], [pr])
                        self.tt("dve", mrow[ms], pt, mbb[sl], ALU.add, [pr, "mbb%d" % sl], ["mrow%d" % ms])
                        S.dma("sp", "mrow%d" % ms, self.modrow[l, b:b + 1, cols], mrow[ms][0:1, :],
                              reads=["mrow%d" % ms], writes=["modrow_%d_%d_%d" % (l, b, blk)])
            si = 0
            for l in range(DEPTH):
                for b in range(ns):
                    sl = si % 2
                    si += 1
                    S.dma("sp", "stg%d" % sl, stg[sl], self.modrow[l, b, :].rearrange("(c p) -> c p", p=128),
                          reads=["modrow_%d_%d_%d" % (l, b, blk) for blk in range(12)], writes=["stg%d" % sl])
                    pt, pr = self.pb[3 + sl]
                    self.tr(pt[:, 0:48], stg[sl], self.ident[0:48, 0:48], ["stg%d" % sl, "ident"], [pr])
                    self.cp("dve", self.modT[:, l, b, :], pt[:, 0:48], [pr], ["modT"])
            for c0 in (8, 32):
                self.ts("dve", self.modT[:, :, :, c0:c0 + 8], self.modT[:, :, :, c0:c0 + 8], 1.0, None, ALU.add, None,
                        ["modT"], ["modT"])

    def load_bcast(self, ph, name, src_row):
        n = src_row.shape[-1]
        t = self.sb(ph, name, [128, n], F32)
        self._n += 1
        res = "%s_%d" % (name, self._n)
        self.S.dma("sp", res, t, src_row.partition_broadcast(128), writes=[res])
        return t, res

    def load_pp(self, ph, name, src2d, pbi=0):
        n = src2d.shape[0]
        stg = self.sb(ph, name + "s", [n, 128], F32)
        t = self.sb(ph, name, [128, n], F32)
        self._n += 1
        res = "%s_%d" % (name, self._n)
        self.S.dma("sp", res + "s", stg, src2d, writes=[res + "s"])
        pt, pr = self.pb[pbi]
        self.tr(pt[:, 0:n], stg, self.ident[0:n, 0:n], [res + "s", "ident"], [pr])
        self.cp("dve", t, pt[:, 0:n], [pr], [res])
        return t, res

    def epi_setup(self, ph, l, which, single_gb=False):
        gcol = 2048 if which == 0 else 5120
        GB = []
        if single_gb:
            t = self.sb(ph, "GBs", [128, D], F32)
            GB = [(t, "GBs")] * self.nseq
        else:
            for b in range(self.nseq):
                t, r = self.load_bcast(ph, "GB%d" % b, self.modrow[l, b, gcol:gcol + D])
                self.ts("dve", t, t, 1.0, None, ALU.add, None, [r], [r])
                GB.append((t, r))
        lng = self.load_bcast(ph, "lng", self.din["ln1_g" if which == 0 else "ln2_g"][l, :])
        lnb = self.load_bcast(ph, "lnb", self.din["ln1_b" if which == 0 else "ln2_b"][l, :])
        eb = []
        for i in range(2):
            eb.append(dict(
                buf=self.sb(ph, "ebuf%d" % i, [128, D], F32), st=self.sb(ph, "est%d" % i, [128, 12], F32),
                mv=self.sb(ph, "emv%d" % i, [128, 2], F32), sm=self.sb(ph, "esm%d" % i, [128, 4], F32),
                res="ebuf%d_%d" % (i, self._n)))
        return dict(GB=GB, lng=lng, lnb=lnb, eb=eb, i=0, single=single_gb, cur=None, l=l, gcol=gcol)

    def epi_select(self, E, b):
        if not E["single"] or E["cur"] == b:
            return
        E["cur"] = b
        t, r = E["GB"][b]
        self.S.dma("sp", r, t, self.modrow[E["l"], b, E["gcol"]:E["gcol"] + D].partition_broadcast(128), writes=[r])
        self.ts("dve", t, t, 1.0, None, ALU.add, None, [r], [r])

    def epilogue(self, E, b, y, yres, x, xres, out_rows, ybias=None):
        S = self.S
        e = E["eb"][E["i"] % 2]
        E["i"] += 1
        buf, r = e["buf"], e["res"]
        GBt, GBr = E["GB"][b]
        if ybias is not None:
            self.tt("dve", buf, y, ybias[0], ALU.add, [yres, ybias[1]], [r])
            self.tt("dve", buf, buf, GBt, ALU.mult, [r, GBr], [r])
        else:
            self.tt("dve", buf, y, GBt, ALU.mult, [yres, GBr], [r])
        self.stt(buf, x, ALPHA, buf, ALU.mult, ALU.add, [xres, r], [r])
        self.ln_rows(buf, r, e)
        self.tt("dve", buf, buf, E["lng"][0], ALU.mult, [r, E["lng"][1]], [r])
        self.tt("pool", buf, buf, E["lnb"][0], ALU.add, [r, E["lnb"][1]], [r])
        S.dma("sp", r, out_rows, buf, reads=[r])

    def ln_rows(self, buf, r, e, out=None, eng_norm="act"):
        S = self.S
        st, mv, sm = e["st"], e["mv"], e["sm"]
        rs = r + "s"
        S.op("dve", lambda en: en.bn_stats(st[:, 0:6], buf[:, 0:512]), [r], [rs])
        S.op("dve", lambda en: en.bn_stats(st[:, 6:12], buf[:, 512:1024]), [r], [rs])
        S.op("dve", lambda en: en.bn_aggr(mv, st), [rs], [rs])
        self.ts("dve", sm[:, 0:1], mv[:, 1:2], EPS, None, ALU.add, None, [rs], [rs])
        self.tt("pool", sm[:, 1:2], sm[:, 0:1], self.mhalf, ALU.pow, [rs, "mhalf"], [rs])
        self.ts("dve", sm[:, 2:3], mv[:, 0:1], sm[:, 1:2], -1.0, ALU.mult, ALU.mult, [rs], [rs])
        o = buf if out is None else out[0]
        wr = [r] if out is None else [out[1]]
        self.act(o, buf, AF.Identity, [r, rs], wr, bias=sm[:, 2:3], scale=sm[:, 1:2])

    def xT_mod(self, xt, xres, nsub, hT, hres, l, b, which, tpr):
        sh0 = 0 if which == 0 else 24
        sc0 = 8 if which == 0 else 32
        for kc in range(8):
            pt, pr = tpr.next()
            for s in range(nsub):
                self.tr(pt[:, s * 128:(s + 1) * 128], xt[:, s, kc * 128:(kc + 1) * 128], self.ident,
                        [xres, "ident"], [pr])
            self.act(hT[:, kc, 0:nsub * 128], pt[:, 0:nsub * 128], AF.Identity, [pr, "modT"], [hres],
                     bias=self.modT[:, l, b, sh0 + kc:sh0 + kc + 1], scale=self.modT[:, l, b, sc0 + kc:sc0 + kc + 1])

    def x_tile(self, X, b, t0, tt):
        return X[b, t0:t0 + tt, :].rearrange("(s p) d -> p s d", p=128)

    def phase_ffn(self, l, Xin, Xout):
        nc, S, ns, T = self.nc, self.S, self.nseq, self.T
        TT = 256
        nsub = TT // 128
        NF = FH // 128
        with ExitStack() as ph:
            w1 = self.sb(ph, "w1", [128, 8, 2 * FH], BF16)
            w2 = self.sb(ph, "w2", [128, NF, D], BF16)
            CBW = FH // 2
            for cb in (0, 2, 1, 3):
                for kc in range(8):
                    S.dma("pool", "w1b%d" % cb, w1[:, kc, cb * CBW:(cb + 1) * CBW],
                          self.din["ffn_w_in"][l, kc * 128:(kc + 1) * 128, cb * CBW:(cb + 1) * CBW], writes=["w1b%d" % cb])
            for fc in range(NF):
                self.load_w_bf16(w2[:, fc, :], self.din["ffn_w_out"][l, fc * 128:(fc + 1) * 128, :], "w2", "w2")
            E = self.epi_setup(ph, l, 1)
            xt = [(self.sb(ph, "xt%d" % i, [128, nsub, D], F32), "xt%d" % i) for i in range(2)]
            hT = [(self.sb(ph, "hT%d" % i, [128, 8, TT], BF16), "hT%d" % i) for i in range(2)]
            aT = (self.sb(ph, "aT", [128, NF, TT], BF16), "aT")
            sg = [(self.sb(ph, "sg%d" % i, [128, TT], F32), "sg%d" % i) for i in range(2)]
            tpr = Ring(self.pb[0:2])
            gur = Ring([(self.pb[2], self.pb[3]), (self.pb[6], self.pb[7])])
            tiles = [(b, t0) for b in range(ns) for t0 in range(0, T, TT)]
            n = len(tiles)

            def loads(i):
                b_, t0_ = tiles[i]
                S.dma("sp", xt[i % 2][1], xt[i % 2][0], self.x_tile(Xin, b_, t0_, TT), writes=[xt[i % 2][1]])

            def prologue(i):
                self.xT_mod(xt[i % 2][0], xt[i % 2][1], nsub, hT[i % 2][0], hT[i % 2][1], l, tiles[i][0], 1, tpr)

            def inproj(i):
                hs, hr = hT[i % 2]
                for fc in range(NF):
                    (pg, pgr), (pu, pur) = gur.next()
                    for kc in range(8):
                        self.mm(pg[:, 0:TT], w1[:, kc, fc * 128:(fc + 1) * 128], hs[:, kc, :], kc == 0, kc == 7,
                                ["w1b%d" % (fc // 11), hr], [pgr])
                    for kc in range(8):
                        self.mm(pu[:, 0:TT], w1[:, kc, FH + fc * 128:FH + (fc + 1) * 128], hs[:, kc, :], kc == 0,
                                kc == 7, ["w1b%d" % (2 + fc // 11), hr], [pur])
                    sgt, sgr = sg[fc % 2]
                    self.act(sgt, pg[:, 0:TT], AF.Silu, [pgr], [sgr])
                    self.tt("dve", aT[0][:, fc, :], sgt, pu[:, 0:TT], ALU.mult, [sgr, pur], [aT[1]])

            def outproj(i):
                b, t0 = tiles[i]
                xs, xr = xt[i % 2]
                for s in range(nsub):
                    pyt, pyr = self.py
                    for hf in range(2):
                        for fc in range(NF):
                            self.mm(pyt[:, hf * 512:(hf + 1) * 512], aT[0][:, fc, s * 128:(s + 1) * 128],
                                    w2[:, fc, hf * 512:(hf + 1) * 512], fc == 0, fc == NF - 1, [aT[1], "w2"], [pyr])
                    self.epilogue(E, b, pyt, pyr, xs[:, s, :], xr, Xout[b, t0 + s * 128:t0 + (s + 1) * 128, :])

            loads(0)
            if n > 1:
                loads(1)
            print("[sbuf] ffn remaining", nc.sbuf_bytes_remaining)
            prologue(0)
            for i in range(n):
                inproj(i)
                if i + 1 < n:
                    prologue(i + 1)
                outproj(i)
                if i + 2 < n:
                    loads(i + 2)

    def phase_gmlp(self, l, Xin, Xout):
        nc, S, ns, T = self.nc, self.S, self.nseq, self.T
        TT = 512
        nsub = TT // 128
        W = self.din
        with ExitStack() as ph:
            wi = self.sb(ph, "wi", [128, 8, 2 * D], BF16)
            wo = self.sb(ph, "wo", [128, 8, D], BF16)
            for kc in range(8):
                self.load_w_bf16(wi[:, kc, :], W["gm_w_in"][0, kc * 128:(kc + 1) * 128, :], "wi", "wi")
                self.load_w_bf16(wo[:, kc, :], W["gm_w_out"][0, kc * 128:(kc + 1) * 128, :], "wo", "wo")
            wsl = self.sb(ph, "wsl", [128, 8, 128], F32)
            wsm = self.sb(ph, "wsm", [128, 8, 128], F32)
            wmT = self.sb(ph, "wmT", [128, 8, 128], BF16)
            S.dma("sp", "wsl", wsl, W["gm_w_s"][0].rearrange("g t s -> t g s"), writes=["wsl"])
            for g in range(8):
                pt, pr = self.pb[g % 2]
                self.tr(pt[:, 0:128], wsl[:, g, :], self.ident, ["wsl", "ident"], [pr])
                self.cp("dve", wsm[:, g, :], pt[:, 0:128], [pr], ["wsm"])
                S.op("pool", lambda e, g=g: e.affine_select(wmT[:, g, :], wsm[:, g, :], [[1, 128]], ALU.is_ge, 0.0,
                                                             base=0, channel_multiplier=-1), ["wsm"], ["wmT"])
            binu, binu_r = self.load_pp(ph, "binu", W["gm_b_in"][0, 0:D].rearrange("(c p) -> c p", p=128), 2)
            binv, binv_r = self.load_bcast(ph, "binv", W["gm_b_in"][0, D:2 * D])
            glg, glg_r = self.load_bcast(ph, "glg", W["gm_ln_g"][0, :])
            glb, glb_r = self.load_bcast(ph, "glb", W["gm_ln_b"][0, :])
            bsb, bsb_r = self.load_bcast(ph, "bsb", W["gm_b_s"][0].rearrange("g t -> (g t)"))
            bsb3 = bsb.rearrange("p (g t) -> p g t", t=128)
            E = self.epi_setup(ph, l, 0, single_gb=True)
            xt = [(self.sb(ph, "xt%d" % i, [128, nsub, D], F32), "xt%d" % i) for i in range(3)]
            hT = [(self.sb(ph, "hT%d" % i, [128, 8, TT], BF16), "hT%d" % i) for i in range(2)]
            vz = [dict(buf=self.sb(ph, "vz%d" % i, [128, D], F32), st=self.sb(ph, "vst%d" % i, [128, 12], F32),
                       mv=self.sb(ph, "vmv%d" % i, [128, 2], F32), sm=self.sb(ph, "vsm%d" % i, [128, 4], F32),
                       res="vz%d" % i) for i in range(2)]
            vn = [[(self.sb(ph, "vn%d_%d" % (k, i), [128, D], BF16), "vn%d_%d" % (k, i)) for i in range(nsub)] for k in range(2)]
            uT = [(self.sb(ph, "uT%d" % i, [128, TT], F32), "uT%d" % i) for i in range(2)]
            tmp = [(self.sb(ph, "tmp%d" % i, [128, TT], F32), "tmp%d" % i) for i in range(2)]
            yT = (self.sb(ph, "yT", [128, 8, TT], BF16), "yT")
            print("[sbuf] gmlp remaining", nc.sbuf_bytes_remaining)
            tpr = Ring(self.pb[0:2])
            tiles = [(b, t0) for b in range(ns) for t0 in range(0, T, TT)]
            cnt = dict(vi=0)

            def loads(i):
                b_, t0_ = tiles[i]
                S.dma("sp", xt[i % 3][1], xt[i % 3][0], self.x_tile(Xin, b_, t0_, TT), writes=[xt[i % 3][1]])

            def stageA(i):
                b, t0 = tiles[i]
                xs, xr = xt[i % 3]
                hs, hr = hT[i % 2]
                self.xT_mod(xs, xr, nsub, hs, hr, l, b, 0, tpr)
                for s in range(nsub):
                    pv, pvr = self.py1
                    for hf in range(2):
                        for kc in range(8):
                            self.mm(pv[:, hf * 512:(hf + 1) * 512], hs[:, kc, s * 128:(s + 1) * 128],
                                    wi[:, kc, D + hf * 512:D + (hf + 1) * 512], kc == 0, kc == 7, [hr, "wi"], [pvr])
                    z = vz[cnt["vi"] % 2]
                    cnt["vi"] += 1
                    vt, vr = vn[i % 2][s]
                    self.tt("dve", z["buf"], pv, binv, ALU.add, [pvr, binv_r], [z["res"]])
                    self.act(z["buf"], z["buf"], AF.Gelu, [z["res"]], [z["res"]])
                    self.ln_rows(z["buf"], z["res"], z)
                    self.tt("dve", z["buf"], z["buf"], glg, ALU.mult, [z["res"], glg_r], [z["res"]])
                    self.tt("pool", vt, z["buf"], glb, ALU.add, [z["res"], glb_r], [vr])

            def stageB(i):
                b, t0 = tiles[i]
                xs, xr = xt[i % 3]
                hs, hr = hT[i % 2]
                self.epi_select(E, b)
                for g in range(8):
                    pu, pur = self.pb[2]
                    psv, psvr = self.pb[3]
                    for kc in range(8):
                        self.mm(pu, wi[:, kc, g * 128:(g + 1) * 128], hs[:, kc, :], kc == 0, kc == 7, ["wi", hr], [pur])
                    ut, utr = uT[g % 2]
                    self.act(ut, pu, AF.Gelu, [pur, binu_r], [utr], bias=binu[:, g:g + 1])
                    for s in range(nsub):
                        vt, vr = vn[i % 2][s]
                        self.mm(psv[:, s * 128:(s + 1) * 128], vt[:, g * 128:(g + 1) * 128], wmT[:, g, :],
                                True, True, [vr, "wmT"], [psvr])
                    tm, tmr = tmp[g % 2]
                    self.tt("dve", tm.rearrange("p (s t) -> p s t", t=128), psv.rearrange("p (s t) -> p s t", t=128),
                            bsb3[:, g:g + 1, :].to_broadcast([128, nsub, 128]), ALU.add, [psvr, bsb_r], [tmr])
                    self.tt("dve" if g % 2 == 0 else "pool", yT[0][:, g, :], tm, ut, ALU.mult, [tmr, utr], [yT[1]])
                for s in range(nsub):
                    pyt, pyr = self.py
                    for hf in range(2):
                        for g in range(8):
                            self.mm(pyt[:, hf * 512:(hf + 1) * 512], yT[0][:, g, s * 128:(s + 1) * 128],
                                    wo[:, g, hf * 512:(hf + 1) * 512], g == 0, g == 7, [yT[1], "wo"], [pyr])
                    self.epilogue(E, b, pyt, pyr, xs[:, s, :], xr, Xout[b, t0 + s * 128:t0 + (s + 1) * 128, :])

            n = len(tiles)
            loads(0)
            if n > 1:
                loads(1)
            for i in range(n):
                stageA(i)
                if i >= 1:
                    stageB(i - 1)
                if i + 2 < n:
                    loads(i + 2)
            stageB(n - 1)

    def phase_conv(self, l, Xin, Xout):
        nc, S, ns, T = self.nc, self.S, self.nseq, self.T
        TT = 256
        nsub = TT // 128
        W = self.din
        HW = CONVW - 1
        with ExitStack() as ph:
            wi = self.sb(ph, "wi", [128, 8, 2 * D], BF16)
            wo = self.sb(ph, "wo", [128, 8, D], BF16)
            for kc in range(8):
                self.load_w_bf16(wi[:, kc, :], W["cv_w_in"][0, kc * 128:(kc + 1) * 128, :], "wi", "wi")
                self.load_w_bf16(wo[:, kc, :], W["cv_w_out"][0, kc * 128:(kc + 1) * 128, :], "wo", "wo")
            bia, bia_r = self.load_pp(ph, "bia", W["cv_b_in"][0, :].rearrange("(c p) -> c p", p=128), 2)
            dwr = W["cv_dw"][0].rearrange("i (c p) -> (i c) p", p=128)
            dwa, dwa_r = self.load_pp(ph, "dwa", dwr[0:124, :], 2)
            dwb_, dwb_r = self.load_pp(ph, "dwb", dwr[124:248, :], 3)
            dwbias, dwbias_r = self.load_pp(ph, "dwbias", W["cv_dw_b"][0, :].rearrange("(c p) -> c p", p=128), 2)
            clg, clg_r = self.load_pp(ph, "clg", W["cv_ln_g"][0, :].rearrange("(c p) -> c p", p=128), 3)
            clb, clb_r = self.load_pp(ph, "clb", W["cv_ln_b"][0, :].rearrange("(c p) -> c p", p=128), 2)
            bo32 = self.sb(ph, "bo32", [1, D], F32)
            bohi = self.sb(ph, "bohi", [1, D], BF16)
            bolo = self.sb(ph, "bolo", [1, D], BF16)
            one1 = self.sb(ph, "one1", [1, 128], BF16)
            S.dma("sp", "bo32", bo32, W["cv_b_out"][0:1, :], writes=["bo"])
            self.cp("dve", bohi, bo32, ["bo"], ["bohi"])
            self.tt("dve", bo32, bo32, bohi, ALU.subtract, ["bo", "bohi"], ["bo"])
            self.cp("dve", bolo, bo32, ["bo"], ["bolo"])
            self.memset("pool", one1, 1.0, ["one1"])
            diag = self.sb(ph, "diag", [128, CONVW * 8, 128], BF16)
            for half, (dt_, dr_) in enumerate(((dwa, dwa_r), (dwb_, dwb_r))):
                self.tt("dve", diag[:, half * 124:(half + 1) * 124, :],
                        self.ident[:, None, :].to_broadcast([128, 124, 128]),
                        dt_[:, :, None].to_broadcast([128, 124, 128]), ALU.mult, ["ident", dr_], ["diag"])
            onesf = self.sb(ph, "onesf", [128, 128], F32)
            self.memset("pool", onesf, 1.0, ["onesf"])
            E = self.epi_setup(ph, l, 0, single_gb=True)
            xt = [(self.sb(ph, "xt%d" % i, [128, nsub, D], F32), "xt%d" % i) for i in range(3)]
            hT = (self.sb(ph, "hT", [128, 8, TT], BF16), "hT")
            ybuf = (self.sb(ph, "ybuf", [128, 8, HW + TT], BF16), "ybuf")
            sgm = [(self.sb(ph, "sgm%d" % i, [128, TT], F32), "sgm%d" % i) for i in range(2)]
            zT = [(self.sb(ph, "zT%d" % i, [128, 8, TT], F32), "zT%d" % i) for i in range(2)]
            zq = [(self.sb(ph, "zq%d" % i, [128, TT], F32), "zq%d" % i) for i in range(2)]
            mean_t = [(self.sb(ph, "mean_t%d" % i, [128, TT], F32), "mean_t%d" % i) for i in range(2)]
            rstd_t = [(self.sb(ph, "rstd_t%d" % i, [128, TT], F32), "rstd_t%d" % i) for i in range(2)]
            sT = (self.sb(ph, "sT", [128, 8, TT], BF16), "sT")
            print("[sbuf] conv remaining", nc.sbuf_bytes_remaining)
            tpr = Ring(self.pb[0:1])
            tiles = [(b, t0) for b in range(ns) for t0 in range(0, T, TT)]

            def loads(i):
                b_, t0_ = tiles[i]
                S.dma("sp", xt[i % 3][1], xt[i % 3][0], self.x_tile(Xin, b_, t0_, TT), writes=[xt[i % 3][1]])

            def stageA(i):
                b, t0 = tiles[i]
                xs, xr = xt[i % 3]
                hs, hr = hT
                self.xT_mod(xs, xr, nsub, hs, hr, l, b, 0, tpr)
                if t0 == 0:
                    self.memset("pool", ybuf[0][:, :, 0:HW], 0.0, [ybuf[1]])
                else:
                    self.cp("pool", ybuf[0][:, :, 0:HW], ybuf[0][:, :, TT:TT + HW], [ybuf[1]], [ybuf[1]])
                for kc in range(8):
                    pa, par = self.pb[2]
                    pg, pgr = self.pb[3]
                    for k in range(8):
                        self.mm(pa[:, 0:TT], wi[:, k, kc * 128:(kc + 1) * 128], hs[:, k, :], k == 0, k == 7, ["wi", hr], [par])
                    for k in range(8):
                        self.mm(pg[:, 0:TT], wi[:, k, D + kc * 128:D + (kc + 1) * 128], hs[:, k, :], k == 0, k == 7,
                                ["wi", hr], [pgr])
                    sg_, sgr = sgm[kc % 2]
                    self.act(sg_, pg[:, 0:TT], AF.Sigmoid, [pgr, bia_r], [sgr], bias=bia[:, 8 + kc:9 + kc])
                    self.stt(ybuf[0][:, kc, HW:HW + TT], pa[:, 0:TT], bia[:, kc:kc + 1], sg_, ALU.add, ALU.mult,
                             [par, sgr, bia_r], [ybuf[1]])
                z_, zr = zT[i % 2]
                p1, p1r = self.pb[7]
                p2, p2r = self.pb[1]
                for kc in range(8):
                    pz, pzr = self.pb[6]
                    for t in range(CONVW):
                        self.mm(pz[:, 0:TT], diag[:, t * 8 + kc, :], ybuf[0][:, kc, t:t + TT], t == 0, t == CONVW - 1,
                                ["diag", ybuf[1]], [pzr])
                    self.act(z_[:, kc, :], pz[:, 0:TT], AF.Identity, [pzr, dwbias_r], [zr], bias=dwbias[:, kc:kc + 1])
                    q_, qr = zq[kc % 2]
                    S.op("act", lambda e, q_=q_, kc=kc: e.activation(q_, z_[:, kc, :], AF.Square), [zr], [qr])
                    self.mm(p1[:, 0:TT], onesf, z_[:, kc, :], kc == 0, kc == 7, ["onesf", zr], [p1r])
                    self.mm(p2[:, 0:TT], onesf, q_, kc == 0, kc == 7, ["onesf", qr], [p2r])
                m_, mr = mean_t[i % 2]
                r_, rr = rstd_t[i % 2]
                q, qr_ = r_, rr
                self.ts("dve", m_, p1[:, 0:TT], 1.0 / D, None, ALU.mult, None, [p1r], [mr])
                self.tt("dve", q, m_, m_, ALU.mult, [mr], [qr_])
                self.stt(q, p2[:, 0:TT], 1.0 / D, q, ALU.mult, ALU.subtract, [p2r, qr_], [qr_])
                self.ts("dve", q, q, EPS, None, ALU.add, None, [qr_], [qr_])
                self.act(r_, q, AF.Sqrt, [qr_], [rr])
                S.op("dve", lambda e, r_=r_: e.reciprocal(r_, r_), [rr], [rr])

            def stageB(i):
                b, t0 = tiles[i]
                xs, xr = xt[i % 3]
                z_, zr = zT[i % 2]
                m_, mr = mean_t[i % 2]
                r_, rr = rstd_t[i % 2]
                self.epi_select(E, b)
                mb = m_[:, None, :].to_broadcast([128, 8, TT])
                rb = r_[:, None, :].to_broadcast([128, 8, TT])
                self.tt("dve", z_, z_, mb, ALU.subtract, [zr, mr], [zr])
                self.tt("dve", z_, z_, rb, ALU.mult, [zr, rr], [zr])
                for kc in range(8):
                    self.act(sT[0][:, kc, :], z_[:, kc, :], AF.Silu, [zr, clg_r, clb_r], [sT[1]], bias=clb[:, kc:kc + 1],
                             scale=clg[:, kc:kc + 1])
                for s in range(nsub):
                    pyt, pyr = self.py
                    for hf in range(2):
                        cs = slice(hf * 512, (hf + 1) * 512)
                        for kc in range(8):
                            self.mm(pyt[:, cs], sT[0][:, kc, s * 128:(s + 1) * 128], wo[:, kc, cs], kc == 0, False,
                                    [sT[1], "wo"], [pyr])
                        self.mm(pyt[:, cs], one1, bohi[:, cs], False, False, ["one1", "bohi"], [pyr])
                        self.mm(pyt[:, cs], one1, bolo[:, cs], False, True, ["one1", "bolo"], [pyr])
                    self.epilogue(E, b, pyt, pyr, xs[:, s, :], xr, Xout[b, t0 + s * 128:t0 + (s + 1) * 128, :])

            n = len(tiles)
            loads(0)
            if n > 1:
                loads(1)
            for i in range(n):
                stageA(i)
                if i >= 1:
                    stageB(i - 1)
                if i + 2 < n:
                    loads(i + 2)
            stageB(n - 1)

    def phase_attn(self, l, kind, Xin, Xout):
        self.attn_proj(l, kind, Xin)
        self.S.barrier()
        if kind == "fox":
            self.attn_core_fox()
        else:
            self.attn_core_sb()
        self.S.barrier()
        self.attn_out(l, kind, Xin, Xout)

    def attn_proj(self, l, kind, Xin):
        nc, S, ns, T = self.nc, self.S, self.nseq, self.T
        TT = 512
        nsub = TT // 128
        W = self.din
        fox = kind == "fox"
        NC = 3 * D + (NH if fox else 0)
        wname = "fox_w_in" if fox else "sb_w_in"
        with ExitStack() as ph:
            wi = self.sb(ph, "wi", [128, 8, NC], BF16)
            for kc in range(8):
                self.load_w_bf16(wi[:, kc, :], W[wname][0, kc * 128:(kc + 1) * 128, :], "wi", "wi", 1024 + (NH if fox else 0))
            xt = [(self.sb(ph, "xt%d" % i, [128, nsub, D], F32), "xt%d" % i) for i in range(2)]
            hT = [(self.sb(ph, "hT%d" % i, [128, 8, TT], BF16), "hT%d" % i) for i in range(2)]
            qst = [(self.sb(ph, "qst%d" % i, [128, 8, TT], BF16), "qst%d" % i) for i in range(2)]
            kst = [(self.sb(ph, "kst%d" % i, [128, 8, TT], BF16), "kst%d" % i) for i in range(2)]
            vst = [(self.sb(ph, "vst%d" % i, [128, nsub, D], BF16), "vst%d" % i) for i in range(2)]
            if fox:
                nbf = self.sb(ph, "nbf", [NH, 1], F32)
                S.dma("sp", "nbf", nbf, W["fox_b_f"][0, :].rearrange("(h o) -> h o", o=1), writes=["nbf"])
                self.ts("dve", nbf, nbf, -1.0, None, ALU.mult, None, ["nbf"], ["nbf"])
                ones16 = self.sb(ph, "ones16", [NH, TT], F32)
                self.memset("dve", ones16, 1.0, ["ones16"])
                ef = self.sb(ph, "ef", [NH, TT], F32)
                cum = [(self.sb(ph, "cum%d" % i, [NH, TT], F32), "cum%d" % i) for i in range(2)]
                s8 = self.sb(ph, "s8", [NH, TT], F32)
                r1 = self.sb(ph, "r1", [NH, TT], F32)
                fk3 = [(self.sb(ph, "fk3%d" % i, [NH, 3, TT], BF16), "fk3%d" % i) for i in range(2)]
                fq3 = [(self.sb(ph, "fq3%d" % i, [NH, 3, TT], BF16), "fq3%d" % i) for i in range(2)]
            tpr = Ring(self.pb[0:2])
            qkr = Ring(self.pb[2:4])
            ev = 0
            tiles = [(b, t0) for b in range(ns) for t0 in range(0, T, TT)]

            def loads(i):
                b_, t0_ = tiles[i]
                S.dma("sp", xt[i % 2][1], xt[i % 2][0], self.x_tile(Xin, b_, t0_, TT), writes=[xt[i % 2][1]])

            loads(0)
            for it, (b, t0) in enumerate(tiles):
                if True:
                    if it + 1 < len(tiles):
                        loads(it + 1)
                    sl = it % 2
                    xs, xr = xt[sl]
                    hs, hr = hT[sl]
                    self.xT_mod(xs, xr, nsub, hs, hr, l, b, 0, tpr)
                    for (stg, c0, dst) in ((qst[sl], 0, self.qt_d), (kst[sl], D, self.kt_d)):
                        for j in range(8):
                            pt, pr = qkr.next()
                            for kc in range(8):
                                self.mm(pt, wi[:, kc, c0 + j * 128:c0 + (j + 1) * 128], hs[:, kc, :], kc == 0, kc == 7,
                                        ["wi", hr], [pr])
                            self.cp("act" if ev % 2 == 0 else "dve", stg[0][:, j, :], pt, [pr], [stg[1]])
                            ev += 1
                        S.dma("sp", stg[1], dst[b].rearrange("(j hh) d t -> (hh d) j t", hh=2)[:, :, t0:t0 + TT], stg[0],
                              reads=[stg[1]])
                    for s in range(nsub):
                        pv, pvr = self.py
                        for hf in range(2):
                            for kc in range(8):
                                self.mm(pv[:, hf * 512:(hf + 1) * 512], hs[:, kc, s * 128:(s + 1) * 128],
                                        wi[:, kc, 2 * D + hf * 512:2 * D + (hf + 1) * 512], kc == 0, kc == 7, [hr, "wi"], [pvr])
                        self.cp("act" if ev % 2 == 0 else "dve", vst[sl][0][:, s, :], pv, [pvr], [vst[sl][1]])
                        ev += 1
                    S.dma("sp", vst[sl][1], self.v_d[b, t0:t0 + TT, :].rearrange("(s p) d -> p s d", p=128), vst[sl][0],
                          reads=[vst[sl][1]])
                    if fox:
                        pf, pfr = self.pb[6]
                        for kc in range(8):
                            self.mm(pf[0:NH, :], wi[:, kc, 3 * D:3 * D + NH], hs[:, kc, :], kc == 0, kc == 7, ["wi", hr], [pfr])
                        self.act(ef, pf[0:NH, :], AF.Exp, [pfr, "nbf"], ["ef"], bias=nbf, scale=-1.0)
                        self.act(ef, ef, AF.Ln, ["ef"], ["ef"], bias=1.0)
                        cm, cmr = cum[sl]
                        pc, pcr = cum[1 - sl]
                        init = 0.0 if t0 == 0 else pc[:, TT - 1:TT]
                        S.op("dve", lambda e, cm=cm, init=init: e.tensor_tensor_scan(cm, ones16, ef, init, ALU.mult, ALU.add),
                             ["ones16", "ef"] + ([] if t0 == 0 else [pcr]), [cmr])
                        fk, fkr = fk3[sl]
                        fq, fqr = fq3[sl]
                        self.ts("dve", s8, cm, 8.0, None, ALU.mult, None, [cmr], ["s8"])
                        self.cp("dve", fk[:, 0, :], s8, ["s8"], [fkr])
                        self.tt("dve", r1, s8, fk[:, 0, :], ALU.subtract, ["s8", fkr], ["r1"])
                        self.cp("dve", fk[:, 1, :], r1, ["r1"], [fkr])
                        self.tt("dve", r1, r1, fk[:, 1, :], ALU.subtract, ["r1", fkr], ["r1"])
                        self.cp("dve", fk[:, 2, :], r1, ["r1"], [fkr])
                        self.ts("dve", fq, fk, -1.0, None, ALU.mult, None, [fkr], [fqr])
                        S.dma("sp", fkr, self.fk_d[b, :, :, t0:t0 + TT], fk, reads=[fkr])
                        S.dma("sp", fqr, self.fq_d[b, :, :, t0:t0 + TT], fq, reads=[fqr])

    def attn_core_fox(self):
        nc, S, ns, T = self.nc, self.S, self.nseq, self.T
        NKB = T // 128
        QW = 1024
        NQ = T // QW
        KA = DH + 6
        LA = 2
        with ExitStack() as ph:
            kta = [(self.sb(ph, "kta%d" % i, [KA, T], BF16), "kta%d" % i) for i in range(3)]
            qta = [(self.sb(ph, "qta%d" % i, [KA, T], BF16), "qta%d" % i) for i in range(3)]
            va = [(self.sb(ph, "va%d" % i, [128, NKB, DH + 1], BF16), "va%d" % i) for i in range(3)]
            PT = [(self.sb(ph, "PT%d" % i, [128, QW], BF16), "PT%d" % i) for i in range(4)]
            ost = [(self.sb(ph, "ost%d" % i, [DH + 1, QW], F32), "ost%d" % i) for i in range(2)]
            for i in range(3):
                self.memset("dve", kta[i][0][DH:KA, :], 1.0, [kta[i][1]])
                self.memset("dve", qta[i][0][DH:KA, :], 1.0, [qta[i][1]])
                self.memset("pool", va[i][0][:, :, DH:DH + 1], 1.0, [va[i][1]])
            psr = Ring(self.pd[0:2])
            por = Ring(self.pd[2:4])
            ptr = Ring(PT)
            osr = Ring(ost)
            pend = []

            def halves(c0):
                return [(max(c0, hf * 512), (hf + 1) * 512) for hf in range(2) if max(c0, hf * 512) < (hf + 1) * 512]

            def stage2(tl):
                (b, h, j, kb, c0, nkb, sl, po, por_, pt, ptr_) = tl
                for (a0, a1) in halves(c0):
                    self.mm(po[0:DH + 1, a0:a1], va[sl][0][:, kb, :], pt[:, a0:a1], kb == 0, kb == nkb - 1,
                            [va[sl][1], ptr_], [por_])
                if kb == nkb - 1:
                    os_, osr_ = osr.next()
                    self.cp("dve", os_, po[0:DH + 1, :], [por_], [osr_])
                    S.dma("sp", osr_, self.o_d[b, h, :, j * QW:(j + 1) * QW], os_[0:DH, :], reads=[osr_])
                    S.dma("sp", osr_, self.den_d[b, h:h + 1, j * QW:(j + 1) * QW], os_[DH:DH + 1, :], reads=[osr_])

            heads = [(b, h) for b in range(ns) for h in range(NH)]

            def loads(i):
                b_, h_ = heads[i]
                sl_ = i % 3
                S.dma("sp", kta[sl_][1], kta[sl_][0][0:DH, :], self.kt_d[b_, h_], writes=[kta[sl_][1]])
                S.dma("sp", kta[sl_][1], kta[sl_][0][DH:DH + 3, :], self.fk_d[b_, h_], writes=[kta[sl_][1]])
                S.dma("sp", qta[sl_][1], qta[sl_][0][0:DH, :], self.qt_d[b_, h_], writes=[qta[sl_][1]])
                S.dma("sp", qta[sl_][1], qta[sl_][0][DH + 3:DH + 6, :], self.fq_d[b_, h_], writes=[qta[sl_][1]])
                S.dma("sp", va[sl_][1], va[sl_][0][:, :, 0:DH],
                      self.v_d[b_, :, h_ * DH:(h_ + 1) * DH].rearrange("(k p) d -> p k d", p=128), writes=[va[sl_][1]])

            loads(0)
            for hi, (b, h) in enumerate(heads):
                sl = hi % 3
                if hi + 1 < len(heads):
                    loads(hi + 1)
                for j in range(NQ):
                    nkb = 8 * j + 8
                    po, por_ = por.next()
                    for kb in range(nkb):
                        c0 = (kb - 8 * j) * 128 if kb >= 8 * j else 0
                        ps, psr_ = psr.next()
                        pt, ptr_ = ptr.next()
                        for (a0, a1) in halves(c0):
                            self.mm(ps[:, a0:a1], kta[sl][0][:, kb * 128:(kb + 1) * 128],
                                    qta[sl][0][:, j * QW + a0:j * QW + a1], True, True, [kta[sl][1], qta[sl][1]], [psr_])
                        self.act(pt[:, c0:QW], ps[:, c0:QW], AF.Exp, [psr_], [ptr_], scale=0.125)
                        if kb >= 8 * j:
                            S.op("pool", lambda e, pt=pt, c0=c0: e.affine_select(
                                pt[:, c0:c0 + 128], pt[:, c0:c0 + 128], [[1, 128]], ALU.is_ge, 0.0, base=0,
                                channel_multiplier=-1), [ptr_], [ptr_])
                        pend.append((b, h, j, kb, c0, nkb, sl, po, por_, pt, ptr_))
                        if len(pend) > LA:
                            stage2(pend.pop(0))
            while pend:
                stage2(pend.pop(0))

    def attn_core_sb(self):
        nc, S, ns, T = self.nc, self.S, self.nseq, self.T
        NKB = T // 128
        QW = 1024
        NQ = T // QW
        with ExitStack() as ph:
            kta = [(self.sb(ph, "kta%d" % i, [DH, T], BF16), "kta%d" % i) for i in range(3)]
            qta = [(self.sb(ph, "qta%d" % i, [DH, T], BF16), "qta%d" % i) for i in range(3)]
            va = [(self.sb(ph, "va%d" % i, [128, NKB, DH], BF16), "va%d" % i) for i in range(3)]
            Et = [(self.sb(ph, "Et%d" % i, [128, QW], F32), "Et%d" % i) for i in range(2)]
            SPt = [(self.sb(ph, "SPt%d" % i, [128, QW], BF16), "SPt%d" % i) for i in range(3)]
            X2 = [(self.sb(ph, "X2%d" % i, [128, QW], F32), "X2%d" % i) for i in range(2)]
            At = [(self.sb(ph, "At%d" % i, [128, QW], BF16), "At%d" % i) for i in range(3)]
            Ccs = [(self.sb(ph, "Cc%d" % i, [128, QW], F32), "Cc%d" % i) for i in range(2)]
            ccr = Ring(Ccs)
            ost = [(self.sb(ph, "ost%d" % i, [DH, QW], F32), "ost%d" % i) for i in range(2)]
            trin = self.sb(ph, "trin", [128, 128], BF16)
            onen = self.sb(ph, "onen", [128, 128], BF16)
            self.memset("pool", trin, -8.0, ["trin"])
            S.op("pool", lambda e: e.affine_select(trin, trin, [[-1, 128]], ALU.is_ge, 0.0, base=0, channel_multiplier=1),
                 ["trin"], ["trin"])
            self.memset("pool", onen, -8.0, ["onen"])
            psr = Ring(self.pd[0:2])
            pyy, pyyr = self.pd[2]
            po_, por_ = self.pd[3]
            etr, spr, x2r, atr, osr = Ring(Et), Ring(SPt), Ring(X2), Ring(At), Ring(ost)
            q1, q2 = [], []

            def halves(c0):
                return [(max(c0, hf * 512), (hf + 1) * 512) for hf in range(2) if max(c0, hf * 512) < (hf + 1) * 512]

            def mask(tile_ap, res, c0):
                S.op("pool", lambda e: e.affine_select(tile_ap[:, c0:c0 + 128], tile_ap[:, c0:c0 + 128], [[1, 128]], ALU.is_gt,
                                                        0.0, base=0, channel_multiplier=-1), [res], [res])

            def stageA1(tl):
                ps, psr_ = tl["ps"]
                c0, sl, kb, j = tl["c0"], tl["sl"], tl["kb"], tl["j"]
                for (a0, a1) in halves(c0):
                    self.mm(ps[:, a0:a1], kta[sl][0][:, kb * 128:(kb + 1) * 128], qta[sl][0][:, j * QW + a0:j * QW + a1],
                            True, True, [kta[sl][1], qta[sl][1]], [psr_])
                et, etr_ = etr.next()
                tl["et"] = (et, etr_)
                self.act(et[:, c0:QW], ps[:, c0:QW], AF.Exp, [psr_], [etr_], scale=0.125)

            def stageA2(tl):
                et, etr_ = tl["et"]
                c0 = tl["c0"]
                sp, spr_ = spr.next()
                tl["sp"] = (sp, spr_)
                self.act(sp[:, c0:QW], et[:, c0:QW], AF.Ln, [etr_], [spr_], bias=1.0)
                if tl["diag"]:
                    mask(sp, spr_, c0)

            def stageB(tl):
                ps, psr_ = tl["ps"]
                sp, spr_ = tl["sp"]
                c0 = tl["c0"]
                Cc = tl["Cc"]
                for (a0, a1) in halves(c0):
                    S.op("pe", lambda e, a0=a0, a1=a1: e.matmul(ps[:, a0:a1], trin, sp[:, a0:a1], start=False, stop=True,
                                                               skip_group_check=True), ["trin", spr_], [psr_])
                for (a0, a1) in halves(c0):
                    self.mm(pyy[:, a0:a1], onen, sp[:, a0:a1], True, True, ["onen", spr_], [pyyr])
                x2, x2r_ = x2r.next()
                at, atr_ = atr.next()
                tl["at"] = (at, atr_)
                self.tt("dve", x2[:, c0:QW], ps[:, c0:QW], Cc[0][:, c0:QW], ALU.add, [psr_, Cc[1]], [x2r_])
                self.tt("dve", Cc[0][:, c0:QW], pyy[:, c0:QW], Cc[0][:, c0:QW], ALU.add, [pyyr, Cc[1]], [Cc[1]])
                self.act(at[:, c0:QW], x2[:, c0:QW], AF.Exp, [x2r_], [atr_], scale=0.125)
                if tl["diag"]:
                    mask(at, atr_, c0)

            def stageC(tl):
                at, atr_ = tl["at"]
                c0, sl, kb = tl["c0"], tl["sl"], tl["kb"]
                st = tl["started"]
                for hf, (a0, a1) in [(a0 // 512, (a0, a1)) for (a0, a1) in halves(c0)]:
                    self.mm(po_[0:DH, a0:a1], va[sl][0][:, kb, :], at[:, a0:a1], not st[hf], tl["last"], [va[sl][1], atr_], [por_])
                    st[hf] = True
                if tl["last"]:
                    os_, osr_ = osr.next()
                    self.cp("act", os_, po_[0:DH, :], [por_], [osr_])
                    S.dma("sp", osr_, self.o_d[tl["b"], tl["h"], :, tl["j"] * QW:(tl["j"] + 1) * QW], os_, reads=[osr_])

            def push(tl):
                stageA1(tl)
                stageA2(tl)
                if q1:
                    t2 = q1.pop(0)
                    stageB(t2)
                    q2.append(t2)
                q1.append(tl)
                while len(q2) > 1:
                    stageC(q2.pop(0))

            heads = [(b, h) for b in range(ns) for h in range(NH)]

            def loads(i):
                b_, h_ = heads[i]
                sl_ = i % 3
                S.dma("sp", kta[sl_][1], kta[sl_][0], self.kt_d[b_, h_], writes=[kta[sl_][1]])
                S.dma("sp", qta[sl_][1], qta[sl_][0], self.qt_d[b_, h_], writes=[qta[sl_][1]])
                S.dma("sp", va[sl_][1], va[sl_][0],
                      self.v_d[b_, :, h_ * DH:(h_ + 1) * DH].rearrange("(k p) d -> p k d", p=128), writes=[va[sl_][1]])

            loads(0)
            for hi, (b, h) in enumerate(heads):
                sl = hi % 3
                if hi + 1 < len(heads):
                    loads(hi + 1)
                for j in range(NQ):
                    nkb = 8 * j + 8
                    Cc = ccr.next()
                    self.memset("pool", Cc[0], 0.0, [Cc[1]])
                    started = [False, False]
                    for n, kb in enumerate(range(nkb - 1, -1, -1)):
                        diag = kb >= 8 * j
                        c0 = (kb - 8 * j) * 128 if diag else 0
                        tl = dict(b=b, h=h, j=j, kb=kb, c0=c0, sl=sl, diag=diag, started=started,
                                  last=(kb == 0), ps=psr.next(), Cc=Cc)
                        push(tl)
            while q1:
                t2 = q1.pop(0)
                stageB(t2)
                q2.append(t2)
            while q2:
                stageC(q2.pop(0))

    def attn_out(self, l, kind, Xin, Xout):
        nc, S, ns, T = self.nc, self.S, self.nseq, self.T
        TT = 512
        nsub = TT // 128
        W = self.din
        fox = kind == "fox"
        with ExitStack() as ph:
            wo = self.sb(ph, "wo", [128, 8, D], BF16)
            for kc in range(8):
                self.load_w_bf16(wo[:, kc, :], W["fox_w_out" if fox else "sb_w_out"][0, kc * 128:(kc + 1) * 128, :], "wo", "wo")
            E = self.epi_setup(ph, l, 0)
            xt = [(self.sb(ph, "xt%d" % i, [128, nsub, D], F32), "xt%d" % i) for i in range(3)]
            oT = [(self.sb(ph, "oT%d" % i, [128, 8, TT], F32), "oT%d" % i) for i in range(2)]
            oTb = [(self.sb(ph, "oTb%d" % i, [128, 8, TT], BF16), "oTb%d" % i) for i in range(2)]
            if fox:
                dnb = [(self.sb(ph, "dnb%d" % i, [128, 8, TT], F32), "dnb%d" % i) for i in range(2)]
            tiles = [(b, t0) for b in range(ns) for t0 in range(0, T, TT)]
            n = len(tiles)

            def loads(i):
                b_, t0_ = tiles[i]
                sl_ = i % 2
                S.dma("sp", xt[i % 3][1], xt[i % 3][0], self.x_tile(Xin, b_, t0_, TT), writes=[xt[i % 3][1]])
                S.dma("sp", oT[sl_][1], oT[sl_][0],
                      self.o_d[b_].rearrange("(j hh) d t -> (hh d) j t", hh=2)[:, :, t0_:t0_ + TT], writes=[oT[sl_][1]])
                if fox:
                    dv = self.den_d[b_].rearrange("(j hh) t -> hh j t", hh=2)
                    for hh in range(2):
                        S.dma("sp", dnb[sl_][1], dnb[sl_][0][hh * DH:(hh + 1) * DH],
                              dv[hh, :, t0_:t0_ + TT].partition_broadcast(DH), writes=[dnb[sl_][1]])

            def norm(i):
                sl = i % 2
                o_, or_ = oT[sl]
                ob, obr = oTb[sl]
                if fox:
                    dn, dnr = dnb[sl]
                    S.op("dve", lambda e, dn=dn: e.reciprocal(dn, dn), [dnr], [dnr])
                    self.tt("dve", ob, o_, dn, ALU.mult, [or_, dnr], [obr])
                else:
                    self.cp("pool", ob, o_, [or_], [obr])

            def main(i):
                b, t0 = tiles[i]
                xs, xr = xt[i % 3]
                ob, obr = oTb[i % 2]
                for s in range(nsub):
                    pyt, pyr = self.py
                    for hf in range(2):
                        for j in range(8):
                            self.mm(pyt[:, hf * 512:(hf + 1) * 512], ob[:, j, s * 128:(s + 1) * 128],
                                    wo[:, j, hf * 512:(hf + 1) * 512], j == 0, j == 7, [obr, "wo"], [pyr])
                    self.epilogue(E, b, pyt, pyr, xs[:, s, :], xr, Xout[b, t0 + s * 128:t0 + (s + 1) * 128, :])

            loads(0)
            if n > 1:
                loads(1)
            norm(0)
            for i in range(n):
                if i + 2 < n:
                    loads(i + 2)
                if i + 1 < n:
                    norm(i + 1)
                main(i)


def _plan_full():
    plan = [("mod",)]
    plan += [("gmlp", 0), ("ffn", 0), ("attn", 1, "fox"), ("ffn", 1), ("attn", 2, "sb"), ("ffn", 2), ("conv", 3), ("ffn", 3)]
    return plan


_CACHE = {}


def kernel(**inputs):
    ncores = 8
    nseq = 2
    T = 4096
    if "nc" not in _CACHE:
        _CACHE["nc"] = Builder(nseq, T, _plan_full()).build()
    nc = _CACHE["nc"]
    x = np.ascontiguousarray(inputs["x"], dtype=np.float32)
    c = np.ascontiguousarray(inputs["c"], dtype=np.float32)
    in_maps = []
    for i in range(ncores):
        m = {k: np.ascontiguousarray(inputs[k], dtype=np.float32) for k in WEIGHT_SHAPES}
        m["x"] = x[i * nseq:(i + 1) * nseq]
        m["c"] = c[i * nseq:(i + 1) * nseq]
        in_maps.append(m)
    res = run_bass_kernel_spmd(nc, in_maps, core_ids=list(range(ncores)))
    return np.concatenate([r["out"] for r in res.results], axis=0)
```
